# Optimizing a Trainium2 kernel written in Bass

```python
import jax
import jax.numpy as jnp
from jax import lax
import numpy as np

D_MODEL = 1024
BATCH = 4
SEQ = 8192
DEPTH = 2

N_GROUPS = 5
HEAD_DIM = 64
GROUP_HEADS = 4
GROUP_W = GROUP_HEADS * HEAD_DIM
MIX_W = N_GROUPS * GROUP_W
N_MEM = 256
CHUNK = 128
CONV_K = 4
RWKV_DECAY_RANK = 16
RWKV_A_RANK = 16
EPS = 1e-6

A_COLS = (GROUP_W, GROUP_W, GROUP_W)
B_COLS = (GROUP_W, GROUP_W, GROUP_W, GROUP_HEADS, GROUP_W)
C_COLS = (2 * GROUP_W, GROUP_W, GROUP_HEADS, GROUP_HEADS, GROUP_W, GROUP_W)
D_COLS = (GROUP_W, RWKV_DECAY_RANK, GROUP_W, GROUP_W, RWKV_A_RANK, GROUP_W)
M_COLS = (GROUP_W, GROUP_W)
GROUP_COLS = (sum(A_COLS), sum(B_COLS), sum(C_COLS), sum(D_COLS), sum(M_COLS))
D_W = sum(D_COLS)
IN_COLS = sum(GROUP_COLS)

kernel_name = "hybrid_parallel_groups_gmlp_fox_mlstm_rwkv7_mem"


def rms_norm(x, g):
    xf = x.astype(jnp.float32)
    y = xf * lax.rsqrt(jnp.mean(xf * xf, axis=-1, keepdims=True) + EPS)
    return (y * g.astype(jnp.float32)).astype(x.dtype)


def split_cols(x, widths):
    return jnp.split(x, np.cumsum(widths)[:-1].tolist(), axis=-1)


def to_heads(t):
    return t.reshape(t.shape[:-1] + (GROUP_HEADS, HEAD_DIM))


def causal_dwconv(x, w, b):
    y = lax.conv_general_dilated(
        x, w[:, None, :].astype(x.dtype), (1,), [(CONV_K - 1, 0)],
        dimension_numbers=("NWC", "WIO", "NWC"), feature_group_count=x.shape[-1])
    return y + b


def token_shift(x, mu):
    prev = jnp.pad(x, ((0, 0), (1, 0), (0, 0)))[:, :-1]
    return x + mu * (prev - x)


def spatial_gating(u, v, norm_g, w_s, b_s):
    bsz, seq, _ = u.shape
    nc = seq // CHUNK
    u = jax.nn.gelu(u)
    v = rms_norm(to_heads(jax.nn.gelu(v)), norm_g.reshape(GROUP_HEADS, HEAD_DIM))
    v = v.reshape(bsz, nc, CHUNK, GROUP_HEADS, HEAD_DIM)
    w = jnp.where(jnp.tril(jnp.ones((CHUNK, CHUNK), bool)), w_s, jnp.zeros_like(w_s))
    mixed = jnp.einsum("gts,bcsgd->bctgd", w, v) + b_s.T[:, :, None]
    return u * mixed.reshape(bsz, seq, GROUP_W)


def forgetting_attention(q, k, v, f_logit, q_g, k_g):
    bsz, seq, _ = q.shape
    nb = seq // CHUNK
    q = rms_norm(to_heads(q), q_g)
    k = rms_norm(to_heads(k), k_g)
    v = to_heads(v)
    big_f = jnp.cumsum(jax.nn.log_sigmoid(f_logit.astype(jnp.float32)), axis=1)
    big_f = big_f.transpose(0, 2, 1)
    q_blocks = q.reshape(bsz, nb, CHUNK, GROUP_HEADS, HEAD_DIM).swapaxes(0, 1)
    f_blocks = big_f.reshape(bsz, GROUP_HEADS, nb, CHUNK).transpose(2, 0, 1, 3)
    k_pos = jnp.arange(seq)
    scale = HEAD_DIM ** -0.5

    def one_block(args):
        qb, fb, start = args
        s = jnp.einsum("bthd,bshd->bhts", qb, k).astype(jnp.float32) * scale
        s = s + fb[..., :, None] - big_f[..., None, :]
        q_pos = start + jnp.arange(CHUNK)
        s = jnp.where(k_pos[None, :] <= q_pos[:, None], s, -jnp.inf)
        p = jax.nn.softmax(s, axis=-1).astype(v.dtype)
        return jnp.einsum("bhts,bshd->bthd", p, v)

    out = lax.map(one_block, (q_blocks, f_blocks, jnp.arange(nb) * CHUNK))
    return out.swapaxes(0, 1).reshape(bsz, seq, GROUP_W)


def mlstm(qk, v, i_logit, f_logit, o_logit, conv_w, conv_b, out_g):
    f32 = jnp.float32
    bsz, seq, _ = v.shape
    nc = seq // CHUNK
    q, k = jnp.split(jax.nn.silu(causal_dwconv(qk, conv_w, conv_b)), 2, axis=-1)
    q = to_heads(q).astype(f32)
    k = to_heads(k).astype(f32) * HEAD_DIM ** -0.5
    vh = to_heads(v).astype(f32)
    log_i = i_logit.astype(f32)
    log_f = jax.nn.log_sigmoid(f_logit.astype(f32))

    def chunks(t):
        return t.reshape((bsz, nc, CHUNK) + t.shape[2:]).swapaxes(0, 1)

    causal = jnp.tril(jnp.ones((CHUNK, CHUNK), bool))

    def step(carry, inp):
        c_state, n_state, m_state = carry
        qc, kc, vc, ic, fc = inp
        b = jnp.cumsum(fc, axis=1).swapaxes(1, 2)
        ic = ic.swapaxes(1, 2)
        g = b[..., -1]
        log_d = jnp.where(causal, b[..., :, None] - b[..., None, :] + ic[..., None, :], -jnp.inf)
        inter = b + m_state[..., None]
        m_t = jnp.maximum(jnp.max(log_d, axis=-1), inter)
        s = jnp.einsum("bthd,bshd->bhts", qc, kc) * jnp.exp(log_d - m_t[..., None])
        w_inter = jnp.exp(inter - m_t)
        num = (jnp.einsum("bhts,bshd->bhtd", s, vc)
               + w_inter[..., None] * jnp.einsum("bthk,bhkd->bhtd", qc, c_state))
        den = jnp.sum(s, axis=-1) + w_inter * jnp.einsum("bthk,bhk->bht", qc, n_state)
        h = num / jnp.maximum(jnp.abs(den), jnp.exp(-m_t))[..., None]
        log_w = g[..., None] - b + ic
        m_new = jnp.maximum(g + m_state, jnp.max(log_w, axis=-1))
        w_new = jnp.exp(log_w - m_new[..., None])
        carry_decay = jnp.exp(g + m_state - m_new)
        c_state = carry_decay[..., None, None] * c_state + jnp.einsum("bhs,bshk,bshd->bhkd", w_new, kc, vc)
        n_state = carry_decay[..., None] * n_state + jnp.einsum("bhs,bshk->bhk", w_new, kc)
        return (c_state, n_state, m_new), h.swapaxes(1, 2)

    init = (jnp.zeros((bsz, GROUP_HEADS, HEAD_DIM, HEAD_DIM), f32),
            jnp.zeros((bsz, GROUP_HEADS, HEAD_DIM), f32),
            jnp.zeros((bsz, GROUP_HEADS), f32))
    _, h = lax.scan(step, init, (chunks(q), chunks(k), chunks(vh), chunks(log_i), chunks(log_f)))
    h = h.swapaxes(0, 1).reshape(bsz, seq, GROUP_HEADS, HEAD_DIM)
    h = jax.nn.sigmoid(to_heads(o_logit).astype(f32)) * h
    return rms_norm(h, out_g.reshape(GROUP_HEADS, HEAD_DIM)).reshape(bsz, seq, GROUP_W)


def rwkv7_time_mix(r, w_lo, k, v, a_lo, w0, w2, a0, a2, k_k, k_a, r_k, ln_g):
    f32 = jnp.float32
    bsz, seq, _ = r.shape
    r, w_lo, k, v, a_lo = (t.astype(f32) for t in (r, w_lo, k, v, a_lo))
    w_log = -jax.nn.softplus(-(w0 + jnp.tanh(w_lo) @ w2)) - 0.5
    decay = jnp.exp(-jnp.exp(w_log))
    a = jax.nn.sigmoid(a0 + a_lo @ a2)
    kk = to_heads(k * k_k)
    kk = kk / jnp.maximum(jnp.linalg.norm(kk, axis=-1, keepdims=True), 1e-12)
    k = k * (1 + (a - 1) * k_a)
    rh, dh, kh, vh, ah = (to_heads(t) for t in (r, decay, k, v, a))

    def step(state, inp):
        r_t, w_t, k_t, v_t, kk_t, a_t = inp
        sa = jnp.einsum("bhvk,bhk->bhv", state, -kk_t)
        state = (state * w_t[:, :, None, :] + sa[..., None] * (kk_t * a_t)[:, :, None, :]
                 + v_t[..., None] * k_t[:, :, None, :])
        return state, jnp.einsum("bhvk,bhk->bhv", state, r_t)

    init = jnp.zeros((bsz, GROUP_HEADS, HEAD_DIM, HEAD_DIM), f32)
    _, y = lax.scan(step, init, tuple(t.swapaxes(0, 1) for t in (rh, dh, kh, vh, kk, ah)))
    y = rms_norm(y.swapaxes(0, 1), ln_g.reshape(GROUP_HEADS, HEAD_DIM))
    y = y + jnp.sum(rh * kh * r_k, axis=-1, keepdims=True) * vh
    return y.reshape(bsz, seq, GROUP_W)


def memory_attention(q, mem_n, w_kv, q_g, k_g):
    bsz, seq, _ = q.shape
    k, v = jnp.split(mem_n @ w_kv, 2, axis=-1)
    q = rms_norm(to_heads(q), q_g)
    k = rms_norm(to_heads(k), k_g)
    v = to_heads(v)
    s = jnp.einsum("bthd,bmhd->bhtm", q, k).astype(jnp.float32) * HEAD_DIM ** -0.5
    p = jax.nn.softmax(s, axis=-1).astype(v.dtype)
    return jnp.einsum("bhtm,bmhd->bthd", p, v).reshape(bsz, seq, GROUP_W)


def hybrid_layer(x, mem, norm_g, w_in, w_out, sgu_norm_g, sgu_w, sgu_b,
                 fox_q_g, fox_k_g, fox_f_b,
                 mlstm_conv_w, mlstm_conv_b, mlstm_i_b, mlstm_f_b, mlstm_out_g,
                 rwkv_mu, rwkv_w0, rwkv_w2, rwkv_a0, rwkv_a2, rwkv_k_k, rwkv_k_a, rwkv_r_k, rwkv_ln_g,
                 mem_norm_g, mem_w_kv, mem_q_g, mem_k_g):
    h = rms_norm(x, norm_g)
    proj = h @ w_in
    pa, pb, pc, pd, pm = split_cols(proj, GROUP_COLS)

    a_u, a_v, a_z = split_cols(pa, A_COLS)
    ya = spatial_gating(a_u, a_v, sgu_norm_g, sgu_w, sgu_b)

    b_q, b_k, b_v, b_f, b_z = split_cols(pb, B_COLS)
    yb = forgetting_attention(b_q, b_k, b_v, b_f + fox_f_b, fox_q_g, fox_k_g)

    c_qk, c_v, c_i, c_f, c_o, c_z = split_cols(pc, C_COLS)
    yc = mlstm(c_qk, c_v, c_i + mlstm_i_b, c_f + mlstm_f_b, c_o, mlstm_conv_w, mlstm_conv_b, mlstm_out_g)

    d_r, d_wlo, d_k, d_v, d_alo, d_z = split_cols(token_shift(pd, rwkv_mu), D_COLS)
    yd = rwkv7_time_mix(d_r, d_wlo, d_k, d_v, d_alo, rwkv_w0, rwkv_w2, rwkv_a0, rwkv_a2,
                        rwkv_k_k, rwkv_k_a, rwkv_r_k, rwkv_ln_g)

    m_q, m_z = split_cols(pm, M_COLS)
    ym = memory_attention(m_q, rms_norm(mem, mem_norm_g), mem_w_kv, mem_q_g, mem_k_g)

    branches = [(ya, a_z), (yb, b_z), (yc, c_z), (yd, d_z), (ym, m_z)]
    mixed = jnp.concatenate([y.astype(x.dtype) * jax.nn.silu(z) for y, z in branches], axis=-1)
    return x + mixed @ w_out


def setup_inputs(seed: int = 0) -> dict:
    key = jax.random.key(seed)
    ks = iter(jax.random.split(key, 32))
    f32 = jnp.float32

    def nrm(shape, scale):
        return jax.random.normal(next(ks), shape, f32) * scale

    def uni(shape, lo, hi):
        return jax.random.uniform(next(ks), shape, f32, lo, hi)

    def gain(shape):
        return 1.0 + nrm(shape, 0.1)

    L = DEPTH
    return {
        "x": nrm((BATCH, SEQ, D_MODEL), 1.0),
        "mem": nrm((BATCH, N_MEM, D_MODEL), 1.0),
        "norm_g": gain((L, D_MODEL)),
        "w_in": nrm((L, D_MODEL, IN_COLS), D_MODEL ** -0.5),
        "w_out": nrm((L, MIX_W, D_MODEL), MIX_W ** -0.5),
        "sgu_norm_g": gain((L, GROUP_W)),
        "sgu_w": nrm((L, GROUP_HEADS, CHUNK, CHUNK), 0.5 * CHUNK ** -0.5),
        "sgu_b": gain((L, GROUP_HEADS, CHUNK)),
        "fox_q_g": gain((L, HEAD_DIM)),
        "fox_k_g": gain((L, HEAD_DIM)),
        "fox_f_b": uni((L, GROUP_HEADS), 1.0, 5.0),
        "mlstm_conv_w": nrm((L, CONV_K, 2 * GROUP_W), CONV_K ** -0.5),
        "mlstm_conv_b": nrm((L, 2 * GROUP_W), 0.02),
        "mlstm_i_b": nrm((L, GROUP_HEADS), 0.1),
        "mlstm_f_b": uni((L, GROUP_HEADS), 3.0, 6.0),
        "mlstm_out_g": gain((L, GROUP_W)),
        "rwkv_mu": uni((L, D_W), 0.0, 1.0),
        "rwkv_w0": uni((L, GROUP_W), -4.0, 0.0),
        "rwkv_w2": nrm((L, RWKV_DECAY_RANK, GROUP_W), 0.5 * RWKV_DECAY_RANK ** -0.5),
        "rwkv_a0": nrm((L, GROUP_W), 0.1),
        "rwkv_a2": nrm((L, RWKV_A_RANK, GROUP_W), 0.5 * RWKV_A_RANK ** -0.5),
        "rwkv_k_k": 0.85 + nrm((L, GROUP_W), 0.05),
        "rwkv_k_a": gain((L, GROUP_W)),
        "rwkv_r_k": nrm((L, GROUP_HEADS, HEAD_DIM), 0.1),
        "rwkv_ln_g": gain((L, GROUP_W)),
        "mem_norm_g": gain((L, D_MODEL)),
        "mem_w_kv": nrm((L, D_MODEL, 2 * GROUP_W), D_MODEL ** -0.5),
        "mem_q_g": gain((L, HEAD_DIM)),
        "mem_k_g": gain((L, HEAD_DIM)),
    }


def reference(x, mem, norm_g, w_in, w_out, sgu_norm_g, sgu_w, sgu_b,
              fox_q_g, fox_k_g, fox_f_b,
              mlstm_conv_w, mlstm_conv_b, mlstm_i_b, mlstm_f_b, mlstm_out_g,
              rwkv_mu, rwkv_w0, rwkv_w2, rwkv_a0, rwkv_a2, rwkv_k_k, rwkv_k_a, rwkv_r_k, rwkv_ln_g,
              mem_norm_g, mem_w_kv, mem_q_g, mem_k_g):
    stacked = (norm_g, w_in, w_out, sgu_norm_g, sgu_w, sgu_b,
               fox_q_g, fox_k_g, fox_f_b,
               mlstm_conv_w, mlstm_conv_b, mlstm_i_b, mlstm_f_b, mlstm_out_g,
               rwkv_mu, rwkv_w0, rwkv_w2, rwkv_a0, rwkv_a2, rwkv_k_k, rwkv_k_a, rwkv_r_k, rwkv_ln_g,
               mem_norm_g, mem_w_kv, mem_q_g, mem_k_g)
    for layer in range(DEPTH):
        x = hybrid_layer(x, mem, *[p[layer] for p in stacked])
    return x
```

```python
from contextlib import ExitStack
import numpy as np
import ml_dtypes
import concourse.bass as bass
import concourse.mybir as mybir
from concourse.bass_utils import run_bass_kernel_spmd

F32 = mybir.dt.float32
BF16 = mybir.dt.bfloat16
ALU = mybir.AluOpType
AF = mybir.ActivationFunctionType
AX = mybir.AxisListType

ENGINES = ("tensor", "vector", "scalar", "gpsimd", "sync")
SEM_CAP = 30000
EPS = 1e-6


class _Op:
    __slots__ = ("eng", "fn", "idx", "deps", "signal", "is_dma", "sem", "val", "pre_wait")

    def __init__(self, eng, fn, is_dma):
        self.eng = eng
        self.fn = fn
        self.is_dma = is_dma
        self.deps = []
        self.signal = False
        self.sem = None
        self.val = None
        self.pre_wait = None


class Tl:
    def __init__(self, name, t):
        self.name = name
        self.t = t

    def __getitem__(self, idx):
        return self.t[idx]


def _norm(rs):
    out = []
    for r in rs:
        if isinstance(r, tuple):
            a, k = r
        else:
            a, k = r, None
        if isinstance(a, Tl):
            a = a.name
        if a.startswith("scr"):
            k = None
        out.append((a, k))
    return out


class Sched:
    def __init__(self, nc, stack, n_dma_sems=16):
        self.nc = nc
        self.stack = stack
        self.ops = {e: [] for e in ENGINES}
        self.state = {}
        self.n_dma_sems = n_dma_sems

    def _entries(self, name, key):
        d = self.state.setdefault(name, {})
        if key is None:
            return list(d.values())
        res = []
        if key in d:
            res.append(d[key])
        if None in d:
            res.append(d[None])
        return res

    def add(self, eng, fn, reads=(), writes=(), dma=False):
        reads = _norm(reads)
        writes = _norm(writes)
        op = _Op(eng, fn, dma)
        deps = []
        for (name, key) in reads:
            for ent in self._entries(name, key):
                if ent[0] is not None:
                    deps.append(ent[0])
                if name[0] == "p" and name[1].isupper():
                    deps.extend(o_ for o_ in ent[1] if o_.eng != eng)
        for (name, key) in writes:
            for ent in self._entries(name, key):
                if ent[0] is not None:
                    deps.append(ent[0])
                deps.extend(ent[1])
        for (name, key) in reads:
            d = self.state.setdefault(name, {})
            if key is None:
                if not d:
                    d[None] = [None, []]
                for ent in d.values():
                    ent[1].append(op)
            else:
                if key not in d:
                    d[key] = [None, []]
                d[key][1].append(op)
        for (name, key) in writes:
            d = self.state.setdefault(name, {})
            if key is None:
                d.clear()
                d[None] = [op, []]
            else:
                d[key] = [op, []]
        op.idx = len(self.ops[eng])
        best = {}
        dl = []
        for dop in deps:
            if dop is op:
                continue
            if dop.is_dma:
                if dop not in dl:
                    dl.append(dop)
            else:
                if dop.eng == "tensor" and eng == "tensor" and not dma:
                    continue
                b = best.get(dop.eng)
                if b is None or dop.idx > b.idx:
                    best[dop.eng] = dop
        op.deps = dl + list(best.values())
        for dop in op.deps:
            dop.signal = True
        self.ops[eng].append(op)
        return op

    def emit(self, final_wait_ops=()):
        nc = self.nc
        for eng in ENGINES:
            cnt = 0
            sem = None
            for op in self.ops[eng]:
                if op.is_dma:
                    continue
                if op.signal:
                    if sem is None or cnt >= SEM_CAP:
                        sem = self.stack.enter_context(nc.semaphore(f"s_{eng}_{op.idx}"))
                        cnt = 0
                    cnt += 1
                    op.sem = sem
                    op.val = cnt
        for eng in ENGINES:
            qpool = []
            k = 0
            for op in self.ops[eng]:
                if not op.is_dma:
                    continue
                if len(qpool) < self.n_dma_sems:
                    s = self.stack.enter_context(nc.semaphore(f"d_{eng}_{len(qpool)}"))
                    qpool.append([s, 0])
                    ent = qpool[-1]
                else:
                    ent = qpool[k % self.n_dma_sems]
                    if ent[1] + 16 > SEM_CAP:
                        ent[0] = self.stack.enter_context(nc.semaphore(f"d_{eng}_x{k}"))
                        ent[1] = 0
                if ent[1] > 0:
                    op.pre_wait = (ent[0], ent[1])
                ent[1] += 16
                op.sem = ent[0]
                op.val = ent[1]
                k += 1
        sched = self

        def run(eng_name, e):
            seen = {}
            for op in sched.ops[eng_name]:
                waits = []
                if op.pre_wait is not None:
                    waits.append(op.pre_wait)
                for dop in op.deps:
                    waits.append((dop.sem, dop.val))
                for (s, v) in waits:
                    key = id(s)
                    if seen.get(key, 0) >= v:
                        continue
                    seen[key] = v
                    e.wait_ge(s, v)
                ins = op.fn(e)
                if op.is_dma:
                    ins.then_inc(op.sem, 16)
                elif op.signal:
                    ins.then_inc(op.sem, 1)
            if eng_name == "sync":
                for fop in final_wait_ops:
                    e.wait_ge(fop.sem, fop.val)

        with nc.Block() as block:
            @block.sync
            def _(e):
                run("sync", e)

            @block.tensor
            def _(e):
                run("tensor", e)

            @block.vector
            def _(e):
                run("vector", e)

            @block.scalar
            def _(e):
                run("scalar", e)

            @block.gpsimd
            def _(e):
                run("gpsimd", e)


D_MODEL = 1024
FM_GROUPS = ["Au", "Az", "Bz", "Cq", "Ck", "Co", "Cz", "Dr", "Dk", "Dv", "Dz", "Mz",
             "Bf", "Ci", "Cf0", "Cf1", "Dw", "Da"]
FM_W = {g: 128 for g in FM_GROUPS}
FM_W["Dw"] = 16
FM_W["Da"] = 16
FM_OFF = {}
_o = 0
for _g in FM_GROUPS:
    FM_OFF[_g] = _o
    _o += FM_W[_g]
NFM = _o
TM_GROUPS = ["Av", "Bq", "Bk", "Bv", "Cv", "Mq"]
NTM = 768
NW = NFM + NTM
PC = {n: i for i, n in enumerate([
    "cq0", "cq1", "cq2", "cq3", "ck0", "ck1", "ck2", "ck3", "cqb", "ckb",
    "mu_r", "mu_k", "mu_v", "mu_z", "mu_w", "mu_a",
    "k_k", "k_a", "a0", "w0", "r_k", "ln_g", "out_g",
    "fb_B", "ib_C", "fb_C0", "fb_C1"])}
NPC = len(PC)
PR_SGU_G, PR_BQG, PR_BKG, PR_MQG, PR_MKG, PR_SGUB = 0, 128, 256, 384, 512, 640
NPR = 768

A_OFF = 0
B_OFF = 768
C_OFF = 768 + 1028
D_OFF = C_OFF + 1288
M_OFF = D_OFF + 1056


def host_layer_params(inp, l, hh):
    f32 = np.float32
    w_in = inp["w_in"][l]
    hs = [2 * hh, 2 * hh + 1]

    def hcols(base):
        return np.concatenate([np.arange(base + h * 64, base + h * 64 + 64) for h in hs])

    cols = {}
    cols["Au"] = hcols(A_OFF)
    cols["Av"] = hcols(A_OFF + 256)
    cols["Az"] = hcols(A_OFF + 512)
    cols["Bq"] = hcols(B_OFF)
    cols["Bk"] = hcols(B_OFF + 256)
    cols["Bv"] = hcols(B_OFF + 512)
    bf = B_OFF + 768
    cols["Bf"] = np.concatenate([np.full(64, bf + hs[1]), np.full(64, bf + hs[0])])
    cols["Bz"] = hcols(B_OFF + 772)
    cols["Cq"] = hcols(C_OFF)
    cols["Ck"] = hcols(C_OFF + 256)
    cols["Cv"] = hcols(C_OFF + 512)
    ci = C_OFF + 768
    cols["Ci"] = np.concatenate([np.full(64, ci + hs[0]), np.full(64, ci + hs[1])])
    cols["Cf0"] = np.full(128, ci + 4 + hs[0])
    cols["Cf1"] = np.full(128, ci + 4 + hs[1])
    cols["Co"] = hcols(C_OFF + 776)
    cols["Cz"] = hcols(C_OFF + 1032)
    cols["Dr"] = hcols(D_OFF)
    cols["Dw"] = np.arange(D_OFF + 256, D_OFF + 272)
    cols["Dk"] = hcols(D_OFF + 272)
    cols["Dv"] = hcols(D_OFF + 528)
    cols["Da"] = np.arange(D_OFF + 784, D_OFF + 800)
    cols["Dz"] = hcols(D_OFF + 800)
    cols["Mq"] = hcols(M_OFF)
    cols["Mz"] = hcols(M_OFF + 256)
    allc = np.concatenate([cols[g] for g in FM_GROUPS] + [cols[g] for g in TM_GROUPS])
    wcat = np.ascontiguousarray(w_in[:, allc])

    hc = hcols(0)
    pc = np.zeros((128, NPC), f32)
    cw = inp["mlstm_conv_w"][l]
    cb = inp["mlstm_conv_b"][l]
    for j in range(4):
        pc[:, PC["cq%d" % j]] = cw[j, hc]
        pc[:, PC["ck%d" % j]] = cw[j, 256 + hc]
    pc[:, PC["cqb"]] = cb[hc]
    pc[:, PC["ckb"]] = cb[256 + hc]
    mu = inp["rwkv_mu"][l]
    pc[:, PC["mu_r"]] = mu[hc]
    pc[:16, PC["mu_w"]] = mu[256:272]
    pc[:, PC["mu_k"]] = mu[272 + hc]
    pc[:, PC["mu_v"]] = mu[528 + hc]
    pc[:16, PC["mu_a"]] = mu[784:800]
    pc[:, PC["mu_z"]] = mu[800 + hc]
    pc[:, PC["k_k"]] = inp["rwkv_k_k"][l][hc]
    pc[:, PC["k_a"]] = inp["rwkv_k_a"][l][hc]
    pc[:, PC["a0"]] = inp["rwkv_a0"][l][hc]
    pc[:, PC["w0"]] = inp["rwkv_w0"][l][hc]
    pc[:, PC["r_k"]] = inp["rwkv_r_k"][l].reshape(-1)[hc]
    pc[:, PC["ln_g"]] = inp["rwkv_ln_g"][l][hc]
    pc[:, PC["out_g"]] = inp["mlstm_out_g"][l][hc]
    fb = inp["fox_f_b"][l]
    pc[:, PC["fb_B"]] = np.concatenate([np.full(64, fb[hs[1]]), np.full(64, fb[hs[0]])])
    ib = inp["mlstm_i_b"][l]
    pc[:, PC["ib_C"]] = np.concatenate([np.full(64, ib[hs[0]]), np.full(64, ib[hs[1]])])
    fbc = inp["mlstm_f_b"][l]
    pc[:, PC["fb_C0"]] = fbc[hs[0]]
    pc[:, PC["fb_C1"]] = fbc[hs[1]]

    pr = np.zeros((128, NPR), f32)
    pr[:, PR_SGU_G:PR_SGU_G + 128] = inp["sgu_norm_g"][l][hc][None, :]
    pr[:, PR_BQG:PR_BQG + 128] = np.tile(inp["fox_q_g"][l], 2)[None, :]
    pr[:, PR_BKG:PR_BKG + 128] = np.tile(inp["fox_k_g"][l], 2)[None, :]
    pr[:, PR_MQG:PR_MQG + 128] = np.tile(inp["mem_q_g"][l], 2)[None, :]
    pr[:, PR_MKG:PR_MKG + 128] = np.tile(inp["mem_k_g"][l], 2)[None, :]
    sb_ = inp["sgu_b"][l]
    pr[:64, PR_SGUB:PR_SGUB + 128] = sb_[hs[0]][None, :]
    pr[64:, PR_SGUB:PR_SGUB + 128] = sb_[hs[1]][None, :]

    d = {
        "wcat": wcat,
        "pc": pc,
        "pr": pr,
        "ng": np.ascontiguousarray(inp["norm_g"][l].reshape(8, 128).T),
        "memg": np.ascontiguousarray(inp["mem_norm_g"][l].reshape(8, 128).T),
        "wkv": np.ascontiguousarray(np.concatenate(
            [inp["mem_w_kv"][l][:, hc], inp["mem_w_kv"][l][:, 256 + hc]], axis=1)),
        "w2": np.ascontiguousarray(inp["rwkv_w2"][l][:, hc]),
        "a2": np.ascontiguousarray(inp["rwkv_a2"][l][:, hc]),
        "sguw": np.ascontiguousarray(inp["sgu_w"][l][hs]),
    }
    return d


class Builder:
    def __init__(self, SEQ, has_prev, branches="ABCDM"):
        self.SEQ = SEQ
        self.has_prev = has_prev
        self.branches = branches
        self.nc = bass.Bass("TRN2", target_bir_lowering=False)
        self.st = ExitStack()
        self.S = Sched(self.nc, self.st)
        self.ps_rr = 0

    def dram(self, name, shape, dt, kind):
        return self.nc.dram_tensor(name, shape, dt, kind=kind).ap()

    def sb(self, name, shape, dt=F32):
        if not hasattr(self, "_tiles"):
            self._tiles = {}
        if name not in self._tiles:
            self._tiles[name] = Tl(name, self.st.enter_context(self.nc.sbuf_tensor(name, shape, dt)))
        return self._tiles[name]

    def psum(self, name, shape, dt=F32):
        if not hasattr(self, "_tiles"):
            self._tiles = {}
        if name not in self._tiles:
            self._tiles[name] = Tl(name, self.st.enter_context(self.nc.psum_tensor(name, shape, dt)))
        return self._tiles[name]

    def gps(self):
        p = self.gp[self.ps_rr % len(self.gp)]
        self.ps_rr += 1
        return p

    def op(self, eng, fn, r=(), w=()):
        return self.S.add(eng, fn, reads=r, writes=w)

    def dma(self, eng, out, in_, r=(), w=()):
        return self.S.add(eng, lambda e: e.dma_start(out=out, in_=in_), reads=r, writes=w, dma=True)

    def mm(self, out, lhsT, rhs, start, stop, r, w):
        return self.S.add("tensor", lambda e: e.matmul(out, lhsT=lhsT, rhs=rhs, start=start, stop=stop),
                          reads=r, writes=w)

    def tr(self, out, in_, ident, r, w):
        return self.S.add("tensor", lambda e: e.transpose(out, in_, ident), reads=r, writes=w)

    def act(self, out, in_, func, r, w, bias=None, scale=None, accum_out=None, eng="scalar"):
        kw = {}
        if bias is not None:
            kw["bias"] = bias
        if scale is not None:
            kw["scale"] = scale
        if accum_out is not None:
            kw["accum_out"] = accum_out
        return self.S.add("scalar", lambda e: e.activation(out=out, in_=in_, func=func, **kw), reads=r, writes=w)

    def tt(self, eng, out, in0, in1, op, r, w):
        return self.S.add(eng, lambda e: e.tensor_tensor(out=out, in0=in0, in1=in1, op=op), reads=r, writes=w)

    def ts(self, eng, out, in0, s1, op0, r, w, s2=None, op1=None):
        if op1 is None:
            return self.S.add(eng, lambda e: e.tensor_scalar(out=out, in0=in0, scalar1=s1, scalar2=None, op0=op0),
                              reads=r, writes=w)
        return self.S.add(eng, lambda e: e.tensor_scalar(out=out, in0=in0, scalar1=s1, scalar2=s2, op0=op0, op1=op1),
                          reads=r, writes=w)

    def stt(self, out, in0, scalar, in1, op0, op1, r, w):
        return self.S.add("vector", lambda e: e.scalar_tensor_tensor(out=out, in0=in0, scalar=scalar, in1=in1,
                                                                      op0=op0, op1=op1), reads=r, writes=w)

    def cp(self, eng, out, in_, r, w):
        if eng == "scalar":
            return self.S.add("scalar", lambda e: e.copy(out=out, in_=in_), reads=r, writes=w)
        return self.S.add(eng, lambda e: e.tensor_copy(out, in_), reads=r, writes=w)

    def recip(self, out, in_, r, w):
        return self.S.add("vector", lambda e: e.reciprocal(out, in_), reads=r, writes=w)

    def memset(self, eng, ap, val, w):
        return self.S.add(eng, lambda e: e.memset(ap, val), writes=w)

    def asel(self, out, in_, cmp, w, r=(), fill=0.0, base=0, cm=-1, pat=None):
        pat = pat or [[1, 128]]
        return self.S.add("gpsimd", lambda e: e.affine_select(out=out, in_=in_, pattern=pat, compare_op=cmp,
                                                              fill=fill, base=base, channel_multiplier=cm),
                          reads=r, writes=w)

    def build(self):
        cfg = dict(tag="", xmode="outproj" if self.has_prev else "ext")
        self.last_out = []
        self.run_pass(cfg)
        self.S.emit(final_wait_ops=self.last_out)
        return self.nc

    def build_fused(self):
        SEQ = self.SEQ
        self.last_out = []
        self.x_ext = self.dram("xin", [SEQ, 1024], F32, "ExternalInput")
        self.mem_ext = self.dram("mem", [256, 1024], F32, "ExternalInput")
        self.mixs = [self.dram("mixs%d" % l, [1280, SEQ], BF16, "Internal") for l in range(2)]
        self.x1s = self.dram("x1s", [SEQ, 1024], F32, "Internal")
        self.wouts = [self.dram("wout%d" % l, [1280, 1024], F32, "ExternalInput") for l in range(2)]
        for l in range(2):
            if l == 1:
                self.begin("oproj")
                self.outproj_pass("ext", 0, self.x1s, "x1s", False)
            for hh in range(2):
                xmode = "ext" if l == 0 else "x1"
                self.run_pass(dict(tag="_%d%d" % (l, hh), xmode=xmode, fused=True, l=l, hh=hh))
        out_d = self.dram("out", [SEQ, 1024], F32, "ExternalOutput")
        self.begin("oproj")
        self.outproj_pass("x1", 1, out_d, "out_d", True)
        self.S.emit(final_wait_ops=self.last_out)
        return self.nc

    def outproj_pass(self, src_kind, l, dst, dst_name, final):
        SEQ = self.SEQ
        pP = [self.psum("pP0", [128, 512]), self.psum("pP1", [128, 512])]
        wb = self.sb("wb", [128, 8, NW], BF16)
        wo = Tl("wb", wb.t[:, :, :].rearrange("p a b -> p (a b)")[:, 0:10240].rearrange("p (c n) -> p c n", c=10))
        wov = self.wouts[l].rearrange("(c p) n -> p c n", p=128)
        for c in range(10):
            self.dma("gpsimd", wo[:, c, :], wov[:, c, :], w=[wo])
        xts = [self.sb("xt%d" % i, [128, 1024]) for i in range(2)]
        tm_ = self.sb("tm", [128, 4, 768], BF16)
        flat = tm_.t[:, :, :].rearrange("p a b -> p (a b)")
        mpvs = [Tl("tm", flat[:, 0:1280].rearrange("p (c t) -> p c t", c=10)),
                Tl("tm", flat[:, 1280:2560].rearrange("p (c t) -> p c t", c=10))]
        for ti in range(SEQ // 128):
            xt = xts[ti % 2]
            mpv = mpvs[ti % 2]
            r0 = ti * 128
            if src_kind == "ext":
                self.dma("sync", xt[:], self.x_ext[r0:r0 + 128, :], w=[xt])
            else:
                self.dma("sync", xt[:], self.x1s[r0:r0 + 128, :], r=[("x1s", ti)], w=[xt])
            self.dma("sync", mpv[:], self.mixs[l][:, r0:r0 + 128].rearrange("(c p) t -> p c t", p=128),
                     r=[("mixs%d" % l, (0, ti // 4)), ("mixs%d" % l, (1, ti // 4))], w=[mpv])
            for hf in range(2):
                p = pP[hf]
                for c in range(10):
                    self.mm(p[:], mpv[:, c, :], wo[:, c, hf * 512:(hf + 1) * 512], c == 0, c == 9,
                            [mpv, wo], [p])
                eng = "vector" if hf == 0 else "gpsimd"
                if eng == "gpsimd":
                    tmpo = self.scr("op_tmp", [128, 512])
                    self.cp("scalar", tmpo[:], p[:], [p], [tmpo])
                    self.tt("gpsimd", xt[:, hf * 512:(hf + 1) * 512], xt[:, hf * 512:(hf + 1) * 512], tmpo[:],
                            ALU.add, [xt, tmpo], [xt])
                else:
                    self.tt("vector", xt[:, hf * 512:(hf + 1) * 512], xt[:, hf * 512:(hf + 1) * 512], p[:],
                            ALU.add, [xt, p], [xt])
            o = self.dma("sync", dst[r0:r0 + 128, :], xt[:], r=[xt], w=[(dst_name, ti)])
            if final:
                self.last_out.append(o)

    def run_pass(self, cfg):
        nc = self.nc
        SEQ = self.SEQ
        NMC = SEQ // 512
        NCH = SEQ // 128
        tag = cfg["tag"]
        xmode = cfg["xmode"]
        fused = cfg.get("fused", False)
        has_prev = xmode == "outproj"
        wcat = self.dram("wcat" + tag, [1024, NW], F32, "ExternalInput")
        pc_d = self.dram("pc" + tag, [128, NPC], F32, "ExternalInput")
        pr_d = self.dram("pr" + tag, [128, NPR], F32, "ExternalInput")
        ng_d = self.dram("ng" + tag, [128, 8], F32, "ExternalInput")
        memg_d = self.dram("memg" + tag, [128, 8], F32, "ExternalInput")
        wkv_d = self.dram("wkv" + tag, [1024, 256], F32, "ExternalInput")
        w2_d = self.dram("w2" + tag, [16, 128], F32, "ExternalInput")
        a2_d = self.dram("a2" + tag, [16, 128], F32, "ExternalInput")
        sguw_d = self.dram("sguw" + tag, [2, 128, 128], F32, "ExternalInput")
        if fused:
            l, hh = cfg["l"], cfg["hh"]
            xin = self.x_ext
            mem_d = self.mem_ext
            mixed_d = self.mixs[l].rearrange("(g two p) t -> two p g t", two=2, p=128)[hh]
            mix_name = "mixs%d" % l
            mix_key = lambda mc: (hh, mc)
            if has_prev:
                mprev_d = self.mixs[0]
                wout_d = self.wouts[0]
                x1_d = self.x1s
        else:
            xin = self.dram("xin", [SEQ, 1024], F32, "ExternalInput")
            mem_d = self.dram("mem", [256, 1024], F32, "ExternalInput")
            mixed_d = self.dram("mixed", [640, SEQ], BF16, "ExternalOutput").rearrange("(g p) t -> p g t", p=128)
            mix_name = "mixed_d"
            mix_key = lambda mc: mc
            if has_prev:
                mprev_d = self.dram("mprev", [1280, SEQ], BF16, "ExternalInput")
                wout_d = self.dram("wout", [1280, 1024], F32, "ExternalInput")
                x1_d = self.dram("x1out", [SEQ, 1024], F32, "ExternalOutput")

        pT = self.psum("pT", [128, 1024], BF16)
        pP = [self.psum("pP0", [128, 512]), self.psum("pP1", [128, 512])]
        pG = [self.psum("pG%d" % i, [128, 512]) for i in range(3)]
        pNum = self.psum("pNum", [128, 512])
        pDen = self.psum("pDen", [128, 512])
        self.gp = [pG[2], pDen]
        self.bps = [pG[0], pG[1]]
        self.pacc = pP

        identf = self.sb("identf", [128, 128])
        ident = self.sb("ident", [128, 128], BF16)
        ones_bf = self.sb("ones_bf", [128, 128], BF16)
        onesf = self.sb("onesf", [128, 128])
        c64f = self.sb("c64f", [128, 128])
        c64b = self.sb("c64b", [128, 128], BF16)
        blk = self.sb("blk", [128, 128], BF16)
        m_ge = self.sb("m_ge", [128, 128], BF16)
        self.memset("vector", identf[:], 1.0, [identf])
        self.asel(identf[:], identf[:], ALU.is_equal, [identf], [identf])
        self.cp("vector", ident[:], identf[:], [identf], [ident])
        self.memset("vector", ones_bf[:], 1.0, [ones_bf])
        self.memset("vector", onesf[:], 1.0, [onesf])
        self.memset("vector", c64f[:], 1.0 / 64, [c64f])
        self.memset("vector", c64b[:], 1.0 / 64, [c64b])
        self.memset("vector", blk[:], 0.0, [blk])
        self.memset("vector", blk[0:64, 0:64], 1.0, [blk])
        self.memset("vector", blk[64:128, 64:128], 1.0, [blk])
        self.memset("vector", m_ge[:], 1.0, [m_ge])
        self.asel(m_ge[:], m_ge[:], ALU.is_ge, [m_ge], [m_ge])

        pc = self.sb("pcs", [128, NPC])
        pr = self.sb("prs", [128, NPR])
        ng = self.sb("ngs", [128, 8])
        memg = self.sb("memgs", [128, 8])
        self.dma("sync", pc[:], pc_d, w=[pc])
        self.dma("sync", pr[:], pr_d, w=[pr])
        self.dma("sync", ng[:], ng_d, w=[ng])
        self.dma("sync", memg[:], memg_d, w=[memg])
        omm = self.sb("omm", [128, 6])
        self.ts("vector", omm[:], pc[:, PC["mu_r"]:PC["mu_r"] + 6], -1.0, ALU.mult, [pc], [omm], s2=1.0, op1=ALU.add)
        nfbB = self.sb("nfbB", [128, 1])
        self.ts("vector", nfbB[:], pc[:, PC["fb_B"]:PC["fb_B"] + 1], -1.0, ALU.mult, [pc], [nfbB])
        nfbC = self.sb("nfbC", [128, 2])
        self.ts("vector", nfbC[:], pc[:, PC["fb_C0"]:PC["fb_C0"] + 2], -1.0, ALU.mult, [pc], [nfbC])
        gq8 = self.sb("gq8", [128, 128])
        self.ts("vector", gq8[:], pr[:, PR_BQG:PR_BQG + 128], 0.125, ALU.mult, [pr], [gq8])
        gmq8 = self.sb("gmq8", [128, 128])
        self.ts("vector", gmq8[:], pr[:, PR_MQG:PR_MQG + 128], 0.125, ALU.mult, [pr], [gmq8])

        wb = self.sb("wb", [128, 8, NW], BF16)
        wv = wcat.rearrange("(c p) n -> p c n", p=128)
        for c in range(8):
            self.dma("gpsimd", wb[:, c, :], wv[:, c, :], w=[(wb, c)])
        for c in range(8):
            eng = "vector" if c % 2 == 0 else "gpsimd"
            self.ts(eng, wb[:, c, :], wb[:, c, :], ng[:, c:c + 1], ALU.mult, [(wb, c), ng], [(wb, c)])
        if has_prev:
            wo = self.sb("wo", [128, 10, 1024], BF16)
            wov = wout_d.rearrange("(c p) n -> p c n", p=128)
            for c in range(10):
                self.dma("gpsimd", wo[:, c, :], wov[:, c, :], w=[(wo, c)])
        w2s = self.sb("w2s", [16, 128])
        a2s = self.sb("a2s", [16, 128])
        self.dma("sync", w2s[:], w2_d, w=[w2s])
        self.dma("sync", a2s[:], a2_d, w=[a2s])

        wsT = self.sb("wsT", [128, 2, 128], BF16)
        self.begin("setupA")
        if "A" in self.branches:
            sgw = self.scr("sgw", [128, 2, 128])
            self.dma("sync", sgw[:], sguw_d.rearrange("h t s -> t h s"), w=[sgw])
            sgwT = self.scr("sgwT", [128, 2, 128])
            for h in range(2):
                p = self.gps()
                self.tr(p[:, 0:128], sgw[:, h, :], identf[:], [sgw, identf], [p])
                self.cp("vector", sgwT[:, h, :], p[:, 0:128], [p], [sgwT])
                self.asel(sgwT[:, h, :], sgwT[:, h, :], ALU.is_ge, [sgwT], [sgwT])
            self.cp("vector", wsT[:], sgwT[:], [sgwT], [wsT])

        kTm = self.sb("kTm", [128, 256], BF16)
        vm = self.sb("vm", [128, 2, 128], BF16)

        self.alloc_state(NCH)

        xts = [self.sb("xt0", [128, 1024]), self.sb("xt1", [128, 1024])]
        hb = self.sb("hb", [128, 1024], BF16)
        hT = self.sb("hT", [128, 8, 512], BF16)
        ss = self.sb("ss", [128, 2])
        GATED = {"Az": AF.Silu, "Bz": AF.Silu, "Cz": AF.Silu, "Mz": AF.Silu, "Co": AF.Sigmoid}
        fm = {}
        for g in FM_GROUPS:
            if g in ("Cq", "Ck"):
                fm[g] = self.sb("fm_" + g, [128, 4 + 512], BF16)
            elif g in ("Dr", "Dk", "Dv", "Dz"):
                fm[g] = self.sb("fm_" + g, [128, 2 + 512], BF16)
            elif g in ("Dw", "Da"):
                fm[g] = self.sb("fm_" + g, [16, 1 + 512])
            elif g in GATED:
                fm[g] = self.sb("fm_" + g, [128, 512], BF16)
            else:
                fm[g] = self.sb("fm_" + g, [128, 512])
        tm = self.sb("tm", [128, 4, 768], BF16)
        mixo = self.sb("mixo", [128, 5, 512], BF16)
        if "M" in self.branches:
            self.setup_mem(mem_d, wkv_d, memg, pr, ident, identf, pT, kTm, vm, xts, hT, tm, hb)
        mpvs = [Tl("tm", tm.t[:, :, :].rearrange("p a b -> p (a b)")[:, 0:1280].rearrange("p (c t) -> p c t", c=10))] * 2
        for g in ("Cq", "Ck"):
            self.memset("vector", fm[g][:, 0:3], 0.0, [(fm[g], "hist")])
        for g in ("Dr", "Dk", "Dv", "Dz"):
            self.memset("vector", fm[g][:, 0:1], 0.0, [(fm[g], "hist")])
        for g in ("Dw", "Da"):
            self.memset("vector", fm[g][:, 0:1], 0.0, [(fm[g], "hist")])

        self.consts = dict(ident=ident, identf=identf, ones_bf=ones_bf, onesf=onesf, c64f=c64f, c64b=c64b,
                           blk=blk, m_ge=m_ge, pc=pc, pr=pr, omm=omm, nfbB=nfbB, nfbC=nfbC,
                           gq8=gq8, gmq8=gmq8, wsT=wsT, kTm=kTm, vm=vm, w2s=w2s, a2s=a2s, pT=pT,
                           pNum=pNum, pDen=pDen)
        last_out = self.last_out
        xi = 0
        xstate = {"xi": 0}

        def xprep(mc):
            t0 = mc * 512
            for tt in range(4):
                xi = xstate["xi"]
                xt = xts[xi % 2]
                r0 = t0 + tt * 128
                ti = mc * 4 + tt
                if xmode == "x1":
                    self.dma("sync", xt[:], self.x1s[r0:r0 + 128, :], r=[("x1s", ti)], w=[xt])
                else:
                    self.dma("sync", xt[:], xin[r0:r0 + 128, :], w=[xt])
                if has_prev:
                    mpv = mpvs[xi % 2]
                    rr = [("mixs0", (0, mc)), ("mixs0", (1, mc))] if fused else []
                    self.dma("sync", mpv[:], mprev_d[:, r0:r0 + 128].rearrange("(c p) t -> p c t", p=128),
                             r=rr, w=[mpv])
                    for hf in range(2):
                        p = pP[hf]
                        for c in range(10):
                            self.mm(p[:], mpv[:, c, :], wo[:, c, hf * 512:(hf + 1) * 512],
                                    c == 0, c == 9, [mpv, (wo, c)], [p])
                        self.tt("vector", xt[:, hf * 512:(hf + 1) * 512], xt[:, hf * 512:(hf + 1) * 512],
                                p[:], ALU.add, [xt, p], [xt])
                    o = self.dma("sync", x1_d[r0:r0 + 128, :], xt[:], r=[xt], w=[("x1s", ti)])
                    if not fused:
                        last_out.append(o)
                xstate["xi"] = xi + 1
                self.act(hb[:], xt[:], AF.Square, [xt], [hb, ss], accum_out=ss[:, 0:1])
                self.act(ss[:, 1:2], ss[:, 0:1], AF.Ln, [ss], [ss], scale=1.0 / 1024, bias=EPS)
                self.act(ss[:, 1:2], ss[:, 1:2], AF.Exp, [ss], [ss], scale=-0.5)
                self.ts("vector", hb[:], xt[:], ss[:, 1:2], ALU.mult, [xt, ss], [hb])
                yield
                for c in range(8):
                    self.tr(pT[:, c * 128:(c + 1) * 128], hb[:, c * 128:(c + 1) * 128], ident[:],
                            [hb, ident], [pT])
                eng = "vector" if tt % 2 == 0 else "scalar"
                self.cp(eng, hT[:, :, tt * 128:(tt + 1) * 128],
                        pT[:, :].rearrange("p (c t) -> p c t", c=8), [pT], [hT])
                yield

        for _ in xprep(0):
            pass
        for mc in range(NMC):
            t0 = mc * 512
            for gi, g in enumerate(FM_GROUPS):
                p = pP[gi % 2]
                wdt = FM_W[g]
                for c in range(8):
                    self.mm(p[0:wdt, :], wb[:, c, FM_OFF[g]:FM_OFF[g] + wdt], hT[:, c, :], c == 0, c == 7,
                            [(wb, c), hT], [p])
                hist = {"Cq": 3, "Ck": 3, "Dr": 1, "Dk": 1, "Dv": 1, "Dz": 1, "Dw": 1, "Da": 1}.get(g, 0)
                dst = fm[g]
                if g in GATED:
                    self.act(dst[:, :], p[:, :], GATED[g], [p], [(dst, "cur")])
                    continue
                if hist and mc > 0:
                    self.cp("vector", dst[0:wdt, 0:hist], dst[0:wdt, 512:512 + hist], [(dst, "cur")], [(dst, "hist")])
                eng = "scalar" if gi % 2 == 0 else "vector"
                self.cp(eng, dst[0:wdt, hist:hist + 512], p[0:wdt, :], [p, (dst, "hist")], [(dst, "cur")])
            for tt in range(4):
                for hf in range(2):
                    p = pP[hf]
                    for c in range(8):
                        self.mm(p[:, 0:384], hT[:, c, tt * 128:(tt + 1) * 128],
                                wb[:, c, NFM + hf * 384:NFM + (hf + 1) * 384], c == 0, c == 7, [hT, (wb, c)], [p])
                    eng = "scalar" if hf == 0 else "vector"
                    self.cp(eng, tm[:, tt, hf * 384:(hf + 1) * 384], p[:, 0:384], [p], [(tm, tt)])
            self.zero_mix = []
            if "A" in self.branches:
                self.branch_A(mc, fm, tm, mixo)
            else:
                self.memset("gpsimd", mixo[:, 0, :], 0.0, [(mixo, 0)])
            gens = []
            nB = 2 * (4 * mc + 4) + 5
            if "B" in self.branches:
                gens.append(["B", self.branch_B(mc, fm, tm, mixo), max(1, int(round(3.0 * nB / 95.0)))])
            else:
                self.memset("gpsimd", mixo[:, 1, :], 0.0, [(mixo, 1)])
            if "D" in self.branches:
                gens.append(["D", self.branch_D(mc, fm, tm, mixo), 3])
            else:
                self.memset("gpsimd", mixo[:, 3, :], 0.0, [(mixo, 3)])
            if mc + 1 < NMC:
                gens.append(["X", xprep(mc + 1), 1])
            while gens:
                for item in list(gens):
                    for rep in range(item[2]):
                        self._br = item[0]
                        try:
                            next(item[1])
                        except StopIteration:
                            gens.remove(item)
                            break
            if "C" in self.branches:
                self._br = "C"
                for _ in self.branch_C(mc, fm, tm, mixo):
                    self._br = "C"
            else:
                self.memset("gpsimd", mixo[:, 2, :], 0.0, [(mixo, 2)])
            if "M" in self.branches:
                self.branch_M(mc, fm, tm, mixo)
            else:
                self.memset("gpsimd", mixo[:, 4, :], 0.0, [(mixo, 4)])
            o = self.dma("sync", mixed_d[:, :, t0:t0 + 512], mixo[:], r=[mixo], w=[(mix_name, mix_key(mc))])
            if not fused:
                last_out.append(o)

    def head_rms_tm(self, src, dst, gain, tag, nt=4):
        sq = self.scr("hr_sq", [128, 4, 128])
        ssq = self.scr("hr_ssq", [128, 8])
        src_ap, src_r = src
        dst_ap, dst_w = dst
        g_ap, g_r = gain
        self.tt("gpsimd", sq[:, 0:nt, :], src_ap, src_ap, ALU.mult, src_r, [sq])
        ssq3 = ssq[:, 0:2 * nt].rearrange("p (t h) -> p t h", h=2)
        self.op("vector", lambda e: e.tensor_reduce(out=ssq3,
                                                     in_=sq[:, 0:nt, :].rearrange("p t (h j) -> p t h j", h=2),
                                                     axis=AX.X, op=ALU.add), [sq], [ssq])
        self.act(ssq[:, 0:2 * nt], ssq[:, 0:2 * nt], AF.Ln, [ssq], [ssq], scale=1.0 / 64, bias=EPS)
        self.act(ssq[:, 0:2 * nt], ssq[:, 0:2 * nt], AF.Exp, [ssq], [ssq], scale=-0.5)
        self.tt("vector", sq[:, 0:nt, :].rearrange("p t (h j) -> p t h j", h=2),
                src_ap.rearrange("p t (h j) -> p t h j", h=2),
                ssq3[:, :, :, None].broadcast_to([128, nt, 2, 64]), ALU.mult, src_r + [ssq], [sq])
        self.tt("vector", dst_ap, sq[:, 0:nt, :], g_ap[:, None, :].broadcast_to([128, nt, 128]), ALU.mult,
                [sq] + g_r, dst_w)

    def begin(self, br):
        self._br = br
        if not hasattr(self, "_brcount"):
            self._brcount = {}
        self._brcount.setdefault(br, {})

    def scr(self, name, shape, dt=F32):
        if not hasattr(self, "_scrmap"):
            self._scrmap = {}
            self._pools = {}
        br = getattr(self, "_br", "x")
        key = (br, name)
        if key in self._scrmap:
            return self._scrmap[key]
        esz = 4 if dt == F32 else 2
        n = 1
        for d_ in shape[1:]:
            n *= d_
        nbytes = n * esz
        cls = 256
        while cls < nbytes:
            cls *= 2
        cnt = self._brcount.setdefault(br, {})
        k = cnt.get(cls, 0)
        cnt[cls] = k + 1
        fam = "B" if br == "B" else ""
        pool = self._pools.setdefault((fam, cls), [])
        if k >= len(pool):
            pname = "scr%s%d_%d" % (fam, cls, k)
            pool.append((pname, self.st.enter_context(self.nc.sbuf_tensor(pname, [128, cls // 4], F32))))
        pname, raw = pool[k]
        h = raw if dt == F32 else raw.bitcast(dt)
        ap = h[0:shape[0], 0:n]
        if len(shape) == 3:
            ap = ap.rearrange("p (a b) -> p a b", a=shape[1])
        elif len(shape) == 4:
            ap = ap.rearrange("p (a b c) -> p a b c", a=shape[1], b=shape[2])
        t = Tl(pname, ap)
        self._scrmap[key] = t
        return t

    def gelu(self, dst_ap, dst_w, src_ap, src_r, shape, tag):
        t1 = self.scr("gl_t1" + tag, shape)
        t2 = self.scr("gl_t2" + tag, shape)
        self.tt("gpsimd", t1[:], src_ap, src_ap, ALU.mult, src_r, [t1])
        self.ts("vector", t1[:], t1[:], 0.044715, ALU.mult, [t1], [t1], s2=1.0, op1=ALU.add)
        self.tt("gpsimd", t2[:], t1[:], src_ap, ALU.mult, [t1] + src_r, [t2])
        self.act(t2[:], t2[:], AF.Sigmoid, [t2], [t2], scale=1.5957691216)
        self.tt("vector", dst_ap, t2[:], src_ap, ALU.mult, [t2] + src_r, dst_w)

    def setup_mem(self, mem_d, wkv_d, memg, pr, ident, identf, pT, kTm, vm, xts, hT, tm, hb):
        self.begin("setup")
        for c in range(2):
            self.dma("sync", xts[c][:], mem_d[c * 128:(c + 1) * 128, :], w=[xts[c]])
        wk = Tl("hT", hT[:, :, 0:256])
        mhT = Tl("hT", hT[:, :, 256:512])
        mh = Tl("tm", tm.t[:, :, :].rearrange("p a b -> p (a b)")[:, 0:2048].rearrange("p (c d) -> p c d", c=2))
        wkv_v = wkv_d.rearrange("(c p) n -> p c n", p=128)
        for c in range(8):
            self.dma("gpsimd", wk[:, c, :], wkv_v[:, c, :], w=[wk])
        for c in range(8):
            self.ts("vector", wk[:, c, :], wk[:, c, :], memg[:, c:c + 1], ALU.mult, [wk, memg], [wk])
        mss = self.scr("m_ss", [128, 2])
        for c in range(2):
            self.act(hb[:], xts[c][:], AF.Square, [xts[c]], [hb, mss], accum_out=mss[:, c:c + 1])
        self.act(mss[:], mss[:], AF.Ln, [mss], [mss], scale=1.0 / 1024, bias=EPS)
        self.act(mss[:], mss[:], AF.Exp, [mss], [mss], scale=-0.5)
        for c in range(2):
            self.ts("vector", mh[:, c, :], xts[c][:], mss[:, c:c + 1], ALU.mult, [xts[c], mss], [mh])
        for c2 in range(2):
            for c in range(8):
                self.tr(pT[:, c * 128:(c + 1) * 128], mh[:, c2, c * 128:(c + 1) * 128], ident[:], [mh, ident], [pT])
            self.cp("vector", mhT[:, :, c2 * 128:(c2 + 1) * 128], pT[:, :].rearrange("p (c t) -> p c t", c=8),
                    [pT], [mhT])
        kvt = self.scr("m_kvt", [128, 2, 256])
        for c2 in range(2):
            p = self.gps()
            for c in range(8):
                self.mm(p[:, 0:256], mhT[:, c, c2 * 128:(c2 + 1) * 128], wk[:, c, :], c == 0, c == 7, [mhT, wk], [p])
            self.cp("vector", kvt[:, c2, :], p[:, 0:256], [p], [kvt])
        self.cp("vector", vm[:], kvt[:, :, 128:256], [kvt], [vm])
        kn = self.scr("m_kn", [128, 2, 128], BF16)
        self.head_rms_tm((kvt[:, :, 0:128], [kvt]), (kn[:], [kn]), (pr[:, PR_MKG:PR_MKG + 128], [pr]), "mk", nt=2)
        for c2 in range(2):
            self.tr(pT[:, c2 * 128:(c2 + 1) * 128], kn[:, c2, :], ident[:], [kn, ident], [pT])
        self.cp("vector", kTm[:], pT[:, 0:256], [pT], [kTm])

    def alloc_state(self, NCH):
        SEQ = self.SEQ
        if "B" in self.branches:
            self.KT = self.sb("KT", [128, SEQ], BF16)
            self.VB = self.sb("VB", [128, NCH, 128], BF16)
            self.Fcol = self.sb("Fcol", [128, 2, NCH])
            self.Fprev = self.sb("Fprev", [128, 1])
            self.memset("vector", self.Fprev[:], 0.0, [self.Fprev])
            self.hm = self.sb("hm", [128, 2])
            self.memset("vector", self.hm[:], 0.0, [self.hm])
            for h in range(2):
                hs = slice(h * 64, (h + 1) * 64)
                fs = slice(64, 128) if h == 0 else slice(0, 64)
                self.memset("vector", self.hm[hs, h:h + 1], 1.0, [self.hm])
        if "C" in self.branches:
            self.Cst = self.sb("Cst", [128, 128])
            self.Cstb = self.sb("Cstb", [128, 128], BF16)
            self.memset("vector", self.Cst[:], 0.0, [self.Cst])
            self.memset("vector", self.Cstb[:], 0.0, [self.Cstb])
        if "D" in self.branches:
            self.Sst = self.sb("Sst", [128, 64])
            self.Sstb = self.sb("Sstb", [128, 64], BF16)
            self.memset("vector", self.Sst[:], 0.0, [self.Sst])
            self.memset("vector", self.Sstb[:], 0.0, [self.Sstb])

    def branch_A(self, mc, fm, tm, mixo):
        self.begin("A")
        c = self.consts
        pr = c["pr"]
        gu = self.scr("A_gu", [128, 512])
        self.gelu(gu[:], [gu], fm["Au"][:, :], [(fm["Au"], "cur")], [128, 512], "u")
        gv = self.scr("A_gv", [128, 4, 128])
        self.gelu(gv[:], [gv], tm[:, :, 0:128], [tm], [128, 4, 128], "v")
        vn = self.scr("A_vn", [128, 4, 128], BF16)
        self.head_rms_tm((gv[:], [gv]), (vn[:], [vn]), (pr[:, PR_SGU_G:PR_SGU_G + 128], [pr]), "av")
        p = self.gps()
        for tt in range(4):
            for h in range(2):
                self.mm(p[h * 64:(h + 1) * 64, tt * 128:(tt + 1) * 128], vn[:, tt, h * 64:(h + 1) * 64],
                        c["wsT"][:, h, :], True, True, [vn, c["wsT"]], [p])
        ya = self.scr("A_ya", [128, 512])
        self.tt("vector", ya[:].rearrange("p (a t) -> p a t", a=4), p[:].rearrange("p (a t) -> p a t", a=4),
                pr[:, None, PR_SGUB:PR_SGUB + 128].broadcast_to([128, 4, 128]), ALU.add, [p, pr], [ya])
        self.tt("gpsimd", ya[:], ya[:], gu[:], ALU.mult, [ya, gu], [ya])
        self.tt("vector", mixo[:, 0, :], ya[:], fm["Az"][:, :], ALU.mult, [ya, fm["Az"]], [(mixo, 0)])

    def branch_M(self, mc, fm, tm, mixo):
        self.begin("M")
        c = self.consts
        pT = c["pT"]
        qn = self.scr("M_qn", [128, 4, 128], BF16)
        self.head_rms_tm((tm[:, :, 640:768], [tm]), (qn[:], [qn]), (c["gmq8"][:], [c["gmq8"]]), "mq")
        qT = self.scr("M_qT", [128, 512], BF16)
        for tt in range(4):
            self.tr(pT[:, tt * 128:(tt + 1) * 128], qn[:, tt, :], c["ident"][:], [qn, c["ident"]], [pT])
        self.cp("vector", qT[:], pT[:, 0:512], [pT], [qT])
        pn = self.pacc[0]
        pd = self.pacc[1]
        E = self.scr("M_E", [128, 512], BF16)
        for h in range(2):
            hs = slice(h * 64, (h + 1) * 64)
            for mcx in range(2):
                ps_ = self.gps()
                self.mm(ps_[:], c["kTm"][hs, mcx * 128:(mcx + 1) * 128], qT[hs, :], True, True, [c["kTm"], qT], [ps_])
                self.act(E[:], ps_[:], AF.Exp, [ps_], [E])
                self.mm(pn[hs, :], c["vm"][:, mcx, hs], E[:], mcx == 0, mcx == 1, [c["vm"], E], [(pn, h)])
                self.mm(pd[hs, :], c["ones_bf"][:, 0:64], E[:], mcx == 0, mcx == 1, [c["ones_bf"], E], [(pd, h)])
        rd = self.scr("M_rd", [128, 512])
        self.recip(rd[:], pd[:], [pd], [rd])
        ym = self.scr("M_ym", [128, 512])
        self.tt("vector", ym[:], pn[:], rd[:], ALU.mult, [pn, rd], [ym])
        self.tt("gpsimd", mixo[:, 4, :], ym[:], fm["Mz"][:, :], ALU.mult, [ym, fm["Mz"]], [(mixo, 4)])

    def branch_B(self, mc, fm, tm, mixo):
        self.begin("B")
        c = self.consts
        pT = c["pT"]
        pr = c["pr"]
        G = mc
        qn = self.scr("B_qn", [128, 4, 128], BF16)
        kn = self.scr("B_kn", [128, 4, 128], BF16)
        self.head_rms_tm((tm[:, :, 128:256], [tm]), (qn[:], [qn]), (c["gq8"][:], [c["gq8"]]), "bq")
        self.head_rms_tm((tm[:, :, 256:384], [tm]), (kn[:], [kn]), (pr[:, PR_BKG:PR_BKG + 128], [pr]), "bk")
        QT = self.scr("B_QT", [128, 512], BF16)
        yield
        for tt in range(4):
            self.tr(pT[:, tt * 128:(tt + 1) * 128], qn[:, tt, :], c["ident"][:], [qn, c["ident"]], [pT])
        for tt in range(4):
            self.tr(pT[:, 512 + tt * 128:512 + (tt + 1) * 128], kn[:, tt, :], c["ident"][:], [kn, c["ident"]], [pT])
        self.cp("vector", QT[:], pT[:, 0:512], [pT], [QT])
        self.cp("scalar", self.KT[:, G * 512:(G + 1) * 512], pT[:, 512:1024], [pT], [(self.KT, G)])
        QTm = [self.scr("B_QTm%d" % h, [128, 512], BF16) for h in range(2)]
        for h in range(2):
            self.ts("gpsimd", QTm[h][:], QT[:], self.hm[:, h:h + 1], ALU.mult, [QT, self.hm], [QTm[h]])
        for tt in range(4):
            self.cp("gpsimd", self.VB[:, G * 4 + tt, :], tm[:, tt, 384:512], [tm], [(self.VB, G * 4 + tt)])
        yield
        e1 = self.scr("B_e1", [128, 512])
        self.act(e1[:], fm["Bf"][:, :], AF.Exp, [(fm["Bf"], "cur")], [e1], scale=-1.0, bias=c["nfbB"][:, 0:1])
        self.act(e1[:], e1[:], AF.Ln, [e1], [e1], bias=1.0)
        Lc = self.scr("B_Lc", [128, 512])
        for q4 in range(4):
            qs = slice(q4 * 128, (q4 + 1) * 128)
            ini = self.Fprev[:, 0:1] if q4 == 0 else Lc[:, q4 * 128 - 1:q4 * 128]
            self.op("vector", lambda e, qs=qs, ini=ini: e.tensor_tensor_scan(
                out=Lc[:, qs], data0=c["onesf"][:, 0:128], data1=e1[:, qs], initial=ini,
                op0=ALU.mult, op1=ALU.add), [c["onesf"], e1, self.Fprev, Lc], [Lc])
        yield
        pcg = self.bps[0]
        for h in range(2):
            fs = slice(64, 128) if h == 0 else slice(0, 64)
            for j in range(4):
                self.mm(pcg[:, h * 8 + j:h * 8 + j + 1], Lc[fs, j * 128:(j + 1) * 128], c["c64f"][fs, 0:1],
                        True, True, [Lc, c["c64f"]], [pcg])
            for hf in range(2):
                col = hf * 256 + 127
                self.mm(pcg[:, 16 + h * 2 + hf:17 + h * 2 + hf], c["c64f"][fs, :], Lc[fs, col:col + 1], True, True,
                        [c["c64f"], Lc], [pcg])
        cg = self.scr("B_cg", [128, 4])
        for h in range(2):
            self.cp("vector", self.Fcol[:, h, G * 4:G * 4 + 4], pcg[:, h * 8:h * 8 + 4], [pcg], [(self.Fcol, G)])
        self.cp("vector", cg[:], pcg[:, 16:20], [pcg], [cg])
        self.cp("vector", self.Fprev[:], Lc[:, 511:512], [Lc], [self.Fprev])
        nkb = 4 * G + 4
        bias = self.scr("B_bias", [128, 4, self.SEQ // 128])
        for h in range(2):
            for hf in range(2):
                self.ts("vector", bias[:, h * 2 + hf, 0:nkb], self.Fcol[:, h, 0:nkb],
                        cg[:, h * 2 + hf:h * 2 + hf + 1], ALU.subtract, [self.Fcol, cg], [bias])
        pNumH = [c["pNum"], c["pNum"]]
        yield
        Es = [self.scr("B_E%d" % i, [128, 512], BF16) for i in range(3)]
        Eacc = [self.scr("B_Eacc", [128, 512])] * 2
        rd = self.scr("B_rd", [128, 512])
        yb = self.scr("B_yb", [128, 512])
        pairs = [(h, kb) for h in range(2) for kb in range(nkb)]
        psl = {}

        def score(i):
            h, kb = pairs[i]
            j = kb - 4 * G
            c0 = 0 if j <= 0 else j * 128
            ps_ = self.bps[i % 2]
            self.mm(ps_[:, c0:512], self.KT[:, kb * 128:(kb + 1) * 128], QTm[h][:, c0:512], True, True,
                    [self.KT, QTm[h]], [ps_])
            psl[i] = ps_
        score(0)
        for i, (h, kb) in enumerate(pairs):
            if i + 1 < len(pairs):
                score(i + 1)
            hs = slice(h * 64, (h + 1) * 64)
            pN = pNumH[h]
            j = kb - 4 * G
            c0 = 0 if j <= 0 else j * 128
            ps_ = psl.pop(i)
            E = Es[i % 3]
            for hf in range(2):
                a_, b_ = max(c0, hf * 256), (hf + 1) * 256
                if a_ >= b_:
                    continue
                self.act(E[:, a_:b_], ps_[:, a_:b_], AF.Exp, [ps_, bias], [E],
                         bias=bias[:, h * 2 + hf, kb:kb + 1])
            if j >= 0:
                self.tt("gpsimd", E[:, c0:c0 + 128], E[:, c0:c0 + 128], c["m_ge"][:], ALU.mult,
                        [E, c["m_ge"]], [E])
            self.mm(pN[:, c0:512], self.VB[:, kb, :], E[:, c0:512], kb == 0, kb == nkb - 1,
                    [self.VB, E], [pN])
            if kb == 0:
                self.cp("vector", Eacc[h][:], E[:], [E], [Eacc[h]])
            else:
                self.tt("vector", Eacc[h][:, c0:512], Eacc[h][:, c0:512], E[:, c0:512], ALU.add,
                        [Eacc[h], E], [Eacc[h]])
            if kb == nkb - 1:
                pd_ = self.bps[i % 2]
                self.mm(pd_[hs, :], c["onesf"][:, 0:64], Eacc[h][:], True, True, [c["onesf"], Eacc[h]], [pd_])
                self.recip(rd[hs, :], pd_[hs, :], [pd_], [rd])
                self.tt("vector", yb[hs, :], pN[hs, :], rd[hs, :], ALU.mult, [pN, rd], [yb])
            yield
        self.tt("gpsimd", mixo[:, 1, :], yb[:], fm["Bz"][:, :], ALU.mult, [yb, fm["Bz"]], [(mixo, 1)])

    def branch_C(self, mc, fm, tm, mixo):
        self.begin("C")
        c = self.consts
        pc = c["pc"]
        pT = c["pT"]
        qc = self.scr("C_qc", [128, 512], BF16)
        kc = self.scr("C_kc", [128, 512], BF16)
        for g, dst, wn, bn in (("Cq", qc, "cq", "cqb"), ("Ck", kc, "ck", "ckb")):
            x = fm[g]
            acc = self.scr("C_acc" + g, [128, 512])
            self.ts("vector", acc[:], x[:, 0:512], pc[:, PC[wn + "0"]:PC[wn + "0"] + 1], ALU.mult, [x, pc], [acc],
                    s2=pc[:, PC[bn]:PC[bn] + 1], op1=ALU.add)
            for j in range(1, 4):
                self.stt(acc[:], x[:, j:j + 512], pc[:, PC[wn + str(j)]:PC[wn + str(j)] + 1], acc[:], ALU.mult,
                         ALU.add, [x, pc, acc], [acc])
            if g == "Cq":
                self.act(dst[:], acc[:], AF.Silu, [acc], [dst])
            else:
                self.act(acc[:], acc[:], AF.Silu, [acc], [acc])
                self.ts("vector", dst[:], acc[:], 0.125, ALU.mult, [acc], [dst])
        yield
        Lb = []
        for h in range(2):
            e1 = fm["Cf%d" % h]
            self.act(e1[:], fm["Cf%d" % h][:, :], AF.Exp, [(fm["Cf%d" % h], "cur")], [e1], scale=-1.0,
                     bias=c["nfbC"][:, h:h + 1])
            self.act(e1[:], e1[:], AF.Ln, [e1], [e1], bias=1.0)
            L = self.scr("C_Lb%d" % h, [128, 512])
            for ch in range(4):
                cs = slice(ch * 128, (ch + 1) * 128)
                self.op("vector", lambda e, L=L, e1=e1, cs=cs: e.tensor_tensor_scan(
                    out=L[:, cs], data0=c["onesf"][:, 0:128], data1=e1[:, cs], initial=0.0,
                    op0=ALU.mult, op1=ALU.add), [c["onesf"], e1], [L])
            Lb.append(L)
        ilog = fm["Ci"]
        self.ts("vector", ilog[:], fm["Ci"][:, :], pc[:, PC["ib_C"]:PC["ib_C"] + 1], ALU.add,
                [(fm["Ci"], "cur"), pc], [ilog])
        yield
        so = fm["Co"]
        sz = fm["Cz"]
        if not hasattr(self, "vaug"):
            self.vaug = [self.sb("C_vaug%d" % h, [128, 128], BF16) for h in range(2)]
            for h in range(2):
                self.memset("vector", self.vaug[h][:], 1.0, [self.vaug[h]])
        vaug = self.vaug
        for ch in range(4):
            cs = slice(ch * 128, (ch + 1) * 128)
            pcol = self.gps()
            for h in range(2):
                hs = slice(h * 64, (h + 1) * 64)
                self.mm(pcol[:, h:h + 1], Lb[h][0:64, cs], c["c64f"][0:64, 0:1], True, True, [Lb[h], c["c64f"]], [pcol])
                self.mm(pcol[:, 2 + h:3 + h], ilog[hs, cs], c["c64f"][hs, 0:1], True, True, [ilog, c["c64f"]], [pcol])
            col = self.scr("C_col", [128, 8])
            self.cp("vector", col[:, 0:4], pcol[:, 0:4], [pcol], [col])
            yield
            self.tt("vector", col[:, 4:6], col[:, 0:2], col[:, 2:4], ALU.add, [col], [col])
            for h in range(2):
                self.ts("vector", col[:, 6 + h:7 + h], col[:, 4 + h:5 + h], Lb[h][:, ch * 128 + 127:ch * 128 + 128],
                        ALU.subtract, [col, Lb[h]], [col])
            wcol = self.scr("C_wcol", [128, 2])
            self.act(wcol[:], col[:, 6:8], AF.Exp, [col], [wcol])
            yield
            eg = self.scr("C_eg", [128, 2])
            for h in range(2):
                self.act(eg[:, h:h + 1], Lb[h][:, ch * 128 + 127:ch * 128 + 128], AF.Exp, [Lb[h]], [eg], scale=-1.0)
            ebq = self.scr("C_ebq", [128, 128])
            for h in range(2):
                hs = slice(h * 64, (h + 1) * 64)
                self.act(ebq[hs, :], Lb[h][hs, cs], AF.Exp, [Lb[h]], [ebq], scale=-1.0)
            qp = self.scr("C_qp", [128, 128], BF16)
            self.tt("vector", qp[:], qc[:, cs], ebq[:], ALU.mult, [qc, ebq], [qp])
            yield
            for h in range(2):
                hs = slice(h * 64, (h + 1) * 64)
                self.cp("gpsimd", vaug[h][:, 0:64], tm[:, ch, 512 + h * 64:512 + (h + 1) * 64], [tm], [vaug[h]])
            pS = self.gps()
            P = []
            for h in range(2):
                hs = slice(h * 64, (h + 1) * 64)
                self.mm(pS[:, h * 128:(h + 1) * 128], kc[hs, cs], qc[hs, cs], True, True, [kc, qc], [pS])
                D = self.scr("C_D%d" % h, [128, 128])
                self.ts("vector", D[:], Lb[h][:, cs], col[:, h:h + 1], ALU.subtract, [Lb[h], col], [D], s2=0.0,
                        op1=ALU.max)
                self.act(D[:], D[:], AF.Exp, [D, col], [D], scale=-1.0, bias=col[:, 2 + h:3 + h])
                self.tt("gpsimd", D[:], D[:], c["m_ge"][:], ALU.mult, [D, c["m_ge"]], [D])
                Ph = self.scr("C_P%d" % h, [128, 128], BF16)
                self.tt("vector", Ph[:], pS[:, h * 128:(h + 1) * 128], D[:], ALU.mult, [pS, D], [Ph])
                P.append(Ph)
            yield
            pnd = self.gps()
            for h in range(2):
                hs = slice(h * 64, (h + 1) * 64)
                self.mm(pnd[hs, 0:128], vaug[h][:, 0:64], P[h][:], True, False, [vaug[h], P[h]], [pnd])
                self.mm(pnd[hs, 0:128], self.Cstb[hs, 0:64], qp[hs, :], False, True, [self.Cstb, qp], [pnd])
            for h in range(2):
                hs = slice(h * 64, (h + 1) * 64)
                self.mm(pnd[hs, 128:256], c["ones_bf"][:, 0:64], P[h][:], True, False, [c["ones_bf"], P[h]], [pnd])
                self.mm(pnd[hs, 128:256], self.Cstb[hs, 64:128], qp[hs, :], False, True, [self.Cstb, qp], [pnd])
            dn = self.scr("C_dn", [128, 128])
            self.act(dn[:], pnd[:, 128:256], AF.Abs, [pnd], [dn])
            self.ts("vector", dn[:], dn[:], 1.0, ALU.max, [dn], [dn])
            self.recip(dn[:], dn[:], [dn], [dn])
            ho = self.scr("C_ho", [128, 128])
            self.tt("vector", ho[:], pnd[:, 0:128], dn[:], ALU.mult, [pnd, dn], [ho])
            yield
            self.tt("gpsimd", ho[:], ho[:], so[:, cs], ALU.mult, [ho, so], [ho])
            yield
            sq = self.scr("C_sq", [128, 128], BF16)
            self.tt("gpsimd", sq[:], ho[:], ho[:], ALU.mult, [ho], [sq])
            pq = self.gps()
            self.mm(pq[:, 0:128], c["blk"][:], sq[:], True, True, [c["blk"], sq], [pq])
            rs = self.scr("C_rs", [128, 128])
            self.act(rs[:], pq[:, 0:128], AF.Ln, [pq], [rs], scale=1.0 / 64, bias=EPS)
            self.act(rs[:], rs[:], AF.Exp, [rs], [rs], scale=-0.5)
            self.stt(ho[:], ho[:], pc[:, PC["out_g"]:PC["out_g"] + 1], rs[:], ALU.mult, ALU.mult, [ho, pc, rs], [ho])
            self.tt("vector", mixo[:, 2, cs], ho[:], sz[:, cs], ALU.mult, [ho, sz], [(mixo, 2)])
            yield
            self.tr(pT[:, 0:128], kc[:, cs], c["ident"][:], [kc, c["ident"]], [pT])
            khat = self.scr("C_khat", [128, 128], BF16)
            for h in range(2):
                hs = slice(h * 64, (h + 1) * 64)
                self.ts("vector", khat[:, hs], pT[:, h * 64:(h + 1) * 64], wcol[:, h:h + 1], ALU.mult, [pT, wcol],
                        [khat])
            yield
            pC = self.gps()
            for h in range(2):
                hs = slice(h * 64, (h + 1) * 64)
                self.mm(pC[hs, 0:128], khat[:, hs], vaug[h][:], True, True, [khat, vaug[h]], [pC])
            for h in range(2):
                hs = slice(h * 64, (h + 1) * 64)
                self.stt(self.Cst[hs, :], self.Cst[hs, :], eg[hs, h:h + 1], pC[hs, 0:128], ALU.mult, ALU.add,
                         [self.Cst, eg, pC], [self.Cst])
            self.cp("vector", self.Cstb[:], self.Cst[:], [self.Cst], [self.Cstb])
            yield

    def branch_D(self, mc, fm, tm, mixo):
        self.begin("D")
        c = self.consts
        pc = c["pc"]
        pT = c["pT"]
        omm = c["omm"]
        if not hasattr(self, "m_gt"):
            self.m_gt = self.sb("m_gt", [128, 128], BF16)
            self.mN_gt = self.sb("mN_gt", [128, 128], BF16)
            self.memset("vector", self.m_gt[:], 1.0, [self.m_gt])
            self.asel(self.m_gt[:], self.m_gt[:], ALU.is_gt, [self.m_gt], [self.m_gt])
            self.memset("vector", self.mN_gt[:], 1.0, [self.mN_gt])
            self.asel(self.mN_gt[:], self.mN_gt[:], ALU.is_gt, [self.mN_gt], [self.mN_gt], cm=1, pat=[[-1, 128]])
        m_gt, mN_gt, m_ge = self.m_gt, self.mN_gt, c["m_ge"]

        def shift(g, idx, rows, name):
            x = fm[g]
            t = self.scr("D_sh_" + name, [128, 512])
            self.ts("vector", t[0:rows, :], x[0:rows, 1:513], omm[0:rows, idx:idx + 1], ALU.mult, [x, omm], [t])
            self.stt(t[0:rows, :], x[0:rows, 0:512], pc[0:rows, PC["mu_r"] + idx:PC["mu_r"] + idx + 1], t[0:rows, :],
                     ALU.mult, ALU.add, [x, pc, t], [t])
            return t
        rs = shift("Dr", 0, 128, "r")
        ks = shift("Dk", 1, 128, "k")
        vs = shift("Dv", 2, 128, "v")
        zs = shift("Dz", 3, 128, "z")
        ws = shift("Dw", 4, 16, "w")
        as_ = shift("Da", 5, 16, "a")
        self.act(ws[0:16, :], ws[0:16, :], AF.Tanh, [ws], [ws])
        pw = self.gps()
        self.mm(pw[:], c["w2s"][:], ws[0:16, :], True, True, [c["w2s"], ws], [pw])
        lw = ws
        self.act(lw[:], pw[:], AF.Sigmoid, [pw, pc], [lw], bias=pc[:, PC["w0"]:PC["w0"] + 1])
        self.ts("gpsimd", lw[:], lw[:], -0.6065306597126334, ALU.mult, [lw], [lw])
        pa = self.gps()
        self.mm(pa[:], c["a2s"][:], as_[0:16, :], True, True, [c["a2s"], as_], [pa])
        aa = as_
        self.act(aa[:], pa[:], AF.Sigmoid, [pa, pc], [aa], bias=pc[:, PC["a0"]:PC["a0"] + 1])
        sz = zs
        self.act(sz[:], zs[:], AF.Silu, [zs], [sz])
        yield
        vb = self.scr("D_vb", [128, 512], BF16)
        self.cp("gpsimd", vb[:], vs[:], [vs], [vb])
        kx = self.scr("D_kx", [128, 512])
        self.ts("vector", kx[:], ks[:], pc[:, PC["k_k"]:PC["k_k"] + 1], ALU.mult, [ks, pc], [kx])
        sqb = self.scr("D_sqb", [128, 512], BF16)
        self.tt("gpsimd", sqb[:], kx[:], kx[:], ALU.mult, [kx], [sqb])
        pq = self.gps()
        self.mm(pq[:], c["blk"][:], sqb[:], True, True, [c["blk"], sqb], [pq])
        rn = self.scr("D_rn", [128, 512])
        self.ts("vector", rn[:], pq[:], 1e-18, ALU.max, [pq], [rn])
        self.act(rn[:], rn[:], AF.Ln, [rn], [rn])
        self.act(rn[:], rn[:], AF.Exp, [rn], [rn], scale=-0.5)
        kk = kx
        self.tt("vector", kk[:], kx[:], rn[:], ALU.mult, [kx, rn], [kk])
        k2 = rn
        self.ts("vector", k2[:], aa[:], -1.0, ALU.add, [aa, pc], [k2], s2=pc[:, PC["k_a"]:PC["k_a"] + 1], op1=ALU.mult)
        self.stt(k2[:], k2[:], 1.0, ks[:], ALU.add, ALU.mult, [k2, ks], [k2])
        bv = ks
        self.tt("gpsimd", bv[:], kk[:], aa[:], ALU.mult, [kk, aa], [bv])
        yield
        rk = self.scr("D_rk", [128, 512], BF16)
        self.stt(rk[:], rs[:], pc[:, PC["r_k"]:PC["r_k"] + 1], k2[:], ALU.mult, ALU.mult, [rs, pc, k2], [rk])
        pb = self.gps()
        self.mm(pb[:], c["blk"][:], rk[:], True, True, [c["blk"], rk], [pb])
        bon = vs
        self.tt("vector", bon[:], pb[:], vs[:], ALU.mult, [pb, vs], [bon])
        cl = self.scr("D_cl", [128, 512])
        for ch in range(4):
            cs = slice(ch * 128, (ch + 1) * 128)
            self.op("vector", lambda e, cs=cs: e.tensor_tensor_scan(
                out=cl[:, cs], data0=c["onesf"][:, 0:128], data1=lw[:, cs], initial=0.0,
                op0=ALU.mult, op1=ALU.add), [c["onesf"], lw], [cl])
        Ecl = self.scr("D_Ecl", [128, 512])
        Encl = self.scr("D_Encl", [128, 512])
        self.act(Ecl[:], cl[:], AF.Exp, [cl], [Ecl])
        self.act(Encl[:], cl[:], AF.Exp, [cl], [Encl], scale=-1.0)
        Ecx = lw
        self.tt("gpsimd", lw[:], cl[:], lw[:], ALU.subtract, [cl, lw], [lw])
        self.act(Ecx[:], lw[:], AF.Exp, [lw], [Ecx])
        yield
        clT = self.scr("D_clT", [128, 4])
        self.cp("vector", clT[:], cl[:].rearrange("p (a t) -> p a t", a=4)[:, :, 127], [cl], [clT])
        Eh = cl
        for ch in range(4):
            cs = slice(ch * 128, (ch + 1) * 128)
            self.act(Eh[:, cs], cl[:, cs], AF.Exp, [cl, clT], [Eh], scale=-1.0, bias=clT[:, ch:ch + 1])
        KR = self.scr("D_KR", [128, 4, 256], BF16)
        kt = self.scr("D_kt", [128, 512], BF16)
        bt = self.scr("D_bt", [128, 512], BF16)
        khat = self.scr("D_khat", [128, 512], BF16)
        nbh = self.scr("D_nbh", [128, 512], BF16)
        self.tt("vector", KR[:, :, 0:128], kk[:].rearrange("p (a t) -> p a t", a=4),
                Ecx[:].rearrange("p (a t) -> p a t", a=4), ALU.mult, [kk, Ecx], [KR])
        self.tt("gpsimd", KR[:, :, 128:256], rs[:].rearrange("p (a t) -> p a t", a=4),
                Ecl[:].rearrange("p (a t) -> p a t", a=4), ALU.mult, [rs, Ecl], [KR])
        self.tt("vector", kt[:], k2[:], Encl[:], ALU.mult, [k2, Encl], [kt])
        self.tt("gpsimd", bt[:], bv[:], Encl[:], ALU.mult, [bv, Encl], [bt])
        self.tt("vector", khat[:], k2[:], Eh[:], ALU.mult, [k2, Eh], [khat])
        self.stt(nbh[:], bv[:], -1.0, Eh[:], ALU.mult, ALU.mult, [bv, Eh], [nbh])
        yraw = kx
        yield
        for ch in range(4):
            cs = slice(ch * 128, (ch + 1) * 128)
            self.tr(pT[:, 0:128], KR[:, ch, 0:128], c["ident"][:], [KR, c["ident"]], [pT])
            self.tr(pT[:, 128:256], vb[:, cs], c["ident"][:], [vb, c["ident"]], [pT])
            self.tr(pT[:, 256:384], khat[:, cs], c["ident"][:], [khat, c["ident"]], [pT])
            self.tr(pT[:, 384:512], nbh[:, cs], c["ident"][:], [nbh, c["ident"]], [pT])
            TMs = self.scr("D_TMs", [128, 4, 128], BF16)
            self.cp("scalar", TMs[:], pT[:, 0:512].rearrange("p (a t) -> p a t", a=4), [pT], [TMs])
            yield
            rpT = self.scr("D_rpT", [128, 128], BF16)
            GT = self.scr("D_GT", [128, 64], BF16)
            Hs = self.scr("D_Hs", [128, 64])
            py = self.pacc[0]
            pHS = self.pacc[1]
            HS = [slice(0, 64), slice(64, 128)]
            LT, nQbT, LkT, QkT, LN, R, R2 = {}, {}, {}, {}, {}, {}, {}
            for h in range(2):
                hs = HS[h]
                p1 = self.gps()
                self.mm(p1[:, 0:256], bt[hs, cs], KR[hs, ch, :], True, True, [bt, KR], [p1])
                self.mm(p1[:, 256:512], kt[hs, cs], KR[hs, ch, :], True, True, [kt, KR], [p1])
                LT[h] = self.scr("D_LT%d" % h, [128, 128], BF16)
                nQbT[h] = self.scr("D_nQbT%d" % h, [128, 128], BF16)
                LkT[h] = self.scr("D_LkT%d" % h, [128, 128], BF16)
                QkT[h] = self.scr("D_QkT%d" % h, [128, 128], BF16)
                self.tt("vector", LT[h][:], p1[:, 0:128], m_gt[:], ALU.mult, [p1, m_gt], [LT[h]])
                self.tt("vector", LkT[h][:], p1[:, 256:384], m_gt[:], ALU.mult, [p1, m_gt], [LkT[h]])
                self.stt(nQbT[h][:], p1[:, 128:256], -1.0, m_ge[:], ALU.mult, ALU.mult, [p1, m_ge], [nQbT[h]])
                self.tt("vector", QkT[h][:], p1[:, 384:512], m_ge[:], ALU.mult, [p1, m_ge], [QkT[h]])
                yield
            for h in range(2):
                hs = HS[h]
                p2 = self.gps()
                self.mm(p2[:, 0:128], KR[hs, ch, 0:128], bt[hs, cs], True, True, [KR, bt], [p2])
                self.mm(p2[:, 128:192], LkT[h][:], TMs[:, 1, hs], True, True, [LkT[h], TMs], [p2])
                LN[h] = self.scr("D_LN%d" % h, [128, 128], BF16)
                self.tt("vector", LN[h][:], p2[:, 0:128], mN_gt[:], ALU.mult, [p2, mN_gt], [LN[h]])
                R[h] = self.scr("D_R%d" % h, [128, 128], BF16)
                self.cp("gpsimd", R[h][:, 0:64], TMs[:, 0, hs], [TMs], [R[h]])
                self.cp("scalar", R[h][:, 64:128], p2[:, 128:192], [p2], [R[h]])
                yield
            Rc, Rn, PTc, PNc = {}, {}, {}, {}
            for h in range(2):
                pr_ = self.gps()
                self.mm(pr_[:, 0:128], LT[h][:], R[h][:], True, True, [LT[h], R[h]], [pr_])
                R2[h] = self.scr("D_Rb%d" % h, [128, 128], BF16)
                self.tt("vector", R2[h][:], R[h][:], pr_[:, 0:128], ALU.subtract, [R[h], pr_], [R2[h]])
                Rc[h], Rn[h] = R2[h], R[h]
                PTc[h], PNc[h] = LT[h], LN[h]
            yield
            for j in range(1, 7):
                PTn, PNn = {}, {}
                for h in range(2):
                    PTn[h] = self.scr("D_PT%d_%d" % (h, j % 2), [128, 128], BF16)
                    PNn[h] = self.scr("D_PN%d_%d" % (h, j % 2), [128, 128], BF16)
                    pp = self.gps()
                    self.mm(pp[:, 0:128], PNc[h][:], PTc[h][:], True, True, [PNc[h], PTc[h]], [pp])
                    if j < 6:
                        self.mm(pp[:, 128:256], PTc[h][:], PNc[h][:], True, True, [PNc[h], PTc[h]], [pp])
                        self.cp("scalar", PTn[h][:], pp[:, 0:128], [pp], [PTn[h]])
                        self.cp("vector", PNn[h][:], pp[:, 128:256], [pp], [PNn[h]])
                    else:
                        self.cp("scalar", PTn[h][:], pp[:, 0:128], [pp], [PTn[h]])
                yield
                for h in range(2):
                    pr_ = self.gps()
                    self.mm(pr_[:, 0:128], PTn[h][:], Rc[h][:], True, True, [PTn[h], Rc[h]], [pr_])
                    self.tt("vector", Rn[h][:], Rc[h][:], pr_[:, 0:128], ALU.add, [Rc[h], pr_], [Rn[h]])
                    Rc[h], Rn[h] = Rn[h], Rc[h]
                    PTc[h], PNc[h] = PTn[h], PNn[h]
                yield
            for h in range(2):
                hs = HS[h]
                Rch = Rc[h]
                pr_ = self.gps()
                self.mm(pr_[hs, 0:128], Rch[:, 0:64], nQbT[h][:], True, True, [Rch, nQbT[h]], [pr_])
                self.tt("vector", rpT[hs, :], pr_[hs, 0:128], KR[hs, ch, 128:256], ALU.add, [pr_, KR], [rpT])
                self.mm(py[hs, 0:128], TMs[:, 1, hs], QkT[h][:], True, False, [TMs, QkT[h]], [(py, h)])
                self.mm(py[hs, 0:128], Rch[:, 64:128], nQbT[h][:], False, False, [Rch, nQbT[h]], [(py, h)])
                self.mm(py[hs, 0:128], self.Sstb[hs, :], rpT[hs, :], False, True, [self.Sstb, rpT], [(py, h)])
                pg = self.gps()
                self.mm(pg[hs, 0:64], Rch[:, 0:64], TMs[:, 3, hs], True, True, [Rch, TMs], [pg])
                self.stt(GT[hs, :], c["identf"][hs, h * 64:(h + 1) * 64], Ecl[hs, ch * 128 + 127:ch * 128 + 128],
                         pg[hs, 0:64], ALU.mult, ALU.add, [c["identf"], Ecl, pg], [GT])
                self.mm(pHS[hs, 0:64], TMs[:, 2, hs], TMs[:, 1, hs], True, False, [TMs], [(pHS, h)])
                self.mm(pHS[hs, 0:64], TMs[:, 3, hs], Rch[:, 64:128], False, True, [TMs, Rch], [(pHS, h)])
                self.cp("scalar", Hs[hs, :], pHS[hs, 0:64], [(pHS, h)], [Hs])
                self.mm(pHS[hs, 64:128], GT[hs, :], self.Sstb[hs, :], True, True, [GT, self.Sstb], [(pHS, h)])
                self.tt("vector", self.Sst[hs, :], pHS[hs, 64:128], Hs[hs, :], ALU.add, [(pHS, h), Hs], [self.Sst])
                yield
            self.cp("vector", self.Sstb[:], self.Sst[:], [self.Sst], [self.Sstb])
            self.cp("scalar", yraw[:, cs], py[:, 0:128], [py], [yraw])
        ysq = sqb
        self.tt("gpsimd", ysq[:], yraw[:], yraw[:], ALU.mult, [yraw], [ysq])
        pq2 = self.gps()
        self.mm(pq2[:], c["blk"][:], ysq[:], True, True, [c["blk"], ysq], [pq2])
        rs2 = rn
        self.act(rs2[:], pq2[:], AF.Ln, [pq2], [rs2], scale=1.0 / 64, bias=EPS)
        self.act(rs2[:], rs2[:], AF.Exp, [rs2], [rs2], scale=-0.5)
        self.stt(yraw[:], yraw[:], pc[:, PC["ln_g"]:PC["ln_g"] + 1], rs2[:], ALU.mult, ALU.mult, [yraw, pc, rs2],
                 [yraw])
        self.tt("vector", yraw[:], yraw[:], bon[:], ALU.add, [yraw, bon], [yraw])
        self.tt("gpsimd", mixo[:, 3, :], yraw[:], sz[:], ALU.mult, [yraw, sz], [(mixo, 3)])


def build_final(SEQ):
    Bd = Builder(SEQ, False)
    nc = Bd.nc
    x1 = Bd.dram("x1", [SEQ, 1024], F32, "ExternalInput")
    mp = Bd.dram("mprev", [1280, SEQ], BF16, "ExternalInput")
    wout_d = Bd.dram("wout", [1280, 1024], F32, "ExternalInput")
    out = Bd.dram("out", [SEQ, 1024], F32, "ExternalOutput")
    pP = [Bd.psum("pP0", [128, 512]), Bd.psum("pP1", [128, 512])]
    wo = Bd.sb("wo", [128, 10, 1024], BF16)
    wov = wout_d.rearrange("(c p) n -> p c n", p=128)
    for c in range(10):
        Bd.dma("gpsimd", wo[:, c, :], wov[:, c, :], w=[(wo, c)])
    xts = [Bd.sb("xt%d" % i, [128, 4, 1024]) for i in range(2)]
    mpvs = [Bd.sb("mpv%d" % i, [128, 10, 512], BF16) for i in range(2)]
    outs = []
    for mc in range(SEQ // 512):
        t0 = mc * 512
        xt = xts[mc % 2]
        mpv = mpvs[mc % 2]
        Bd.dma("sync", xt[:], x1[t0:t0 + 512, :].rearrange("(tt p) d -> p tt d", p=128), w=[xt])
        Bd.dma("sync", mpv[:], mp[:, t0:t0 + 512].rearrange("(c p) t -> p c t", p=128), w=[mpv])
        for tt in range(4):
            for hf in range(2):
                p = pP[hf]
                for c in range(10):
                    Bd.mm(p[:], mpv[:, c, tt * 128:(tt + 1) * 128], wo[:, c, hf * 512:(hf + 1) * 512],
                          c == 0, c == 9, [mpv, (wo, c)], [p])
                Bd.tt("vector", xt[:, tt, hf * 512:(hf + 1) * 512], xt[:, tt, hf * 512:(hf + 1) * 512],
                      p[:], ALU.add, [xt, p], [xt])
        o = Bd.dma("sync", out[t0:t0 + 512, :].rearrange("(tt p) d -> p tt d", p=128), xt[:], r=[xt], w=["out_d"])
        outs.append(o)
    Bd.S.emit(final_wait_ops=outs)
    return nc


_CACHE = {}


def _get_prog(key, fn):
    if key not in _CACHE:
        _CACHE[key] = fn()
    return _CACHE[key]


def kernel_unfused(**inputs):
    inp = {k: np.asarray(v) for k, v in inputs.items()}
    x = inp["x"]
    BATCH, SEQ, _ = x.shape
    n = 8
    cores = [(b, hh) for b in range(BATCH) for hh in range(2)]
    xcur = [np.ascontiguousarray(x[b]) for b in range(BATCH)]
    mprev = None
    for l in range(2):
        has_prev = l > 0
        nc = _get_prog(("layer", SEQ, has_prev), lambda: Builder(SEQ, has_prev).build())
        in_maps = []
        for (b, hh) in cores:
            d = host_layer_params(inp, l, hh)
            d["xin"] = xcur[b]
            d["mem"] = np.ascontiguousarray(inp["mem"][b])
            if has_prev:
                d["mprev"] = mprev[b]
                d["wout"] = np.ascontiguousarray(inp["w_out"][l - 1])
            in_maps.append(d)
        res = run_bass_kernel_spmd(nc, in_maps, core_ids=list(range(n)))
        new_m = []
        for b in range(BATCH):
            full = np.zeros((1280, SEQ), dtype=ml_dtypes.bfloat16)
            for hh in range(2):
                m = np.asarray(res.results[b * 2 + hh]["mixed"])
                for g in range(5):
                    full[g * 256 + hh * 128:g * 256 + hh * 128 + 128] = m[g * 128:(g + 1) * 128]
            new_m.append(full)
            if has_prev:
                xcur[b] = np.asarray(res.results[b * 2]["x1out"])
        mprev = new_m
    ncf = _get_prog(("final", SEQ), lambda: build_final(SEQ // 2))
    in_maps = []
    H = SEQ // 2
    for (b, hh) in cores:
        in_maps.append({"x1": np.ascontiguousarray(xcur[b][hh * H:(hh + 1) * H]),
                        "mprev": np.ascontiguousarray(mprev[b][:, hh * H:(hh + 1) * H]),
                        "wout": np.ascontiguousarray(inp["w_out"][1])})
    res = run_bass_kernel_spmd(ncf, in_maps, core_ids=list(range(n)))
    out = np.zeros((BATCH, SEQ, 1024), np.float32)
    for i, (b, hh) in enumerate(cores):
        out[b, hh * H:(hh + 1) * H] = np.asarray(res.results[i]["out"])
    return out


def kernel(**inputs):
    inp = {k: np.asarray(v) for k, v in inputs.items()}
    x = inp["x"]
    BATCH, SEQ, _ = x.shape
    nc = _get_prog(("fused", SEQ), lambda: Builder(SEQ, False).build_fused())
    per = {}
    for l in range(2):
        for hh in range(2):
            d = host_layer_params(inp, l, hh)
            for k, v in d.items():
                per["%s_%d%d" % (k, l, hh)] = v
    in_maps = []
    for b in range(BATCH):
        d = dict(per)
        d["xin"] = np.ascontiguousarray(x[b])
        d["mem"] = np.ascontiguousarray(inp["mem"][b])
        d["wout0"] = np.ascontiguousarray(inp["w_out"][0])
        d["wout1"] = np.ascontiguousarray(inp["w_out"][1])
        in_maps.append(d)
    res = run_bass_kernel_spmd(nc, in_maps, core_ids=list(range(BATCH)))
    out = np.stack([np.asarray(res.results[b]["out"]) for b in range(BATCH)], axis=0)
    return out.astype(np.float32)
```

```python
from contextlib import ExitStack
import numpy as np
import ml_dtypes
import concourse.bass as bass
import concourse.mybir as mybir
from concourse.bass_utils import run_bass_kernel_spmd

F32 = mybir.dt.float32
BF16 = mybir.dt.bfloat16
ALU = mybir.AluOpType
AF = mybir.ActivationFunctionType
AX = mybir.AxisListType

ENGINES = ("tensor", "vector", "scalar", "gpsimd", "sync")
SEM_CAP = 30000
EPS = 1e-6


class _Op:
    __slots__ = ("eng", "fn", "idx", "deps", "signal", "is_dma", "sem", "val", "pre_wait")

    def __init__(self, eng, fn, is_dma):
        self.eng = eng
        self.fn = fn
        self.is_dma = is_dma
        self.deps = []
        self.signal = False
        self.sem = None
        self.val = None
        self.pre_wait = None


class Tl:
    def __init__(self, name, t):
        self.name = name
        self.t = t

    def __getitem__(self, idx):
        return self.t[idx]


def _norm(rs):
    out = []
    for r in rs:
        if isinstance(r, tuple):
            a, k = r
        else:
            a, k = r, None
        if isinstance(a, Tl):
            a = a.name
        if a.startswith("scr"):
            k = None
        out.append((a, k))
    return out


class Sched:
    def __init__(self, nc, stack, n_dma_sems=16):
        self.nc = nc
        self.stack = stack
        self.ops = {e: [] for e in ENGINES}
        self.state = {}
        self.n_dma_sems = n_dma_sems

    def _entries(self, name, key):
        d = self.state.setdefault(name, {})
        if key is None:
            return list(d.values())
        res = []
        if key in d:
            res.append(d[key])
        if None in d:
            res.append(d[None])
        return res

    def add(self, eng, fn, reads=(), writes=(), dma=False):
        reads = _norm(reads)
        writes = _norm(writes)
        op = _Op(eng, fn, dma)
        deps = []
        for (name, key) in reads:
            for ent in self._entries(name, key):
                if ent[0] is not None:
                    deps.append(ent[0])
                if name[0] == "p" and name[1].isupper():
                    deps.extend(o_ for o_ in ent[1] if o_.eng != eng)
        for (name, key) in writes:
            for ent in self._entries(name, key):
                if ent[0] is not None:
                    deps.append(ent[0])
                deps.extend(ent[1])
        for (name, key) in reads:
            d = self.state.setdefault(name, {})
            if key is None:
                if not d:
                    d[None] = [None, []]
                for ent in d.values():
                    ent[1].append(op)
            else:
                if key not in d:
                    d[key] = [None, []]
                d[key][1].append(op)
        for (name, key) in writes:
            d = self.state.setdefault(name, {})
            if key is None:
                d.clear()
                d[None] = [op, []]
            else:
                d[key] = [op, []]
        op.idx = len(self.ops[eng])
        best = {}
        dl = []
        for dop in deps:
            if dop is op:
                continue
            if dop.is_dma:
                if dop not in dl:
                    dl.append(dop)
            else:
                if dop.eng == "tensor" and eng == "tensor" and not dma:
                    continue
                b = best.get(dop.eng)
                if b is None or dop.idx > b.idx:
                    best[dop.eng] = dop
        op.deps = dl + list(best.values())
        for dop in op.deps:
            dop.signal = True
        self.ops[eng].append(op)
        return op

    def emit(self, final_wait_ops=()):
        nc = self.nc
        for eng in ENGINES:
            cnt = 0
            sem = None
            for op in self.ops[eng]:
                if op.is_dma:
                    continue
                if op.signal:
                    if sem is None or cnt >= SEM_CAP:
                        sem = self.stack.enter_context(nc.semaphore(f"s_{eng}_{op.idx}"))
                        cnt = 0
                    cnt += 1
                    op.sem = sem
                    op.val = cnt
        for eng in ENGINES:
            qpool = []
            k = 0
            for op in self.ops[eng]:
                if not op.is_dma:
                    continue
                if len(qpool) < self.n_dma_sems:
                    s = self.stack.enter_context(nc.semaphore(f"d_{eng}_{len(qpool)}"))
                    qpool.append([s, 0])
                    ent = qpool[-1]
                else:
                    ent = qpool[k % self.n_dma_sems]
                    if ent[1] + 16 > SEM_CAP:
                        ent[0] = self.stack.enter_context(nc.semaphore(f"d_{eng}_x{k}"))
                        ent[1] = 0
                if ent[1] > 0:
                    op.pre_wait = (ent[0], ent[1])
                ent[1] += 16
                op.sem = ent[0]
                op.val = ent[1]
                k += 1
        sched = self

        def run(eng_name, e):
            seen = {}
            for op in sched.ops[eng_name]:
                waits = []
                if op.pre_wait is not None:
                    waits.append(op.pre_wait)
                for dop in op.deps:
                    waits.append((dop.sem, dop.val))
                for (s, v) in waits:
                    key = id(s)
                    if seen.get(key, 0) >= v:
                        continue
                    seen[key] = v
                    e.wait_ge(s, v)
                ins = op.fn(e)
                if op.is_dma:
                    ins.then_inc(op.sem, 16)
                elif op.signal:
                    ins.then_inc(op.sem, 1)
            if eng_name == "sync":
                for fop in final_wait_ops:
                    e.wait_ge(fop.sem, fop.val)

        with nc.Block() as block:
            @block.sync
            def _(e):
                run("sync", e)

            @block.tensor
            def _(e):
                run("tensor", e)

            @block.vector
            def _(e):
                run("vector", e)

            @block.scalar
            def _(e):
                run("scalar", e)

            @block.gpsimd
            def _(e):
                run("gpsimd", e)


D_MODEL = 1024
FM_GROUPS = ["Au", "Az", "Bz", "Cq", "Ck", "Co", "Cz", "Dr", "Dk", "Dv", "Dz", "Mz",
             "Bf", "Ci", "Cf0", "Cf1", "Dw", "Da"]
FM_W = {g: 128 for g in FM_GROUPS}
FM_W["Dw"] = 16
FM_W["Da"] = 16
FM_OFF = {}
_o = 0
for _g in FM_GROUPS:
    FM_OFF[_g] = _o
    _o += FM_W[_g]
NFM = _o
TM_GROUPS = ["Av", "Bq", "Bk", "Bv", "Cv", "Mq"]
NTM = 768
NW = NFM + NTM
PC = {n: i for i, n in enumerate([
    "cq0", "cq1", "cq2", "cq3", "ck0", "ck1", "ck2", "ck3", "cqb", "ckb",
    "mu_r", "mu_k", "mu_v", "mu_z", "mu_w", "mu_a",
    "k_k", "k_a", "a0", "w0", "r_k", "ln_g", "out_g",
    "fb_B", "ib_C", "fb_C0", "fb_C1"])}
NPC = len(PC)
PR_SGU_G, PR_BQG, PR_BKG, PR_MQG, PR_MKG, PR_SGUB = 0, 128, 256, 384, 512, 640
NPR = 768

A_OFF = 0
B_OFF = 768
C_OFF = 768 + 1028
D_OFF = C_OFF + 1288
M_OFF = D_OFF + 1056


def host_layer_params(inp, l, hh):
    f32 = np.float32
    w_in = inp["w_in"][l]
    hs = [2 * hh, 2 * hh + 1]

    def hcols(base):
        return np.concatenate([np.arange(base + h * 64, base + h * 64 + 64) for h in hs])

    cols = {}
    cols["Au"] = hcols(A_OFF)
    cols["Av"] = hcols(A_OFF + 256)
    cols["Az"] = hcols(A_OFF + 512)
    cols["Bq"] = hcols(B_OFF)
    cols["Bk"] = hcols(B_OFF + 256)
    cols["Bv"] = hcols(B_OFF + 512)
    bf = B_OFF + 768
    cols["Bf"] = np.concatenate([np.full(64, bf + hs[1]), np.full(64, bf + hs[0])])
    cols["Bz"] = hcols(B_OFF + 772)
    cols["Cq"] = hcols(C_OFF)
    cols["Ck"] = hcols(C_OFF + 256)
    cols["Cv"] = hcols(C_OFF + 512)
    ci = C_OFF + 768
    cols["Ci"] = np.concatenate([np.full(64, ci + hs[0]), np.full(64, ci + hs[1])])
    cols["Cf0"] = np.full(128, ci + 4 + hs[0])
    cols["Cf1"] = np.full(128, ci + 4 + hs[1])
    cols["Co"] = hcols(C_OFF + 776)
    cols["Cz"] = hcols(C_OFF + 1032)
    cols["Dr"] = hcols(D_OFF)
    cols["Dw"] = np.arange(D_OFF + 256, D_OFF + 272)
    cols["Dk"] = hcols(D_OFF + 272)
    cols["Dv"] = hcols(D_OFF + 528)
    cols["Da"] = np.arange(D_OFF + 784, D_OFF + 800)
    cols["Dz"] = hcols(D_OFF + 800)
    cols["Mq"] = hcols(M_OFF)
    cols["Mz"] = hcols(M_OFF + 256)
    allc = np.concatenate([cols[g] for g in FM_GROUPS] + [cols[g] for g in TM_GROUPS])
    wcat = np.ascontiguousarray(w_in[:, allc])

    hc = hcols(0)
    pc = np.zeros((128, NPC), f32)
    cw = inp["mlstm_conv_w"][l]
    cb = inp["mlstm_conv_b"][l]
    for j in range(4):
        pc[:, PC["cq%d" % j]] = cw[j, hc]
        pc[:, PC["ck%d" % j]] = cw[j, 256 + hc]
    pc[:, PC["cqb"]] = cb[hc]
    pc[:, PC["ckb"]] = cb[256 + hc]
    mu = inp["rwkv_mu"][l]
    pc[:, PC["mu_r"]] = mu[hc]
    pc[:16, PC["mu_w"]] = mu[256:272]
    pc[:, PC["mu_k"]] = mu[272 + hc]
    pc[:, PC["mu_v"]] = mu[528 + hc]
    pc[:16, PC["mu_a"]] = mu[784:800]
    pc[:, PC["mu_z"]] = mu[800 + hc]
    pc[:, PC["k_k"]] = inp["rwkv_k_k"][l][hc]
    pc[:, PC["k_a"]] = inp["rwkv_k_a"][l][hc]
    pc[:, PC["a0"]] = inp["rwkv_a0"][l][hc]
    pc[:, PC["w0"]] = inp["rwkv_w0"][l][hc]
    pc[:, PC["r_k"]] = inp["rwkv_r_k"][l].reshape(-1)[hc]
    pc[:, PC["ln_g"]] = inp["rwkv_ln_g"][l][hc]
    pc[:, PC["out_g"]] = inp["mlstm_out_g"][l][hc]
    fb = inp["fox_f_b"][l]
    pc[:, PC["fb_B"]] = np.concatenate([np.full(64, fb[hs[1]]), np.full(64, fb[hs[0]])])
    ib = inp["mlstm_i_b"][l]
    pc[:, PC["ib_C"]] = np.concatenate([np.full(64, ib[hs[0]]), np.full(64, ib[hs[1]])])
    fbc = inp["mlstm_f_b"][l]
    pc[:, PC["fb_C0"]] = fbc[hs[0]]
    pc[:, PC["fb_C1"]] = fbc[hs[1]]

    pr = np.zeros((128, NPR), f32)
    pr[:, PR_SGU_G:PR_SGU_G + 128] = inp["sgu_norm_g"][l][hc][None, :]
    pr[:, PR_BQG:PR_BQG + 128] = np.tile(inp["fox_q_g"][l], 2)[None, :]
    pr[:, PR_BKG:PR_BKG + 128] = np.tile(inp["fox_k_g"][l], 2)[None, :]
    pr[:, PR_MQG:PR_MQG + 128] = np.tile(inp["mem_q_g"][l], 2)[None, :]
    pr[:, PR_MKG:PR_MKG + 128] = np.tile(inp["mem_k_g"][l], 2)[None, :]
    sb_ = inp["sgu_b"][l]
    pr[:64, PR_SGUB:PR_SGUB + 128] = sb_[hs[0]][None, :]
    pr[64:, PR_SGUB:PR_SGUB + 128] = sb_[hs[1]][None, :]

    d = {
        "wcat": wcat,
        "pc": pc,
        "pr": pr,
        "ng": np.ascontiguousarray(inp["norm_g"][l].reshape(8, 128).T),
        "memg": np.ascontiguousarray(inp["mem_norm_g"][l].reshape(8, 128).T),
        "wkv": np.ascontiguousarray(np.concatenate(
            [inp["mem_w_kv"][l][:, hc], inp["mem_w_kv"][l][:, 256 + hc]], axis=1)),
        "w2": np.ascontiguousarray(inp["rwkv_w2"][l][:, hc]),
        "a2": np.ascontiguousarray(inp["rwkv_a2"][l][:, hc]),
        "sguw": np.ascontiguousarray(inp["sgu_w"][l][hs]),
    }
    return d


class Builder:
    def __init__(self, SEQ, has_prev, branches="ABCDM"):
        self.SEQ = SEQ
        self.has_prev = has_prev
        self.branches = branches
        self.nc = bass.Bass("TRN2", target_bir_lowering=False)
        self.st = ExitStack()
        self.S = Sched(self.nc, self.st)
        self.ps_rr = 0

    def dram(self, name, shape, dt, kind):
        return self.nc.dram_tensor(name, shape, dt, kind=kind).ap()

    def sb(self, name, shape, dt=F32):
        if not hasattr(self, "_tiles"):
            self._tiles = {}
        if name not in self._tiles:
            self._tiles[name] = Tl(name, self.st.enter_context(self.nc.sbuf_tensor(name, shape, dt)))
        return self._tiles[name]

    def psum(self, name, shape, dt=F32):
        if not hasattr(self, "_tiles"):
            self._tiles = {}
        if name not in self._tiles:
            self._tiles[name] = Tl(name, self.st.enter_context(self.nc.psum_tensor(name, shape, dt)))
        return self._tiles[name]

    def gps(self):
        p = self.gp[self.ps_rr % len(self.gp)]
        self.ps_rr += 1
        return p

    def op(self, eng, fn, r=(), w=()):
        return self.S.add(eng, fn, reads=r, writes=w)

    def dma(self, eng, out, in_, r=(), w=()):
        return self.S.add(eng, lambda e: e.dma_start(out=out, in_=in_), reads=r, writes=w, dma=True)

    def mm(self, out, lhsT, rhs, start, stop, r, w):
        return self.S.add("tensor", lambda e: e.matmul(out, lhsT=lhsT, rhs=rhs, start=start, stop=stop),
                          reads=r, writes=w)

    def tr(self, out, in_, ident, r, w):
        return self.S.add("tensor", lambda e: e.transpose(out, in_, ident), reads=r, writes=w)

    def act(self, out, in_, func, r, w, bias=None, scale=None, accum_out=None, eng="scalar"):
        kw = {}
        if bias is not None:
            kw["bias"] = bias
        if scale is not None:
            kw["scale"] = scale
        if accum_out is not None:
            kw["accum_out"] = accum_out
        return self.S.add("scalar", lambda e: e.activation(out=out, in_=in_, func=func, **kw), reads=r, writes=w)

    def tt(self, eng, out, in0, in1, op, r, w):
        return self.S.add(eng, lambda e: e.tensor_tensor(out=out, in0=in0, in1=in1, op=op), reads=r, writes=w)

    def ts(self, eng, out, in0, s1, op0, r, w, s2=None, op1=None):
        if op1 is None:
            return self.S.add(eng, lambda e: e.tensor_scalar(out=out, in0=in0, scalar1=s1, scalar2=None, op0=op0),
                              reads=r, writes=w)
        return self.S.add(eng, lambda e: e.tensor_scalar(out=out, in0=in0, scalar1=s1, scalar2=s2, op0=op0, op1=op1),
                          reads=r, writes=w)

    def stt(self, out, in0, scalar, in1, op0, op1, r, w):
        return self.S.add("vector", lambda e: e.scalar_tensor_tensor(out=out, in0=in0, scalar=scalar, in1=in1,
                                                                      op0=op0, op1=op1), reads=r, writes=w)

    def cp(self, eng, out, in_, r, w):
        if eng == "scalar":
            return self.S.add("scalar", lambda e: e.copy(out=out, in_=in_), reads=r, writes=w)
        return self.S.add(eng, lambda e: e.tensor_copy(out, in_), reads=r, writes=w)

    def recip(self, out, in_, r, w):
        return self.S.add("vector", lambda e: e.reciprocal(out, in_), reads=r, writes=w)

    def memset(self, eng, ap, val, w):
        return self.S.add(eng, lambda e: e.memset(ap, val), writes=w)

    def asel(self, out, in_, cmp, w, r=(), fill=0.0, base=0, cm=-1, pat=None):
        pat = pat or [[1, 128]]
        return self.S.add("gpsimd", lambda e: e.affine_select(out=out, in_=in_, pattern=pat, compare_op=cmp,
                                                              fill=fill, base=base, channel_multiplier=cm),
                          reads=r, writes=w)

    def build(self):
        cfg = dict(tag="", xmode="outproj" if self.has_prev else "ext")
        self.last_out = []
        self.run_pass(cfg)
        self.S.emit(final_wait_ops=self.last_out)
        return self.nc

    def build_fused(self):
        SEQ = self.SEQ
        self.last_out = []
        self.x_ext = self.dram("xin", [SEQ, 1024], F32, "ExternalInput")
        self.mem_ext = self.dram("mem", [256, 1024], F32, "ExternalInput")
        self.mixs = [self.dram("mixs%d" % l, [1280, SEQ], BF16, "Internal") for l in range(2)]
        self.x1s = self.dram("x1s", [SEQ, 1024], F32, "Internal")
        self.wouts = [self.dram("wout%d" % l, [1280, 1024], F32, "ExternalInput") for l in range(2)]
        for l in range(2):
            if l == 1:
                self.begin("oproj")
                self.outproj_pass("ext", 0, self.x1s, "x1s", False)
            for hh in range(2):
                xmode = "ext" if l == 0 else "x1"
                self.run_pass(dict(tag="_%d%d" % (l, hh), xmode=xmode, fused=True, l=l, hh=hh))
        out_d = self.dram("out", [SEQ, 1024], F32, "ExternalOutput")
        self.begin("oproj")
        self.outproj_pass("x1", 1, out_d, "out_d", True)
        self.S.emit(final_wait_ops=self.last_out)
        return self.nc

    def outproj_pass(self, src_kind, l, dst, dst_name, final):
        SEQ = self.SEQ
        pP = [self.psum("pP0", [128, 512]), self.psum("pP1", [128, 512])]
        wb = self.sb("wb", [128, 8, NW], BF16)
        wo = Tl("wb", wb.t[:, :, :].rearrange("p a b -> p (a b)")[:, 0:10240].rearrange("p (c n) -> p c n", c=10))
        wov = self.wouts[l].rearrange("(c p) n -> p c n", p=128)
        for c in range(10):
            self.dma("gpsimd", wo[:, c, :], wov[:, c, :], w=[wo])
        xts = [self.sb("xt%d" % i, [128, 1024]) for i in range(2)]
        tm_ = self.sb("tm", [128, 4, 768], BF16)
        flat = tm_.t[:, :, :].rearrange("p a b -> p (a b)")
        mpvs = [Tl("tm", flat[:, 0:1280].rearrange("p (c t) -> p c t", c=10)),
                Tl("tm", flat[:, 1280:2560].rearrange("p (c t) -> p c t", c=10))]
        for ti in range(SEQ // 128):
            xt = xts[ti % 2]
            mpv = mpvs[ti % 2]
            r0 = ti * 128
            if src_kind == "ext":
                self.dma("sync", xt[:], self.x_ext[r0:r0 + 128, :], w=[xt])
            else:
                self.dma("sync", xt[:], self.x1s[r0:r0 + 128, :], r=[("x1s", ti)], w=[xt])
            self.dma("sync", mpv[:], self.mixs[l][:, r0:r0 + 128].rearrange("(c p) t -> p c t", p=128),
                     r=[("mixs%d" % l, (0, ti // 4)), ("mixs%d" % l, (1, ti // 4))], w=[mpv])
            for hf in range(2):
                p = pP[hf]
                for c in range(10):
                    self.mm(p[:], mpv[:, c, :], wo[:, c, hf * 512:(hf + 1) * 512], c == 0, c == 9,
                            [mpv, wo], [p])
                eng = "vector" if hf == 0 else "gpsimd"
                if eng == "gpsimd":
                    tmpo = self.scr("op_tmp", [128, 512])
                    self.cp("scalar", tmpo[:], p[:], [p], [tmpo])
                    self.tt("gpsimd", xt[:, hf * 512:(hf + 1) * 512], xt[:, hf * 512:(hf + 1) * 512], tmpo[:],
                            ALU.add, [xt, tmpo], [xt])
                else:
                    self.tt("vector", xt[:, hf * 512:(hf + 1) * 512], xt[:, hf * 512:(hf + 1) * 512], p[:],
                            ALU.add, [xt, p], [xt])
            o = self.dma("sync", dst[r0:r0 + 128, :], xt[:], r=[xt], w=[(dst_name, ti)])
            if final:
                self.last_out.append(o)

    def run_pass(self, cfg):
        nc = self.nc
        SEQ = self.SEQ
        NMC = SEQ // 512
        NCH = SEQ // 128
        tag = cfg["tag"]
        xmode = cfg["xmode"]
        fused = cfg.get("fused", False)
        has_prev = xmode == "outproj"
        wcat = self.dram("wcat" + tag, [1024, NW], F32, "ExternalInput")
        pc_d = self.dram("pc" + tag, [128, NPC], F32, "ExternalInput")
        pr_d = self.dram("pr" + tag, [128, NPR], F32, "ExternalInput")
        ng_d = self.dram("ng" + tag, [128, 8], F32, "ExternalInput")
        memg_d = self.dram("memg" + tag, [128, 8], F32, "ExternalInput")
        wkv_d = self.dram("wkv" + tag, [1024, 256], F32, "ExternalInput")
        w2_d = self.dram("w2" + tag, [16, 128], F32, "ExternalInput")
        a2_d = self.dram("a2" + tag, [16, 128], F32, "ExternalInput")
        sguw_d = self.dram("sguw" + tag, [2, 128, 128], F32, "ExternalInput")
        if fused:
            l, hh = cfg["l"], cfg["hh"]
            xin = self.x_ext
            mem_d = self.mem_ext
            mixed_d = self.mixs[l].rearrange("(g two p) t -> two p g t", two=2, p=128)[hh]
            mix_name = "mixs%d" % l
            mix_key = lambda mc: (hh, mc)
            if has_prev:
                mprev_d = self.mixs[0]
                wout_d = self.wouts[0]
                x1_d = self.x1s
        else:
            xin = self.dram("xin", [SEQ, 1024], F32, "ExternalInput")
            mem_d = self.dram("mem", [256, 1024], F32, "ExternalInput")
            mixed_d = self.dram("mixed", [640, SEQ], BF16, "ExternalOutput").rearrange("(g p) t -> p g t", p=128)
            mix_name = "mixed_d"
            mix_key = lambda mc: mc
            if has_prev:
                mprev_d = self.dram("mprev", [1280, SEQ], BF16, "ExternalInput")
                wout_d = self.dram("wout", [1280, 1024], F32, "ExternalInput")
                x1_d = self.dram("x1out", [SEQ, 1024], F32, "ExternalOutput")

        pT = self.psum("pT", [128, 1024], BF16)
        pP = [self.psum("pP0", [128, 512]), self.psum("pP1", [128, 512])]
        self.gp = [self.psum("pG%d" % i, [128, 512]) for i in range(3)]
        pNum = self.psum("pNum", [128, 512])
        pDen = self.psum("pDen", [128, 512])

        identf = self.sb("identf", [128, 128])
        ident = self.sb("ident", [128, 128], BF16)
        ones_bf = self.sb("ones_bf", [128, 128], BF16)
        onesf = self.sb("onesf", [128, 128])
        c64f = self.sb("c64f", [128, 128])
        c64b = self.sb("c64b", [128, 128], BF16)
        blk = self.sb("blk", [128, 128], BF16)
        m_ge = self.sb("m_ge", [128, 128], BF16)
        self.memset("vector", identf[:], 1.0, [identf])
        self.asel(identf[:], identf[:], ALU.is_equal, [identf], [identf])
        self.cp("vector", ident[:], identf[:], [identf], [ident])
        self.memset("vector", ones_bf[:], 1.0, [ones_bf])
        self.memset("vector", onesf[:], 1.0, [onesf])
        self.memset("vector", c64f[:], 1.0 / 64, [c64f])
        self.memset("vector", c64b[:], 1.0 / 64, [c64b])
        self.memset("vector", blk[:], 0.0, [blk])
        self.memset("vector", blk[0:64, 0:64], 1.0, [blk])
        self.memset("vector", blk[64:128, 64:128], 1.0, [blk])
        self.memset("vector", m_ge[:], 1.0, [m_ge])
        self.asel(m_ge[:], m_ge[:], ALU.is_ge, [m_ge], [m_ge])

        pc = self.sb("pcs", [128, NPC])
        pr = self.sb("prs", [128, NPR])
        ng = self.sb("ngs", [128, 8])
        memg = self.sb("memgs", [128, 8])
        self.dma("sync", pc[:], pc_d, w=[pc])
        self.dma("sync", pr[:], pr_d, w=[pr])
        self.dma("sync", ng[:], ng_d, w=[ng])
        self.dma("sync", memg[:], memg_d, w=[memg])
        omm = self.sb("omm", [128, 6])
        self.ts("vector", omm[:], pc[:, PC["mu_r"]:PC["mu_r"] + 6], -1.0, ALU.mult, [pc], [omm], s2=1.0, op1=ALU.add)
        nfbB = self.sb("nfbB", [128, 1])
        self.ts("vector", nfbB[:], pc[:, PC["fb_B"]:PC["fb_B"] + 1], -1.0, ALU.mult, [pc], [nfbB])
        nfbC = self.sb("nfbC", [128, 2])
        self.ts("vector", nfbC[:], pc[:, PC["fb_C0"]:PC["fb_C0"] + 2], -1.0, ALU.mult, [pc], [nfbC])
        gq8 = self.sb("gq8", [128, 128])
        self.ts("vector", gq8[:], pr[:, PR_BQG:PR_BQG + 128], 0.125, ALU.mult, [pr], [gq8])
        gmq8 = self.sb("gmq8", [128, 128])
        self.ts("vector", gmq8[:], pr[:, PR_MQG:PR_MQG + 128], 0.125, ALU.mult, [pr], [gmq8])

        wb = self.sb("wb", [128, 8, NW], BF16)
        wv = wcat.rearrange("(c p) n -> p c n", p=128)
        for c in range(8):
            self.dma("gpsimd", wb[:, c, :], wv[:, c, :], w=[(wb, c)])
        for c in range(8):
            eng = "vector" if c % 2 == 0 else "gpsimd"
            self.ts(eng, wb[:, c, :], wb[:, c, :], ng[:, c:c + 1], ALU.mult, [(wb, c), ng], [(wb, c)])
        if has_prev:
            wo = self.sb("wo", [128, 10, 1024], BF16)
            wov = wout_d.rearrange("(c p) n -> p c n", p=128)
            for c in range(10):
                self.dma("gpsimd", wo[:, c, :], wov[:, c, :], w=[(wo, c)])
        w2s = self.sb("w2s", [16, 128])
        a2s = self.sb("a2s", [16, 128])
        self.dma("sync", w2s[:], w2_d, w=[w2s])
        self.dma("sync", a2s[:], a2_d, w=[a2s])

        wsT = self.sb("wsT", [128, 2, 128], BF16)
        self.begin("setupA")
        if "A" in self.branches:
            sgw = self.scr("sgw", [128, 2, 128])
            self.dma("sync", sgw[:], sguw_d.rearrange("h t s -> t h s"), w=[sgw])
            sgwT = self.scr("sgwT", [128, 2, 128])
            for h in range(2):
                p = self.gps()
                self.tr(p[:, 0:128], sgw[:, h, :], identf[:], [sgw, identf], [p])
                self.cp("vector", sgwT[:, h, :], p[:, 0:128], [p], [sgwT])
                self.asel(sgwT[:, h, :], sgwT[:, h, :], ALU.is_ge, [sgwT], [sgwT])
            self.cp("vector", wsT[:], sgwT[:], [sgwT], [wsT])

        kTm = self.sb("kTm", [128, 256], BF16)
        vm = self.sb("vm", [128, 2, 128], BF16)

        self.alloc_state(NCH)

        xts = [self.sb("xt0", [128, 1024]), self.sb("xt1", [128, 1024])]
        hb = self.sb("hb", [128, 1024], BF16)
        hT = self.sb("hT", [128, 8, 512], BF16)
        ss = self.sb("ss", [128, 2])
        GATED = {"Az": AF.Silu, "Bz": AF.Silu, "Cz": AF.Silu, "Mz": AF.Silu, "Co": AF.Sigmoid}
        fm = {}
        for g in FM_GROUPS:
            if g in ("Cq", "Ck"):
                fm[g] = self.sb("fm_" + g, [128, 4 + 512], BF16)
            elif g in ("Dr", "Dk", "Dv", "Dz"):
                fm[g] = self.sb("fm_" + g, [128, 2 + 512], BF16)
            elif g in ("Dw", "Da"):
                fm[g] = self.sb("fm_" + g, [16, 1 + 512])
            elif g in GATED:
                fm[g] = self.sb("fm_" + g, [128, 512], BF16)
            else:
                fm[g] = self.sb("fm_" + g, [128, 512])
        tm = self.sb("tm", [128, 4, 768], BF16)
        mixo = self.sb("mixo", [128, 5, 512], BF16)
        if "M" in self.branches:
            self.setup_mem(mem_d, wkv_d, memg, pr, ident, identf, pT, kTm, vm, xts, hT, tm, hb)
        mpvs = [Tl("tm", tm.t[:, :, :].rearrange("p a b -> p (a b)")[:, 0:1280].rearrange("p (c t) -> p c t", c=10))] * 2
        for g in ("Cq", "Ck"):
            self.memset("vector", fm[g][:, 0:3], 0.0, [(fm[g], "hist")])
        for g in ("Dr", "Dk", "Dv", "Dz"):
            self.memset("vector", fm[g][:, 0:1], 0.0, [(fm[g], "hist")])
        for g in ("Dw", "Da"):
            self.memset("vector", fm[g][:, 0:1], 0.0, [(fm[g], "hist")])

        self.consts = dict(ident=ident, identf=identf, ones_bf=ones_bf, onesf=onesf, c64f=c64f, c64b=c64b,
                           blk=blk, m_ge=m_ge, pc=pc, pr=pr, omm=omm, nfbB=nfbB, nfbC=nfbC,
                           gq8=gq8, gmq8=gmq8, wsT=wsT, kTm=kTm, vm=vm, w2s=w2s, a2s=a2s, pT=pT,
                           pNum=pNum, pDen=pDen)
        last_out = self.last_out
        xi = 0
        xstate = {"xi": 0}

        def xprep(mc):
            t0 = mc * 512
            for tt in range(4):
                xi = xstate["xi"]
                xt = xts[xi % 2]
                r0 = t0 + tt * 128
                ti = mc * 4 + tt
                if xmode == "x1":
                    self.dma("sync", xt[:], self.x1s[r0:r0 + 128, :], r=[("x1s", ti)], w=[xt])
                else:
                    self.dma("sync", xt[:], xin[r0:r0 + 128, :], w=[xt])
                if has_prev:
                    mpv = mpvs[xi % 2]
                    rr = [("mixs0", (0, mc)), ("mixs0", (1, mc))] if fused else []
                    self.dma("sync", mpv[:], mprev_d[:, r0:r0 + 128].rearrange("(c p) t -> p c t", p=128),
                             r=rr, w=[mpv])
                    for hf in range(2):
                        p = pP[hf]
                        for c in range(10):
                            self.mm(p[:], mpv[:, c, :], wo[:, c, hf * 512:(hf + 1) * 512],
                                    c == 0, c == 9, [mpv, (wo, c)], [p])
                        self.tt("vector", xt[:, hf * 512:(hf + 1) * 512], xt[:, hf * 512:(hf + 1) * 512],
                                p[:], ALU.add, [xt, p], [xt])
                    o = self.dma("sync", x1_d[r0:r0 + 128, :], xt[:], r=[xt], w=[("x1s", ti)])
                    if not fused:
                        last_out.append(o)
                xstate["xi"] = xi + 1
                self.act(hb[:], xt[:], AF.Square, [xt], [hb, ss], accum_out=ss[:, 0:1])
                self.act(ss[:, 1:2], ss[:, 0:1], AF.Ln, [ss], [ss], scale=1.0 / 1024, bias=EPS)
                self.act(ss[:, 1:2], ss[:, 1:2], AF.Exp, [ss], [ss], scale=-0.5)
                self.ts("vector", hb[:], xt[:], ss[:, 1:2], ALU.mult, [xt, ss], [hb])
                yield
                for c in range(8):
                    self.tr(pT[:, c * 128:(c + 1) * 128], hb[:, c * 128:(c + 1) * 128], ident[:],
                            [hb, ident], [pT])
                eng = "vector" if tt % 2 == 0 else "scalar"
                self.cp(eng, hT[:, :, tt * 128:(tt + 1) * 128],
                        pT[:, :].rearrange("p (c t) -> p c t", c=8), [pT], [hT])
                yield

        for _ in xprep(0):
            pass
        for mc in range(NMC):
            t0 = mc * 512
            for gi, g in enumerate(FM_GROUPS):
                p = pP[gi % 2]
                wdt = FM_W[g]
                for c in range(8):
                    self.mm(p[0:wdt, :], wb[:, c, FM_OFF[g]:FM_OFF[g] + wdt], hT[:, c, :], c == 0, c == 7,
                            [(wb, c), hT], [p])
                hist = {"Cq": 3, "Ck": 3, "Dr": 1, "Dk": 1, "Dv": 1, "Dz": 1, "Dw": 1, "Da": 1}.get(g, 0)
                dst = fm[g]
                if g in GATED:
                    self.act(dst[:, :], p[:, :], GATED[g], [p], [(dst, "cur")])
                    continue
                if hist and mc > 0:
                    self.cp("vector", dst[0:wdt, 0:hist], dst[0:wdt, 512:512 + hist], [(dst, "cur")], [(dst, "hist")])
                eng = "scalar" if gi % 2 == 0 else "vector"
                self.cp(eng, dst[0:wdt, hist:hist + 512], p[0:wdt, :], [p, (dst, "hist")], [(dst, "cur")])
            for tt in range(4):
                for hf in range(2):
                    p = pP[hf]
                    for c in range(8):
                        self.mm(p[:, 0:384], hT[:, c, tt * 128:(tt + 1) * 128],
                                wb[:, c, NFM + hf * 384:NFM + (hf + 1) * 384], c == 0, c == 7, [hT, (wb, c)], [p])
                    eng = "scalar" if hf == 0 else "vector"
                    self.cp(eng, tm[:, tt, hf * 384:(hf + 1) * 384], p[:, 0:384], [p], [(tm, tt)])
            self.zero_mix = []
            if "A" in self.branches:
                self.branch_A(mc, fm, tm, mixo)
            else:
                self.memset("gpsimd", mixo[:, 0, :], 0.0, [(mixo, 0)])
            if "B" in self.branches:
                self.branch_B(mc, fm, tm, mixo)
            else:
                self.memset("gpsimd", mixo[:, 1, :], 0.0, [(mixo, 1)])
            gens = []
            if mc + 1 < NMC:
                gens.append(("X", xprep(mc + 1)))
            if "D" in self.branches:
                gens.append(("D", self.branch_D(mc, fm, tm, mixo)))
            else:
                self.memset("gpsimd", mixo[:, 3, :], 0.0, [(mixo, 3)])
            if "C" in self.branches:
                gens.append(("C", self.branch_C(mc, fm, tm, mixo)))
            else:
                self.memset("gpsimd", mixo[:, 2, :], 0.0, [(mixo, 2)])
            while gens:
                for item in list(gens):
                    for rep in range(3 if item[0] == "D" else 1):
                        self._br = item[0]
                        try:
                            next(item[1])
                        except StopIteration:
                            gens.remove(item)
                            break
            if "M" in self.branches:
                self.branch_M(mc, fm, tm, mixo)
            else:
                self.memset("gpsimd", mixo[:, 4, :], 0.0, [(mixo, 4)])
            o = self.dma("sync", mixed_d[:, :, t0:t0 + 512], mixo[:], r=[mixo], w=[(mix_name, mix_key(mc))])
            if not fused:
                last_out.append(o)

    def head_rms_tm(self, src, dst, gain, tag, nt=4):
        sq = self.scr("hr_sq", [128, 4, 128])
        ssq = self.scr("hr_ssq", [128, 8])
        src_ap, src_r = src
        dst_ap, dst_w = dst
        g_ap, g_r = gain
        self.tt("gpsimd", sq[:, 0:nt, :], src_ap, src_ap, ALU.mult, src_r, [sq])
        ssq3 = ssq[:, 0:2 * nt].rearrange("p (t h) -> p t h", h=2)
        self.op("vector", lambda e: e.tensor_reduce(out=ssq3,
                                                     in_=sq[:, 0:nt, :].rearrange("p t (h j) -> p t h j", h=2),
                                                     axis=AX.X, op=ALU.add), [sq], [ssq])
        self.act(ssq[:, 0:2 * nt], ssq[:, 0:2 * nt], AF.Ln, [ssq], [ssq], scale=1.0 / 64, bias=EPS)
        self.act(ssq[:, 0:2 * nt], ssq[:, 0:2 * nt], AF.Exp, [ssq], [ssq], scale=-0.5)
        self.tt("vector", sq[:, 0:nt, :].rearrange("p t (h j) -> p t h j", h=2),
                src_ap.rearrange("p t (h j) -> p t h j", h=2),
                ssq3[:, :, :, None].broadcast_to([128, nt, 2, 64]), ALU.mult, src_r + [ssq], [sq])
        self.tt("vector", dst_ap, sq[:, 0:nt, :], g_ap[:, None, :].broadcast_to([128, nt, 128]), ALU.mult,
                [sq] + g_r, dst_w)

    def begin(self, br):
        self._br = br
        if not hasattr(self, "_brcount"):
            self._brcount = {}
        self._brcount.setdefault(br, {})

    def scr(self, name, shape, dt=F32):
        if not hasattr(self, "_scrmap"):
            self._scrmap = {}
            self._pools = {}
        br = getattr(self, "_br", "x")
        key = (br, name)
        if key in self._scrmap:
            return self._scrmap[key]
        esz = 4 if dt == F32 else 2
        n = 1
        for d_ in shape[1:]:
            n *= d_
        nbytes = n * esz
        cls = 256
        while cls < nbytes:
            cls *= 2
        cnt = self._brcount.setdefault(br, {})
        k = cnt.get(cls, 0)
        cnt[cls] = k + 1
        fam = "C" if br == "C" else ""
        pool = self._pools.setdefault((fam, cls), [])
        if k >= len(pool):
            pname = "scr%s%d_%d" % (fam, cls, k)
            pool.append((pname, self.st.enter_context(self.nc.sbuf_tensor(pname, [128, cls // 4], F32))))
        pname, raw = pool[k]
        h = raw if dt == F32 else raw.bitcast(dt)
        ap = h[0:shape[0], 0:n]
        if len(shape) == 3:
            ap = ap.rearrange("p (a b) -> p a b", a=shape[1])
        elif len(shape) == 4:
            ap = ap.rearrange("p (a b c) -> p a b c", a=shape[1], b=shape[2])
        t = Tl(pname, ap)
        self._scrmap[key] = t
        return t

    def gelu(self, dst_ap, dst_w, src_ap, src_r, shape, tag):
        t1 = self.scr("gl_t1" + tag, shape)
        t2 = self.scr("gl_t2" + tag, shape)
        self.tt("gpsimd", t1[:], src_ap, src_ap, ALU.mult, src_r, [t1])
        self.ts("vector", t1[:], t1[:], 0.044715, ALU.mult, [t1], [t1], s2=1.0, op1=ALU.add)
        self.tt("gpsimd", t2[:], t1[:], src_ap, ALU.mult, [t1] + src_r, [t2])
        self.act(t2[:], t2[:], AF.Sigmoid, [t2], [t2], scale=1.5957691216)
        self.tt("vector", dst_ap, t2[:], src_ap, ALU.mult, [t2] + src_r, dst_w)

    def setup_mem(self, mem_d, wkv_d, memg, pr, ident, identf, pT, kTm, vm, xts, hT, tm, hb):
        self.begin("setup")
        for c in range(2):
            self.dma("sync", xts[c][:], mem_d[c * 128:(c + 1) * 128, :], w=[xts[c]])
        wk = Tl("hT", hT[:, :, 0:256])
        mhT = Tl("hT", hT[:, :, 256:512])
        mh = Tl("tm", tm.t[:, :, :].rearrange("p a b -> p (a b)")[:, 0:2048].rearrange("p (c d) -> p c d", c=2))
        wkv_v = wkv_d.rearrange("(c p) n -> p c n", p=128)
        for c in range(8):
            self.dma("gpsimd", wk[:, c, :], wkv_v[:, c, :], w=[wk])
        for c in range(8):
            self.ts("vector", wk[:, c, :], wk[:, c, :], memg[:, c:c + 1], ALU.mult, [wk, memg], [wk])
        mss = self.scr("m_ss", [128, 2])
        for c in range(2):
            self.act(hb[:], xts[c][:], AF.Square, [xts[c]], [hb, mss], accum_out=mss[:, c:c + 1])
        self.act(mss[:], mss[:], AF.Ln, [mss], [mss], scale=1.0 / 1024, bias=EPS)
        self.act(mss[:], mss[:], AF.Exp, [mss], [mss], scale=-0.5)
        for c in range(2):
            self.ts("vector", mh[:, c, :], xts[c][:], mss[:, c:c + 1], ALU.mult, [xts[c], mss], [mh])
        for c2 in range(2):
            for c in range(8):
                self.tr(pT[:, c * 128:(c + 1) * 128], mh[:, c2, c * 128:(c + 1) * 128], ident[:], [mh, ident], [pT])
            self.cp("vector", mhT[:, :, c2 * 128:(c2 + 1) * 128], pT[:, :].rearrange("p (c t) -> p c t", c=8),
                    [pT], [mhT])
        kvt = self.scr("m_kvt", [128, 2, 256])
        for c2 in range(2):
            p = self.gps()
            for c in range(8):
                self.mm(p[:, 0:256], mhT[:, c, c2 * 128:(c2 + 1) * 128], wk[:, c, :], c == 0, c == 7, [mhT, wk], [p])
            self.cp("vector", kvt[:, c2, :], p[:, 0:256], [p], [kvt])
        self.cp("vector", vm[:], kvt[:, :, 128:256], [kvt], [vm])
        kn = self.scr("m_kn", [128, 2, 128], BF16)
        self.head_rms_tm((kvt[:, :, 0:128], [kvt]), (kn[:], [kn]), (pr[:, PR_MKG:PR_MKG + 128], [pr]), "mk", nt=2)
        for c2 in range(2):
            self.tr(pT[:, c2 * 128:(c2 + 1) * 128], kn[:, c2, :], ident[:], [kn, ident], [pT])
        self.cp("vector", kTm[:], pT[:, 0:256], [pT], [kTm])

    def alloc_state(self, NCH):
        SEQ = self.SEQ
        if "B" in self.branches:
            self.KT = self.sb("KT", [128, SEQ], BF16)
            self.VB = self.sb("VB", [128, NCH, 128], BF16)
            self.Fcol = self.sb("Fcol", [128, 2, NCH])
            self.Fprev = self.sb("Fprev", [128, 1])
            self.memset("vector", self.Fprev[:], 0.0, [self.Fprev])
            self.c64h = [self.sb("c64h%d" % h, [128, 128], BF16) for h in range(2)]
            self.hm = self.sb("hm", [128, 2])
            self.memset("vector", self.hm[:], 0.0, [self.hm])
            for h in range(2):
                hs = slice(h * 64, (h + 1) * 64)
                fs = slice(64, 128) if h == 0 else slice(0, 64)
                self.memset("vector", self.c64h[h][:], 0.0, [self.c64h[h]])
                self.memset("vector", self.c64h[h][fs, :], 1.0 / 64, [self.c64h[h]])
                self.memset("vector", self.hm[hs, h:h + 1], 1.0, [self.hm])
        if "C" in self.branches:
            self.Cst = self.sb("Cst", [128, 128])
            self.Cstb = self.sb("Cstb", [128, 128], BF16)
            self.memset("vector", self.Cst[:], 0.0, [self.Cst])
            self.memset("vector", self.Cstb[:], 0.0, [self.Cstb])
        if "D" in self.branches:
            self.Sst = self.sb("Sst", [128, 64])
            self.Sstb = self.sb("Sstb", [128, 64], BF16)
            self.memset("vector", self.Sst[:], 0.0, [self.Sst])
            self.memset("vector", self.Sstb[:], 0.0, [self.Sstb])

    def branch_A(self, mc, fm, tm, mixo):
        self.begin("A")
        c = self.consts
        pr = c["pr"]
        gu = self.scr("A_gu", [128, 512])
        self.gelu(gu[:], [gu], fm["Au"][:, :], [(fm["Au"], "cur")], [128, 512], "u")
        gv = self.scr("A_gv", [128, 4, 128])
        self.gelu(gv[:], [gv], tm[:, :, 0:128], [tm], [128, 4, 128], "v")
        vn = self.scr("A_vn", [128, 4, 128], BF16)
        self.head_rms_tm((gv[:], [gv]), (vn[:], [vn]), (pr[:, PR_SGU_G:PR_SGU_G + 128], [pr]), "av")
        p = self.gps()
        for tt in range(4):
            for h in range(2):
                self.mm(p[h * 64:(h + 1) * 64, tt * 128:(tt + 1) * 128], vn[:, tt, h * 64:(h + 1) * 64],
                        c["wsT"][:, h, :], True, True, [vn, c["wsT"]], [p])
        ya = self.scr("A_ya", [128, 512])
        self.tt("vector", ya[:].rearrange("p (a t) -> p a t", a=4), p[:].rearrange("p (a t) -> p a t", a=4),
                pr[:, None, PR_SGUB:PR_SGUB + 128].broadcast_to([128, 4, 128]), ALU.add, [p, pr], [ya])
        self.tt("gpsimd", ya[:], ya[:], gu[:], ALU.mult, [ya, gu], [ya])
        self.tt("vector", mixo[:, 0, :], ya[:], fm["Az"][:, :], ALU.mult, [ya, fm["Az"]], [(mixo, 0)])

    def branch_M(self, mc, fm, tm, mixo):
        self.begin("M")
        c = self.consts
        pT = c["pT"]
        qn = self.scr("M_qn", [128, 4, 128], BF16)
        self.head_rms_tm((tm[:, :, 640:768], [tm]), (qn[:], [qn]), (c["gmq8"][:], [c["gmq8"]]), "mq")
        qT = self.scr("M_qT", [128, 512], BF16)
        for tt in range(4):
            self.tr(pT[:, tt * 128:(tt + 1) * 128], qn[:, tt, :], c["ident"][:], [qn, c["ident"]], [pT])
        self.cp("vector", qT[:], pT[:, 0:512], [pT], [qT])
        pn = c["pNum"]
        pd = c["pDen"]
        E = self.scr("M_E", [128, 512], BF16)
        for h in range(2):
            hs = slice(h * 64, (h + 1) * 64)
            for mcx in range(2):
                ps_ = self.gps()
                self.mm(ps_[:], c["kTm"][hs, mcx * 128:(mcx + 1) * 128], qT[hs, :], True, True, [c["kTm"], qT], [ps_])
                self.act(E[:], ps_[:], AF.Exp, [ps_], [E])
                self.mm(pn[hs, :], c["vm"][:, mcx, hs], E[:], mcx == 0, mcx == 1, [c["vm"], E], [(pn, h)])
                self.mm(pd[hs, :], c["ones_bf"][:, 0:64], E[:], mcx == 0, mcx == 1, [c["ones_bf"], E], [(pd, h)])
        rd = self.scr("M_rd", [128, 512])
        self.recip(rd[:], pd[:], [pd], [rd])
        ym = self.scr("M_ym", [128, 512])
        self.tt("vector", ym[:], pn[:], rd[:], ALU.mult, [pn, rd], [ym])
        self.tt("gpsimd", mixo[:, 4, :], ym[:], fm["Mz"][:, :], ALU.mult, [ym, fm["Mz"]], [(mixo, 4)])

    def branch_B(self, mc, fm, tm, mixo):
        self.begin("B")
        c = self.consts
        pT = c["pT"]
        pr = c["pr"]
        G = mc
        if getattr(self, "bstage", 9) < 1:
            return
        qn = self.scr("B_qn", [128, 4, 128], BF16)
        kn = self.scr("B_kn", [128, 4, 128], BF16)
        self.head_rms_tm((tm[:, :, 128:256], [tm]), (qn[:], [qn]), (c["gq8"][:], [c["gq8"]]), "bq")
        self.head_rms_tm((tm[:, :, 256:384], [tm]), (kn[:], [kn]), (pr[:, PR_BKG:PR_BKG + 128], [pr]), "bk")
        QT = self.scr("B_QT", [128, 512], BF16)
        if getattr(self, "bstage", 9) < 0.3:
            return
        for tt in range(4):
            self.tr(pT[:, tt * 128:(tt + 1) * 128], qn[:, tt, :], c["ident"][:], [qn, c["ident"]], [pT])
        for tt in range(4):
            self.tr(pT[:, 512 + tt * 128:512 + (tt + 1) * 128], kn[:, tt, :], c["ident"][:], [kn, c["ident"]], [pT])
        self.cp("vector", QT[:], pT[:, 0:512], [pT], [QT])
        self.cp("scalar", self.KT[:, G * 512:(G + 1) * 512], pT[:, 512:1024], [pT], [(self.KT, G)])
        QTm = [self.scr("B_QTm%d" % h, [128, 512], BF16) for h in range(2)]
        for h in range(2):
            self.ts("gpsimd", QTm[h][:], QT[:], self.hm[:, h:h + 1], ALU.mult, [QT, self.hm], [QTm[h]])
        for tt in range(4):
            self.cp("gpsimd", self.VB[:, G * 4 + tt, :], tm[:, tt, 384:512], [tm], [(self.VB, G * 4 + tt)])
        e1 = self.scr("B_e1", [128, 512])
        self.act(e1[:], fm["Bf"][:, :], AF.Exp, [(fm["Bf"], "cur")], [e1], scale=-1.0, bias=c["nfbB"][:, 0:1])
        self.act(e1[:], e1[:], AF.Ln, [e1], [e1], bias=1.0)
        Lc = self.scr("B_Lc", [128, 512])
        for q4 in range(4):
            qs = slice(q4 * 128, (q4 + 1) * 128)
            ini = self.Fprev[:, 0:1] if q4 == 0 else Lc[:, q4 * 128 - 1:q4 * 128]
            self.op("vector", lambda e, qs=qs, ini=ini: e.tensor_tensor_scan(
                out=Lc[:, qs], data0=c["onesf"][:, 0:128], data1=e1[:, qs], initial=ini,
                op0=ALU.mult, op1=ALU.add), [c["onesf"], e1, self.Fprev, Lc], [Lc])
        pcg = self.gps()
        for h in range(2):
            fs = slice(64, 128) if h == 0 else slice(0, 64)
            for j in range(4):
                self.mm(pcg[:, h * 8 + j:h * 8 + j + 1], Lc[fs, j * 128:(j + 1) * 128], c["c64f"][fs, 0:1],
                        True, True, [Lc, c["c64f"]], [pcg])
            for hf in range(2):
                col = hf * 256 + 127
                self.mm(pcg[:, 16 + h * 2 + hf:17 + h * 2 + hf], c["c64f"][fs, :], Lc[fs, col:col + 1], True, True,
                        [c["c64f"], Lc], [pcg])
        cg = self.scr("B_cg", [128, 4])
        for h in range(2):
            self.cp("vector", self.Fcol[:, h, G * 4:G * 4 + 4], pcg[:, h * 8:h * 8 + 4], [pcg], [(self.Fcol, G)])
        self.cp("vector", cg[:], pcg[:, 16:20], [pcg], [cg])
        self.cp("vector", self.Fprev[:], Lc[:, 511:512], [Lc], [self.Fprev])
        nkb = 4 * G + 4
        bias = self.scr("B_bias", [128, 4, self.SEQ // 128])
        for h in range(2):
            for hf in range(2):
                self.ts("vector", bias[:, h * 2 + hf, 0:nkb], self.Fcol[:, h, 0:nkb],
                        cg[:, h * 2 + hf:h * 2 + hf + 1], ALU.subtract, [self.Fcol, cg], [bias])
        pNumH = [c["pNum"], c["pDen"]]
        Es = [self.scr("B_E%d" % i, [128, 512], BF16) for i in range(4)]
        Eacc = [self.scr("B_Eacc%d" % h, [128, 512]) for h in range(2)]
        rd = self.scr("B_rd", [128, 512])
        yb = self.scr("B_yb", [128, 512])
        pairs = [(h, kb) for h in range(2) for kb in range(nkb)]
        psl = {}

        def score(i):
            h, kb = pairs[i]
            j = kb - 4 * G
            c0 = 0 if j <= 0 else j * 128
            ps_ = self.gps()
            self.mm(ps_[:, c0:512], self.KT[:, kb * 128:(kb + 1) * 128], QTm[h][:, c0:512], True, True,
                    [self.KT, QTm[h]], [ps_])
            psl[i] = ps_
        score(0)
        for i, (h, kb) in enumerate(pairs):
            if i + 1 < len(pairs):
                score(i + 1)
            hs = slice(h * 64, (h + 1) * 64)
            pN = pNumH[h]
            j = kb - 4 * G
            c0 = 0 if j <= 0 else j * 128
            ps_ = psl.pop(i)
            E = Es[i % 4]
            for hf in range(2):
                a_, b_ = max(c0, hf * 256), (hf + 1) * 256
                if a_ >= b_:
                    continue
                self.act(E[:, a_:b_], ps_[:, a_:b_], AF.Exp, [ps_, bias], [E],
                         bias=bias[:, h * 2 + hf, kb:kb + 1])
            if j >= 0:
                self.tt("gpsimd", E[:, c0:c0 + 128], E[:, c0:c0 + 128], c["m_ge"][:], ALU.mult,
                        [E, c["m_ge"]], [E])
            self.mm(pN[:, c0:512], self.VB[:, kb, :], E[:, c0:512], kb == 0, kb == nkb - 1,
                    [self.VB, E], [pN])
            if kb == 0:
                self.cp("vector", Eacc[h][:], E[:], [E], [Eacc[h]])
            else:
                self.tt("vector", Eacc[h][:, c0:512], Eacc[h][:, c0:512], E[:, c0:512], ALU.add,
                        [Eacc[h], E], [Eacc[h]])
            if kb == nkb - 1:
                pd_ = self.gps()
                self.mm(pd_[hs, :], c["onesf"][:, 0:64], Eacc[h][:], True, True, [c["onesf"], Eacc[h]], [pd_])
                self.recip(rd[hs, :], pd_[hs, :], [pd_], [rd])
                self.tt("vector", yb[hs, :], pN[hs, :], rd[hs, :], ALU.mult, [pN, rd], [yb])
        self.tt("gpsimd", mixo[:, 1, :], yb[:], fm["Bz"][:, :], ALU.mult, [yb, fm["Bz"]], [(mixo, 1)])

    def branch_C(self, mc, fm, tm, mixo):
        self.begin("C")
        c = self.consts
        pc = c["pc"]
        pT = c["pT"]
        qc = self.scr("C_qc", [128, 512], BF16)
        kc = self.scr("C_kc", [128, 512], BF16)
        for g, dst, wn, bn in (("Cq", qc, "cq", "cqb"), ("Ck", kc, "ck", "ckb")):
            x = fm[g]
            acc = self.scr("C_acc" + g, [128, 512])
            self.ts("vector", acc[:], x[:, 0:512], pc[:, PC[wn + "0"]:PC[wn + "0"] + 1], ALU.mult, [x, pc], [acc],
                    s2=pc[:, PC[bn]:PC[bn] + 1], op1=ALU.add)
            for j in range(1, 4):
                self.stt(acc[:], x[:, j:j + 512], pc[:, PC[wn + str(j)]:PC[wn + str(j)] + 1], acc[:], ALU.mult,
                         ALU.add, [x, pc, acc], [acc])
            if g == "Cq":
                self.act(dst[:], acc[:], AF.Silu, [acc], [dst])
            else:
                self.act(acc[:], acc[:], AF.Silu, [acc], [acc])
                self.ts("vector", dst[:], acc[:], 0.125, ALU.mult, [acc], [dst])
        yield
        Lb = []
        for h in range(2):
            e1 = fm["Cf%d" % h]
            self.act(e1[:], fm["Cf%d" % h][:, :], AF.Exp, [(fm["Cf%d" % h], "cur")], [e1], scale=-1.0,
                     bias=c["nfbC"][:, h:h + 1])
            self.act(e1[:], e1[:], AF.Ln, [e1], [e1], bias=1.0)
            L = self.scr("C_Lb%d" % h, [128, 512])
            for ch in range(4):
                cs = slice(ch * 128, (ch + 1) * 128)
                self.op("vector", lambda e, L=L, e1=e1, cs=cs: e.tensor_tensor_scan(
                    out=L[:, cs], data0=c["onesf"][:, 0:128], data1=e1[:, cs], initial=0.0,
                    op0=ALU.mult, op1=ALU.add), [c["onesf"], e1], [L])
            Lb.append(L)
        ilog = fm["Ci"]
        self.ts("vector", ilog[:], fm["Ci"][:, :], pc[:, PC["ib_C"]:PC["ib_C"] + 1], ALU.add,
                [(fm["Ci"], "cur"), pc], [ilog])
        yield
        so = fm["Co"]
        sz = fm["Cz"]
        if not hasattr(self, "vaug"):
            self.vaug = [self.sb("C_vaug%d" % h, [128, 128], BF16) for h in range(2)]
            for h in range(2):
                self.memset("vector", self.vaug[h][:], 1.0, [self.vaug[h]])
        vaug = self.vaug
        for ch in range(4):
            cs = slice(ch * 128, (ch + 1) * 128)
            pcol = self.gps()
            for h in range(2):
                hs = slice(h * 64, (h + 1) * 64)
                self.mm(pcol[:, h:h + 1], Lb[h][0:64, cs], c["c64f"][0:64, 0:1], True, True, [Lb[h], c["c64f"]], [pcol])
                self.mm(pcol[:, 2 + h:3 + h], ilog[hs, cs], c["c64f"][hs, 0:1], True, True, [ilog, c["c64f"]], [pcol])
            col = self.scr("C_col", [128, 8])
            self.cp("vector", col[:, 0:4], pcol[:, 0:4], [pcol], [col])
            yield
            self.tt("vector", col[:, 4:6], col[:, 0:2], col[:, 2:4], ALU.add, [col], [col])
            for h in range(2):
                self.ts("vector", col[:, 6 + h:7 + h], col[:, 4 + h:5 + h], Lb[h][:, ch * 128 + 127:ch * 128 + 128],
                        ALU.subtract, [col, Lb[h]], [col])
            wcol = self.scr("C_wcol", [128, 2])
            self.act(wcol[:], col[:, 6:8], AF.Exp, [col], [wcol])
            yield
            eg = self.scr("C_eg", [128, 2])
            for h in range(2):
                self.act(eg[:, h:h + 1], Lb[h][:, ch * 128 + 127:ch * 128 + 128], AF.Exp, [Lb[h]], [eg], scale=-1.0)
            ebq = self.scr("C_ebq", [128, 128])
            for h in range(2):
                hs = slice(h * 64, (h + 1) * 64)
                self.act(ebq[hs, :], Lb[h][hs, cs], AF.Exp, [Lb[h]], [ebq], scale=-1.0)
            qp = self.scr("C_qp", [128, 128], BF16)
            self.tt("vector", qp[:], qc[:, cs], ebq[:], ALU.mult, [qc, ebq], [qp])
            yield
            for h in range(2):
                hs = slice(h * 64, (h + 1) * 64)
                self.cp("gpsimd", vaug[h][:, 0:64], tm[:, ch, 512 + h * 64:512 + (h + 1) * 64], [tm], [vaug[h]])
            pS = self.gps()
            P = []
            for h in range(2):
                hs = slice(h * 64, (h + 1) * 64)
                self.mm(pS[:, h * 128:(h + 1) * 128], kc[hs, cs], qc[hs, cs], True, True, [kc, qc], [pS])
                D = self.scr("C_D%d" % h, [128, 128])
                self.ts("vector", D[:], Lb[h][:, cs], col[:, h:h + 1], ALU.subtract, [Lb[h], col], [D], s2=0.0,
                        op1=ALU.max)
                self.act(D[:], D[:], AF.Exp, [D, col], [D], scale=-1.0, bias=col[:, 2 + h:3 + h])
                self.tt("gpsimd", D[:], D[:], c["m_ge"][:], ALU.mult, [D, c["m_ge"]], [D])
                Ph = self.scr("C_P%d" % h, [128, 128], BF16)
                self.tt("vector", Ph[:], pS[:, h * 128:(h + 1) * 128], D[:], ALU.mult, [pS, D], [Ph])
                P.append(Ph)
            yield
            pnd = self.gps()
            for h in range(2):
                hs = slice(h * 64, (h + 1) * 64)
                self.mm(pnd[hs, 0:128], vaug[h][:, 0:64], P[h][:], True, False, [vaug[h], P[h]], [pnd])
                self.mm(pnd[hs, 0:128], self.Cstb[hs, 0:64], qp[hs, :], False, True, [self.Cstb, qp], [pnd])
            for h in range(2):
                hs = slice(h * 64, (h + 1) * 64)
                self.mm(pnd[hs, 128:256], c["ones_bf"][:, 0:64], P[h][:], True, False, [c["ones_bf"], P[h]], [pnd])
                self.mm(pnd[hs, 128:256], self.Cstb[hs, 64:128], qp[hs, :], False, True, [self.Cstb, qp], [pnd])
            dn = self.scr("C_dn", [128, 128])
            self.act(dn[:], pnd[:, 128:256], AF.Abs, [pnd], [dn])
            self.ts("vector", dn[:], dn[:], 1.0, ALU.max, [dn], [dn])
            self.recip(dn[:], dn[:], [dn], [dn])
            ho = self.scr("C_ho", [128, 128])
            self.tt("vector", ho[:], pnd[:, 0:128], dn[:], ALU.mult, [pnd, dn], [ho])
            yield
            self.tt("gpsimd", ho[:], ho[:], so[:, cs], ALU.mult, [ho, so], [ho])
            yield
            sq = self.scr("C_sq", [128, 128], BF16)
            self.tt("gpsimd", sq[:], ho[:], ho[:], ALU.mult, [ho], [sq])
            pq = self.gps()
            self.mm(pq[:, 0:128], c["blk"][:], sq[:], True, True, [c["blk"], sq], [pq])
            rs = self.scr("C_rs", [128, 128])
            self.act(rs[:], pq[:, 0:128], AF.Ln, [pq], [rs], scale=1.0 / 64, bias=EPS)
            self.act(rs[:], rs[:], AF.Exp, [rs], [rs], scale=-0.5)
            self.stt(ho[:], ho[:], pc[:, PC["out_g"]:PC["out_g"] + 1], rs[:], ALU.mult, ALU.mult, [ho, pc, rs], [ho])
            self.tt("vector", mixo[:, 2, cs], ho[:], sz[:, cs], ALU.mult, [ho, sz], [(mixo, 2)])
            yield
            self.tr(pT[:, 0:128], kc[:, cs], c["ident"][:], [kc, c["ident"]], [pT])
            khat = self.scr("C_khat", [128, 128], BF16)
            for h in range(2):
                hs = slice(h * 64, (h + 1) * 64)
                self.ts("vector", khat[:, hs], pT[:, h * 64:(h + 1) * 64], wcol[:, h:h + 1], ALU.mult, [pT, wcol],
                        [khat])
            yield
            pC = self.gps()
            for h in range(2):
                hs = slice(h * 64, (h + 1) * 64)
                self.mm(pC[hs, 0:128], khat[:, hs], vaug[h][:], True, True, [khat, vaug[h]], [pC])
            for h in range(2):
                hs = slice(h * 64, (h + 1) * 64)
                self.stt(self.Cst[hs, :], self.Cst[hs, :], eg[hs, h:h + 1], pC[hs, 0:128], ALU.mult, ALU.add,
                         [self.Cst, eg, pC], [self.Cst])
            self.cp("vector", self.Cstb[:], self.Cst[:], [self.Cst], [self.Cstb])
            yield

    def branch_D(self, mc, fm, tm, mixo):
        self.begin("D")
        c = self.consts
        pc = c["pc"]
        pT = c["pT"]
        omm = c["omm"]
        if not hasattr(self, "m_gt"):
            self.m_gt = self.sb("m_gt", [128, 128], BF16)
            self.mN_gt = self.sb("mN_gt", [128, 128], BF16)
            self.memset("vector", self.m_gt[:], 1.0, [self.m_gt])
            self.asel(self.m_gt[:], self.m_gt[:], ALU.is_gt, [self.m_gt], [self.m_gt])
            self.memset("vector", self.mN_gt[:], 1.0, [self.mN_gt])
            self.asel(self.mN_gt[:], self.mN_gt[:], ALU.is_gt, [self.mN_gt], [self.mN_gt], cm=1, pat=[[-1, 128]])
        m_gt, mN_gt, m_ge = self.m_gt, self.mN_gt, c["m_ge"]

        def shift(g, idx, rows, name):
            x = fm[g]
            t = self.scr("D_sh_" + name, [128, 512])
            self.ts("vector", t[0:rows, :], x[0:rows, 1:513], omm[0:rows, idx:idx + 1], ALU.mult, [x, omm], [t])
            self.stt(t[0:rows, :], x[0:rows, 0:512], pc[0:rows, PC["mu_r"] + idx:PC["mu_r"] + idx + 1], t[0:rows, :],
                     ALU.mult, ALU.add, [x, pc, t], [t])
            return t
        rs = shift("Dr", 0, 128, "r")
        ks = shift("Dk", 1, 128, "k")
        vs = shift("Dv", 2, 128, "v")
        zs = shift("Dz", 3, 128, "z")
        ws = shift("Dw", 4, 16, "w")
        as_ = shift("Da", 5, 16, "a")
        self.act(ws[0:16, :], ws[0:16, :], AF.Tanh, [ws], [ws])
        pw = self.gps()
        self.mm(pw[:], c["w2s"][:], ws[0:16, :], True, True, [c["w2s"], ws], [pw])
        lw = ws
        self.act(lw[:], pw[:], AF.Sigmoid, [pw, pc], [lw], bias=pc[:, PC["w0"]:PC["w0"] + 1])
        self.ts("gpsimd", lw[:], lw[:], -0.6065306597126334, ALU.mult, [lw], [lw])
        pa = self.gps()
        self.mm(pa[:], c["a2s"][:], as_[0:16, :], True, True, [c["a2s"], as_], [pa])
        aa = as_
        self.act(aa[:], pa[:], AF.Sigmoid, [pa, pc], [aa], bias=pc[:, PC["a0"]:PC["a0"] + 1])
        sz = zs
        self.act(sz[:], zs[:], AF.Silu, [zs], [sz])
        yield
        vb = self.scr("D_vb", [128, 512], BF16)
        self.cp("gpsimd", vb[:], vs[:], [vs], [vb])
        kx = self.scr("D_kx", [128, 512])
        self.ts("vector", kx[:], ks[:], pc[:, PC["k_k"]:PC["k_k"] + 1], ALU.mult, [ks, pc], [kx])
        sqb = self.scr("D_sqb", [128, 512], BF16)
        self.tt("gpsimd", sqb[:], kx[:], kx[:], ALU.mult, [kx], [sqb])
        pq = self.gps()
        self.mm(pq[:], c["blk"][:], sqb[:], True, True, [c["blk"], sqb], [pq])
        rn = self.scr("D_rn", [128, 512])
        self.ts("vector", rn[:], pq[:], 1e-18, ALU.max, [pq], [rn])
        self.act(rn[:], rn[:], AF.Ln, [rn], [rn])
        self.act(rn[:], rn[:], AF.Exp, [rn], [rn], scale=-0.5)
        kk = kx
        self.tt("vector", kk[:], kx[:], rn[:], ALU.mult, [kx, rn], [kk])
        k2 = rn
        self.ts("vector", k2[:], aa[:], -1.0, ALU.add, [aa, pc], [k2], s2=pc[:, PC["k_a"]:PC["k_a"] + 1], op1=ALU.mult)
        self.stt(k2[:], k2[:], 1.0, ks[:], ALU.add, ALU.mult, [k2, ks], [k2])
        bv = ks
        self.tt("gpsimd", bv[:], kk[:], aa[:], ALU.mult, [kk, aa], [bv])
        yield
        rk = self.scr("D_rk", [128, 512], BF16)
        self.stt(rk[:], rs[:], pc[:, PC["r_k"]:PC["r_k"] + 1], k2[:], ALU.mult, ALU.mult, [rs, pc, k2], [rk])
        pb = self.gps()
        self.mm(pb[:], c["blk"][:], rk[:], True, True, [c["blk"], rk], [pb])
        bon = vs
        self.tt("vector", bon[:], pb[:], vs[:], ALU.mult, [pb, vs], [bon])
        cl = self.scr("D_cl", [128, 512])
        for ch in range(4):
            cs = slice(ch * 128, (ch + 1) * 128)
            self.op("vector", lambda e, cs=cs: e.tensor_tensor_scan(
                out=cl[:, cs], data0=c["onesf"][:, 0:128], data1=lw[:, cs], initial=0.0,
                op0=ALU.mult, op1=ALU.add), [c["onesf"], lw], [cl])
        Ecl = self.scr("D_Ecl", [128, 512])
        Encl = self.scr("D_Encl", [128, 512])
        self.act(Ecl[:], cl[:], AF.Exp, [cl], [Ecl])
        self.act(Encl[:], cl[:], AF.Exp, [cl], [Encl], scale=-1.0)
        Ecx = lw
        self.tt("gpsimd", lw[:], cl[:], lw[:], ALU.subtract, [cl, lw], [lw])
        self.act(Ecx[:], lw[:], AF.Exp, [lw], [Ecx])
        yield
        clT = self.scr("D_clT", [128, 4])
        self.cp("vector", clT[:], cl[:].rearrange("p (a t) -> p a t", a=4)[:, :, 127], [cl], [clT])
        Eh = cl
        for ch in range(4):
            cs = slice(ch * 128, (ch + 1) * 128)
            self.act(Eh[:, cs], cl[:, cs], AF.Exp, [cl, clT], [Eh], scale=-1.0, bias=clT[:, ch:ch + 1])
        KR = self.scr("D_KR", [128, 4, 256], BF16)
        kt = self.scr("D_kt", [128, 512], BF16)
        bt = self.scr("D_bt", [128, 512], BF16)
        khat = self.scr("D_khat", [128, 512], BF16)
        nbh = self.scr("D_nbh", [128, 512], BF16)
        self.tt("vector", KR[:, :, 0:128], kk[:].rearrange("p (a t) -> p a t", a=4),
                Ecx[:].rearrange("p (a t) -> p a t", a=4), ALU.mult, [kk, Ecx], [KR])
        self.tt("gpsimd", KR[:, :, 128:256], rs[:].rearrange("p (a t) -> p a t", a=4),
                Ecl[:].rearrange("p (a t) -> p a t", a=4), ALU.mult, [rs, Ecl], [KR])
        self.tt("vector", kt[:], k2[:], Encl[:], ALU.mult, [k2, Encl], [kt])
        self.tt("gpsimd", bt[:], bv[:], Encl[:], ALU.mult, [bv, Encl], [bt])
        self.tt("vector", khat[:], k2[:], Eh[:], ALU.mult, [k2, Eh], [khat])
        self.stt(nbh[:], bv[:], -1.0, Eh[:], ALU.mult, ALU.mult, [bv, Eh], [nbh])
        yraw = kx
        yield
        def chunk(ch, par):
            PFX = "p%d_" % par
            po = par * 128
            cs = slice(ch * 128, (ch + 1) * 128)
            cs = slice(ch * 128, (ch + 1) * 128)
            self.tr(pT[:, 0:128], KR[:, ch, 0:128], c["ident"][:], [KR, c["ident"]], [pT])
            self.tr(pT[:, 128:256], vb[:, cs], c["ident"][:], [vb, c["ident"]], [pT])
            self.tr(pT[:, 256:384], khat[:, cs], c["ident"][:], [khat, c["ident"]], [pT])
            self.tr(pT[:, 384:512], nbh[:, cs], c["ident"][:], [nbh, c["ident"]], [pT])
            TMs = self.scr(PFX + "D_TMs", [128, 4, 128], BF16)
            self.cp("scalar", TMs[:], pT[:, 0:512].rearrange("p (a t) -> p a t", a=4), [pT], [TMs])
            yield
            rpT = self.scr(PFX + "D_rpT", [128, 128], BF16)
            GT = self.scr(PFX + "D_GT", [128, 64], BF16)
            Hs = self.scr(PFX + "D_Hs", [128, 64])
            py = c["pNum"]
            pHS = c["pDen"]
            HS = [slice(0, 64), slice(64, 128)]
            LT, nQbT, LkT, QkT, LN, R, R2 = {}, {}, {}, {}, {}, {}, {}
            for h in range(2):
                hs = HS[h]
                p1 = self.gps()
                self.mm(p1[:, 0:256], bt[hs, cs], KR[hs, ch, :], True, True, [bt, KR], [p1])
                self.mm(p1[:, 256:512], kt[hs, cs], KR[hs, ch, :], True, True, [kt, KR], [p1])
                LT[h] = self.scr(PFX + "D_LT%d" % h, [128, 128], BF16)
                nQbT[h] = self.scr(PFX + "D_nQbT%d" % h, [128, 128], BF16)
                LkT[h] = self.scr(PFX + "D_LkT%d" % h, [128, 128], BF16)
                QkT[h] = self.scr(PFX + "D_QkT%d" % h, [128, 128], BF16)
                self.tt("vector", LT[h][:], p1[:, 0:128], m_gt[:], ALU.mult, [p1, m_gt], [LT[h]])
                self.tt("vector", LkT[h][:], p1[:, 256:384], m_gt[:], ALU.mult, [p1, m_gt], [LkT[h]])
                self.stt(nQbT[h][:], p1[:, 128:256], -1.0, m_ge[:], ALU.mult, ALU.mult, [p1, m_ge], [nQbT[h]])
                self.tt("vector", QkT[h][:], p1[:, 384:512], m_ge[:], ALU.mult, [p1, m_ge], [QkT[h]])
                yield
            for h in range(2):
                hs = HS[h]
                p2 = self.gps()
                self.mm(p2[:, 0:128], KR[hs, ch, 0:128], bt[hs, cs], True, True, [KR, bt], [p2])
                self.mm(p2[:, 128:192], LkT[h][:], TMs[:, 1, hs], True, True, [LkT[h], TMs], [p2])
                LN[h] = self.scr(PFX + "D_LN%d" % h, [128, 128], BF16)
                self.tt("vector", LN[h][:], p2[:, 0:128], mN_gt[:], ALU.mult, [p2, mN_gt], [LN[h]])
                R[h] = self.scr(PFX + "D_R%d" % h, [128, 128], BF16)
                self.cp("gpsimd", R[h][:, 0:64], TMs[:, 0, hs], [TMs], [R[h]])
                self.cp("scalar", R[h][:, 64:128], p2[:, 128:192], [p2], [R[h]])
                yield
            Rc, Rn, PTc, PNc = {}, {}, {}, {}
            for h in range(2):
                pr_ = self.gps()
                self.mm(pr_[:, 0:128], LT[h][:], R[h][:], True, True, [LT[h], R[h]], [pr_])
                R2[h] = self.scr(PFX + "D_Rb%d" % h, [128, 128], BF16)
                self.tt("vector", R2[h][:], R[h][:], pr_[:, 0:128], ALU.subtract, [R[h], pr_], [R2[h]])
                Rc[h], Rn[h] = R2[h], R[h]
                PTc[h], PNc[h] = LT[h], LN[h]
            yield
            for j in range(1, 7):
                PTn, PNn = {}, {}
                for h in range(2):
                    PTn[h] = self.scr(PFX + "D_PT%d_%d" % (h, j % 2), [128, 128], BF16)
                    PNn[h] = self.scr(PFX + "D_PN%d_%d" % (h, j % 2), [128, 128], BF16)
                    pp = self.gps()
                    self.mm(pp[:, 0:128], PNc[h][:], PTc[h][:], True, True, [PNc[h], PTc[h]], [pp])
                    if j < 6:
                        self.mm(pp[:, 128:256], PTc[h][:], PNc[h][:], True, True, [PNc[h], PTc[h]], [pp])
                        self.cp("scalar", PTn[h][:], pp[:, 0:128], [pp], [PTn[h]])
                        self.cp("vector", PNn[h][:], pp[:, 128:256], [pp], [PNn[h]])
                    else:
                        self.cp("scalar", PTn[h][:], pp[:, 0:128], [pp], [PTn[h]])
                yield
                for h in range(2):
                    pr_ = self.gps()
                    self.mm(pr_[:, 0:128], PTn[h][:], Rc[h][:], True, True, [PTn[h], Rc[h]], [pr_])
                    self.tt("vector", Rn[h][:], Rc[h][:], pr_[:, 0:128], ALU.add, [Rc[h], pr_], [Rn[h]])
                    Rc[h], Rn[h] = Rn[h], Rc[h]
                    PTc[h], PNc[h] = PTn[h], PNn[h]
                yield
            while self.d_done < ch:
                yield
            for h in range(2):
                hs = HS[h]
                Rch = Rc[h]
                pr_ = self.gps()
                self.mm(pr_[hs, 0:128], Rch[:, 0:64], nQbT[h][:], True, True, [Rch, nQbT[h]], [pr_])
                self.tt("vector", rpT[hs, :], pr_[hs, 0:128], KR[hs, ch, 128:256], ALU.add, [pr_, KR], [rpT])
                self.mm(py[hs, po:po + 128], TMs[:, 1, hs], QkT[h][:], True, False, [TMs, QkT[h]], [(py, h)])
                self.mm(py[hs, po:po + 128], Rch[:, 64:128], nQbT[h][:], False, False, [Rch, nQbT[h]], [(py, h)])
                self.mm(py[hs, po:po + 128], self.Sstb[hs, :], rpT[hs, :], False, True, [self.Sstb, rpT], [(py, h)])
                pg = self.gps()
                self.mm(pg[hs, 0:64], Rch[:, 0:64], TMs[:, 3, hs], True, True, [Rch, TMs], [pg])
                self.stt(GT[hs, :], c["identf"][hs, h * 64:(h + 1) * 64], Ecl[hs, ch * 128 + 127:ch * 128 + 128],
                         pg[hs, 0:64], ALU.mult, ALU.add, [c["identf"], Ecl, pg], [GT])
                self.mm(pHS[hs, po:po + 64], TMs[:, 2, hs], TMs[:, 1, hs], True, False, [TMs], [(pHS, h)])
                self.mm(pHS[hs, po:po + 64], TMs[:, 3, hs], Rch[:, 64:128], False, True, [TMs, Rch], [(pHS, h)])
                self.cp("scalar", Hs[hs, :], pHS[hs, po:po + 64], [(pHS, h)], [Hs])
                self.mm(pHS[hs, po + 64:po + 128], GT[hs, :], self.Sstb[hs, :], True, True, [GT, self.Sstb], [(pHS, h)])
                self.tt("vector", self.Sst[hs, :], pHS[hs, po + 64:po + 128], Hs[hs, :], ALU.add, [(pHS, h), Hs], [self.Sst])
                yield
            self.cp("vector", self.Sstb[:], self.Sst[:], [self.Sst], [self.Sstb])
            self.cp("scalar", yraw[:, cs], py[:, po:po + 128], [py], [yraw])
            self.d_done = ch + 1

        self.d_done = 0
        for pair in ((0, 1), (2, 3)):
            gs = [chunk(pair[0], 0), chunk(pair[1], 1)]
            while gs:
                for g in list(gs):
                    try:
                        next(g)
                    except StopIteration:
                        gs.remove(g)
                yield
        ysq = sqb
        self.tt("gpsimd", ysq[:], yraw[:], yraw[:], ALU.mult, [yraw], [ysq])
        pq2 = self.gps()
        self.mm(pq2[:], c["blk"][:], ysq[:], True, True, [c["blk"], ysq], [pq2])
        rs2 = rn
        self.act(rs2[:], pq2[:], AF.Ln, [pq2], [rs2], scale=1.0 / 64, bias=EPS)
        self.act(rs2[:], rs2[:], AF.Exp, [rs2], [rs2], scale=-0.5)
        self.stt(yraw[:], yraw[:], pc[:, PC["ln_g"]:PC["ln_g"] + 1], rs2[:], ALU.mult, ALU.mult, [yraw, pc, rs2],
                 [yraw])
        self.tt("vector", yraw[:], yraw[:], bon[:], ALU.add, [yraw, bon], [yraw])
        self.tt("gpsimd", mixo[:, 3, :], yraw[:], sz[:], ALU.mult, [yraw, sz], [(mixo, 3)])


def build_final(SEQ):
    Bd = Builder(SEQ, False)
    nc = Bd.nc
    x1 = Bd.dram("x1", [SEQ, 1024], F32, "ExternalInput")
    mp = Bd.dram("mprev", [1280, SEQ], BF16, "ExternalInput")
    wout_d = Bd.dram("wout", [1280, 1024], F32, "ExternalInput")
    out = Bd.dram("out", [SEQ, 1024], F32, "ExternalOutput")
    pP = [Bd.psum("pP0", [128, 512]), Bd.psum("pP1", [128, 512])]
    wo = Bd.sb("wo", [128, 10, 1024], BF16)
    wov = wout_d.rearrange("(c p) n -> p c n", p=128)
    for c in range(10):
        Bd.dma("gpsimd", wo[:, c, :], wov[:, c, :], w=[(wo, c)])
    xts = [Bd.sb("xt%d" % i, [128, 4, 1024]) for i in range(2)]
    mpvs = [Bd.sb("mpv%d" % i, [128, 10, 512], BF16) for i in range(2)]
    outs = []
    for mc in range(SEQ // 512):
        t0 = mc * 512
        xt = xts[mc % 2]
        mpv = mpvs[mc % 2]
        Bd.dma("sync", xt[:], x1[t0:t0 + 512, :].rearrange("(tt p) d -> p tt d", p=128), w=[xt])
        Bd.dma("sync", mpv[:], mp[:, t0:t0 + 512].rearrange("(c p) t -> p c t", p=128), w=[mpv])
        for tt in range(4):
            for hf in range(2):
                p = pP[hf]
                for c in range(10):
                    Bd.mm(p[:], mpv[:, c, tt * 128:(tt + 1) * 128], wo[:, c, hf * 512:(hf + 1) * 512],
                          c == 0, c == 9, [mpv, (wo, c)], [p])
                Bd.tt("vector", xt[:, tt, hf * 512:(hf + 1) * 512], xt[:, tt, hf * 512:(hf + 1) * 512],
                      p[:], ALU.add, [xt, p], [xt])
        o = Bd.dma("sync", out[t0:t0 + 512, :].rearrange("(tt p) d -> p tt d", p=128), xt[:], r=[xt], w=["out_d"])
        outs.append(o)
    Bd.S.emit(final_wait_ops=outs)
    return nc


_CACHE = {}


def _get_prog(key, fn):
    if key not in _CACHE:
        _CACHE[key] = fn()
    return _CACHE[key]


def kernel_unfused(**inputs):
    inp = {k: np.asarray(v) for k, v in inputs.items()}
    x = inp["x"]
    BATCH, SEQ, _ = x.shape
    n = 8
    cores = [(b, hh) for b in range(BATCH) for hh in range(2)]
    xcur = [np.ascontiguousarray(x[b]) for b in range(BATCH)]
    mprev = None
    for l in range(2):
        has_prev = l > 0
        nc = _get_prog(("layer", SEQ, has_prev), lambda: Builder(SEQ, has_prev).build())
        in_maps = []
        for (b, hh) in cores:
            d = host_layer_params(inp, l, hh)
            d["xin"] = xcur[b]
            d["mem"] = np.ascontiguousarray(inp["mem"][b])
            if has_prev:
                d["mprev"] = mprev[b]
                d["wout"] = np.ascontiguousarray(inp["w_out"][l - 1])
            in_maps.append(d)
        res = run_bass_kernel_spmd(nc, in_maps, core_ids=list(range(n)))
        new_m = []
        for b in range(BATCH):
            full = np.zeros((1280, SEQ), dtype=ml_dtypes.bfloat16)
            for hh in range(2):
                m = np.asarray(res.results[b * 2 + hh]["mixed"])
                for g in range(5):
                    full[g * 256 + hh * 128:g * 256 + hh * 128 + 128] = m[g * 128:(g + 1) * 128]
            new_m.append(full)
            if has_prev:
                xcur[b] = np.asarray(res.results[b * 2]["x1out"])
        mprev = new_m
    ncf = _get_prog(("final", SEQ), lambda: build_final(SEQ // 2))
    in_maps = []
    H = SEQ // 2
    for (b, hh) in cores:
        in_maps.append({"x1": np.ascontiguousarray(xcur[b][hh * H:(hh + 1) * H]),
                        "mprev": np.ascontiguousarray(mprev[b][:, hh * H:(hh + 1) * H]),
                        "wout": np.ascontiguousarray(inp["w_out"][1])})
    res = run_bass_kernel_spmd(ncf, in_maps, core_ids=list(range(n)))
    out = np.zeros((BATCH, SEQ, 1024), np.float32)
    for i, (b, hh) in enumerate(cores):
        out[b, hh * H:(hh + 1) * H] = np.asarray(res.results[i]["out"])
    return out


def kernel(**inputs):
    inp = {k: np.asarray(v) for k, v in inputs.items()}
    x = inp["x"]
    BATCH, SEQ, _ = x.shape
    nc = _get_prog(("fused", SEQ), lambda: Builder(SEQ, False).build_fused())
    per = {}
    for l in range(2):
        for hh in range(2):
            d = host_layer_params(inp, l, hh)
            for k, v in d.items():
                per["%s_%d%d" % (k, l, hh)] = v
    in_maps = []
    for b in range(BATCH):
        d = dict(per)
        d["xin"] = np.ascontiguousarray(x[b])
        d["mem"] = np.ascontiguousarray(inp["mem"][b])
        d["wout0"] = np.ascontiguousarray(inp["w_out"][0])
        d["wout1"] = np.ascontiguousarray(inp["w_out"][1])
        in_maps.append(d)
    res = run_bass_kernel_spmd(nc, in_maps, core_ids=list(range(BATCH)))
    out = np.stack([np.asarray(res.results[b]["out"]) for b in range(BATCH)], axis=0)
    return out.astype(np.float32)
```

```python
from contextlib import ExitStack
import numpy as np
import ml_dtypes
import concourse.bass as bass
import concourse.mybir as mybir
from concourse.bass_utils import run_bass_kernel_spmd

F32 = mybir.dt.float32
BF16 = mybir.dt.bfloat16
ALU = mybir.AluOpType
AF = mybir.ActivationFunctionType
AX = mybir.AxisListType

ENGINES = ("tensor", "vector", "scalar", "gpsimd", "sync")
SEM_CAP = 30000
EPS = 1e-6


class _Op:
    __slots__ = ("eng", "fn", "idx", "deps", "signal", "is_dma", "sem", "val", "pre_wait")

    def __init__(self, eng, fn, is_dma):
        self.eng = eng
        self.fn = fn
        self.is_dma = is_dma
        self.deps = []
        self.signal = False
        self.sem = None
        self.val = None
        self.pre_wait = None


class Tl:
    def __init__(self, name, t):
        self.name = name
        self.t = t

    def __getitem__(self, idx):
        return self.t[idx]


def _norm(rs):
    out = []
    for r in rs:
        if isinstance(r, tuple):
            a, k = r
        else:
            a, k = r, None
        if isinstance(a, Tl):
            a = a.name
        if a.startswith("scr"):
            k = None
        out.append((a, k))
    return out


class Sched:
    def __init__(self, nc, stack, n_dma_sems=16):
        self.nc = nc
        self.stack = stack
        self.ops = {e: [] for e in ENGINES}
        self.state = {}
        self.n_dma_sems = n_dma_sems

    def _entries(self, name, key):
        d = self.state.setdefault(name, {})
        if key is None:
            return list(d.values())
        res = []
        if key in d:
            res.append(d[key])
        if None in d:
            res.append(d[None])
        return res

    def add(self, eng, fn, reads=(), writes=(), dma=False):
        reads = _norm(reads)
        writes = _norm(writes)
        op = _Op(eng, fn, dma)
        deps = []
        for (name, key) in reads:
            for ent in self._entries(name, key):
                if ent[0] is not None:
                    deps.append(ent[0])
                if name[0] == "p" and name[1].isupper():
                    deps.extend(o_ for o_ in ent[1] if o_.eng != eng)
        for (name, key) in writes:
            for ent in self._entries(name, key):
                if ent[0] is not None:
                    deps.append(ent[0])
                deps.extend(ent[1])
        for (name, key) in reads:
            d = self.state.setdefault(name, {})
            if key is None:
                if not d:
                    d[None] = [None, []]
                for ent in d.values():
                    ent[1].append(op)
            else:
                if key not in d:
                    d[key] = [None, []]
                d[key][1].append(op)
        for (name, key) in writes:
            d = self.state.setdefault(name, {})
            if key is None:
                d.clear()
                d[None] = [op, []]
            else:
                d[key] = [op, []]
        op.idx = len(self.ops[eng])
        best = {}
        dl = []
        for dop in deps:
            if dop is op:
                continue
            if dop.is_dma:
                if dop not in dl:
                    dl.append(dop)
            else:
                if dop.eng == "tensor" and eng == "tensor" and not dma:
                    continue
                b = best.get(dop.eng)
                if b is None or dop.idx > b.idx:
                    best[dop.eng] = dop
        op.deps = dl + list(best.values())
        for dop in op.deps:
            dop.signal = True
        self.ops[eng].append(op)
        return op

    def emit(self, final_wait_ops=()):
        nc = self.nc
        for eng in ENGINES:
            cnt = 0
            sem = None
            for op in self.ops[eng]:
                if op.is_dma:
                    continue
                if op.signal:
                    if sem is None or cnt >= SEM_CAP:
                        sem = self.stack.enter_context(nc.semaphore(f"s_{eng}_{op.idx}"))
                        cnt = 0
                    cnt += 1
                    op.sem = sem
                    op.val = cnt
        for eng in ENGINES:
            qpool = []
            k = 0
            for op in self.ops[eng]:
                if not op.is_dma:
                    continue
                if len(qpool) < self.n_dma_sems:
                    s = self.stack.enter_context(nc.semaphore(f"d_{eng}_{len(qpool)}"))
                    qpool.append([s, 0])
                    ent = qpool[-1]
                else:
                    ent = qpool[k % self.n_dma_sems]
                    if ent[1] + 16 > SEM_CAP:
                        ent[0] = self.stack.enter_context(nc.semaphore(f"d_{eng}_x{k}"))
                        ent[1] = 0
                if ent[1] > 0:
                    op.pre_wait = (ent[0], ent[1])
                ent[1] += 16
                op.sem = ent[0]
                op.val = ent[1]
                k += 1
        sched = self

        def run(eng_name, e):
            seen = {}
            for op in sched.ops[eng_name]:
                waits = []
                if op.pre_wait is not None:
                    waits.append(op.pre_wait)
                for dop in op.deps:
                    waits.append((dop.sem, dop.val))
                for (s, v) in waits:
                    key = id(s)
                    if seen.get(key, 0) >= v:
                        continue
                    seen[key] = v
                    e.wait_ge(s, v)
                ins = op.fn(e)
                if op.is_dma:
                    ins.then_inc(op.sem, 16)
                elif op.signal:
                    ins.then_inc(op.sem, 1)
            if eng_name == "sync":
                for fop in final_wait_ops:
                    e.wait_ge(fop.sem, fop.val)

        with nc.Block() as block:
            @block.sync
            def _(e):
                run("sync", e)

            @block.tensor
            def _(e):
                run("tensor", e)

            @block.vector
            def _(e):
                run("vector", e)

            @block.scalar
            def _(e):
                run("scalar", e)

            @block.gpsimd
            def _(e):
                run("gpsimd", e)


D_MODEL = 1024
FM_GROUPS = ["Au", "Az", "Bz", "Cq", "Ck", "Co", "Cz", "Dr", "Dk", "Dv", "Dz", "Mz",
             "Bf", "Ci", "Cf0", "Cf1", "Dw", "Da"]
FM_W = {g: 128 for g in FM_GROUPS}
FM_W["Dw"] = 16
FM_W["Da"] = 16
FM_OFF = {}
_o = 0
for _g in FM_GROUPS:
    FM_OFF[_g] = _o
    _o += FM_W[_g]
NFM = _o
TM_GROUPS = ["Av", "Bq", "Bk", "Bv", "Cv", "Mq"]
NTM = 768
NW = NFM + NTM
PC = {n: i for i, n in enumerate([
    "cq0", "cq1", "cq2", "cq3", "ck0", "ck1", "ck2", "ck3", "cqb", "ckb",
    "mu_r", "mu_k", "mu_v", "mu_z", "mu_w", "mu_a",
    "k_k", "k_a", "a0", "w0", "r_k", "ln_g", "out_g",
    "fb_B", "ib_C", "fb_C0", "fb_C1"])}
NPC = len(PC)
PR_SGU_G, PR_BQG, PR_BKG, PR_MQG, PR_MKG, PR_SGUB = 0, 128, 256, 384, 512, 640
NPR = 768

A_OFF = 0
B_OFF = 768
C_OFF = 768 + 1028
D_OFF = C_OFF + 1288
M_OFF = D_OFF + 1056


def host_layer_params(inp, l, hh):
    f32 = np.float32
    w_in = inp["w_in"][l]
    hs = [2 * hh, 2 * hh + 1]

    def hcols(base):
        return np.concatenate([np.arange(base + h * 64, base + h * 64 + 64) for h in hs])

    cols = {}
    cols["Au"] = hcols(A_OFF)
    cols["Av"] = hcols(A_OFF + 256)
    cols["Az"] = hcols(A_OFF + 512)
    cols["Bq"] = hcols(B_OFF)
    cols["Bk"] = hcols(B_OFF + 256)
    cols["Bv"] = hcols(B_OFF + 512)
    bf = B_OFF + 768
    cols["Bf"] = np.concatenate([np.full(64, bf + hs[1]), np.full(64, bf + hs[0])])
    cols["Bz"] = hcols(B_OFF + 772)
    cols["Cq"] = hcols(C_OFF)
    cols["Ck"] = hcols(C_OFF + 256)
    cols["Cv"] = hcols(C_OFF + 512)
    ci = C_OFF + 768
    cols["Ci"] = np.concatenate([np.full(64, ci + hs[0]), np.full(64, ci + hs[1])])
    cols["Cf0"] = np.full(128, ci + 4 + hs[0])
    cols["Cf1"] = np.full(128, ci + 4 + hs[1])
    cols["Co"] = hcols(C_OFF + 776)
    cols["Cz"] = hcols(C_OFF + 1032)
    cols["Dr"] = hcols(D_OFF)
    cols["Dw"] = np.arange(D_OFF + 256, D_OFF + 272)
    cols["Dk"] = hcols(D_OFF + 272)
    cols["Dv"] = hcols(D_OFF + 528)
    cols["Da"] = np.arange(D_OFF + 784, D_OFF + 800)
    cols["Dz"] = hcols(D_OFF + 800)
    cols["Mq"] = hcols(M_OFF)
    cols["Mz"] = hcols(M_OFF + 256)
    allc = np.concatenate([cols[g] for g in FM_GROUPS] + [cols[g] for g in TM_GROUPS])
    wcat = np.ascontiguousarray(w_in[:, allc])

    hc = hcols(0)
    pc = np.zeros((128, NPC), f32)
    cw = inp["mlstm_conv_w"][l]
    cb = inp["mlstm_conv_b"][l]
    for j in range(4):
        pc[:, PC["cq%d" % j]] = cw[j, hc]
        pc[:, PC["ck%d" % j]] = cw[j, 256 + hc]
    pc[:, PC["cqb"]] = cb[hc]
    pc[:, PC["ckb"]] = cb[256 + hc]
    mu = inp["rwkv_mu"][l]
    pc[:, PC["mu_r"]] = mu[hc]
    pc[:16, PC["mu_w"]] = mu[256:272]
    pc[:, PC["mu_k"]] = mu[272 + hc]
    pc[:, PC["mu_v"]] = mu[528 + hc]
    pc[:16, PC["mu_a"]] = mu[784:800]
    pc[:, PC["mu_z"]] = mu[800 + hc]
    pc[:, PC["k_k"]] = inp["rwkv_k_k"][l][hc]
    pc[:, PC["k_a"]] = inp["rwkv_k_a"][l][hc]
    pc[:, PC["a0"]] = inp["rwkv_a0"][l][hc]
    pc[:, PC["w0"]] = inp["rwkv_w0"][l][hc]
    pc[:, PC["r_k"]] = inp["rwkv_r_k"][l].reshape(-1)[hc]
    pc[:, PC["ln_g"]] = inp["rwkv_ln_g"][l][hc]
    pc[:, PC["out_g"]] = inp["mlstm_out_g"][l][hc]
    fb = inp["fox_f_b"][l]
    pc[:, PC["fb_B"]] = np.concatenate([np.full(64, fb[hs[1]]), np.full(64, fb[hs[0]])])
    ib = inp["mlstm_i_b"][l]
    pc[:, PC["ib_C"]] = np.concatenate([np.full(64, ib[hs[0]]), np.full(64, ib[hs[1]])])
    fbc = inp["mlstm_f_b"][l]
    pc[:, PC["fb_C0"]] = fbc[hs[0]]
    pc[:, PC["fb_C1"]] = fbc[hs[1]]

    pr = np.zeros((128, NPR), f32)
    pr[:, PR_SGU_G:PR_SGU_G + 128] = inp["sgu_norm_g"][l][hc][None, :]
    pr[:, PR_BQG:PR_BQG + 128] = np.tile(inp["fox_q_g"][l], 2)[None, :]
    pr[:, PR_BKG:PR_BKG + 128] = np.tile(inp["fox_k_g"][l], 2)[None, :]
    pr[:, PR_MQG:PR_MQG + 128] = np.tile(inp["mem_q_g"][l], 2)[None, :]
    pr[:, PR_MKG:PR_MKG + 128] = np.tile(inp["mem_k_g"][l], 2)[None, :]
    sb_ = inp["sgu_b"][l]
    pr[:64, PR_SGUB:PR_SGUB + 128] = sb_[hs[0]][None, :]
    pr[64:, PR_SGUB:PR_SGUB + 128] = sb_[hs[1]][None, :]

    d = {
        "wcat": wcat,
        "pc": pc,
        "pr": pr,
        "ng": np.ascontiguousarray(inp["norm_g"][l].reshape(8, 128).T),
        "memg": np.ascontiguousarray(inp["mem_norm_g"][l].reshape(8, 128).T),
        "wkv": np.ascontiguousarray(np.concatenate(
            [inp["mem_w_kv"][l][:, hc], inp["mem_w_kv"][l][:, 256 + hc]], axis=1)),
        "w2": np.ascontiguousarray(inp["rwkv_w2"][l][:, hc]),
        "a2": np.ascontiguousarray(inp["rwkv_a2"][l][:, hc]),
        "sguw": np.ascontiguousarray(inp["sgu_w"][l][hs]),
    }
    return d


class Builder:
    def __init__(self, SEQ, has_prev, branches="ABCDM"):
        self.SEQ = SEQ
        self.has_prev = has_prev
        self.branches = branches
        self.nc = bass.Bass("TRN2", target_bir_lowering=False)
        self.st = ExitStack()
        self.S = Sched(self.nc, self.st)
        self.ps_rr = 0

    def dram(self, name, shape, dt, kind):
        return self.nc.dram_tensor(name, shape, dt, kind=kind).ap()

    def sb(self, name, shape, dt=F32):
        if not hasattr(self, "_tiles"):
            self._tiles = {}
        if name not in self._tiles:
            self._tiles[name] = Tl(name, self.st.enter_context(self.nc.sbuf_tensor(name, shape, dt)))
        return self._tiles[name]

    def psum(self, name, shape, dt=F32):
        if not hasattr(self, "_tiles"):
            self._tiles = {}
        if name not in self._tiles:
            self._tiles[name] = Tl(name, self.st.enter_context(self.nc.psum_tensor(name, shape, dt)))
        return self._tiles[name]

    def gps(self):
        p = self.gp[self.ps_rr % len(self.gp)]
        self.ps_rr += 1
        return p

    def op(self, eng, fn, r=(), w=()):
        return self.S.add(eng, fn, reads=r, writes=w)

    def dma(self, eng, out, in_, r=(), w=()):
        return self.S.add(eng, lambda e: e.dma_start(out=out, in_=in_), reads=r, writes=w, dma=True)

    def mm(self, out, lhsT, rhs, start, stop, r, w):
        return self.S.add("tensor", lambda e: e.matmul(out, lhsT=lhsT, rhs=rhs, start=start, stop=stop),
                          reads=r, writes=w)

    def tr(self, out, in_, ident, r, w):
        return self.S.add("tensor", lambda e: e.transpose(out, in_, ident), reads=r, writes=w)

    def act(self, out, in_, func, r, w, bias=None, scale=None, accum_out=None, eng="scalar"):
        kw = {}
        if bias is not None:
            kw["bias"] = bias
        if scale is not None:
            kw["scale"] = scale
        if accum_out is not None:
            kw["accum_out"] = accum_out
        return self.S.add("scalar", lambda e: e.activation(out=out, in_=in_, func=func, **kw), reads=r, writes=w)

    def tt(self, eng, out, in0, in1, op, r, w):
        return self.S.add(eng, lambda e: e.tensor_tensor(out=out, in0=in0, in1=in1, op=op), reads=r, writes=w)

    def ts(self, eng, out, in0, s1, op0, r, w, s2=None, op1=None):
        if op1 is None:
            return self.S.add(eng, lambda e: e.tensor_scalar(out=out, in0=in0, scalar1=s1, scalar2=None, op0=op0),
                              reads=r, writes=w)
        return self.S.add(eng, lambda e: e.tensor_scalar(out=out, in0=in0, scalar1=s1, scalar2=s2, op0=op0, op1=op1),
                          reads=r, writes=w)

    def stt(self, out, in0, scalar, in1, op0, op1, r, w):
        return self.S.add("vector", lambda e: e.scalar_tensor_tensor(out=out, in0=in0, scalar=scalar, in1=in1,
                                                                      op0=op0, op1=op1), reads=r, writes=w)

    def cp(self, eng, out, in_, r, w):
        if eng == "scalar":
            return self.S.add("scalar", lambda e: e.copy(out=out, in_=in_), reads=r, writes=w)
        return self.S.add(eng, lambda e: e.tensor_copy(out, in_), reads=r, writes=w)

    def recip(self, out, in_, r, w):
        return self.S.add("vector", lambda e: e.reciprocal(out, in_), reads=r, writes=w)

    def memset(self, eng, ap, val, w):
        return self.S.add(eng, lambda e: e.memset(ap, val), writes=w)

    def asel(self, out, in_, cmp, w, r=(), fill=0.0, base=0, cm=-1, pat=None):
        pat = pat or [[1, 128]]
        return self.S.add("gpsimd", lambda e: e.affine_select(out=out, in_=in_, pattern=pat, compare_op=cmp,
                                                              fill=fill, base=base, channel_multiplier=cm),
                          reads=r, writes=w)

    def build(self):
        cfg = dict(tag="", xmode="outproj" if self.has_prev else "ext")
        self.last_out = []
        self.run_pass(cfg)
        self.S.emit(final_wait_ops=self.last_out)
        return self.nc

    def build_fused(self):
        SEQ = self.SEQ
        self.last_out = []
        self.x_ext = self.dram("xin", [SEQ, 1024], F32, "ExternalInput")
        self.mem_ext = self.dram("mem", [256, 1024], F32, "ExternalInput")
        self.mixs = [self.dram("mixs%d" % l, [1280, SEQ], BF16, "Internal") for l in range(2)]
        self.x1s = self.dram("x1s", [SEQ, 1024], F32, "Internal")
        self.wouts = [self.dram("wout%d" % l, [1280, 1024], F32, "ExternalInput") for l in range(2)]
        for l in range(2):
            if l == 1:
                self.begin("oproj")
                self.outproj_pass("ext", 0, self.x1s, "x1s", False)
            for hh in range(2):
                xmode = "ext" if l == 0 else "x1"
                self.run_pass(dict(tag="_%d%d" % (l, hh), xmode=xmode, fused=True, l=l, hh=hh))
        out_d = self.dram("out", [SEQ, 1024], F32, "ExternalOutput")
        self.begin("oproj")
        self.outproj_pass("x1", 1, out_d, "out_d", True)
        self.S.emit(final_wait_ops=self.last_out)
        return self.nc

    def outproj_pass(self, src_kind, l, dst, dst_name, final):
        SEQ = self.SEQ
        pP = [self.psum("pP0", [128, 512]), self.psum("pP1", [128, 512])]
        wb = self.sb("wb", [128, 8, NW], BF16)
        wo = Tl("wb", wb.t[:, :, :].rearrange("p a b -> p (a b)")[:, 0:10240].rearrange("p (c n) -> p c n", c=10))
        wov = self.wouts[l].rearrange("(c p) n -> p c n", p=128)
        for c in range(10):
            self.dma("gpsimd", wo[:, c, :], wov[:, c, :], w=[wo])
        xts = [self.sb("xt%d" % i, [128, 1024]) for i in range(2)]
        tm_ = self.sb("tm", [128, 4, 768], BF16)
        flat = tm_.t[:, :, :].rearrange("p a b -> p (a b)")
        mpvs = [Tl("tm", flat[:, 0:1280].rearrange("p (c t) -> p c t", c=10)),
                Tl("tm", flat[:, 1280:2560].rearrange("p (c t) -> p c t", c=10))]
        for ti in range(SEQ // 128):
            xt = xts[ti % 2]
            mpv = mpvs[ti % 2]
            r0 = ti * 128
            if src_kind == "ext":
                self.dma("sync", xt[:], self.x_ext[r0:r0 + 128, :], w=[xt])
            else:
                self.dma("sync", xt[:], self.x1s[r0:r0 + 128, :], r=[("x1s", ti)], w=[xt])
            self.dma("sync", mpv[:], self.mixs[l][:, r0:r0 + 128].rearrange("(c p) t -> p c t", p=128),
                     r=[("mixs%d" % l, (0, ti // 4)), ("mixs%d" % l, (1, ti // 4))], w=[mpv])
            for hf in range(2):
                p = pP[hf]
                for c in range(10):
                    self.mm(p[:], mpv[:, c, :], wo[:, c, hf * 512:(hf + 1) * 512], c == 0, c == 9,
                            [mpv, wo], [p])
                eng = "vector" if hf == 0 else "gpsimd"
                if eng == "gpsimd":
                    tmpo = self.scr("op_tmp", [128, 512])
                    self.cp("scalar", tmpo[:], p[:], [p], [tmpo])
                    self.tt("gpsimd", xt[:, hf * 512:(hf + 1) * 512], xt[:, hf * 512:(hf + 1) * 512], tmpo[:],
                            ALU.add, [xt, tmpo], [xt])
                else:
                    self.tt("vector", xt[:, hf * 512:(hf + 1) * 512], xt[:, hf * 512:(hf + 1) * 512], p[:],
                            ALU.add, [xt, p], [xt])
            o = self.dma("sync", dst[r0:r0 + 128, :], xt[:], r=[xt], w=[(dst_name, ti)])
            if final:
                self.last_out.append(o)

    def run_pass(self, cfg):
        nc = self.nc
        SEQ = self.SEQ
        NMC = SEQ // 512
        NCH = SEQ // 128
        tag = cfg["tag"]
        xmode = cfg["xmode"]
        fused = cfg.get("fused", False)
        has_prev = xmode == "outproj"
        wcat = self.dram("wcat" + tag, [1024, NW], F32, "ExternalInput")
        pc_d = self.dram("pc" + tag, [128, NPC], F32, "ExternalInput")
        pr_d = self.dram("pr" + tag, [128, NPR], F32, "ExternalInput")
        ng_d = self.dram("ng" + tag, [128, 8], F32, "ExternalInput")
        memg_d = self.dram("memg" + tag, [128, 8], F32, "ExternalInput")
        wkv_d = self.dram("wkv" + tag, [1024, 256], F32, "ExternalInput")
        w2_d = self.dram("w2" + tag, [16, 128], F32, "ExternalInput")
        a2_d = self.dram("a2" + tag, [16, 128], F32, "ExternalInput")
        sguw_d = self.dram("sguw" + tag, [2, 128, 128], F32, "ExternalInput")
        if fused:
            l, hh = cfg["l"], cfg["hh"]
            xin = self.x_ext
            mem_d = self.mem_ext
            mixed_d = self.mixs[l].rearrange("(g two p) t -> two p g t", two=2, p=128)[hh]
            mix_name = "mixs%d" % l
            mix_key = lambda mc: (hh, mc)
            if has_prev:
                mprev_d = self.mixs[0]
                wout_d = self.wouts[0]
                x1_d = self.x1s
        else:
            xin = self.dram("xin", [SEQ, 1024], F32, "ExternalInput")
            mem_d = self.dram("mem", [256, 1024], F32, "ExternalInput")
            mixed_d = self.dram("mixed", [640, SEQ], BF16, "ExternalOutput").rearrange("(g p) t -> p g t", p=128)
            mix_name = "mixed_d"
            mix_key = lambda mc: mc
            if has_prev:
                mprev_d = self.dram("mprev", [1280, SEQ], BF16, "ExternalInput")
                wout_d = self.dram("wout", [1280, 1024], F32, "ExternalInput")
                x1_d = self.dram("x1out", [SEQ, 1024], F32, "ExternalOutput")

        pT = self.psum("pT", [128, 1024], BF16)
        pP = [self.psum("pP0", [128, 512]), self.psum("pP1", [128, 512])]
        self.gp = [self.psum("pG%d" % i, [128, 512]) for i in range(3)]
        pNum = self.psum("pNum", [128, 512])
        pDen = self.psum("pDen", [128, 512])

        identf = self.sb("identf", [128, 128])
        ident = self.sb("ident", [128, 128], BF16)
        ones_bf = self.sb("ones_bf", [128, 128], BF16)
        onesf = self.sb("onesf", [128, 128])
        c64f = self.sb("c64f", [128, 128])
        c64b = self.sb("c64b", [128, 128], BF16)
        blk = self.sb("blk", [128, 128], BF16)
        m_ge = self.sb("m_ge", [128, 128], BF16)
        self.memset("vector", identf[:], 1.0, [identf])
        self.asel(identf[:], identf[:], ALU.is_equal, [identf], [identf])
        self.cp("vector", ident[:], identf[:], [identf], [ident])
        self.memset("vector", ones_bf[:], 1.0, [ones_bf])
        self.memset("vector", onesf[:], 1.0, [onesf])
        self.memset("vector", c64f[:], 1.0 / 64, [c64f])
        self.memset("vector", c64b[:], 1.0 / 64, [c64b])
        self.memset("vector", blk[:], 0.0, [blk])
        self.memset("vector", blk[0:64, 0:64], 1.0, [blk])
        self.memset("vector", blk[64:128, 64:128], 1.0, [blk])
        self.memset("vector", m_ge[:], 1.0, [m_ge])
        self.asel(m_ge[:], m_ge[:], ALU.is_ge, [m_ge], [m_ge])

        pc = self.sb("pcs", [128, NPC])
        pr = self.sb("prs", [128, NPR])
        ng = self.sb("ngs", [128, 8])
        memg = self.sb("memgs", [128, 8])
        self.dma("sync", pc[:], pc_d, w=[pc])
        self.dma("sync", pr[:], pr_d, w=[pr])
        self.dma("sync", ng[:], ng_d, w=[ng])
        self.dma("sync", memg[:], memg_d, w=[memg])
        omm = self.sb("omm", [128, 6])
        self.ts("vector", omm[:], pc[:, PC["mu_r"]:PC["mu_r"] + 6], -1.0, ALU.mult, [pc], [omm], s2=1.0, op1=ALU.add)
        nfbB = self.sb("nfbB", [128, 1])
        self.ts("vector", nfbB[:], pc[:, PC["fb_B"]:PC["fb_B"] + 1], -1.0, ALU.mult, [pc], [nfbB])
        nfbC = self.sb("nfbC", [128, 2])
        self.ts("vector", nfbC[:], pc[:, PC["fb_C0"]:PC["fb_C0"] + 2], -1.0, ALU.mult, [pc], [nfbC])
        gq8 = self.sb("gq8", [128, 128])
        self.ts("vector", gq8[:], pr[:, PR_BQG:PR_BQG + 128], 0.125, ALU.mult, [pr], [gq8])
        gmq8 = self.sb("gmq8", [128, 128])
        self.ts("vector", gmq8[:], pr[:, PR_MQG:PR_MQG + 128], 0.125, ALU.mult, [pr], [gmq8])

        wb = self.sb("wb", [128, 8, NW], BF16)
        wv = wcat.rearrange("(c p) n -> p c n", p=128)
        for c in range(8):
            self.dma("gpsimd", wb[:, c, :], wv[:, c, :], w=[(wb, c)])
        for c in range(8):
            eng = "vector" if c % 2 == 0 else "gpsimd"
            self.ts(eng, wb[:, c, :], wb[:, c, :], ng[:, c:c + 1], ALU.mult, [(wb, c), ng], [(wb, c)])
        if has_prev:
            wo = self.sb("wo", [128, 10, 1024], BF16)
            wov = wout_d.rearrange("(c p) n -> p c n", p=128)
            for c in range(10):
                self.dma("gpsimd", wo[:, c, :], wov[:, c, :], w=[(wo, c)])
        w2s = self.sb("w2s", [16, 128])
        a2s = self.sb("a2s", [16, 128])
        self.dma("sync", w2s[:], w2_d, w=[w2s])
        self.dma("sync", a2s[:], a2_d, w=[a2s])

        wsT = self.sb("wsT", [128, 2, 128], BF16)
        self.begin("setupA")
        if "A" in self.branches:
            sgw = self.scr("sgw", [128, 2, 128])
            self.dma("sync", sgw[:], sguw_d.rearrange("h t s -> t h s"), w=[sgw])
            sgwT = self.scr("sgwT", [128, 2, 128])
            for h in range(2):
                p = self.gps()
                self.tr(p[:, 0:128], sgw[:, h, :], identf[:], [sgw, identf], [p])
                self.cp("vector", sgwT[:, h, :], p[:, 0:128], [p], [sgwT])
                self.asel(sgwT[:, h, :], sgwT[:, h, :], ALU.is_ge, [sgwT], [sgwT])
            self.cp("vector", wsT[:], sgwT[:], [sgwT], [wsT])

        kTm = self.sb("kTm", [128, 256], BF16)
        vm = self.sb("vm", [128, 2, 128], BF16)

        self.alloc_state(NCH)

        xts = [self.sb("xt0", [128, 1024]), self.sb("xt1", [128, 1024])]
        hb = self.sb("hb", [128, 1024], BF16)
        hT = self.sb("hT", [128, 8, 512], BF16)
        ss = self.sb("ss", [128, 2])
        GATED = {"Az": AF.Silu, "Bz": AF.Silu, "Cz": AF.Silu, "Mz": AF.Silu, "Co": AF.Sigmoid}
        fm = {}
        for g in FM_GROUPS:
            if g in ("Cq", "Ck"):
                fm[g] = self.sb("fm_" + g, [128, 4 + 512], BF16)
            elif g in ("Dr", "Dk", "Dv", "Dz"):
                fm[g] = self.sb("fm_" + g, [128, 2 + 512], BF16)
            elif g in ("Dw", "Da"):
                fm[g] = self.sb("fm_" + g, [16, 1 + 512])
            elif g in GATED:
                fm[g] = self.sb("fm_" + g, [128, 512], BF16)
            else:
                fm[g] = self.sb("fm_" + g, [128, 512])
        tm = self.sb("tm", [128, 4, 768], BF16)
        mixo = self.sb("mixo", [128, 5, 512], BF16)
        if "M" in self.branches:
            self.setup_mem(mem_d, wkv_d, memg, pr, ident, identf, pT, kTm, vm, xts, hT, tm, hb)
        mpvs = [Tl("tm", tm.t[:, :, :].rearrange("p a b -> p (a b)")[:, 0:1280].rearrange("p (c t) -> p c t", c=10))] * 2
        for g in ("Cq", "Ck"):
            self.memset("vector", fm[g][:, 0:3], 0.0, [(fm[g], "hist")])
        for g in ("Dr", "Dk", "Dv", "Dz"):
            self.memset("vector", fm[g][:, 0:1], 0.0, [(fm[g], "hist")])
        for g in ("Dw", "Da"):
            self.memset("vector", fm[g][:, 0:1], 0.0, [(fm[g], "hist")])

        self.consts = dict(ident=ident, identf=identf, ones_bf=ones_bf, onesf=onesf, c64f=c64f, c64b=c64b,
                           blk=blk, m_ge=m_ge, pc=pc, pr=pr, omm=omm, nfbB=nfbB, nfbC=nfbC,
                           gq8=gq8, gmq8=gmq8, wsT=wsT, kTm=kTm, vm=vm, w2s=w2s, a2s=a2s, pT=pT,
                           pNum=pNum, pDen=pDen)
        last_out = self.last_out
        xi = 0
        xstate = {"xi": 0}

        def xprep(mc):
            t0 = mc * 512
            for tt in range(4):
                xi = xstate["xi"]
                xt = xts[xi % 2]
                r0 = t0 + tt * 128
                ti = mc * 4 + tt
                if xmode == "x1":
                    self.dma("sync", xt[:], self.x1s[r0:r0 + 128, :], r=[("x1s", ti)], w=[xt])
                else:
                    self.dma("sync", xt[:], xin[r0:r0 + 128, :], w=[xt])
                if has_prev:
                    mpv = mpvs[xi % 2]
                    rr = [("mixs0", (0, mc)), ("mixs0", (1, mc))] if fused else []
                    self.dma("sync", mpv[:], mprev_d[:, r0:r0 + 128].rearrange("(c p) t -> p c t", p=128),
                             r=rr, w=[mpv])
                    for hf in range(2):
                        p = pP[hf]
                        for c in range(10):
                            self.mm(p[:], mpv[:, c, :], wo[:, c, hf * 512:(hf + 1) * 512],
                                    c == 0, c == 9, [mpv, (wo, c)], [p])
                        self.tt("vector", xt[:, hf * 512:(hf + 1) * 512], xt[:, hf * 512:(hf + 1) * 512],
                                p[:], ALU.add, [xt, p], [xt])
                    o = self.dma("sync", x1_d[r0:r0 + 128, :], xt[:], r=[xt], w=[("x1s", ti)])
                    if not fused:
                        last_out.append(o)
                xstate["xi"] = xi + 1
                self.act(hb[:], xt[:], AF.Square, [xt], [hb, ss], accum_out=ss[:, 0:1])
                self.act(ss[:, 1:2], ss[:, 0:1], AF.Ln, [ss], [ss], scale=1.0 / 1024, bias=EPS)
                self.act(ss[:, 1:2], ss[:, 1:2], AF.Exp, [ss], [ss], scale=-0.5)
                self.ts("vector", hb[:], xt[:], ss[:, 1:2], ALU.mult, [xt, ss], [hb])
                yield
                for c in range(8):
                    self.tr(pT[:, c * 128:(c + 1) * 128], hb[:, c * 128:(c + 1) * 128], ident[:],
                            [hb, ident], [pT])
                eng = "vector" if tt % 2 == 0 else "scalar"
                self.cp(eng, hT[:, :, tt * 128:(tt + 1) * 128],
                        pT[:, :].rearrange("p (c t) -> p c t", c=8), [pT], [hT])
                yield

        for _ in xprep(0):
            pass
        for mc in range(NMC):
            t0 = mc * 512
            for gi, g in enumerate(FM_GROUPS):
                p = pP[gi % 2]
                wdt = FM_W[g]
                for c in range(8):
                    self.mm(p[0:wdt, :], wb[:, c, FM_OFF[g]:FM_OFF[g] + wdt], hT[:, c, :], c == 0, c == 7,
                            [(wb, c), hT], [p])
                hist = {"Cq": 3, "Ck": 3, "Dr": 1, "Dk": 1, "Dv": 1, "Dz": 1, "Dw": 1, "Da": 1}.get(g, 0)
                dst = fm[g]
                if g in GATED:
                    self.act(dst[:, :], p[:, :], GATED[g], [p], [(dst, "cur")])
                    continue
                if hist and mc > 0:
                    self.cp("vector", dst[0:wdt, 0:hist], dst[0:wdt, 512:512 + hist], [(dst, "cur")], [(dst, "hist")])
                eng = "scalar" if gi % 2 == 0 else "vector"
                self.cp(eng, dst[0:wdt, hist:hist + 512], p[0:wdt, :], [p, (dst, "hist")], [(dst, "cur")])
            for tt in range(4):
                for hf in range(2):
                    p = pP[hf]
                    for c in range(8):
                        self.mm(p[:, 0:384], hT[:, c, tt * 128:(tt + 1) * 128],
                                wb[:, c, NFM + hf * 384:NFM + (hf + 1) * 384], c == 0, c == 7, [hT, (wb, c)], [p])
                    eng = "scalar" if hf == 0 else "vector"
                    self.cp(eng, tm[:, tt, hf * 384:(hf + 1) * 384], p[:, 0:384], [p], [(tm, tt)])
            self.zero_mix = []
            if "A" in self.branches:
                self.branch_A(mc, fm, tm, mixo)
            else:
                self.memset("gpsimd", mixo[:, 0, :], 0.0, [(mixo, 0)])
            if "B" in self.branches:
                self.branch_B(mc, fm, tm, mixo)
            else:
                self.memset("gpsimd", mixo[:, 1, :], 0.0, [(mixo, 1)])
            gens = []
            if mc + 1 < NMC:
                gens.append(("X", xprep(mc + 1)))
            if "D" in self.branches:
                gens.append(("D", self.branch_D(mc, fm, tm, mixo)))
            else:
                self.memset("gpsimd", mixo[:, 3, :], 0.0, [(mixo, 3)])
            if "C" in self.branches:
                gens.append(("C", self.branch_C(mc, fm, tm, mixo)))
            else:
                self.memset("gpsimd", mixo[:, 2, :], 0.0, [(mixo, 2)])
            while gens:
                for item in list(gens):
                    for rep in range({"D": 2, "C": 2}.get(item[0], 1)):
                        self._br = item[0]
                        try:
                            next(item[1])
                        except StopIteration:
                            gens.remove(item)
                            break
            if "M" in self.branches:
                self.branch_M(mc, fm, tm, mixo)
            else:
                self.memset("gpsimd", mixo[:, 4, :], 0.0, [(mixo, 4)])
            o = self.dma("sync", mixed_d[:, :, t0:t0 + 512], mixo[:], r=[mixo], w=[(mix_name, mix_key(mc))])
            if not fused:
                last_out.append(o)

    def head_rms_tm(self, src, dst, gain, tag, nt=4):
        sq = self.scr("hr_sq", [128, 4, 128])
        ssq = self.scr("hr_ssq", [128, 8])
        src_ap, src_r = src
        dst_ap, dst_w = dst
        g_ap, g_r = gain
        self.tt("gpsimd", sq[:, 0:nt, :], src_ap, src_ap, ALU.mult, src_r, [sq])
        ssq3 = ssq[:, 0:2 * nt].rearrange("p (t h) -> p t h", h=2)
        self.op("vector", lambda e: e.tensor_reduce(out=ssq3,
                                                     in_=sq[:, 0:nt, :].rearrange("p t (h j) -> p t h j", h=2),
                                                     axis=AX.X, op=ALU.add), [sq], [ssq])
        self.act(ssq[:, 0:2 * nt], ssq[:, 0:2 * nt], AF.Ln, [ssq], [ssq], scale=1.0 / 64, bias=EPS)
        self.act(ssq[:, 0:2 * nt], ssq[:, 0:2 * nt], AF.Exp, [ssq], [ssq], scale=-0.5)
        self.tt("vector", sq[:, 0:nt, :].rearrange("p t (h j) -> p t h j", h=2),
                src_ap.rearrange("p t (h j) -> p t h j", h=2),
                ssq3[:, :, :, None].broadcast_to([128, nt, 2, 64]), ALU.mult, src_r + [ssq], [sq])
        self.tt("vector", dst_ap, sq[:, 0:nt, :], g_ap[:, None, :].broadcast_to([128, nt, 128]), ALU.mult,
                [sq] + g_r, dst_w)

    def begin(self, br):
        self._br = br
        if not hasattr(self, "_brcount"):
            self._brcount = {}
        self._brcount.setdefault(br, {})

    def scr(self, name, shape, dt=F32):
        if not hasattr(self, "_scrmap"):
            self._scrmap = {}
            self._pools = {}
        br = getattr(self, "_br", "x")
        key = (br, name)
        if key in self._scrmap:
            return self._scrmap[key]
        esz = 4 if dt == F32 else 2
        n = 1
        for d_ in shape[1:]:
            n *= d_
        nbytes = n * esz
        cls = 256
        while cls < nbytes:
            cls *= 2
        cnt = self._brcount.setdefault(br, {})
        k = cnt.get(cls, 0)
        cnt[cls] = k + 1
        fam = "C" if br == "C" else ""
        pool = self._pools.setdefault((fam, cls), [])
        if k >= len(pool):
            pname = "scr%s%d_%d" % (fam, cls, k)
            pool.append((pname, self.st.enter_context(self.nc.sbuf_tensor(pname, [128, cls // 4], F32))))
        pname, raw = pool[k]
        h = raw if dt == F32 else raw.bitcast(dt)
        ap = h[0:shape[0], 0:n]
        if len(shape) == 3:
            ap = ap.rearrange("p (a b) -> p a b", a=shape[1])
        elif len(shape) == 4:
            ap = ap.rearrange("p (a b c) -> p a b c", a=shape[1], b=shape[2])
        t = Tl(pname, ap)
        self._scrmap[key] = t
        return t

    def gelu(self, dst_ap, dst_w, src_ap, src_r, shape, tag):
        t1 = self.scr("gl_t1" + tag, shape)
        t2 = self.scr("gl_t2" + tag, shape)
        self.tt("gpsimd", t1[:], src_ap, src_ap, ALU.mult, src_r, [t1])
        self.ts("vector", t1[:], t1[:], 0.044715, ALU.mult, [t1], [t1], s2=1.0, op1=ALU.add)
        self.tt("gpsimd", t2[:], t1[:], src_ap, ALU.mult, [t1] + src_r, [t2])
        self.act(t2[:], t2[:], AF.Sigmoid, [t2], [t2], scale=1.5957691216)
        self.tt("vector", dst_ap, t2[:], src_ap, ALU.mult, [t2] + src_r, dst_w)

    def setup_mem(self, mem_d, wkv_d, memg, pr, ident, identf, pT, kTm, vm, xts, hT, tm, hb):
        self.begin("setup")
        for c in range(2):
            self.dma("sync", xts[c][:], mem_d[c * 128:(c + 1) * 128, :], w=[xts[c]])
        wk = Tl("hT", hT[:, :, 0:256])
        mhT = Tl("hT", hT[:, :, 256:512])
        mh = Tl("tm", tm.t[:, :, :].rearrange("p a b -> p (a b)")[:, 0:2048].rearrange("p (c d) -> p c d", c=2))
        wkv_v = wkv_d.rearrange("(c p) n -> p c n", p=128)
        for c in range(8):
            self.dma("gpsimd", wk[:, c, :], wkv_v[:, c, :], w=[wk])
        for c in range(8):
            self.ts("vector", wk[:, c, :], wk[:, c, :], memg[:, c:c + 1], ALU.mult, [wk, memg], [wk])
        mss = self.scr("m_ss", [128, 2])
        for c in range(2):
            self.act(hb[:], xts[c][:], AF.Square, [xts[c]], [hb, mss], accum_out=mss[:, c:c + 1])
        self.act(mss[:], mss[:], AF.Ln, [mss], [mss], scale=1.0 / 1024, bias=EPS)
        self.act(mss[:], mss[:], AF.Exp, [mss], [mss], scale=-0.5)
        for c in range(2):
            self.ts("vector", mh[:, c, :], xts[c][:], mss[:, c:c + 1], ALU.mult, [xts[c], mss], [mh])
        for c2 in range(2):
            for c in range(8):
                self.tr(pT[:, c * 128:(c + 1) * 128], mh[:, c2, c * 128:(c + 1) * 128], ident[:], [mh, ident], [pT])
            self.cp("vector", mhT[:, :, c2 * 128:(c2 + 1) * 128], pT[:, :].rearrange("p (c t) -> p c t", c=8),
                    [pT], [mhT])
        kvt = self.scr("m_kvt", [128, 2, 256])
        for c2 in range(2):
            p = self.gps()
            for c in range(8):
                self.mm(p[:, 0:256], mhT[:, c, c2 * 128:(c2 + 1) * 128], wk[:, c, :], c == 0, c == 7, [mhT, wk], [p])
            self.cp("vector", kvt[:, c2, :], p[:, 0:256], [p], [kvt])
        self.cp("vector", vm[:], kvt[:, :, 128:256], [kvt], [vm])
        kn = self.scr("m_kn", [128, 2, 128], BF16)
        self.head_rms_tm((kvt[:, :, 0:128], [kvt]), (kn[:], [kn]), (pr[:, PR_MKG:PR_MKG + 128], [pr]), "mk", nt=2)
        for c2 in range(2):
            self.tr(pT[:, c2 * 128:(c2 + 1) * 128], kn[:, c2, :], ident[:], [kn, ident], [pT])
        self.cp("vector", kTm[:], pT[:, 0:256], [pT], [kTm])

    def alloc_state(self, NCH):
        SEQ = self.SEQ
        if "B" in self.branches:
            self.KT = self.sb("KT", [128, SEQ], BF16)
            self.VB = self.sb("VB", [128, NCH, 128], BF16)
            self.Fcol = self.sb("Fcol", [128, 2, NCH])
            self.Fprev = self.sb("Fprev", [128, 1])
            self.memset("vector", self.Fprev[:], 0.0, [self.Fprev])
            self.c64h = [self.sb("c64h%d" % h, [128, 128], BF16) for h in range(2)]
            self.hm = self.sb("hm", [128, 2])
            self.memset("vector", self.hm[:], 0.0, [self.hm])
            for h in range(2):
                hs = slice(h * 64, (h + 1) * 64)
                fs = slice(64, 128) if h == 0 else slice(0, 64)
                self.memset("vector", self.c64h[h][:], 0.0, [self.c64h[h]])
                self.memset("vector", self.c64h[h][fs, :], 1.0 / 64, [self.c64h[h]])
                self.memset("vector", self.hm[hs, h:h + 1], 1.0, [self.hm])
        if "C" in self.branches:
            self.Cst = self.sb("Cst", [128, 128])
            self.Cstb = self.sb("Cstb", [128, 128], BF16)
            self.memset("vector", self.Cst[:], 0.0, [self.Cst])
            self.memset("vector", self.Cstb[:], 0.0, [self.Cstb])
        if "D" in self.branches:
            self.Sst = self.sb("Sst", [128, 64])
            self.Sstb = self.sb("Sstb", [128, 64], BF16)
            self.memset("vector", self.Sst[:], 0.0, [self.Sst])
            self.memset("vector", self.Sstb[:], 0.0, [self.Sstb])

    def branch_A(self, mc, fm, tm, mixo):
        self.begin("A")
        c = self.consts
        pr = c["pr"]
        gu = self.scr("A_gu", [128, 512])
        self.gelu(gu[:], [gu], fm["Au"][:, :], [(fm["Au"], "cur")], [128, 512], "u")
        gv = self.scr("A_gv", [128, 4, 128])
        self.gelu(gv[:], [gv], tm[:, :, 0:128], [tm], [128, 4, 128], "v")
        vn = self.scr("A_vn", [128, 4, 128], BF16)
        self.head_rms_tm((gv[:], [gv]), (vn[:], [vn]), (pr[:, PR_SGU_G:PR_SGU_G + 128], [pr]), "av")
        p = self.gps()
        for tt in range(4):
            for h in range(2):
                self.mm(p[h * 64:(h + 1) * 64, tt * 128:(tt + 1) * 128], vn[:, tt, h * 64:(h + 1) * 64],
                        c["wsT"][:, h, :], True, True, [vn, c["wsT"]], [p])
        ya = self.scr("A_ya", [128, 512])
        self.tt("vector", ya[:].rearrange("p (a t) -> p a t", a=4), p[:].rearrange("p (a t) -> p a t", a=4),
                pr[:, None, PR_SGUB:PR_SGUB + 128].broadcast_to([128, 4, 128]), ALU.add, [p, pr], [ya])
        self.tt("gpsimd", ya[:], ya[:], gu[:], ALU.mult, [ya, gu], [ya])
        self.tt("vector", mixo[:, 0, :], ya[:], fm["Az"][:, :], ALU.mult, [ya, fm["Az"]], [(mixo, 0)])

    def branch_M(self, mc, fm, tm, mixo):
        self.begin("M")
        c = self.consts
        pT = c["pT"]
        qn = self.scr("M_qn", [128, 4, 128], BF16)
        self.head_rms_tm((tm[:, :, 640:768], [tm]), (qn[:], [qn]), (c["gmq8"][:], [c["gmq8"]]), "mq")
        qT = self.scr("M_qT", [128, 512], BF16)
        for tt in range(4):
            self.tr(pT[:, tt * 128:(tt + 1) * 128], qn[:, tt, :], c["ident"][:], [qn, c["ident"]], [pT])
        self.cp("vector", qT[:], pT[:, 0:512], [pT], [qT])
        pn = c["pNum"]
        pd = c["pDen"]
        E = self.scr("M_E", [128, 512], BF16)
        for h in range(2):
            hs = slice(h * 64, (h + 1) * 64)
            for mcx in range(2):
                ps_ = self.gps()
                self.mm(ps_[:], c["kTm"][hs, mcx * 128:(mcx + 1) * 128], qT[hs, :], True, True, [c["kTm"], qT], [ps_])
                self.act(E[:], ps_[:], AF.Exp, [ps_], [E])
                self.mm(pn[hs, :], c["vm"][:, mcx, hs], E[:], mcx == 0, mcx == 1, [c["vm"], E], [(pn, h)])
                self.mm(pd[hs, :], c["ones_bf"][:, 0:64], E[:], mcx == 0, mcx == 1, [c["ones_bf"], E], [(pd, h)])
        rd = self.scr("M_rd", [128, 512])
        self.recip(rd[:], pd[:], [pd], [rd])
        ym = self.scr("M_ym", [128, 512])
        self.tt("vector", ym[:], pn[:], rd[:], ALU.mult, [pn, rd], [ym])
        self.tt("gpsimd", mixo[:, 4, :], ym[:], fm["Mz"][:, :], ALU.mult, [ym, fm["Mz"]], [(mixo, 4)])

    def branch_B(self, mc, fm, tm, mixo):
        self.begin("B")
        c = self.consts
        pT = c["pT"]
        pr = c["pr"]
        G = mc
        if getattr(self, "bstage", 9) < 1:
            return
        qn = self.scr("B_qn", [128, 4, 128], BF16)
        kn = self.scr("B_kn", [128, 4, 128], BF16)
        self.head_rms_tm((tm[:, :, 128:256], [tm]), (qn[:], [qn]), (c["gq8"][:], [c["gq8"]]), "bq")
        self.head_rms_tm((tm[:, :, 256:384], [tm]), (kn[:], [kn]), (pr[:, PR_BKG:PR_BKG + 128], [pr]), "bk")
        QT = self.scr("B_QT", [128, 512], BF16)
        if getattr(self, "bstage", 9) < 0.3:
            return
        for tt in range(4):
            self.tr(pT[:, tt * 128:(tt + 1) * 128], qn[:, tt, :], c["ident"][:], [qn, c["ident"]], [pT])
        for tt in range(4):
            self.tr(pT[:, 512 + tt * 128:512 + (tt + 1) * 128], kn[:, tt, :], c["ident"][:], [kn, c["ident"]], [pT])
        self.cp("vector", QT[:], pT[:, 0:512], [pT], [QT])
        self.cp("scalar", self.KT[:, G * 512:(G + 1) * 512], pT[:, 512:1024], [pT], [(self.KT, G)])
        QTm = [self.scr("B_QTm%d" % h, [128, 512], BF16) for h in range(2)]
        for h in range(2):
            self.ts("gpsimd", QTm[h][:], QT[:], self.hm[:, h:h + 1], ALU.mult, [QT, self.hm], [QTm[h]])
        for tt in range(4):
            self.cp("gpsimd", self.VB[:, G * 4 + tt, :], tm[:, tt, 384:512], [tm], [(self.VB, G * 4 + tt)])
        e1 = self.scr("B_e1", [128, 512])
        self.act(e1[:], fm["Bf"][:, :], AF.Exp, [(fm["Bf"], "cur")], [e1], scale=-1.0, bias=c["nfbB"][:, 0:1])
        self.act(e1[:], e1[:], AF.Ln, [e1], [e1], bias=1.0)
        Lc = self.scr("B_Lc", [128, 512])
        for q4 in range(4):
            qs = slice(q4 * 128, (q4 + 1) * 128)
            ini = self.Fprev[:, 0:1] if q4 == 0 else Lc[:, q4 * 128 - 1:q4 * 128]
            self.op("vector", lambda e, qs=qs, ini=ini: e.tensor_tensor_scan(
                out=Lc[:, qs], data0=c["onesf"][:, 0:128], data1=e1[:, qs], initial=ini,
                op0=ALU.mult, op1=ALU.add), [c["onesf"], e1, self.Fprev, Lc], [Lc])
        pcg = self.gps()
        for h in range(2):
            fs = slice(64, 128) if h == 0 else slice(0, 64)
            for j in range(4):
                self.mm(pcg[:, h * 8 + j:h * 8 + j + 1], Lc[fs, j * 128:(j + 1) * 128], c["c64f"][fs, 0:1],
                        True, True, [Lc, c["c64f"]], [pcg])
            for hf in range(2):
                col = hf * 256 + 127
                self.mm(pcg[:, 16 + h * 2 + hf:17 + h * 2 + hf], c["c64f"][fs, :], Lc[fs, col:col + 1], True, True,
                        [c["c64f"], Lc], [pcg])
        cg = self.scr("B_cg", [128, 4])
        for h in range(2):
            self.cp("vector", self.Fcol[:, h, G * 4:G * 4 + 4], pcg[:, h * 8:h * 8 + 4], [pcg], [(self.Fcol, G)])
        self.cp("vector", cg[:], pcg[:, 16:20], [pcg], [cg])
        self.cp("vector", self.Fprev[:], Lc[:, 511:512], [Lc], [self.Fprev])
        nkb = 4 * G + 4
        bias = self.scr("B_bias", [128, 4, self.SEQ // 128])
        for h in range(2):
            for hf in range(2):
                self.ts("vector", bias[:, h * 2 + hf, 0:nkb], self.Fcol[:, h, 0:nkb],
                        cg[:, h * 2 + hf:h * 2 + hf + 1], ALU.subtract, [self.Fcol, cg], [bias])
        pNumH = [c["pNum"], c["pDen"]]
        Es = [self.scr("B_E%d" % i, [128, 512], BF16) for i in range(4)]
        Eacc = [self.scr("B_Eacc%d" % h, [128, 512]) for h in range(2)]
        rd = self.scr("B_rd", [128, 512])
        yb = self.scr("B_yb", [128, 512])
        pairs = [(h, kb) for h in range(2) for kb in range(nkb)]
        psl = {}

        def score(i):
            h, kb = pairs[i]
            j = kb - 4 * G
            c0 = 0 if j <= 0 else j * 128
            ps_ = self.gps()
            self.mm(ps_[:, c0:512], self.KT[:, kb * 128:(kb + 1) * 128], QTm[h][:, c0:512], True, True,
                    [self.KT, QTm[h]], [ps_])
            psl[i] = ps_
        score(0)
        for i, (h, kb) in enumerate(pairs):
            if i + 1 < len(pairs):
                score(i + 1)
            hs = slice(h * 64, (h + 1) * 64)
            pN = pNumH[h]
            j = kb - 4 * G
            c0 = 0 if j <= 0 else j * 128
            ps_ = psl.pop(i)
            E = Es[i % 4]
            for hf in range(2):
                a_, b_ = max(c0, hf * 256), (hf + 1) * 256
                if a_ >= b_:
                    continue
                self.act(E[:, a_:b_], ps_[:, a_:b_], AF.Exp, [ps_, bias], [E],
                         bias=bias[:, h * 2 + hf, kb:kb + 1])
            if j >= 0:
                self.tt("gpsimd", E[:, c0:c0 + 128], E[:, c0:c0 + 128], c["m_ge"][:], ALU.mult,
                        [E, c["m_ge"]], [E])
            self.mm(pN[:, c0:512], self.VB[:, kb, :], E[:, c0:512], kb == 0, kb == nkb - 1,
                    [self.VB, E], [pN])
            if kb == 0:
                self.cp("vector", Eacc[h][:], E[:], [E], [Eacc[h]])
            else:
                self.tt("vector", Eacc[h][:, c0:512], Eacc[h][:, c0:512], E[:, c0:512], ALU.add,
                        [Eacc[h], E], [Eacc[h]])
            if kb == nkb - 1:
                pd_ = self.gps()
                self.mm(pd_[hs, :], c["onesf"][:, 0:64], Eacc[h][:], True, True, [c["onesf"], Eacc[h]], [pd_])
                self.recip(rd[hs, :], pd_[hs, :], [pd_], [rd])
                self.tt("vector", yb[hs, :], pN[hs, :], rd[hs, :], ALU.mult, [pN, rd], [yb])
        self.tt("gpsimd", mixo[:, 1, :], yb[:], fm["Bz"][:, :], ALU.mult, [yb, fm["Bz"]], [(mixo, 1)])

    def branch_C(self, mc, fm, tm, mixo):
        self.begin("C")
        c = self.consts
        pc = c["pc"]
        pT = c["pT"]
        qc = self.scr("C_qc", [128, 512], BF16)
        kc = self.scr("C_kc", [128, 512], BF16)
        for g, dst, wn, bn in (("Cq", qc, "cq", "cqb"), ("Ck", kc, "ck", "ckb")):
            x = fm[g]
            acc = self.scr("C_acc" + g, [128, 512])
            self.ts("vector", acc[:], x[:, 0:512], pc[:, PC[wn + "0"]:PC[wn + "0"] + 1], ALU.mult, [x, pc], [acc],
                    s2=pc[:, PC[bn]:PC[bn] + 1], op1=ALU.add)
            for j in range(1, 4):
                self.stt(acc[:], x[:, j:j + 512], pc[:, PC[wn + str(j)]:PC[wn + str(j)] + 1], acc[:], ALU.mult,
                         ALU.add, [x, pc, acc], [acc])
            if g == "Cq":
                self.act(dst[:], acc[:], AF.Silu, [acc], [dst])
            else:
                self.act(acc[:], acc[:], AF.Silu, [acc], [acc])
                self.ts("vector", dst[:], acc[:], 0.125, ALU.mult, [acc], [dst])
        yield
        Lb = []
        for h in range(2):
            e1 = fm["Cf%d" % h]
            self.act(e1[:], fm["Cf%d" % h][:, :], AF.Exp, [(fm["Cf%d" % h], "cur")], [e1], scale=-1.0,
                     bias=c["nfbC"][:, h:h + 1])
            self.act(e1[:], e1[:], AF.Ln, [e1], [e1], bias=1.0)
            L = self.scr("C_Lb%d" % h, [128, 512])
            for ch in range(4):
                cs = slice(ch * 128, (ch + 1) * 128)
                self.op("vector", lambda e, L=L, e1=e1, cs=cs: e.tensor_tensor_scan(
                    out=L[:, cs], data0=c["onesf"][:, 0:128], data1=e1[:, cs], initial=0.0,
                    op0=ALU.mult, op1=ALU.add), [c["onesf"], e1], [L])
            Lb.append(L)
        ilog = fm["Ci"]
        self.ts("vector", ilog[:], fm["Ci"][:, :], pc[:, PC["ib_C"]:PC["ib_C"] + 1], ALU.add,
                [(fm["Ci"], "cur"), pc], [ilog])
        yield
        so = fm["Co"]
        sz = fm["Cz"]
        if not hasattr(self, "vaug"):
            self.vaug = [self.sb("C_vaug%d" % h, [128, 128], BF16) for h in range(2)]
            for h in range(2):
                self.memset("vector", self.vaug[h][:], 1.0, [self.vaug[h]])
        vaug = self.vaug
        for ch in range(4):
            cs = slice(ch * 128, (ch + 1) * 128)
            pcol = self.gps()
            for h in range(2):
                hs = slice(h * 64, (h + 1) * 64)
                self.mm(pcol[:, h:h + 1], Lb[h][0:64, cs], c["c64f"][0:64, 0:1], True, True, [Lb[h], c["c64f"]], [pcol])
                self.mm(pcol[:, 2 + h:3 + h], ilog[hs, cs], c["c64f"][hs, 0:1], True, True, [ilog, c["c64f"]], [pcol])
            col = self.scr("C_col", [128, 8])
            self.cp("vector", col[:, 0:4], pcol[:, 0:4], [pcol], [col])
            yield
            self.tt("vector", col[:, 4:6], col[:, 0:2], col[:, 2:4], ALU.add, [col], [col])
            for h in range(2):
                self.ts("vector", col[:, 6 + h:7 + h], col[:, 4 + h:5 + h], Lb[h][:, ch * 128 + 127:ch * 128 + 128],
                        ALU.subtract, [col, Lb[h]], [col])
            wcol = self.scr("C_wcol", [128, 2])
            self.act(wcol[:], col[:, 6:8], AF.Exp, [col], [wcol])
            yield
            eg = self.scr("C_eg", [128, 2])
            for h in range(2):
                self.act(eg[:, h:h + 1], Lb[h][:, ch * 128 + 127:ch * 128 + 128], AF.Exp, [Lb[h]], [eg], scale=-1.0)
            ebq = self.scr("C_ebq", [128, 128])
            for h in range(2):
                hs = slice(h * 64, (h + 1) * 64)
                self.act(ebq[hs, :], Lb[h][hs, cs], AF.Exp, [Lb[h]], [ebq], scale=-1.0)
            qp = self.scr("C_qp", [128, 128], BF16)
            self.tt("vector", qp[:], qc[:, cs], ebq[:], ALU.mult, [qc, ebq], [qp])
            yield
            for h in range(2):
                hs = slice(h * 64, (h + 1) * 64)
                self.cp("gpsimd", vaug[h][:, 0:64], tm[:, ch, 512 + h * 64:512 + (h + 1) * 64], [tm], [vaug[h]])
            pS = self.gps()
            P = []
            for h in range(2):
                hs = slice(h * 64, (h + 1) * 64)
                self.mm(pS[:, h * 128:(h + 1) * 128], kc[hs, cs], qc[hs, cs], True, True, [kc, qc], [pS])
                D = self.scr("C_D%d" % h, [128, 128])
                self.ts("vector", D[:], Lb[h][:, cs], col[:, h:h + 1], ALU.subtract, [Lb[h], col], [D], s2=0.0,
                        op1=ALU.max)
                self.act(D[:], D[:], AF.Exp, [D, col], [D], scale=-1.0, bias=col[:, 2 + h:3 + h])
                self.tt("gpsimd", D[:], D[:], c["m_ge"][:], ALU.mult, [D, c["m_ge"]], [D])
                Ph = self.scr("C_P%d" % h, [128, 128], BF16)
                self.tt("vector", Ph[:], pS[:, h * 128:(h + 1) * 128], D[:], ALU.mult, [pS, D], [Ph])
                P.append(Ph)
            yield
            pnd = self.gps()
            for h in range(2):
                hs = slice(h * 64, (h + 1) * 64)
                self.mm(pnd[hs, 0:128], vaug[h][:, 0:64], P[h][:], True, False, [vaug[h], P[h]], [pnd])
                self.mm(pnd[hs, 0:128], self.Cstb[hs, 0:64], qp[hs, :], False, True, [self.Cstb, qp], [pnd])
            for h in range(2):
                hs = slice(h * 64, (h + 1) * 64)
                self.mm(pnd[hs, 128:256], c["ones_bf"][:, 0:64], P[h][:], True, False, [c["ones_bf"], P[h]], [pnd])
                self.mm(pnd[hs, 128:256], self.Cstb[hs, 64:128], qp[hs, :], False, True, [self.Cstb, qp], [pnd])
            dn = self.scr("C_dn", [128, 128])
            self.act(dn[:], pnd[:, 128:256], AF.Abs, [pnd], [dn])
            self.ts("vector", dn[:], dn[:], 1.0, ALU.max, [dn], [dn])
            self.recip(dn[:], dn[:], [dn], [dn])
            ho = self.scr("C_ho", [128, 128])
            self.tt("vector", ho[:], pnd[:, 0:128], dn[:], ALU.mult, [pnd, dn], [ho])
            yield
            self.tt("gpsimd", ho[:], ho[:], so[:, cs], ALU.mult, [ho, so], [ho])
            yield
            sq = self.scr("C_sq", [128, 128], BF16)
            self.tt("gpsimd", sq[:], ho[:], ho[:], ALU.mult, [ho], [sq])
            pq = self.gps()
            self.mm(pq[:, 0:128], c["blk"][:], sq[:], True, True, [c["blk"], sq], [pq])
            rs = self.scr("C_rs", [128, 128])
            self.act(rs[:], pq[:, 0:128], AF.Ln, [pq], [rs], scale=1.0 / 64, bias=EPS)
            self.act(rs[:], rs[:], AF.Exp, [rs], [rs], scale=-0.5)
            self.stt(ho[:], ho[:], pc[:, PC["out_g"]:PC["out_g"] + 1], rs[:], ALU.mult, ALU.mult, [ho, pc, rs], [ho])
            self.tt("vector", mixo[:, 2, cs], ho[:], sz[:, cs], ALU.mult, [ho, sz], [(mixo, 2)])
            yield
            self.tr(pT[:, 0:128], kc[:, cs], c["ident"][:], [kc, c["ident"]], [pT])
            khat = self.scr("C_khat", [128, 128], BF16)
            for h in range(2):
                hs = slice(h * 64, (h + 1) * 64)
                self.ts("vector", khat[:, hs], pT[:, h * 64:(h + 1) * 64], wcol[:, h:h + 1], ALU.mult, [pT, wcol],
                        [khat])
            yield
            pC = self.gps()
            for h in range(2):
                hs = slice(h * 64, (h + 1) * 64)
                self.mm(pC[hs, 0:128], khat[:, hs], vaug[h][:], True, True, [khat, vaug[h]], [pC])
            for h in range(2):
                hs = slice(h * 64, (h + 1) * 64)
                self.stt(self.Cst[hs, :], self.Cst[hs, :], eg[hs, h:h + 1], pC[hs, 0:128], ALU.mult, ALU.add,
                         [self.Cst, eg, pC], [self.Cst])
            self.cp("vector", self.Cstb[:], self.Cst[:], [self.Cst], [self.Cstb])
            yield

    def branch_D(self, mc, fm, tm, mixo):
        self.begin("D")
        c = self.consts
        pc = c["pc"]
        pT = c["pT"]
        omm = c["omm"]
        if not hasattr(self, "m_gt"):
            self.m_gt = self.sb("m_gt", [128, 128], BF16)
            self.mN_gt = self.sb("mN_gt", [128, 128], BF16)
            self.memset("vector", self.m_gt[:], 1.0, [self.m_gt])
            self.asel(self.m_gt[:], self.m_gt[:], ALU.is_gt, [self.m_gt], [self.m_gt])
            self.memset("vector", self.mN_gt[:], 1.0, [self.mN_gt])
            self.asel(self.mN_gt[:], self.mN_gt[:], ALU.is_gt, [self.mN_gt], [self.mN_gt], cm=1, pat=[[-1, 128]])
        m_gt, mN_gt, m_ge = self.m_gt, self.mN_gt, c["m_ge"]

        def shift(g, idx, rows, name):
            x = fm[g]
            t = self.scr("D_sh_" + name, [128, 512])
            self.ts("vector", t[0:rows, :], x[0:rows, 1:513], omm[0:rows, idx:idx + 1], ALU.mult, [x, omm], [t])
            self.stt(t[0:rows, :], x[0:rows, 0:512], pc[0:rows, PC["mu_r"] + idx:PC["mu_r"] + idx + 1], t[0:rows, :],
                     ALU.mult, ALU.add, [x, pc, t], [t])
            return t
        rs = shift("Dr", 0, 128, "r")
        ks = shift("Dk", 1, 128, "k")
        vs = shift("Dv", 2, 128, "v")
        zs = shift("Dz", 3, 128, "z")
        ws = shift("Dw", 4, 16, "w")
        as_ = shift("Da", 5, 16, "a")
        self.act(ws[0:16, :], ws[0:16, :], AF.Tanh, [ws], [ws])
        pw = self.gps()
        self.mm(pw[:], c["w2s"][:], ws[0:16, :], True, True, [c["w2s"], ws], [pw])
        lw = ws
        self.act(lw[:], pw[:], AF.Sigmoid, [pw, pc], [lw], bias=pc[:, PC["w0"]:PC["w0"] + 1])
        self.ts("gpsimd", lw[:], lw[:], -0.6065306597126334, ALU.mult, [lw], [lw])
        pa = self.gps()
        self.mm(pa[:], c["a2s"][:], as_[0:16, :], True, True, [c["a2s"], as_], [pa])
        aa = as_
        self.act(aa[:], pa[:], AF.Sigmoid, [pa, pc], [aa], bias=pc[:, PC["a0"]:PC["a0"] + 1])
        sz = zs
        self.act(sz[:], zs[:], AF.Silu, [zs], [sz])
        yield
        vb = self.scr("D_vb", [128, 512], BF16)
        self.cp("gpsimd", vb[:], vs[:], [vs], [vb])
        kx = self.scr("D_kx", [128, 512])
        self.ts("vector", kx[:], ks[:], pc[:, PC["k_k"]:PC["k_k"] + 1], ALU.mult, [ks, pc], [kx])
        sqb = self.scr("D_sqb", [128, 512], BF16)
        self.tt("gpsimd", sqb[:], kx[:], kx[:], ALU.mult, [kx], [sqb])
        pq = self.gps()
        self.mm(pq[:], c["blk"][:], sqb[:], True, True, [c["blk"], sqb], [pq])
        rn = self.scr("D_rn", [128, 512])
        self.ts("vector", rn[:], pq[:], 1e-18, ALU.max, [pq], [rn])
        self.act(rn[:], rn[:], AF.Ln, [rn], [rn])
        self.act(rn[:], rn[:], AF.Exp, [rn], [rn], scale=-0.5)
        kk = kx
        self.tt("vector", kk[:], kx[:], rn[:], ALU.mult, [kx, rn], [kk])
        k2 = rn
        self.ts("vector", k2[:], aa[:], -1.0, ALU.add, [aa, pc], [k2], s2=pc[:, PC["k_a"]:PC["k_a"] + 1], op1=ALU.mult)
        self.stt(k2[:], k2[:], 1.0, ks[:], ALU.add, ALU.mult, [k2, ks], [k2])
        bv = ks
        self.tt("gpsimd", bv[:], kk[:], aa[:], ALU.mult, [kk, aa], [bv])
        yield
        rk = self.scr("D_rk", [128, 512], BF16)
        self.stt(rk[:], rs[:], pc[:, PC["r_k"]:PC["r_k"] + 1], k2[:], ALU.mult, ALU.mult, [rs, pc, k2], [rk])
        pb = self.gps()
        self.mm(pb[:], c["blk"][:], rk[:], True, True, [c["blk"], rk], [pb])
        bon = vs
        self.tt("vector", bon[:], pb[:], vs[:], ALU.mult, [pb, vs], [bon])
        cl = self.scr("D_cl", [128, 512])
        for ch in range(4):
            cs = slice(ch * 128, (ch + 1) * 128)
            self.op("vector", lambda e, cs=cs: e.tensor_tensor_scan(
                out=cl[:, cs], data0=c["onesf"][:, 0:128], data1=lw[:, cs], initial=0.0,
                op0=ALU.mult, op1=ALU.add), [c["onesf"], lw], [cl])
        Ecl = self.scr("D_Ecl", [128, 512])
        Encl = self.scr("D_Encl", [128, 512])
        self.act(Ecl[:], cl[:], AF.Exp, [cl], [Ecl])
        self.act(Encl[:], cl[:], AF.Exp, [cl], [Encl], scale=-1.0)
        Ecx = lw
        self.tt("gpsimd", lw[:], cl[:], lw[:], ALU.subtract, [cl, lw], [lw])
        self.act(Ecx[:], lw[:], AF.Exp, [lw], [Ecx])
        yield
        clT = self.scr("D_clT", [128, 4])
        self.cp("vector", clT[:], cl[:].rearrange("p (a t) -> p a t", a=4)[:, :, 127], [cl], [clT])
        Eh = cl
        for ch in range(4):
            cs = slice(ch * 128, (ch + 1) * 128)
            self.act(Eh[:, cs], cl[:, cs], AF.Exp, [cl, clT], [Eh], scale=-1.0, bias=clT[:, ch:ch + 1])
        KR = self.scr("D_KR", [128, 4, 256], BF16)
        kt = self.scr("D_kt", [128, 512], BF16)
        bt = self.scr("D_bt", [128, 512], BF16)
        khat = self.scr("D_khat", [128, 512], BF16)
        nbh = self.scr("D_nbh", [128, 512], BF16)
        self.tt("vector", KR[:, :, 0:128], kk[:].rearrange("p (a t) -> p a t", a=4),
                Ecx[:].rearrange("p (a t) -> p a t", a=4), ALU.mult, [kk, Ecx], [KR])
        self.tt("gpsimd", KR[:, :, 128:256], rs[:].rearrange("p (a t) -> p a t", a=4),
                Ecl[:].rearrange("p (a t) -> p a t", a=4), ALU.mult, [rs, Ecl], [KR])
        self.tt("vector", kt[:], k2[:], Encl[:], ALU.mult, [k2, Encl], [kt])
        self.tt("gpsimd", bt[:], bv[:], Encl[:], ALU.mult, [bv, Encl], [bt])
        self.tt("vector", khat[:], k2[:], Eh[:], ALU.mult, [k2, Eh], [khat])
        self.stt(nbh[:], bv[:], -1.0, Eh[:], ALU.mult, ALU.mult, [bv, Eh], [nbh])
        yraw = kx
        yield
        def chunk(ch, par):
            PFX = "p%d_" % par
            po = par * 128
            cs = slice(ch * 128, (ch + 1) * 128)
            cs = slice(ch * 128, (ch + 1) * 128)
            self.tr(pT[:, 0:128], KR[:, ch, 0:128], c["ident"][:], [KR, c["ident"]], [pT])
            self.tr(pT[:, 128:256], vb[:, cs], c["ident"][:], [vb, c["ident"]], [pT])
            self.tr(pT[:, 256:384], khat[:, cs], c["ident"][:], [khat, c["ident"]], [pT])
            self.tr(pT[:, 384:512], nbh[:, cs], c["ident"][:], [nbh, c["ident"]], [pT])
            TMs = self.scr(PFX + "D_TMs", [128, 4, 128], BF16)
            self.cp("scalar", TMs[:], pT[:, 0:512].rearrange("p (a t) -> p a t", a=4), [pT], [TMs])
            yield
            rpT = self.scr(PFX + "D_rpT", [128, 128], BF16)
            GT = self.scr(PFX + "D_GT", [128, 64], BF16)
            Hs = self.scr(PFX + "D_Hs", [128, 64])
            py = c["pNum"]
            pHS = c["pDen"]
            HS = [slice(0, 64), slice(64, 128)]
            LT, nQbT, LkT, QkT, LN, R, R2 = {}, {}, {}, {}, {}, {}, {}
            for h in range(2):
                hs = HS[h]
                p1 = self.gps()
                self.mm(p1[:, 0:256], bt[hs, cs], KR[hs, ch, :], True, True, [bt, KR], [p1])
                self.mm(p1[:, 256:512], kt[hs, cs], KR[hs, ch, :], True, True, [kt, KR], [p1])
                LT[h] = self.scr(PFX + "D_LT%d" % h, [128, 128], BF16)
                nQbT[h] = self.scr(PFX + "D_nQbT%d" % h, [128, 128], BF16)
                LkT[h] = self.scr(PFX + "D_LkT%d" % h, [128, 128], BF16)
                QkT[h] = self.scr(PFX + "D_QkT%d" % h, [128, 128], BF16)
                self.tt("vector", LT[h][:], p1[:, 0:128], m_gt[:], ALU.mult, [p1, m_gt], [LT[h]])
                self.tt("vector", LkT[h][:], p1[:, 256:384], m_gt[:], ALU.mult, [p1, m_gt], [LkT[h]])
                self.stt(nQbT[h][:], p1[:, 128:256], -1.0, m_ge[:], ALU.mult, ALU.mult, [p1, m_ge], [nQbT[h]])
                self.tt("vector", QkT[h][:], p1[:, 384:512], m_ge[:], ALU.mult, [p1, m_ge], [QkT[h]])
                yield
            for h in range(2):
                hs = HS[h]
                p2 = self.gps()
                self.mm(p2[:, 0:128], KR[hs, ch, 0:128], bt[hs, cs], True, True, [KR, bt], [p2])
                self.mm(p2[:, 128:192], LkT[h][:], TMs[:, 1, hs], True, True, [LkT[h], TMs], [p2])
                LN[h] = self.scr(PFX + "D_LN%d" % h, [128, 128], BF16)
                self.tt("vector", LN[h][:], p2[:, 0:128], mN_gt[:], ALU.mult, [p2, mN_gt], [LN[h]])
                R[h] = self.scr(PFX + "D_R%d" % h, [128, 128], BF16)
                self.cp("gpsimd", R[h][:, 0:64], TMs[:, 0, hs], [TMs], [R[h]])
                self.cp("scalar", R[h][:, 64:128], p2[:, 128:192], [p2], [R[h]])
                yield
            Rc, Rn, PTc, PNc = {}, {}, {}, {}
            for h in range(2):
                pr_ = self.gps()
                self.mm(pr_[:, 0:128], LT[h][:], R[h][:], True, True, [LT[h], R[h]], [pr_])
                R2[h] = self.scr(PFX + "D_Rb%d" % h, [128, 128], BF16)
                self.tt("vector", R2[h][:], R[h][:], pr_[:, 0:128], ALU.subtract, [R[h], pr_], [R2[h]])
                Rc[h], Rn[h] = R2[h], R[h]
                PTc[h], PNc[h] = LT[h], LN[h]
            yield
            for j in range(1, 7):
                PTn, PNn = {}, {}
                for h in range(2):
                    PTn[h] = self.scr(PFX + "D_PT%d_%d" % (h, j % 2), [128, 128], BF16)
                    PNn[h] = self.scr(PFX + "D_PN%d_%d" % (h, j % 2), [128, 128], BF16)
                    pp = self.gps()
                    self.mm(pp[:, 0:128], PNc[h][:], PTc[h][:], True, True, [PNc[h], PTc[h]], [pp])
                    if j < 6:
                        self.mm(pp[:, 128:256], PTc[h][:], PNc[h][:], True, True, [PNc[h], PTc[h]], [pp])
                        self.cp("scalar", PTn[h][:], pp[:, 0:128], [pp], [PTn[h]])
                        self.cp("vector", PNn[h][:], pp[:, 128:256], [pp], [PNn[h]])
                    else:
                        self.cp("scalar", PTn[h][:], pp[:, 0:128], [pp], [PTn[h]])
                yield
                for h in range(2):
                    pr_ = self.gps()
                    self.mm(pr_[:, 0:128], PTn[h][:], Rc[h][:], True, True, [PTn[h], Rc[h]], [pr_])
                    self.tt("vector", Rn[h][:], Rc[h][:], pr_[:, 0:128], ALU.add, [Rc[h], pr_], [Rn[h]])
                    Rc[h], Rn[h] = Rn[h], Rc[h]
                    PTc[h], PNc[h] = PTn[h], PNn[h]
                yield
            while self.d_done < ch:
                yield
            for h in range(2):
                hs = HS[h]
                Rch = Rc[h]
                pr_ = self.gps()
                self.mm(pr_[hs, 0:128], Rch[:, 0:64], nQbT[h][:], True, True, [Rch, nQbT[h]], [pr_])
                self.tt("vector", rpT[hs, :], pr_[hs, 0:128], KR[hs, ch, 128:256], ALU.add, [pr_, KR], [rpT])
                self.mm(py[hs, po:po + 128], TMs[:, 1, hs], QkT[h][:], True, False, [TMs, QkT[h]], [(py, h)])
                self.mm(py[hs, po:po + 128], Rch[:, 64:128], nQbT[h][:], False, False, [Rch, nQbT[h]], [(py, h)])
                self.mm(py[hs, po:po + 128], self.Sstb[hs, :], rpT[hs, :], False, True, [self.Sstb, rpT], [(py, h)])
                pg = self.gps()
                self.mm(pg[hs, 0:64], Rch[:, 0:64], TMs[:, 3, hs], True, True, [Rch, TMs], [pg])
                self.stt(GT[hs, :], c["identf"][hs, h * 64:(h + 1) * 64], Ecl[hs, ch * 128 + 127:ch * 128 + 128],
                         pg[hs, 0:64], ALU.mult, ALU.add, [c["identf"], Ecl, pg], [GT])
                self.mm(pHS[hs, po:po + 64], TMs[:, 2, hs], TMs[:, 1, hs], True, False, [TMs], [(pHS, h)])
                self.mm(pHS[hs, po:po + 64], TMs[:, 3, hs], Rch[:, 64:128], False, True, [TMs, Rch], [(pHS, h)])
                self.cp("scalar", Hs[hs, :], pHS[hs, po:po + 64], [(pHS, h)], [Hs])
                self.mm(pHS[hs, po + 64:po + 128], GT[hs, :], self.Sstb[hs, :], True, True, [GT, self.Sstb], [(pHS, h)])
                self.tt("vector", self.Sst[hs, :], pHS[hs, po + 64:po + 128], Hs[hs, :], ALU.add, [(pHS, h), Hs], [self.Sst])
                yield
            self.cp("vector", self.Sstb[:], self.Sst[:], [self.Sst], [self.Sstb])
            self.cp("scalar", yraw[:, cs], py[:, po:po + 128], [py], [yraw])
            self.d_done = ch + 1

        self.d_done = 0
        for pair in ((0, 1), (2, 3)):
            gs = [chunk(pair[0], 0), chunk(pair[1], 1)]
            while gs:
                for g in list(gs):
                    try:
                        next(g)
                    except StopIteration:
                        gs.remove(g)
                yield
        ysq = sqb
        self.tt("gpsimd", ysq[:], yraw[:], yraw[:], ALU.mult, [yraw], [ysq])
        pq2 = self.gps()
        self.mm(pq2[:], c["blk"][:], ysq[:], True, True, [c["blk"], ysq], [pq2])
        rs2 = rn
        self.act(rs2[:], pq2[:], AF.Ln, [pq2], [rs2], scale=1.0 / 64, bias=EPS)
        self.act(rs2[:], rs2[:], AF.Exp, [rs2], [rs2], scale=-0.5)
        self.stt(yraw[:], yraw[:], pc[:, PC["ln_g"]:PC["ln_g"] + 1], rs2[:], ALU.mult, ALU.mult, [yraw, pc, rs2],
                 [yraw])
        self.tt("vector", yraw[:], yraw[:], bon[:], ALU.add, [yraw, bon], [yraw])
        self.tt("gpsimd", mixo[:, 3, :], yraw[:], sz[:], ALU.mult, [yraw, sz], [(mixo, 3)])


def build_final(SEQ):
    Bd = Builder(SEQ, False)
    nc = Bd.nc
    x1 = Bd.dram("x1", [SEQ, 1024], F32, "ExternalInput")
    mp = Bd.dram("mprev", [1280, SEQ], BF16, "ExternalInput")
    wout_d = Bd.dram("wout", [1280, 1024], F32, "ExternalInput")
    out = Bd.dram("out", [SEQ, 1024], F32, "ExternalOutput")
    pP = [Bd.psum("pP0", [128, 512]), Bd.psum("pP1", [128, 512])]
    wo = Bd.sb("wo", [128, 10, 1024], BF16)
    wov = wout_d.rearrange("(c p) n -> p c n", p=128)
    for c in range(10):
        Bd.dma("gpsimd", wo[:, c, :], wov[:, c, :], w=[(wo, c)])
    xts = [Bd.sb("xt%d" % i, [128, 4, 1024]) for i in range(2)]
    mpvs = [Bd.sb("mpv%d" % i, [128, 10, 512], BF16) for i in range(2)]
    outs = []
    for mc in range(SEQ // 512):
        t0 = mc * 512
        xt = xts[mc % 2]
        mpv = mpvs[mc % 2]
        Bd.dma("sync", xt[:], x1[t0:t0 + 512, :].rearrange("(tt p) d -> p tt d", p=128), w=[xt])
        Bd.dma("sync", mpv[:], mp[:, t0:t0 + 512].rearrange("(c p) t -> p c t", p=128), w=[mpv])
        for tt in range(4):
            for hf in range(2):
                p = pP[hf]
                for c in range(10):
                    Bd.mm(p[:], mpv[:, c, tt * 128:(tt + 1) * 128], wo[:, c, hf * 512:(hf + 1) * 512],
                          c == 0, c == 9, [mpv, (wo, c)], [p])
                Bd.tt("vector", xt[:, tt, hf * 512:(hf + 1) * 512], xt[:, tt, hf * 512:(hf + 1) * 512],
                      p[:], ALU.add, [xt, p], [xt])
        o = Bd.dma("sync", out[t0:t0 + 512, :].rearrange("(tt p) d -> p tt d", p=128), xt[:], r=[xt], w=["out_d"])
        outs.append(o)
    Bd.S.emit(final_wait_ops=outs)
    return nc


_CACHE = {}


def _get_prog(key, fn):
    if key not in _CACHE:
        _CACHE[key] = fn()
    return _CACHE[key]


def kernel_unfused(**inputs):
    inp = {k: np.asarray(v) for k, v in inputs.items()}
    x = inp["x"]
    BATCH, SEQ, _ = x.shape
    n = 8
    cores = [(b, hh) for b in range(BATCH) for hh in range(2)]
    xcur = [np.ascontiguousarray(x[b]) for b in range(BATCH)]
    mprev = None
    for l in range(2):
        has_prev = l > 0
        nc = _get_prog(("layer", SEQ, has_prev), lambda: Builder(SEQ, has_prev).build())
        in_maps = []
        for (b, hh) in cores:
            d = host_layer_params(inp, l, hh)
            d["xin"] = xcur[b]
            d["mem"] = np.ascontiguousarray(inp["mem"][b])
            if has_prev:
                d["mprev"] = mprev[b]
                d["wout"] = np.ascontiguousarray(inp["w_out"][l - 1])
            in_maps.append(d)
        res = run_bass_kernel_spmd(nc, in_maps, core_ids=list(range(n)))
        new_m = []
        for b in range(BATCH):
            full = np.zeros((1280, SEQ), dtype=ml_dtypes.bfloat16)
            for hh in range(2):
                m = np.asarray(res.results[b * 2 + hh]["mixed"])
                for g in range(5):
                    full[g * 256 + hh * 128:g * 256 + hh * 128 + 128] = m[g * 128:(g + 1) * 128]
            new_m.append(full)
            if has_prev:
                xcur[b] = np.asarray(res.results[b * 2]["x1out"])
        mprev = new_m
    ncf = _get_prog(("final", SEQ), lambda: build_final(SEQ // 2))
    in_maps = []
    H = SEQ // 2
    for (b, hh) in cores:
        in_maps.append({"x1": np.ascontiguousarray(xcur[b][hh * H:(hh + 1) * H]),
                        "mprev": np.ascontiguousarray(mprev[b][:, hh * H:(hh + 1) * H]),
                        "wout": np.ascontiguousarray(inp["w_out"][1])})
    res = run_bass_kernel_spmd(ncf, in_maps, core_ids=list(range(n)))
    out = np.zeros((BATCH, SEQ, 1024), np.float32)
    for i, (b, hh) in enumerate(cores):
        out[b, hh * H:(hh + 1) * H] = np.asarray(res.results[i]["out"])
    return out


def kernel(**inputs):
    inp = {k: np.asarray(v) for k, v in inputs.items()}
    x = inp["x"]
    BATCH, SEQ, _ = x.shape
    nc = _get_prog(("fused", SEQ), lambda: Builder(SEQ, False).build_fused())
    per = {}
    for l in range(2):
        for hh in range(2):
            d = host_layer_params(inp, l, hh)
            for k, v in d.items():
                per["%s_%d%d" % (k, l, hh)] = v
    in_maps = []
    for b in range(BATCH):
        d = dict(per)
        d["xin"] = np.ascontiguousarray(x[b])
        d["mem"] = np.ascontiguousarray(inp["mem"][b])
        d["wout0"] = np.ascontiguousarray(inp["w_out"][0])
        d["wout1"] = np.ascontiguousarray(inp["w_out"][1])
        in_maps.append(d)
    res = run_bass_kernel_spmd(nc, in_maps, core_ids=list(range(BATCH)))
    out = np.stack([np.asarray(res.results[b]["out"]) for b in range(BATCH)], axis=0)
    return out.astype(np.float32)
```

```python
from contextlib import ExitStack
import numpy as np
import ml_dtypes
import concourse.bass as bass
import concourse.mybir as mybir
from concourse.bass_utils import run_bass_kernel_spmd

F32 = mybir.dt.float32
BF16 = mybir.dt.bfloat16
ALU = mybir.AluOpType
AF = mybir.ActivationFunctionType
AX = mybir.AxisListType

ENGINES = ("tensor", "vector", "scalar", "gpsimd", "sync")
SEM_CAP = 30000
EPS = 1e-6


class _Op:
    __slots__ = ("eng", "fn", "idx", "deps", "signal", "is_dma", "sem", "val", "pre_wait")

    def __init__(self, eng, fn, is_dma):
        self.eng = eng
        self.fn = fn
        self.is_dma = is_dma
        self.deps = []
        self.signal = False
        self.sem = None
        self.val = None
        self.pre_wait = None


class Tl:
    def __init__(self, name, t):
        self.name = name
        self.t = t

    def __getitem__(self, idx):
        return self.t[idx]


def _norm(rs):
    out = []
    for r in rs:
        if isinstance(r, tuple):
            a, k = r
        else:
            a, k = r, None
        if isinstance(a, Tl):
            a = a.name
        if a.startswith("scr"):
            k = None
        out.append((a, k))
    return out


class Sched:
    def __init__(self, nc, stack, n_dma_sems=16):
        self.nc = nc
        self.stack = stack
        self.ops = {e: [] for e in ENGINES}
        self.state = {}
        self.n_dma_sems = n_dma_sems

    def _entries(self, name, key):
        d = self.state.setdefault(name, {})
        if key is None:
            return list(d.values())
        res = []
        if key in d:
            res.append(d[key])
        if None in d:
            res.append(d[None])
        return res

    def add(self, eng, fn, reads=(), writes=(), dma=False):
        reads = _norm(reads)
        writes = _norm(writes)
        op = _Op(eng, fn, dma)
        deps = []
        for (name, key) in reads:
            for ent in self._entries(name, key):
                if ent[0] is not None:
                    deps.append(ent[0])
                if name[0] == "p" and name[1].isupper():
                    deps.extend(o_ for o_ in ent[1] if o_.eng != eng)
        for (name, key) in writes:
            for ent in self._entries(name, key):
                if ent[0] is not None:
                    deps.append(ent[0])
                deps.extend(ent[1])
        for (name, key) in reads:
            d = self.state.setdefault(name, {})
            if key is None:
                if not d:
                    d[None] = [None, []]
                for ent in d.values():
                    ent[1].append(op)
            else:
                if key not in d:
                    d[key] = [None, []]
                d[key][1].append(op)
        for (name, key) in writes:
            d = self.state.setdefault(name, {})
            if key is None:
                d.clear()
                d[None] = [op, []]
            else:
                d[key] = [op, []]
        op.idx = len(self.ops[eng])
        best = {}
        dl = []
        for dop in deps:
            if dop is op:
                continue
            if dop.is_dma:
                if dop not in dl:
                    dl.append(dop)
            else:
                if dop.eng == "tensor" and eng == "tensor" and not dma:
                    continue
                b = best.get(dop.eng)
                if b is None or dop.idx > b.idx:
                    best[dop.eng] = dop
        op.deps = dl + list(best.values())
        for dop in op.deps:
            dop.signal = True
        self.ops[eng].append(op)
        return op

    def emit(self, final_wait_ops=()):
        nc = self.nc
        for eng in ENGINES:
            cnt = 0
            sem = None
            for op in self.ops[eng]:
                if op.is_dma:
                    continue
                if op.signal:
                    if sem is None or cnt >= SEM_CAP:
                        sem = self.stack.enter_context(nc.semaphore(f"s_{eng}_{op.idx}"))
                        cnt = 0
                    cnt += 1
                    op.sem = sem
                    op.val = cnt
        for eng in ENGINES:
            qpool = []
            k = 0
            for op in self.ops[eng]:
                if not op.is_dma:
                    continue
                if len(qpool) < self.n_dma_sems:
                    s = self.stack.enter_context(nc.semaphore(f"d_{eng}_{len(qpool)}"))
                    qpool.append([s, 0])
                    ent = qpool[-1]
                else:
                    ent = qpool[k % self.n_dma_sems]
                    if ent[1] + 16 > SEM_CAP:
                        ent[0] = self.stack.enter_context(nc.semaphore(f"d_{eng}_x{k}"))
                        ent[1] = 0
                if ent[1] > 0:
                    op.pre_wait = (ent[0], ent[1])
                ent[1] += 16
                op.sem = ent[0]
                op.val = ent[1]
                k += 1
        sched = self

        def run(eng_name, e):
            seen = {}
            for op in sched.ops[eng_name]:
                waits = []
                if op.pre_wait is not None:
                    waits.append(op.pre_wait)
                for dop in op.deps:
                    waits.append((dop.sem, dop.val))
                for (s, v) in waits:
                    key = id(s)
                    if seen.get(key, 0) >= v:
                        continue
                    seen[key] = v
                    e.wait_ge(s, v)
                ins = op.fn(e)
                if op.is_dma:
                    ins.then_inc(op.sem, 16)
                elif op.signal:
                    ins.then_inc(op.sem, 1)
            if eng_name == "sync":
                for fop in final_wait_ops:
                    e.wait_ge(fop.sem, fop.val)

        with nc.Block() as block:
            @block.sync
            def _(e):
                run("sync", e)

            @block.tensor
            def _(e):
                run("tensor", e)

            @block.vector
            def _(e):
                run("vector", e)

            @block.scalar
            def _(e):
                run("scalar", e)

            @block.gpsimd
            def _(e):
                run("gpsimd", e)


D_MODEL = 1024
FM_GROUPS = ["Au", "Az", "Bz", "Cq", "Ck", "Co", "Cz", "Dr", "Dk", "Dv", "Dz", "Mz",
             "Bf", "Ci", "Cf0", "Cf1", "Dw", "Da"]
FM_W = {g: 128 for g in FM_GROUPS}
FM_W["Dw"] = 16
FM_W["Da"] = 16
FM_OFF = {}
_o = 0
for _g in FM_GROUPS:
    FM_OFF[_g] = _o
    _o += FM_W[_g]
NFM = _o
TM_GROUPS = ["Av", "Bq", "Bk", "Bv", "Cv", "Mq"]
NTM = 768
NW = NFM + NTM
PC = {n: i for i, n in enumerate([
    "cq0", "cq1", "cq2", "cq3", "ck0", "ck1", "ck2", "ck3", "cqb", "ckb",
    "mu_r", "mu_k", "mu_v", "mu_z", "mu_w", "mu_a",
    "k_k", "k_a", "a0", "w0", "r_k", "ln_g", "out_g",
    "fb_B", "ib_C", "fb_C0", "fb_C1"])}
NPC = len(PC)
PR_SGU_G, PR_BQG, PR_BKG, PR_MQG, PR_MKG, PR_SGUB = 0, 128, 256, 384, 512, 640
NPR = 768

A_OFF = 0
B_OFF = 768
C_OFF = 768 + 1028
D_OFF = C_OFF + 1288
M_OFF = D_OFF + 1056


def host_layer_params(inp, l, hh):
    f32 = np.float32
    w_in = inp["w_in"][l]
    hs = [2 * hh, 2 * hh + 1]

    def hcols(base):
        return np.concatenate([np.arange(base + h * 64, base + h * 64 + 64) for h in hs])

    cols = {}
    cols["Au"] = hcols(A_OFF)
    cols["Av"] = hcols(A_OFF + 256)
    cols["Az"] = hcols(A_OFF + 512)
    cols["Bq"] = hcols(B_OFF)
    cols["Bk"] = hcols(B_OFF + 256)
    cols["Bv"] = hcols(B_OFF + 512)
    bf = B_OFF + 768
    cols["Bf"] = np.concatenate([np.full(64, bf + hs[1]), np.full(64, bf + hs[0])])
    cols["Bz"] = hcols(B_OFF + 772)
    cols["Cq"] = hcols(C_OFF)
    cols["Ck"] = hcols(C_OFF + 256)
    cols["Cv"] = hcols(C_OFF + 512)
    ci = C_OFF + 768
    cols["Ci"] = np.concatenate([np.full(64, ci + hs[0]), np.full(64, ci + hs[1])])
    cols["Cf0"] = np.full(128, ci + 4 + hs[0])
    cols["Cf1"] = np.full(128, ci + 4 + hs[1])
    cols["Co"] = hcols(C_OFF + 776)
    cols["Cz"] = hcols(C_OFF + 1032)
    cols["Dr"] = hcols(D_OFF)
    cols["Dw"] = np.arange(D_OFF + 256, D_OFF + 272)
    cols["Dk"] = hcols(D_OFF + 272)
    cols["Dv"] = hcols(D_OFF + 528)
    cols["Da"] = np.arange(D_OFF + 784, D_OFF + 800)
    cols["Dz"] = hcols(D_OFF + 800)
    cols["Mq"] = hcols(M_OFF)
    cols["Mz"] = hcols(M_OFF + 256)
    allc = np.concatenate([cols[g] for g in FM_GROUPS] + [cols[g] for g in TM_GROUPS])
    wcat = np.ascontiguousarray(w_in[:, allc])

    hc = hcols(0)
    pc = np.zeros((128, NPC), f32)
    cw = inp["mlstm_conv_w"][l]
    cb = inp["mlstm_conv_b"][l]
    for j in range(4):
        pc[:, PC["cq%d" % j]] = cw[j, hc]
        pc[:, PC["ck%d" % j]] = cw[j, 256 + hc]
    pc[:, PC["cqb"]] = cb[hc]
    pc[:, PC["ckb"]] = cb[256 + hc]
    mu = inp["rwkv_mu"][l]
    pc[:, PC["mu_r"]] = mu[hc]
    pc[:16, PC["mu_w"]] = mu[256:272]
    pc[:, PC["mu_k"]] = mu[272 + hc]
    pc[:, PC["mu_v"]] = mu[528 + hc]
    pc[:16, PC["mu_a"]] = mu[784:800]
    pc[:, PC["mu_z"]] = mu[800 + hc]
    pc[:, PC["k_k"]] = inp["rwkv_k_k"][l][hc]
    pc[:, PC["k_a"]] = inp["rwkv_k_a"][l][hc]
    pc[:, PC["a0"]] = inp["rwkv_a0"][l][hc]
    pc[:, PC["w0"]] = inp["rwkv_w0"][l][hc]
    pc[:, PC["r_k"]] = inp["rwkv_r_k"][l].reshape(-1)[hc]
    pc[:, PC["ln_g"]] = inp["rwkv_ln_g"][l][hc]
    pc[:, PC["out_g"]] = inp["mlstm_out_g"][l][hc]
    fb = inp["fox_f_b"][l]
    pc[:, PC["fb_B"]] = np.concatenate([np.full(64, fb[hs[1]]), np.full(64, fb[hs[0]])])
    ib = inp["mlstm_i_b"][l]
    pc[:, PC["ib_C"]] = np.concatenate([np.full(64, ib[hs[0]]), np.full(64, ib[hs[1]])])
    fbc = inp["mlstm_f_b"][l]
    pc[:, PC["fb_C0"]] = fbc[hs[0]]
    pc[:, PC["fb_C1"]] = fbc[hs[1]]

    pr = np.zeros((128, NPR), f32)
    pr[:, PR_SGU_G:PR_SGU_G + 128] = inp["sgu_norm_g"][l][hc][None, :]
    pr[:, PR_BQG:PR_BQG + 128] = np.tile(inp["fox_q_g"][l], 2)[None, :]
    pr[:, PR_BKG:PR_BKG + 128] = np.tile(inp["fox_k_g"][l], 2)[None, :]
    pr[:, PR_MQG:PR_MQG + 128] = np.tile(inp["mem_q_g"][l], 2)[None, :]
    pr[:, PR_MKG:PR_MKG + 128] = np.tile(inp["mem_k_g"][l], 2)[None, :]
    sb_ = inp["sgu_b"][l]
    pr[:64, PR_SGUB:PR_SGUB + 128] = sb_[hs[0]][None, :]
    pr[64:, PR_SGUB:PR_SGUB + 128] = sb_[hs[1]][None, :]

    d = {
        "wcat": wcat,
        "pc": pc,
        "pr": pr,
        "ng": np.ascontiguousarray(inp["norm_g"][l].reshape(8, 128).T),
        "memg": np.ascontiguousarray(inp["mem_norm_g"][l].reshape(8, 128).T),
        "wkv": np.ascontiguousarray(np.concatenate(
            [inp["mem_w_kv"][l][:, hc], inp["mem_w_kv"][l][:, 256 + hc]], axis=1)),
        "w2": np.ascontiguousarray(inp["rwkv_w2"][l][:, hc]),
        "a2": np.ascontiguousarray(inp["rwkv_a2"][l][:, hc]),
        "sguw": np.ascontiguousarray(inp["sgu_w"][l][hs]),
    }
    return d


class Builder:
    def __init__(self, SEQ, has_prev, branches="ABCDM"):
        self.SEQ = SEQ
        self.has_prev = has_prev
        self.branches = branches
        self.nc = bass.Bass("TRN2", target_bir_lowering=False)
        self.st = ExitStack()
        self.S = Sched(self.nc, self.st)
        self.ps_rr = 0

    def dram(self, name, shape, dt, kind):
        return self.nc.dram_tensor(name, shape, dt, kind=kind).ap()

    def sb(self, name, shape, dt=F32):
        if not hasattr(self, "_tiles"):
            self._tiles = {}
        if name not in self._tiles:
            self._tiles[name] = Tl(name, self.st.enter_context(self.nc.sbuf_tensor(name, shape, dt)))
        return self._tiles[name]

    def psum(self, name, shape, dt=F32):
        if not hasattr(self, "_tiles"):
            self._tiles = {}
        if name not in self._tiles:
            self._tiles[name] = Tl(name, self.st.enter_context(self.nc.psum_tensor(name, shape, dt)))
        return self._tiles[name]

    def gps(self):
        p = self.gp[self.ps_rr % len(self.gp)]
        self.ps_rr += 1
        return p

    def op(self, eng, fn, r=(), w=()):
        return self.S.add(eng, fn, reads=r, writes=w)

    def dma(self, eng, out, in_, r=(), w=()):
        return self.S.add(eng, lambda e: e.dma_start(out=out, in_=in_), reads=r, writes=w, dma=True)

    def mm(self, out, lhsT, rhs, start, stop, r, w):
        return self.S.add("tensor", lambda e: e.matmul(out, lhsT=lhsT, rhs=rhs, start=start, stop=stop),
                          reads=r, writes=w)

    def tr(self, out, in_, ident, r, w):
        return self.S.add("tensor", lambda e: e.transpose(out, in_, ident), reads=r, writes=w)

    def act(self, out, in_, func, r, w, bias=None, scale=None, accum_out=None, eng="scalar"):
        kw = {}
        if bias is not None:
            kw["bias"] = bias
        if scale is not None:
            kw["scale"] = scale
        if accum_out is not None:
            kw["accum_out"] = accum_out
        return self.S.add("scalar", lambda e: e.activation(out=out, in_=in_, func=func, **kw), reads=r, writes=w)

    def tt(self, eng, out, in0, in1, op, r, w):
        return self.S.add(eng, lambda e: e.tensor_tensor(out=out, in0=in0, in1=in1, op=op), reads=r, writes=w)

    def ts(self, eng, out, in0, s1, op0, r, w, s2=None, op1=None):
        if op1 is None:
            return self.S.add(eng, lambda e: e.tensor_scalar(out=out, in0=in0, scalar1=s1, scalar2=None, op0=op0),
                              reads=r, writes=w)
        return self.S.add(eng, lambda e: e.tensor_scalar(out=out, in0=in0, scalar1=s1, scalar2=s2, op0=op0, op1=op1),
                          reads=r, writes=w)

    def stt(self, out, in0, scalar, in1, op0, op1, r, w):
        return self.S.add("vector", lambda e: e.scalar_tensor_tensor(out=out, in0=in0, scalar=scalar, in1=in1,
                                                                      op0=op0, op1=op1), reads=r, writes=w)

    def cp(self, eng, out, in_, r, w):
        if eng == "scalar":
            return self.S.add("scalar", lambda e: e.copy(out=out, in_=in_), reads=r, writes=w)
        return self.S.add(eng, lambda e: e.tensor_copy(out, in_), reads=r, writes=w)

    def recip(self, out, in_, r, w):
        return self.S.add("vector", lambda e: e.reciprocal(out, in_), reads=r, writes=w)

    def memset(self, eng, ap, val, w):
        return self.S.add(eng, lambda e: e.memset(ap, val), writes=w)

    def asel(self, out, in_, cmp, w, r=(), fill=0.0, base=0, cm=-1, pat=None):
        pat = pat or [[1, 128]]
        return self.S.add("gpsimd", lambda e: e.affine_select(out=out, in_=in_, pattern=pat, compare_op=cmp,
                                                              fill=fill, base=base, channel_multiplier=cm),
                          reads=r, writes=w)

    def build(self):
        cfg = dict(tag="", xmode="outproj" if self.has_prev else "ext")
        self.last_out = []
        self.run_pass(cfg)
        self.S.emit(final_wait_ops=self.last_out)
        return self.nc

    def build_fused(self):
        SEQ = self.SEQ
        self.last_out = []
        self.x_ext = self.dram("xin", [SEQ, 1024], F32, "ExternalInput")
        self.mem_ext = self.dram("mem", [256, 1024], F32, "ExternalInput")
        self.mixs = [self.dram("mixs%d" % l, [1280, SEQ], BF16, "Internal") for l in range(2)]
        self.x1s = self.dram("x1s", [SEQ, 1024], F32, "Internal")
        self.wouts = [self.dram("wout%d" % l, [1280, 1024], F32, "ExternalInput") for l in range(2)]
        for l in range(2):
            if l == 1:
                self.begin("oproj")
                self.outproj_pass("ext", 0, self.x1s, "x1s", False)
            for hh in range(2):
                xmode = "ext" if l == 0 else "x1"
                self.run_pass(dict(tag="_%d%d" % (l, hh), xmode=xmode, fused=True, l=l, hh=hh))
        out_d = self.dram("out", [SEQ, 1024], F32, "ExternalOutput")
        self.begin("oproj")
        self.outproj_pass("x1", 1, out_d, "out_d", True)
        self.S.emit(final_wait_ops=self.last_out)
        return self.nc

    def outproj_pass(self, src_kind, l, dst, dst_name, final):
        SEQ = self.SEQ
        pP = [self.psum("pP0", [128, 512]), self.psum("pP1", [128, 512])]
        wb = self.sb("wb", [128, 8, NW], BF16)
        wo = Tl("wb", wb.t[:, :, :].rearrange("p a b -> p (a b)")[:, 0:10240].rearrange("p (c n) -> p c n", c=10))
        wov = self.wouts[l].rearrange("(c p) n -> p c n", p=128)
        for c in range(10):
            self.dma("gpsimd", wo[:, c, :], wov[:, c, :], w=[wo])
        xts = [self.sb("xt%d" % i, [128, 1024]) for i in range(2)]
        tm_ = self.sb("tm", [128, 4, 768], BF16)
        flat = tm_.t[:, :, :].rearrange("p a b -> p (a b)")
        mpvs = [Tl("tm", flat[:, 0:1280].rearrange("p (c t) -> p c t", c=10)),
                Tl("tm", flat[:, 1280:2560].rearrange("p (c t) -> p c t", c=10))]
        for ti in range(SEQ // 128):
            xt = xts[ti % 2]
            mpv = mpvs[ti % 2]
            r0 = ti * 128
            if src_kind == "ext":
                self.dma("sync", xt[:], self.x_ext[r0:r0 + 128, :], w=[xt])
            else:
                self.dma("sync", xt[:], self.x1s[r0:r0 + 128, :], r=[("x1s", ti)], w=[xt])
            self.dma("sync", mpv[:], self.mixs[l][:, r0:r0 + 128].rearrange("(c p) t -> p c t", p=128),
                     r=[("mixs%d" % l, (0, ti // 4)), ("mixs%d" % l, (1, ti // 4))], w=[mpv])
            for hf in range(2):
                p = pP[hf]
                for c in range(10):
                    self.mm(p[:], mpv[:, c, :], wo[:, c, hf * 512:(hf + 1) * 512], c == 0, c == 9,
                            [mpv, wo], [p])
                eng = "vector" if hf == 0 else "gpsimd"
                if eng == "gpsimd":
                    tmpo = self.scr("op_tmp", [128, 512])
                    self.cp("scalar", tmpo[:], p[:], [p], [tmpo])
                    self.tt("gpsimd", xt[:, hf * 512:(hf + 1) * 512], xt[:, hf * 512:(hf + 1) * 512], tmpo[:],
                            ALU.add, [xt, tmpo], [xt])
                else:
                    self.tt("vector", xt[:, hf * 512:(hf + 1) * 512], xt[:, hf * 512:(hf + 1) * 512], p[:],
                            ALU.add, [xt, p], [xt])
            o = self.dma("sync", dst[r0:r0 + 128, :], xt[:], r=[xt], w=[(dst_name, ti)])
            if final:
                self.last_out.append(o)

    def run_pass(self, cfg):
        nc = self.nc
        SEQ = self.SEQ
        NMC = SEQ // 512
        NCH = SEQ // 128
        tag = cfg["tag"]
        xmode = cfg["xmode"]
        fused = cfg.get("fused", False)
        has_prev = xmode == "outproj"
        wcat = self.dram("wcat" + tag, [1024, NW], F32, "ExternalInput")
        pc_d = self.dram("pc" + tag, [128, NPC], F32, "ExternalInput")
        pr_d = self.dram("pr" + tag, [128, NPR], F32, "ExternalInput")
        ng_d = self.dram("ng" + tag, [128, 8], F32, "ExternalInput")
        memg_d = self.dram("memg" + tag, [128, 8], F32, "ExternalInput")
        wkv_d = self.dram("wkv" + tag, [1024, 256], F32, "ExternalInput")
        w2_d = self.dram("w2" + tag, [16, 128], F32, "ExternalInput")
        a2_d = self.dram("a2" + tag, [16, 128], F32, "ExternalInput")
        sguw_d = self.dram("sguw" + tag, [2, 128, 128], F32, "ExternalInput")
        if fused:
            l, hh = cfg["l"], cfg["hh"]
            xin = self.x_ext
            mem_d = self.mem_ext
            mixed_d = self.mixs[l].rearrange("(g two p) t -> two p g t", two=2, p=128)[hh]
            mix_name = "mixs%d" % l
            mix_key = lambda mc: (hh, mc)
            if has_prev:
                mprev_d = self.mixs[0]
                wout_d = self.wouts[0]
                x1_d = self.x1s
        else:
            xin = self.dram("xin", [SEQ, 1024], F32, "ExternalInput")
            mem_d = self.dram("mem", [256, 1024], F32, "ExternalInput")
            mixed_d = self.dram("mixed", [640, SEQ], BF16, "ExternalOutput").rearrange("(g p) t -> p g t", p=128)
            mix_name = "mixed_d"
            mix_key = lambda mc: mc
            if has_prev:
                mprev_d = self.dram("mprev", [1280, SEQ], BF16, "ExternalInput")
                wout_d = self.dram("wout", [1280, 1024], F32, "ExternalInput")
                x1_d = self.dram("x1out", [SEQ, 1024], F32, "ExternalOutput")

        pT = self.psum("pT", [128, 1024], BF16)
        pP = [self.psum("pP0", [128, 512]), self.psum("pP1", [128, 512])]
        self.gp = [self.psum("pG%d" % i, [128, 512]) for i in range(3)]
        self.pacc = pP
        pNum = self.psum("pNum", [128, 512])
        pDen = self.psum("pDen", [128, 512])

        identf = self.sb("identf", [128, 128])
        ident = self.sb("ident", [128, 128], BF16)
        ones_bf = self.sb("ones_bf", [128, 128], BF16)
        onesf = self.sb("onesf", [128, 128])
        c64f = self.sb("c64f", [128, 128])
        c64b = self.sb("c64b", [128, 128], BF16)
        blk = self.sb("blk", [128, 128], BF16)
        m_ge = self.sb("m_ge", [128, 128], BF16)
        self.memset("vector", identf[:], 1.0, [identf])
        self.asel(identf[:], identf[:], ALU.is_equal, [identf], [identf])
        self.cp("vector", ident[:], identf[:], [identf], [ident])
        self.memset("vector", ones_bf[:], 1.0, [ones_bf])
        self.memset("vector", onesf[:], 1.0, [onesf])
        self.memset("vector", c64f[:], 1.0 / 64, [c64f])
        self.memset("vector", c64b[:], 1.0 / 64, [c64b])
        self.memset("vector", blk[:], 0.0, [blk])
        self.memset("vector", blk[0:64, 0:64], 1.0, [blk])
        self.memset("vector", blk[64:128, 64:128], 1.0, [blk])
        self.memset("vector", m_ge[:], 1.0, [m_ge])
        self.asel(m_ge[:], m_ge[:], ALU.is_ge, [m_ge], [m_ge])

        pc = self.sb("pcs", [128, NPC])
        pr = self.sb("prs", [128, NPR])
        ng = self.sb("ngs", [128, 8])
        memg = self.sb("memgs", [128, 8])
        self.dma("sync", pc[:], pc_d, w=[pc])
        self.dma("sync", pr[:], pr_d, w=[pr])
        self.dma("sync", ng[:], ng_d, w=[ng])
        self.dma("sync", memg[:], memg_d, w=[memg])
        omm = self.sb("omm", [128, 6])
        self.ts("vector", omm[:], pc[:, PC["mu_r"]:PC["mu_r"] + 6], -1.0, ALU.mult, [pc], [omm], s2=1.0, op1=ALU.add)
        nfbB = self.sb("nfbB", [128, 1])
        self.ts("vector", nfbB[:], pc[:, PC["fb_B"]:PC["fb_B"] + 1], -1.0, ALU.mult, [pc], [nfbB])
        nfbC = self.sb("nfbC", [128, 2])
        self.ts("vector", nfbC[:], pc[:, PC["fb_C0"]:PC["fb_C0"] + 2], -1.0, ALU.mult, [pc], [nfbC])
        gq8 = self.sb("gq8", [128, 128])
        self.ts("vector", gq8[:], pr[:, PR_BQG:PR_BQG + 128], 0.125, ALU.mult, [pr], [gq8])
        gmq8 = self.sb("gmq8", [128, 128])
        self.ts("vector", gmq8[:], pr[:, PR_MQG:PR_MQG + 128], 0.125, ALU.mult, [pr], [gmq8])

        wb = self.sb("wb", [128, 8, NW], BF16)
        wv = wcat.rearrange("(c p) n -> p c n", p=128)
        for c in range(8):
            self.dma("gpsimd", wb[:, c, :], wv[:, c, :], w=[(wb, c)])
        for c in range(8):
            eng = "vector" if c % 2 == 0 else "gpsimd"
            self.ts(eng, wb[:, c, :], wb[:, c, :], ng[:, c:c + 1], ALU.mult, [(wb, c), ng], [(wb, c)])
        if has_prev:
            wo = self.sb("wo", [128, 10, 1024], BF16)
            wov = wout_d.rearrange("(c p) n -> p c n", p=128)
            for c in range(10):
                self.dma("gpsimd", wo[:, c, :], wov[:, c, :], w=[(wo, c)])
        w2s = self.sb("w2s", [16, 128])
        a2s = self.sb("a2s", [16, 128])
        self.dma("sync", w2s[:], w2_d, w=[w2s])
        self.dma("sync", a2s[:], a2_d, w=[a2s])

        wsT = self.sb("wsT", [128, 2, 128], BF16)
        self.begin("setupA")
        if "A" in self.branches:
            sgw = self.scr("sgw", [128, 2, 128])
            self.dma("sync", sgw[:], sguw_d.rearrange("h t s -> t h s"), w=[sgw])
            sgwT = self.scr("sgwT", [128, 2, 128])
            for h in range(2):
                p = self.gps()
                self.tr(p[:, 0:128], sgw[:, h, :], identf[:], [sgw, identf], [p])
                self.cp("vector", sgwT[:, h, :], p[:, 0:128], [p], [sgwT])
                self.asel(sgwT[:, h, :], sgwT[:, h, :], ALU.is_ge, [sgwT], [sgwT])
            self.cp("vector", wsT[:], sgwT[:], [sgwT], [wsT])

        kTm = self.sb("kTm", [128, 256], BF16)
        vm = self.sb("vm", [128, 2, 128], BF16)

        self.alloc_state(NCH)

        xts = [self.sb("xt0", [128, 1024]), self.sb("xt1", [128, 1024])]
        hb = self.sb("hb", [128, 1024], BF16)
        hT = self.sb("hT", [128, 8, 512], BF16)
        ss = self.sb("ss", [128, 2])
        GATED = {"Az": AF.Silu, "Bz": AF.Silu, "Cz": AF.Silu, "Mz": AF.Silu, "Co": AF.Sigmoid}
        fm = {}
        for g in FM_GROUPS:
            if g in ("Cq", "Ck"):
                fm[g] = self.sb("fm_" + g, [128, 4 + 512], BF16)
            elif g in ("Dr", "Dk", "Dv", "Dz"):
                fm[g] = self.sb("fm_" + g, [128, 2 + 512], BF16)
            elif g in ("Dw", "Da"):
                fm[g] = self.sb("fm_" + g, [16, 1 + 512])
            elif g in GATED:
                fm[g] = self.sb("fm_" + g, [128, 512], BF16)
            else:
                fm[g] = self.sb("fm_" + g, [128, 512])
        tm = self.sb("tm", [128, 4, 768], BF16)
        mixo = self.sb("mixo", [128, 5, 512], BF16)
        if "M" in self.branches:
            self.setup_mem(mem_d, wkv_d, memg, pr, ident, identf, pT, kTm, vm, xts, hT, tm, hb)
        mpvs = [Tl("tm", tm.t[:, :, :].rearrange("p a b -> p (a b)")[:, 0:1280].rearrange("p (c t) -> p c t", c=10))] * 2
        for g in ("Cq", "Ck"):
            self.memset("vector", fm[g][:, 0:3], 0.0, [(fm[g], "hist")])
        for g in ("Dr", "Dk", "Dv", "Dz"):
            self.memset("vector", fm[g][:, 0:1], 0.0, [(fm[g], "hist")])
        for g in ("Dw", "Da"):
            self.memset("vector", fm[g][:, 0:1], 0.0, [(fm[g], "hist")])

        self.consts = dict(ident=ident, identf=identf, ones_bf=ones_bf, onesf=onesf, c64f=c64f, c64b=c64b,
                           blk=blk, m_ge=m_ge, pc=pc, pr=pr, omm=omm, nfbB=nfbB, nfbC=nfbC,
                           gq8=gq8, gmq8=gmq8, wsT=wsT, kTm=kTm, vm=vm, w2s=w2s, a2s=a2s, pT=pT,
                           pNum=pNum, pDen=pDen)
        last_out = self.last_out
        xi = 0
        xstate = {"xi": 0}

        def xprep(mc):
            t0 = mc * 512
            for tt in range(4):
                xi = xstate["xi"]
                xt = xts[xi % 2]
                r0 = t0 + tt * 128
                ti = mc * 4 + tt
                if xmode == "x1":
                    self.dma("sync", xt[:], self.x1s[r0:r0 + 128, :], r=[("x1s", ti)], w=[xt])
                else:
                    self.dma("sync", xt[:], xin[r0:r0 + 128, :], w=[xt])
                if has_prev:
                    mpv = mpvs[xi % 2]
                    rr = [("mixs0", (0, mc)), ("mixs0", (1, mc))] if fused else []
                    self.dma("sync", mpv[:], mprev_d[:, r0:r0 + 128].rearrange("(c p) t -> p c t", p=128),
                             r=rr, w=[mpv])
                    for hf in range(2):
                        p = pP[hf]
                        for c in range(10):
                            self.mm(p[:], mpv[:, c, :], wo[:, c, hf * 512:(hf + 1) * 512],
                                    c == 0, c == 9, [mpv, (wo, c)], [p])
                        self.tt("vector", xt[:, hf * 512:(hf + 1) * 512], xt[:, hf * 512:(hf + 1) * 512],
                                p[:], ALU.add, [xt, p], [xt])
                    o = self.dma("sync", x1_d[r0:r0 + 128, :], xt[:], r=[xt], w=[("x1s", ti)])
                    if not fused:
                        last_out.append(o)
                xstate["xi"] = xi + 1
                self.act(hb[:], xt[:], AF.Square, [xt], [hb, ss], accum_out=ss[:, 0:1])
                self.act(ss[:, 1:2], ss[:, 0:1], AF.Ln, [ss], [ss], scale=1.0 / 1024, bias=EPS)
                self.act(ss[:, 1:2], ss[:, 1:2], AF.Exp, [ss], [ss], scale=-0.5)
                self.ts("vector", hb[:], xt[:], ss[:, 1:2], ALU.mult, [xt, ss], [hb])
                yield
                for c in range(8):
                    self.tr(pT[:, c * 128:(c + 1) * 128], hb[:, c * 128:(c + 1) * 128], ident[:],
                            [hb, ident], [pT])
                eng = "vector" if tt % 2 == 0 else "scalar"
                self.cp(eng, hT[:, :, tt * 128:(tt + 1) * 128],
                        pT[:, :].rearrange("p (c t) -> p c t", c=8), [pT], [hT])
                yield

        for _ in xprep(0):
            pass
        for mc in range(NMC):
            t0 = mc * 512
            for gi, g in enumerate(FM_GROUPS):
                p = pP[gi % 2]
                wdt = FM_W[g]
                for c in range(8):
                    self.mm(p[0:wdt, :], wb[:, c, FM_OFF[g]:FM_OFF[g] + wdt], hT[:, c, :], c == 0, c == 7,
                            [(wb, c), hT], [p])
                hist = {"Cq": 3, "Ck": 3, "Dr": 1, "Dk": 1, "Dv": 1, "Dz": 1, "Dw": 1, "Da": 1}.get(g, 0)
                dst = fm[g]
                if g in GATED:
                    self.act(dst[:, :], p[:, :], GATED[g], [p], [(dst, "cur")])
                    continue
                if hist and mc > 0:
                    self.cp("vector", dst[0:wdt, 0:hist], dst[0:wdt, 512:512 + hist], [(dst, "cur")], [(dst, "hist")])
                eng = "scalar" if gi % 2 == 0 else "vector"
                self.cp(eng, dst[0:wdt, hist:hist + 512], p[0:wdt, :], [p, (dst, "hist")], [(dst, "cur")])
            for tt in range(4):
                for hf in range(2):
                    p = pP[hf]
                    for c in range(8):
                        self.mm(p[:, 0:384], hT[:, c, tt * 128:(tt + 1) * 128],
                                wb[:, c, NFM + hf * 384:NFM + (hf + 1) * 384], c == 0, c == 7, [hT, (wb, c)], [p])
                    eng = "scalar" if hf == 0 else "vector"
                    self.cp(eng, tm[:, tt, hf * 384:(hf + 1) * 384], p[:, 0:384], [p], [(tm, tt)])
            self.zero_mix = []
            if "A" in self.branches:
                self.branch_A(mc, fm, tm, mixo)
            else:
                self.memset("gpsimd", mixo[:, 0, :], 0.0, [(mixo, 0)])
            if "B" in self.branches:
                self.branch_B(mc, fm, tm, mixo)
            else:
                self.memset("gpsimd", mixo[:, 1, :], 0.0, [(mixo, 1)])
            gens = []
            if mc + 1 < NMC:
                gens.append(("X", xprep(mc + 1)))
            if "M" in self.branches:
                gens.append(("M", self.branch_M(mc, fm, tm, mixo)))
            else:
                self.memset("gpsimd", mixo[:, 4, :], 0.0, [(mixo, 4)])
            if "D" in self.branches:
                gens.append(("D", self.branch_D(mc, fm, tm, mixo)))
            else:
                self.memset("gpsimd", mixo[:, 3, :], 0.0, [(mixo, 3)])
            if "C" in self.branches:
                gens.append(("C", self.branch_C(mc, fm, tm, mixo)))
            else:
                self.memset("gpsimd", mixo[:, 2, :], 0.0, [(mixo, 2)])
            while gens:
                for item in list(gens):
                    for rep in range(3 if item[0] == "D" else 1):
                        self._br = item[0]
                        try:
                            next(item[1])
                        except StopIteration:
                            gens.remove(item)
                            break
            o = self.dma("sync", mixed_d[:, :, t0:t0 + 512], mixo[:], r=[mixo], w=[(mix_name, mix_key(mc))])
            if not fused:
                last_out.append(o)

    def head_rms_tm(self, src, dst, gain, tag, nt=4):
        sq = self.scr("hr_sq", [128, 4, 128])
        ssq = self.scr("hr_ssq", [128, 8])
        src_ap, src_r = src
        dst_ap, dst_w = dst
        g_ap, g_r = gain
        self.tt("gpsimd", sq[:, 0:nt, :], src_ap, src_ap, ALU.mult, src_r, [sq])
        ssq3 = ssq[:, 0:2 * nt].rearrange("p (t h) -> p t h", h=2)
        self.op("vector", lambda e: e.tensor_reduce(out=ssq3,
                                                     in_=sq[:, 0:nt, :].rearrange("p t (h j) -> p t h j", h=2),
                                                     axis=AX.X, op=ALU.add), [sq], [ssq])
        self.act(ssq[:, 0:2 * nt], ssq[:, 0:2 * nt], AF.Ln, [ssq], [ssq], scale=1.0 / 64, bias=EPS)
        self.act(ssq[:, 0:2 * nt], ssq[:, 0:2 * nt], AF.Exp, [ssq], [ssq], scale=-0.5)
        self.tt("vector", sq[:, 0:nt, :].rearrange("p t (h j) -> p t h j", h=2),
                src_ap.rearrange("p t (h j) -> p t h j", h=2),
                ssq3[:, :, :, None].broadcast_to([128, nt, 2, 64]), ALU.mult, src_r + [ssq], [sq])
        self.tt("vector", dst_ap, sq[:, 0:nt, :], g_ap[:, None, :].broadcast_to([128, nt, 128]), ALU.mult,
                [sq] + g_r, dst_w)

    def begin(self, br):
        self._br = br
        if not hasattr(self, "_brcount"):
            self._brcount = {}
        self._brcount.setdefault(br, {})

    def scr(self, name, shape, dt=F32):
        if not hasattr(self, "_scrmap"):
            self._scrmap = {}
            self._pools = {}
        br = getattr(self, "_br", "x")
        key = (br, name)
        if key in self._scrmap:
            return self._scrmap[key]
        esz = 4 if dt == F32 else 2
        n = 1
        for d_ in shape[1:]:
            n *= d_
        nbytes = n * esz
        cls = 256
        while cls < nbytes:
            cls *= 2
        cnt = self._brcount.setdefault(br, {})
        k = cnt.get(cls, 0)
        cnt[cls] = k + 1
        fam = br if br in ("C", "M") else ""
        pool = self._pools.setdefault((fam, cls), [])
        if k >= len(pool):
            pname = "scr%s%d_%d" % (fam, cls, k)
            pool.append((pname, self.st.enter_context(self.nc.sbuf_tensor(pname, [128, cls // 4], F32))))
        pname, raw = pool[k]
        h = raw if dt == F32 else raw.bitcast(dt)
        ap = h[0:shape[0], 0:n]
        if len(shape) == 3:
            ap = ap.rearrange("p (a b) -> p a b", a=shape[1])
        elif len(shape) == 4:
            ap = ap.rearrange("p (a b c) -> p a b c", a=shape[1], b=shape[2])
        t = Tl(pname, ap)
        self._scrmap[key] = t
        return t

    def gelu(self, dst_ap, dst_w, src_ap, src_r, shape, tag):
        t1 = self.scr("gl_t1" + tag, shape)
        t2 = self.scr("gl_t2" + tag, shape)
        self.tt("gpsimd", t1[:], src_ap, src_ap, ALU.mult, src_r, [t1])
        self.ts("vector", t1[:], t1[:], 0.044715, ALU.mult, [t1], [t1], s2=1.0, op1=ALU.add)
        self.tt("gpsimd", t2[:], t1[:], src_ap, ALU.mult, [t1] + src_r, [t2])
        self.act(t2[:], t2[:], AF.Sigmoid, [t2], [t2], scale=1.5957691216)
        self.tt("vector", dst_ap, t2[:], src_ap, ALU.mult, [t2] + src_r, dst_w)

    def setup_mem(self, mem_d, wkv_d, memg, pr, ident, identf, pT, kTm, vm, xts, hT, tm, hb):
        self.begin("setup")
        for c in range(2):
            self.dma("sync", xts[c][:], mem_d[c * 128:(c + 1) * 128, :], w=[xts[c]])
        wk = Tl("hT", hT[:, :, 0:256])
        mhT = Tl("hT", hT[:, :, 256:512])
        mh = Tl("tm", tm.t[:, :, :].rearrange("p a b -> p (a b)")[:, 0:2048].rearrange("p (c d) -> p c d", c=2))
        wkv_v = wkv_d.rearrange("(c p) n -> p c n", p=128)
        for c in range(8):
            self.dma("gpsimd", wk[:, c, :], wkv_v[:, c, :], w=[wk])
        for c in range(8):
            self.ts("vector", wk[:, c, :], wk[:, c, :], memg[:, c:c + 1], ALU.mult, [wk, memg], [wk])
        mss = self.scr("m_ss", [128, 2])
        for c in range(2):
            self.act(hb[:], xts[c][:], AF.Square, [xts[c]], [hb, mss], accum_out=mss[:, c:c + 1])
        self.act(mss[:], mss[:], AF.Ln, [mss], [mss], scale=1.0 / 1024, bias=EPS)
        self.act(mss[:], mss[:], AF.Exp, [mss], [mss], scale=-0.5)
        for c in range(2):
            self.ts("vector", mh[:, c, :], xts[c][:], mss[:, c:c + 1], ALU.mult, [xts[c], mss], [mh])
        for c2 in range(2):
            for c in range(8):
                self.tr(pT[:, c * 128:(c + 1) * 128], mh[:, c2, c * 128:(c + 1) * 128], ident[:], [mh, ident], [pT])
            self.cp("vector", mhT[:, :, c2 * 128:(c2 + 1) * 128], pT[:, :].rearrange("p (c t) -> p c t", c=8),
                    [pT], [mhT])
        kvt = self.scr("m_kvt", [128, 2, 256])
        for c2 in range(2):
            p = self.gps()
            for c in range(8):
                self.mm(p[:, 0:256], mhT[:, c, c2 * 128:(c2 + 1) * 128], wk[:, c, :], c == 0, c == 7, [mhT, wk], [p])
            self.cp("vector", kvt[:, c2, :], p[:, 0:256], [p], [kvt])
        self.cp("vector", vm[:], kvt[:, :, 128:256], [kvt], [vm])
        kn = self.scr("m_kn", [128, 2, 128], BF16)
        self.head_rms_tm((kvt[:, :, 0:128], [kvt]), (kn[:], [kn]), (pr[:, PR_MKG:PR_MKG + 128], [pr]), "mk", nt=2)
        for c2 in range(2):
            self.tr(pT[:, c2 * 128:(c2 + 1) * 128], kn[:, c2, :], ident[:], [kn, ident], [pT])
        self.cp("vector", kTm[:], pT[:, 0:256], [pT], [kTm])

    def alloc_state(self, NCH):
        SEQ = self.SEQ
        if "B" in self.branches:
            self.KT = self.sb("KT", [128, SEQ], BF16)
            self.VB = self.sb("VB", [128, NCH, 128], BF16)
            self.Fcol = self.sb("Fcol", [128, 2, NCH])
            self.Fprev = self.sb("Fprev", [128, 1])
            self.memset("vector", self.Fprev[:], 0.0, [self.Fprev])
            self.c64h = [self.sb("c64h%d" % h, [128, 128], BF16) for h in range(2)]
            self.hm = self.sb("hm", [128, 2])
            self.memset("vector", self.hm[:], 0.0, [self.hm])
            for h in range(2):
                hs = slice(h * 64, (h + 1) * 64)
                fs = slice(64, 128) if h == 0 else slice(0, 64)
                self.memset("vector", self.c64h[h][:], 0.0, [self.c64h[h]])
                self.memset("vector", self.c64h[h][fs, :], 1.0 / 64, [self.c64h[h]])
                self.memset("vector", self.hm[hs, h:h + 1], 1.0, [self.hm])
        if "C" in self.branches:
            self.Cst = self.sb("Cst", [128, 128])
            self.Cstb = self.sb("Cstb", [128, 128], BF16)
            self.memset("vector", self.Cst[:], 0.0, [self.Cst])
            self.memset("vector", self.Cstb[:], 0.0, [self.Cstb])
        if "D" in self.branches:
            self.Sst = self.sb("Sst", [128, 64])
            self.Sstb = self.sb("Sstb", [128, 64], BF16)
            self.memset("vector", self.Sst[:], 0.0, [self.Sst])
            self.memset("vector", self.Sstb[:], 0.0, [self.Sstb])

    def branch_A(self, mc, fm, tm, mixo):
        self.begin("A")
        c = self.consts
        pr = c["pr"]
        gu = self.scr("A_gu", [128, 512])
        self.gelu(gu[:], [gu], fm["Au"][:, :], [(fm["Au"], "cur")], [128, 512], "u")
        gv = self.scr("A_gv", [128, 4, 128])
        self.gelu(gv[:], [gv], tm[:, :, 0:128], [tm], [128, 4, 128], "v")
        vn = self.scr("A_vn", [128, 4, 128], BF16)
        self.head_rms_tm((gv[:], [gv]), (vn[:], [vn]), (pr[:, PR_SGU_G:PR_SGU_G + 128], [pr]), "av")
        p = self.gps()
        for tt in range(4):
            for h in range(2):
                self.mm(p[h * 64:(h + 1) * 64, tt * 128:(tt + 1) * 128], vn[:, tt, h * 64:(h + 1) * 64],
                        c["wsT"][:, h, :], True, True, [vn, c["wsT"]], [p])
        ya = self.scr("A_ya", [128, 512])
        self.tt("vector", ya[:].rearrange("p (a t) -> p a t", a=4), p[:].rearrange("p (a t) -> p a t", a=4),
                pr[:, None, PR_SGUB:PR_SGUB + 128].broadcast_to([128, 4, 128]), ALU.add, [p, pr], [ya])
        self.tt("gpsimd", ya[:], ya[:], gu[:], ALU.mult, [ya, gu], [ya])
        self.tt("vector", mixo[:, 0, :], ya[:], fm["Az"][:, :], ALU.mult, [ya, fm["Az"]], [(mixo, 0)])

    def branch_M(self, mc, fm, tm, mixo):
        self.begin("M")
        c = self.consts
        pT = c["pT"]
        qn = self.scr("M_qn", [128, 4, 128], BF16)
        self.head_rms_tm((tm[:, :, 640:768], [tm]), (qn[:], [qn]), (c["gmq8"][:], [c["gmq8"]]), "mq")
        yield
        qT = self.scr("M_qT", [128, 512], BF16)
        for tt in range(4):
            self.tr(pT[:, tt * 128:(tt + 1) * 128], qn[:, tt, :], c["ident"][:], [qn, c["ident"]], [pT])
        self.cp("vector", qT[:], pT[:, 0:512], [pT], [qT])
        yield
        pn = self.pacc[0]
        pd = self.pacc[1]
        E = self.scr("M_E", [128, 512], BF16)
        for h in range(2):
            hs = slice(h * 64, (h + 1) * 64)
            for mcx in range(2):
                ps_ = self.gps()
                self.mm(ps_[:], c["kTm"][hs, mcx * 128:(mcx + 1) * 128], qT[hs, :], True, True, [c["kTm"], qT], [ps_])
                self.act(E[:], ps_[:], AF.Exp, [ps_], [E])
                yield
                self.mm(pn[hs, :], c["vm"][:, mcx, hs], E[:], mcx == 0, mcx == 1, [c["vm"], E], [(pn, h)])
                self.mm(pd[hs, :], c["ones_bf"][:, 0:64], E[:], mcx == 0, mcx == 1, [c["ones_bf"], E], [(pd, h)])
                yield
        rd = self.scr("M_rd", [128, 512])
        self.recip(rd[:], pd[:], [pd], [rd])
        self.tt("vector", rd[:], pn[:], rd[:], ALU.mult, [pn, rd], [rd])
        yield
        self.tt("gpsimd", mixo[:, 4, :], rd[:], fm["Mz"][:, :], ALU.mult, [rd, fm["Mz"]], [(mixo, 4)])

    def branch_B(self, mc, fm, tm, mixo):
        self.begin("B")
        c = self.consts
        pT = c["pT"]
        pr = c["pr"]
        G = mc
        if getattr(self, "bstage", 9) < 1:
            return
        qn = self.scr("B_qn", [128, 4, 128], BF16)
        kn = self.scr("B_kn", [128, 4, 128], BF16)
        self.head_rms_tm((tm[:, :, 128:256], [tm]), (qn[:], [qn]), (c["gq8"][:], [c["gq8"]]), "bq")
        self.head_rms_tm((tm[:, :, 256:384], [tm]), (kn[:], [kn]), (pr[:, PR_BKG:PR_BKG + 128], [pr]), "bk")
        QT = self.scr("B_QT", [128, 512], BF16)
        if getattr(self, "bstage", 9) < 0.3:
            return
        for tt in range(4):
            self.tr(pT[:, tt * 128:(tt + 1) * 128], qn[:, tt, :], c["ident"][:], [qn, c["ident"]], [pT])
        for tt in range(4):
            self.tr(pT[:, 512 + tt * 128:512 + (tt + 1) * 128], kn[:, tt, :], c["ident"][:], [kn, c["ident"]], [pT])
        self.cp("vector", QT[:], pT[:, 0:512], [pT], [QT])
        self.cp("scalar", self.KT[:, G * 512:(G + 1) * 512], pT[:, 512:1024], [pT], [(self.KT, G)])
        QTm = [self.scr("B_QTm%d" % h, [128, 512], BF16) for h in range(2)]
        for h in range(2):
            self.ts("gpsimd", QTm[h][:], QT[:], self.hm[:, h:h + 1], ALU.mult, [QT, self.hm], [QTm[h]])
        for tt in range(4):
            self.cp("gpsimd", self.VB[:, G * 4 + tt, :], tm[:, tt, 384:512], [tm], [(self.VB, G * 4 + tt)])
        e1 = self.scr("B_e1", [128, 512])
        self.act(e1[:], fm["Bf"][:, :], AF.Exp, [(fm["Bf"], "cur")], [e1], scale=-1.0, bias=c["nfbB"][:, 0:1])
        self.act(e1[:], e1[:], AF.Ln, [e1], [e1], bias=1.0)
        Lc = self.scr("B_Lc", [128, 512])
        for q4 in range(4):
            qs = slice(q4 * 128, (q4 + 1) * 128)
            ini = self.Fprev[:, 0:1] if q4 == 0 else Lc[:, q4 * 128 - 1:q4 * 128]
            self.op("vector", lambda e, qs=qs, ini=ini: e.tensor_tensor_scan(
                out=Lc[:, qs], data0=c["onesf"][:, 0:128], data1=e1[:, qs], initial=ini,
                op0=ALU.mult, op1=ALU.add), [c["onesf"], e1, self.Fprev, Lc], [Lc])
        pcg = self.gps()
        for h in range(2):
            fs = slice(64, 128) if h == 0 else slice(0, 64)
            for j in range(4):
                self.mm(pcg[:, h * 8 + j:h * 8 + j + 1], Lc[fs, j * 128:(j + 1) * 128], c["c64f"][fs, 0:1],
                        True, True, [Lc, c["c64f"]], [pcg])
            for hf in range(2):
                col = hf * 256 + 127
                self.mm(pcg[:, 16 + h * 2 + hf:17 + h * 2 + hf], c["c64f"][fs, :], Lc[fs, col:col + 1], True, True,
                        [c["c64f"], Lc], [pcg])
        cg = self.scr("B_cg", [128, 4])
        for h in range(2):
            self.cp("vector", self.Fcol[:, h, G * 4:G * 4 + 4], pcg[:, h * 8:h * 8 + 4], [pcg], [(self.Fcol, G)])
        self.cp("vector", cg[:], pcg[:, 16:20], [pcg], [cg])
        self.cp("vector", self.Fprev[:], Lc[:, 511:512], [Lc], [self.Fprev])
        nkb = 4 * G + 4
        bias = self.scr("B_bias", [128, 4, self.SEQ // 128])
        for h in range(2):
            for hf in range(2):
                self.ts("vector", bias[:, h * 2 + hf, 0:nkb], self.Fcol[:, h, 0:nkb],
                        cg[:, h * 2 + hf:h * 2 + hf + 1], ALU.subtract, [self.Fcol, cg], [bias])
        pNumH = [c["pNum"], c["pDen"]]
        Es = [self.scr("B_E%d" % i, [128, 512], BF16) for i in range(4)]
        Eacc = [self.scr("B_Eacc%d" % h, [128, 512]) for h in range(2)]
        rd = self.scr("B_rd", [128, 512])
        yb = self.scr("B_yb", [128, 512])
        pairs = [(h, kb) for h in range(2) for kb in range(nkb)]
        psl = {}

        def score(i):
            h, kb = pairs[i]
            j = kb - 4 * G
            c0 = 0 if j <= 0 else j * 128
            ps_ = self.gps()
            self.mm(ps_[:, c0:512], self.KT[:, kb * 128:(kb + 1) * 128], QTm[h][:, c0:512], True, True,
                    [self.KT, QTm[h]], [ps_])
            psl[i] = ps_
        score(0)
        for i, (h, kb) in enumerate(pairs):
            if i + 1 < len(pairs):
                score(i + 1)
            hs = slice(h * 64, (h + 1) * 64)
            pN = pNumH[h]
            j = kb - 4 * G
            c0 = 0 if j <= 0 else j * 128
            ps_ = psl.pop(i)
            E = Es[i % 4]
            for hf in range(2):
                a_, b_ = max(c0, hf * 256), (hf + 1) * 256
                if a_ >= b_:
                    continue
                self.act(E[:, a_:b_], ps_[:, a_:b_], AF.Exp, [ps_, bias], [E],
                         bias=bias[:, h * 2 + hf, kb:kb + 1])
            if j >= 0:
                self.tt("gpsimd", E[:, c0:c0 + 128], E[:, c0:c0 + 128], c["m_ge"][:], ALU.mult,
                        [E, c["m_ge"]], [E])
            self.mm(pN[:, c0:512], self.VB[:, kb, :], E[:, c0:512], kb == 0, kb == nkb - 1,
                    [self.VB, E], [pN])
            if kb == 0:
                self.cp("vector", Eacc[h][:], E[:], [E], [Eacc[h]])
            else:
                self.tt("vector", Eacc[h][:, c0:512], Eacc[h][:, c0:512], E[:, c0:512], ALU.add,
                        [Eacc[h], E], [Eacc[h]])
            if kb == nkb - 1:
                pd_ = self.gps()
                self.mm(pd_[hs, :], c["onesf"][:, 0:64], Eacc[h][:], True, True, [c["onesf"], Eacc[h]], [pd_])
                self.recip(rd[hs, :], pd_[hs, :], [pd_], [rd])
                self.tt("vector", yb[hs, :], pN[hs, :], rd[hs, :], ALU.mult, [pN, rd], [yb])
        self.tt("gpsimd", mixo[:, 1, :], yb[:], fm["Bz"][:, :], ALU.mult, [yb, fm["Bz"]], [(mixo, 1)])

    def branch_C(self, mc, fm, tm, mixo):
        self.begin("C")
        c = self.consts
        pc = c["pc"]
        pT = c["pT"]
        qc = self.scr("C_qc", [128, 512], BF16)
        kc = self.scr("C_kc", [128, 512], BF16)
        for g, dst, wn, bn in (("Cq", qc, "cq", "cqb"), ("Ck", kc, "ck", "ckb")):
            x = fm[g]
            acc = self.scr("C_acc" + g, [128, 512])
            self.ts("vector", acc[:], x[:, 0:512], pc[:, PC[wn + "0"]:PC[wn + "0"] + 1], ALU.mult, [x, pc], [acc],
                    s2=pc[:, PC[bn]:PC[bn] + 1], op1=ALU.add)
            for j in range(1, 4):
                self.stt(acc[:], x[:, j:j + 512], pc[:, PC[wn + str(j)]:PC[wn + str(j)] + 1], acc[:], ALU.mult,
                         ALU.add, [x, pc, acc], [acc])
            if g == "Cq":
                self.act(dst[:], acc[:], AF.Silu, [acc], [dst])
            else:
                self.act(acc[:], acc[:], AF.Silu, [acc], [acc])
                self.ts("vector", dst[:], acc[:], 0.125, ALU.mult, [acc], [dst])
        yield
        Lb = []
        for h in range(2):
            e1 = fm["Cf%d" % h]
            self.act(e1[:], fm["Cf%d" % h][:, :], AF.Exp, [(fm["Cf%d" % h], "cur")], [e1], scale=-1.0,
                     bias=c["nfbC"][:, h:h + 1])
            self.act(e1[:], e1[:], AF.Ln, [e1], [e1], bias=1.0)
            L = self.scr("C_Lb%d" % h, [128, 512])
            for ch in range(4):
                cs = slice(ch * 128, (ch + 1) * 128)
                self.op("vector", lambda e, L=L, e1=e1, cs=cs: e.tensor_tensor_scan(
                    out=L[:, cs], data0=c["onesf"][:, 0:128], data1=e1[:, cs], initial=0.0,
                    op0=ALU.mult, op1=ALU.add), [c["onesf"], e1], [L])
            Lb.append(L)
        ilog = fm["Ci"]
        self.ts("vector", ilog[:], fm["Ci"][:, :], pc[:, PC["ib_C"]:PC["ib_C"] + 1], ALU.add,
                [(fm["Ci"], "cur"), pc], [ilog])
        yield
        so = fm["Co"]
        sz = fm["Cz"]
        if not hasattr(self, "vaug"):
            self.vaug = [self.sb("C_vaug%d" % h, [128, 128], BF16) for h in range(2)]
            for h in range(2):
                self.memset("vector", self.vaug[h][:], 1.0, [self.vaug[h]])
        vaug = self.vaug
        for ch in range(4):
            cs = slice(ch * 128, (ch + 1) * 128)
            pcol = self.gps()
            for h in range(2):
                hs = slice(h * 64, (h + 1) * 64)
                self.mm(pcol[:, h:h + 1], Lb[h][0:64, cs], c["c64f"][0:64, 0:1], True, True, [Lb[h], c["c64f"]], [pcol])
                self.mm(pcol[:, 2 + h:3 + h], ilog[hs, cs], c["c64f"][hs, 0:1], True, True, [ilog, c["c64f"]], [pcol])
            col = self.scr("C_col", [128, 8])
            self.cp("vector", col[:, 0:4], pcol[:, 0:4], [pcol], [col])
            yield
            self.tt("vector", col[:, 4:6], col[:, 0:2], col[:, 2:4], ALU.add, [col], [col])
            for h in range(2):
                self.ts("vector", col[:, 6 + h:7 + h], col[:, 4 + h:5 + h], Lb[h][:, ch * 128 + 127:ch * 128 + 128],
                        ALU.subtract, [col, Lb[h]], [col])
            wcol = self.scr("C_wcol", [128, 2])
            self.act(wcol[:], col[:, 6:8], AF.Exp, [col], [wcol])
            yield
            eg = self.scr("C_eg", [128, 2])
            for h in range(2):
                self.act(eg[:, h:h + 1], Lb[h][:, ch * 128 + 127:ch * 128 + 128], AF.Exp, [Lb[h]], [eg], scale=-1.0)
            ebq = self.scr("C_ebq", [128, 128])
            for h in range(2):
                hs = slice(h * 64, (h + 1) * 64)
                self.act(ebq[hs, :], Lb[h][hs, cs], AF.Exp, [Lb[h]], [ebq], scale=-1.0)
            qp = self.scr("C_qp", [128, 128], BF16)
            self.tt("vector", qp[:], qc[:, cs], ebq[:], ALU.mult, [qc, ebq], [qp])
            yield
            for h in range(2):
                hs = slice(h * 64, (h + 1) * 64)
                self.cp("gpsimd", vaug[h][:, 0:64], tm[:, ch, 512 + h * 64:512 + (h + 1) * 64], [tm], [vaug[h]])
            pS = self.gps()
            P = []
            for h in range(2):
                hs = slice(h * 64, (h + 1) * 64)
                self.mm(pS[:, h * 128:(h + 1) * 128], kc[hs, cs], qc[hs, cs], True, True, [kc, qc], [pS])
                D = self.scr("C_D%d" % h, [128, 128])
                self.ts("vector", D[:], Lb[h][:, cs], col[:, h:h + 1], ALU.subtract, [Lb[h], col], [D], s2=0.0,
                        op1=ALU.max)
                self.act(D[:], D[:], AF.Exp, [D, col], [D], scale=-1.0, bias=col[:, 2 + h:3 + h])
                self.tt("gpsimd", D[:], D[:], c["m_ge"][:], ALU.mult, [D, c["m_ge"]], [D])
                Ph = self.scr("C_P%d" % h, [128, 128], BF16)
                self.tt("vector", Ph[:], pS[:, h * 128:(h + 1) * 128], D[:], ALU.mult, [pS, D], [Ph])
                P.append(Ph)
            yield
            pnd = self.gps()
            for h in range(2):
                hs = slice(h * 64, (h + 1) * 64)
                self.mm(pnd[hs, 0:128], vaug[h][:, 0:64], P[h][:], True, False, [vaug[h], P[h]], [pnd])
                self.mm(pnd[hs, 0:128], self.Cstb[hs, 0:64], qp[hs, :], False, True, [self.Cstb, qp], [pnd])
            for h in range(2):
                hs = slice(h * 64, (h + 1) * 64)
                self.mm(pnd[hs, 128:256], c["ones_bf"][:, 0:64], P[h][:], True, False, [c["ones_bf"], P[h]], [pnd])
                self.mm(pnd[hs, 128:256], self.Cstb[hs, 64:128], qp[hs, :], False, True, [self.Cstb, qp], [pnd])
            dn = self.scr("C_dn", [128, 128])
            self.act(dn[:], pnd[:, 128:256], AF.Abs, [pnd], [dn])
            self.ts("vector", dn[:], dn[:], 1.0, ALU.max, [dn], [dn])
            self.recip(dn[:], dn[:], [dn], [dn])
            ho = self.scr("C_ho", [128, 128])
            self.tt("vector", ho[:], pnd[:, 0:128], dn[:], ALU.mult, [pnd, dn], [ho])
            yield
            self.tt("gpsimd", ho[:], ho[:], so[:, cs], ALU.mult, [ho, so], [ho])
            yield
            sq = self.scr("C_sq", [128, 128], BF16)
            self.tt("gpsimd", sq[:], ho[:], ho[:], ALU.mult, [ho], [sq])
            pq = self.gps()
            self.mm(pq[:, 0:128], c["blk"][:], sq[:], True, True, [c["blk"], sq], [pq])
            rs = self.scr("C_rs", [128, 128])
            self.act(rs[:], pq[:, 0:128], AF.Ln, [pq], [rs], scale=1.0 / 64, bias=EPS)
            self.act(rs[:], rs[:], AF.Exp, [rs], [rs], scale=-0.5)
            self.stt(ho[:], ho[:], pc[:, PC["out_g"]:PC["out_g"] + 1], rs[:], ALU.mult, ALU.mult, [ho, pc, rs], [ho])
            self.tt("vector", mixo[:, 2, cs], ho[:], sz[:, cs], ALU.mult, [ho, sz], [(mixo, 2)])
            yield
            self.tr(pT[:, 0:128], kc[:, cs], c["ident"][:], [kc, c["ident"]], [pT])
            khat = self.scr("C_khat", [128, 128], BF16)
            for h in range(2):
                hs = slice(h * 64, (h + 1) * 64)
                self.ts("vector", khat[:, hs], pT[:, h * 64:(h + 1) * 64], wcol[:, h:h + 1], ALU.mult, [pT, wcol],
                        [khat])
            yield
            pC = self.gps()
            for h in range(2):
                hs = slice(h * 64, (h + 1) * 64)
                self.mm(pC[hs, 0:128], khat[:, hs], vaug[h][:], True, True, [khat, vaug[h]], [pC])
            for h in range(2):
                hs = slice(h * 64, (h + 1) * 64)
                self.stt(self.Cst[hs, :], self.Cst[hs, :], eg[hs, h:h + 1], pC[hs, 0:128], ALU.mult, ALU.add,
                         [self.Cst, eg, pC], [self.Cst])
            self.cp("vector", self.Cstb[:], self.Cst[:], [self.Cst], [self.Cstb])
            yield

    def branch_D(self, mc, fm, tm, mixo):
        self.begin("D")
        c = self.consts
        pc = c["pc"]
        pT = c["pT"]
        omm = c["omm"]
        if not hasattr(self, "m_gt"):
            self.m_gt = self.sb("m_gt", [128, 128], BF16)
            self.mN_gt = self.sb("mN_gt", [128, 128], BF16)
            self.memset("vector", self.m_gt[:], 1.0, [self.m_gt])
            self.asel(self.m_gt[:], self.m_gt[:], ALU.is_gt, [self.m_gt], [self.m_gt])
            self.memset("vector", self.mN_gt[:], 1.0, [self.mN_gt])
            self.asel(self.mN_gt[:], self.mN_gt[:], ALU.is_gt, [self.mN_gt], [self.mN_gt], cm=1, pat=[[-1, 128]])
        m_gt, mN_gt, m_ge = self.m_gt, self.mN_gt, c["m_ge"]

        def shift(g, idx, rows, name):
            x = fm[g]
            t = self.scr("D_sh_" + name, [128, 512])
            self.ts("vector", t[0:rows, :], x[0:rows, 1:513], omm[0:rows, idx:idx + 1], ALU.mult, [x, omm], [t])
            self.stt(t[0:rows, :], x[0:rows, 0:512], pc[0:rows, PC["mu_r"] + idx:PC["mu_r"] + idx + 1], t[0:rows, :],
                     ALU.mult, ALU.add, [x, pc, t], [t])
            return t
        rs = shift("Dr", 0, 128, "r")
        ks = shift("Dk", 1, 128, "k")
        vs = shift("Dv", 2, 128, "v")
        zs = shift("Dz", 3, 128, "z")
        ws = shift("Dw", 4, 16, "w")
        as_ = shift("Da", 5, 16, "a")
        self.act(ws[0:16, :], ws[0:16, :], AF.Tanh, [ws], [ws])
        pw = self.gps()
        self.mm(pw[:], c["w2s"][:], ws[0:16, :], True, True, [c["w2s"], ws], [pw])
        lw = ws
        self.act(lw[:], pw[:], AF.Sigmoid, [pw, pc], [lw], bias=pc[:, PC["w0"]:PC["w0"] + 1])
        self.ts("gpsimd", lw[:], lw[:], -0.6065306597126334, ALU.mult, [lw], [lw])
        pa = self.gps()
        self.mm(pa[:], c["a2s"][:], as_[0:16, :], True, True, [c["a2s"], as_], [pa])
        aa = as_
        self.act(aa[:], pa[:], AF.Sigmoid, [pa, pc], [aa], bias=pc[:, PC["a0"]:PC["a0"] + 1])
        sz = zs
        self.act(sz[:], zs[:], AF.Silu, [zs], [sz])
        yield
        vb = self.scr("D_vb", [128, 512], BF16)
        self.cp("gpsimd", vb[:], vs[:], [vs], [vb])
        kx = self.scr("D_kx", [128, 512])
        self.ts("vector", kx[:], ks[:], pc[:, PC["k_k"]:PC["k_k"] + 1], ALU.mult, [ks, pc], [kx])
        sqb = self.scr("D_sqb", [128, 512], BF16)
        self.tt("gpsimd", sqb[:], kx[:], kx[:], ALU.mult, [kx], [sqb])
        pq = self.gps()
        self.mm(pq[:], c["blk"][:], sqb[:], True, True, [c["blk"], sqb], [pq])
        rn = self.scr("D_rn", [128, 512])
        self.ts("vector", rn[:], pq[:], 1e-18, ALU.max, [pq], [rn])
        self.act(rn[:], rn[:], AF.Ln, [rn], [rn])
        self.act(rn[:], rn[:], AF.Exp, [rn], [rn], scale=-0.5)
        kk = kx
        self.tt("vector", kk[:], kx[:], rn[:], ALU.mult, [kx, rn], [kk])
        k2 = rn
        self.ts("vector", k2[:], aa[:], -1.0, ALU.add, [aa, pc], [k2], s2=pc[:, PC["k_a"]:PC["k_a"] + 1], op1=ALU.mult)
        self.stt(k2[:], k2[:], 1.0, ks[:], ALU.add, ALU.mult, [k2, ks], [k2])
        bv = ks
        self.tt("gpsimd", bv[:], kk[:], aa[:], ALU.mult, [kk, aa], [bv])
        yield
        rk = self.scr("D_rk", [128, 512], BF16)
        self.stt(rk[:], rs[:], pc[:, PC["r_k"]:PC["r_k"] + 1], k2[:], ALU.mult, ALU.mult, [rs, pc, k2], [rk])
        pb = self.gps()
        self.mm(pb[:], c["blk"][:], rk[:], True, True, [c["blk"], rk], [pb])
        bon = vs
        self.tt("vector", bon[:], pb[:], vs[:], ALU.mult, [pb, vs], [bon])
        cl = self.scr("D_cl", [128, 512])
        for ch in range(4):
            cs = slice(ch * 128, (ch + 1) * 128)
            self.op("vector", lambda e, cs=cs: e.tensor_tensor_scan(
                out=cl[:, cs], data0=c["onesf"][:, 0:128], data1=lw[:, cs], initial=0.0,
                op0=ALU.mult, op1=ALU.add), [c["onesf"], lw], [cl])
        Ecl = self.scr("D_Ecl", [128, 512])
        Encl = self.scr("D_Encl", [128, 512])
        self.act(Ecl[:], cl[:], AF.Exp, [cl], [Ecl])
        self.act(Encl[:], cl[:], AF.Exp, [cl], [Encl], scale=-1.0)
        Ecx = lw
        self.tt("gpsimd", lw[:], cl[:], lw[:], ALU.subtract, [cl, lw], [lw])
        self.act(Ecx[:], lw[:], AF.Exp, [lw], [Ecx])
        yield
        clT = self.scr("D_clT", [128, 4])
        self.cp("vector", clT[:], cl[:].rearrange("p (a t) -> p a t", a=4)[:, :, 127], [cl], [clT])
        Eh = cl
        for ch in range(4):
            cs = slice(ch * 128, (ch + 1) * 128)
            self.act(Eh[:, cs], cl[:, cs], AF.Exp, [cl, clT], [Eh], scale=-1.0, bias=clT[:, ch:ch + 1])
        KR = self.scr("D_KR", [128, 4, 256], BF16)
        kt = self.scr("D_kt", [128, 512], BF16)
        bt = self.scr("D_bt", [128, 512], BF16)
        khat = self.scr("D_khat", [128, 512], BF16)
        nbh = self.scr("D_nbh", [128, 512], BF16)
        self.tt("vector", KR[:, :, 0:128], kk[:].rearrange("p (a t) -> p a t", a=4),
                Ecx[:].rearrange("p (a t) -> p a t", a=4), ALU.mult, [kk, Ecx], [KR])
        self.tt("gpsimd", KR[:, :, 128:256], rs[:].rearrange("p (a t) -> p a t", a=4),
                Ecl[:].rearrange("p (a t) -> p a t", a=4), ALU.mult, [rs, Ecl], [KR])
        self.tt("vector", kt[:], k2[:], Encl[:], ALU.mult, [k2, Encl], [kt])
        self.tt("gpsimd", bt[:], bv[:], Encl[:], ALU.mult, [bv, Encl], [bt])
        self.tt("vector", khat[:], k2[:], Eh[:], ALU.mult, [k2, Eh], [khat])
        self.stt(nbh[:], bv[:], -1.0, Eh[:], ALU.mult, ALU.mult, [bv, Eh], [nbh])
        yraw = kx
        yield
        for ch in range(4):
            cs = slice(ch * 128, (ch + 1) * 128)
            self.tr(pT[:, 0:128], KR[:, ch, 0:128], c["ident"][:], [KR, c["ident"]], [pT])
            self.tr(pT[:, 128:256], vb[:, cs], c["ident"][:], [vb, c["ident"]], [pT])
            self.tr(pT[:, 256:384], khat[:, cs], c["ident"][:], [khat, c["ident"]], [pT])
            self.tr(pT[:, 384:512], nbh[:, cs], c["ident"][:], [nbh, c["ident"]], [pT])
            TMs = self.scr("D_TMs", [128, 4, 128], BF16)
            self.cp("scalar", TMs[:], pT[:, 0:512].rearrange("p (a t) -> p a t", a=4), [pT], [TMs])
            yield
            rpT = self.scr("D_rpT", [128, 128], BF16)
            GT = self.scr("D_GT", [128, 64], BF16)
            Hs = self.scr("D_Hs", [128, 64])
            py = c["pNum"]
            pHS = c["pDen"]
            HS = [slice(0, 64), slice(64, 128)]
            LT, nQbT, LkT, QkT, LN, R, R2 = {}, {}, {}, {}, {}, {}, {}
            for h in range(2):
                hs = HS[h]
                p1 = self.gps()
                self.mm(p1[:, 0:256], bt[hs, cs], KR[hs, ch, :], True, True, [bt, KR], [p1])
                self.mm(p1[:, 256:512], kt[hs, cs], KR[hs, ch, :], True, True, [kt, KR], [p1])
                LT[h] = self.scr("D_LT%d" % h, [128, 128], BF16)
                nQbT[h] = self.scr("D_nQbT%d" % h, [128, 128], BF16)
                LkT[h] = self.scr("D_LkT%d" % h, [128, 128], BF16)
                QkT[h] = self.scr("D_QkT%d" % h, [128, 128], BF16)
                self.tt("vector", LT[h][:], p1[:, 0:128], m_gt[:], ALU.mult, [p1, m_gt], [LT[h]])
                self.tt("vector", LkT[h][:], p1[:, 256:384], m_gt[:], ALU.mult, [p1, m_gt], [LkT[h]])
                self.stt(nQbT[h][:], p1[:, 128:256], -1.0, m_ge[:], ALU.mult, ALU.mult, [p1, m_ge], [nQbT[h]])
                self.tt("vector", QkT[h][:], p1[:, 384:512], m_ge[:], ALU.mult, [p1, m_ge], [QkT[h]])
                yield
            for h in range(2):
                hs = HS[h]
                p2 = self.gps()
                self.mm(p2[:, 0:128], KR[hs, ch, 0:128], bt[hs, cs], True, True, [KR, bt], [p2])
                self.mm(p2[:, 128:192], LkT[h][:], TMs[:, 1, hs], True, True, [LkT[h], TMs], [p2])
                LN[h] = self.scr("D_LN%d" % h, [128, 128], BF16)
                self.tt("vector", LN[h][:], p2[:, 0:128], mN_gt[:], ALU.mult, [p2, mN_gt], [LN[h]])
                R[h] = self.scr("D_R%d" % h, [128, 128], BF16)
                self.cp("gpsimd", R[h][:, 0:64], TMs[:, 0, hs], [TMs], [R[h]])
                self.cp("scalar", R[h][:, 64:128], p2[:, 128:192], [p2], [R[h]])
                yield
            Rc, Rn, PTc, PNc = {}, {}, {}, {}
            for h in range(2):
                pr_ = self.gps()
                self.mm(pr_[:, 0:128], LT[h][:], R[h][:], True, True, [LT[h], R[h]], [pr_])
                R2[h] = self.scr("D_Rb%d" % h, [128, 128], BF16)
                self.tt("vector", R2[h][:], R[h][:], pr_[:, 0:128], ALU.subtract, [R[h], pr_], [R2[h]])
                Rc[h], Rn[h] = R2[h], R[h]
                PTc[h], PNc[h] = LT[h], LN[h]
            yield
            for j in range(1, 7):
                PTn, PNn = {}, {}
                for h in range(2):
                    PTn[h] = self.scr("D_PT%d_%d" % (h, j % 2), [128, 128], BF16)
                    PNn[h] = self.scr("D_PN%d_%d" % (h, j % 2), [128, 128], BF16)
                    pp = self.gps()
                    self.mm(pp[:, 0:128], PNc[h][:], PTc[h][:], True, True, [PNc[h], PTc[h]], [pp])
                    if j < 6:
                        self.mm(pp[:, 128:256], PTc[h][:], PNc[h][:], True, True, [PNc[h], PTc[h]], [pp])
                        self.cp("scalar", PTn[h][:], pp[:, 0:128], [pp], [PTn[h]])
                        self.cp("vector", PNn[h][:], pp[:, 128:256], [pp], [PNn[h]])
                    else:
                        self.cp("scalar", PTn[h][:], pp[:, 0:128], [pp], [PTn[h]])
                yield
                for h in range(2):
                    pr_ = self.gps()
                    self.mm(pr_[:, 0:128], PTn[h][:], Rc[h][:], True, True, [PTn[h], Rc[h]], [pr_])
                    self.tt("vector", Rn[h][:], Rc[h][:], pr_[:, 0:128], ALU.add, [Rc[h], pr_], [Rn[h]])
                    Rc[h], Rn[h] = Rn[h], Rc[h]
                    PTc[h], PNc[h] = PTn[h], PNn[h]
                yield
            for h in range(2):
                hs = HS[h]
                Rch = Rc[h]
                pr_ = self.gps()
                self.mm(pr_[hs, 0:128], Rch[:, 0:64], nQbT[h][:], True, True, [Rch, nQbT[h]], [pr_])
                self.tt("vector", rpT[hs, :], pr_[hs, 0:128], KR[hs, ch, 128:256], ALU.add, [pr_, KR], [rpT])
                self.mm(py[hs, 0:128], TMs[:, 1, hs], QkT[h][:], True, False, [TMs, QkT[h]], [(py, h)])
                self.mm(py[hs, 0:128], Rch[:, 64:128], nQbT[h][:], False, False, [Rch, nQbT[h]], [(py, h)])
                self.mm(py[hs, 0:128], self.Sstb[hs, :], rpT[hs, :], False, True, [self.Sstb, rpT], [(py, h)])
                pg = self.gps()
                self.mm(pg[hs, 0:64], Rch[:, 0:64], TMs[:, 3, hs], True, True, [Rch, TMs], [pg])
                self.stt(GT[hs, :], c["identf"][hs, h * 64:(h + 1) * 64], Ecl[hs, ch * 128 + 127:ch * 128 + 128],
                         pg[hs, 0:64], ALU.mult, ALU.add, [c["identf"], Ecl, pg], [GT])
                self.mm(pHS[hs, 0:64], TMs[:, 2, hs], TMs[:, 1, hs], True, False, [TMs], [(pHS, h)])
                self.mm(pHS[hs, 0:64], TMs[:, 3, hs], Rch[:, 64:128], False, True, [TMs, Rch], [(pHS, h)])
                self.cp("scalar", Hs[hs, :], pHS[hs, 0:64], [(pHS, h)], [Hs])
                self.mm(pHS[hs, 64:128], GT[hs, :], self.Sstb[hs, :], True, True, [GT, self.Sstb], [(pHS, h)])
                self.tt("vector", self.Sst[hs, :], pHS[hs, 64:128], Hs[hs, :], ALU.add, [(pHS, h), Hs], [self.Sst])
                yield
            self.cp("vector", self.Sstb[:], self.Sst[:], [self.Sst], [self.Sstb])
            self.cp("scalar", yraw[:, cs], py[:, 0:128], [py], [yraw])
        ysq = sqb
        self.tt("gpsimd", ysq[:], yraw[:], yraw[:], ALU.mult, [yraw], [ysq])
        pq2 = self.gps()
        self.mm(pq2[:], c["blk"][:], ysq[:], True, True, [c["blk"], ysq], [pq2])
        rs2 = rn
        self.act(rs2[:], pq2[:], AF.Ln, [pq2], [rs2], scale=1.0 / 64, bias=EPS)
        self.act(rs2[:], rs2[:], AF.Exp, [rs2], [rs2], scale=-0.5)
        self.stt(yraw[:], yraw[:], pc[:, PC["ln_g"]:PC["ln_g"] + 1], rs2[:], ALU.mult, ALU.mult, [yraw, pc, rs2],
                 [yraw])
        self.tt("vector", yraw[:], yraw[:], bon[:], ALU.add, [yraw, bon], [yraw])
        self.tt("gpsimd", mixo[:, 3, :], yraw[:], sz[:], ALU.mult, [yraw, sz], [(mixo, 3)])


def build_final(SEQ):
    Bd = Builder(SEQ, False)
    nc = Bd.nc
    x1 = Bd.dram("x1", [SEQ, 1024], F32, "ExternalInput")
    mp = Bd.dram("mprev", [1280, SEQ], BF16, "ExternalInput")
    wout_d = Bd.dram("wout", [1280, 1024], F32, "ExternalInput")
    out = Bd.dram("out", [SEQ, 1024], F32, "ExternalOutput")
    pP = [Bd.psum("pP0", [128, 512]), Bd.psum("pP1", [128, 512])]
    wo = Bd.sb("wo", [128, 10, 1024], BF16)
    wov = wout_d.rearrange("(c p) n -> p c n", p=128)
    for c in range(10):
        Bd.dma("gpsimd", wo[:, c, :], wov[:, c, :], w=[(wo, c)])
    xts = [Bd.sb("xt%d" % i, [128, 4, 1024]) for i in range(2)]
    mpvs = [Bd.sb("mpv%d" % i, [128, 10, 512], BF16) for i in range(2)]
    outs = []
    for mc in range(SEQ // 512):
        t0 = mc * 512
        xt = xts[mc % 2]
        mpv = mpvs[mc % 2]
        Bd.dma("sync", xt[:], x1[t0:t0 + 512, :].rearrange("(tt p) d -> p tt d", p=128), w=[xt])
        Bd.dma("sync", mpv[:], mp[:, t0:t0 + 512].rearrange("(c p) t -> p c t", p=128), w=[mpv])
        for tt in range(4):
            for hf in range(2):
                p = pP[hf]
                for c in range(10):
                    Bd.mm(p[:], mpv[:, c, tt * 128:(tt + 1) * 128], wo[:, c, hf * 512:(hf + 1) * 512],
                          c == 0, c == 9, [mpv, (wo, c)], [p])
                Bd.tt("vector", xt[:, tt, hf * 512:(hf + 1) * 512], xt[:, tt, hf * 512:(hf + 1) * 512],
                      p[:], ALU.add, [xt, p], [xt])
        o = Bd.dma("sync", out[t0:t0 + 512, :].rearrange("(tt p) d -> p tt d", p=128), xt[:], r=[xt], w=["out_d"])
        outs.append(o)
    Bd.S.emit(final_wait_ops=outs)
    return nc


_CACHE = {}


def _get_prog(key, fn):
    if key not in _CACHE:
        _CACHE[key] = fn()
    return _CACHE[key]


def kernel_unfused(**inputs):
    inp = {k: np.asarray(v) for k, v in inputs.items()}
    x = inp["x"]
    BATCH, SEQ, _ = x.shape
    n = 8
    cores = [(b, hh) for b in range(BATCH) for hh in range(2)]
    xcur = [np.ascontiguousarray(x[b]) for b in range(BATCH)]
    mprev = None
    for l in range(2):
        has_prev = l > 0
        nc = _get_prog(("layer", SEQ, has_prev), lambda: Builder(SEQ, has_prev).build())
        in_maps = []
        for (b, hh) in cores:
            d = host_layer_params(inp, l, hh)
            d["xin"] = xcur[b]
            d["mem"] = np.ascontiguousarray(inp["mem"][b])
            if has_prev:
                d["mprev"] = mprev[b]
                d["wout"] = np.ascontiguousarray(inp["w_out"][l - 1])
            in_maps.append(d)
        res = run_bass_kernel_spmd(nc, in_maps, core_ids=list(range(n)))
        new_m = []
        for b in range(BATCH):
            full = np.zeros((1280, SEQ), dtype=ml_dtypes.bfloat16)
            for hh in range(2):
                m = np.asarray(res.results[b * 2 + hh]["mixed"])
                for g in range(5):
                    full[g * 256 + hh * 128:g * 256 + hh * 128 + 128] = m[g * 128:(g + 1) * 128]
            new_m.append(full)
            if has_prev:
                xcur[b] = np.asarray(res.results[b * 2]["x1out"])
        mprev = new_m
    ncf = _get_prog(("final", SEQ), lambda: build_final(SEQ // 2))
    in_maps = []
    H = SEQ // 2
    for (b, hh) in cores:
        in_maps.append({"x1": np.ascontiguousarray(xcur[b][hh * H:(hh + 1) * H]),
                        "mprev": np.ascontiguousarray(mprev[b][:, hh * H:(hh + 1) * H]),
                        "wout": np.ascontiguousarray(inp["w_out"][1])})
    res = run_bass_kernel_spmd(ncf, in_maps, core_ids=list(range(n)))
    out = np.zeros((BATCH, SEQ, 1024), np.float32)
    for i, (b, hh) in enumerate(cores):
        out[b, hh * H:(hh + 1) * H] = np.asarray(res.results[i]["out"])
    return out


def kernel(**inputs):
    inp = {k: np.asarray(v) for k, v in inputs.items()}
    x = inp["x"]
    BATCH, SEQ, _ = x.shape
    nc = _get_prog(("fused", SEQ), lambda: Builder(SEQ, False).build_fused())
    per = {}
    for l in range(2):
        for hh in range(2):
            d = host_layer_params(inp, l, hh)
            for k, v in d.items():
                per["%s_%d%d" % (k, l, hh)] = v
    in_maps = []
    for b in range(BATCH):
        d = dict(per)
        d["xin"] = np.ascontiguousarray(x[b])
        d["mem"] = np.ascontiguousarray(inp["mem"][b])
        d["wout0"] = np.ascontiguousarray(inp["w_out"][0])
        d["wout1"] = np.ascontiguousarray(inp["w_out"][1])
        in_maps.append(d)
    res = run_bass_kernel_spmd(nc, in_maps, core_ids=list(range(BATCH)))
    out = np.stack([np.asarray(res.results[b]["out"]) for b in range(BATCH)], axis=0)
    return out.astype(np.float32)
```

```python
from contextlib import ExitStack
import numpy as np
import ml_dtypes
import concourse.bass as bass
import concourse.mybir as mybir
from concourse.bass_utils import run_bass_kernel_spmd

F32 = mybir.dt.float32
BF16 = mybir.dt.bfloat16
ALU = mybir.AluOpType
AF = mybir.ActivationFunctionType
AX = mybir.AxisListType

ENGINES = ("tensor", "vector", "scalar", "gpsimd", "sync")
SEM_CAP = 30000
EPS = 1e-6


class _Op:
    __slots__ = ("eng", "fn", "idx", "deps", "signal", "is_dma", "sem", "val", "pre_wait")

    def __init__(self, eng, fn, is_dma):
        self.eng = eng
        self.fn = fn
        self.is_dma = is_dma
        self.deps = []
        self.signal = False
        self.sem = None
        self.val = None
        self.pre_wait = None


class Tl:
    def __init__(self, name, t):
        self.name = name
        self.t = t

    def __getitem__(self, idx):
        return self.t[idx]


def _norm(rs):
    out = []
    for r in rs:
        if isinstance(r, tuple):
            a, k = r
        else:
            a, k = r, None
        if isinstance(a, Tl):
            a = a.name
        if a.startswith("scr"):
            k = None
        out.append((a, k))
    return out


class Sched:
    def __init__(self, nc, stack, n_dma_sems=16):
        self.nc = nc
        self.stack = stack
        self.ops = {e: [] for e in ENGINES}
        self.state = {}
        self.n_dma_sems = n_dma_sems

    def _entries(self, name, key):
        d = self.state.setdefault(name, {})
        if key is None:
            return list(d.values())
        res = []
        if key in d:
            res.append(d[key])
        if None in d:
            res.append(d[None])
        return res

    def add(self, eng, fn, reads=(), writes=(), dma=False):
        reads = _norm(reads)
        writes = _norm(writes)
        op = _Op(eng, fn, dma)
        deps = []
        for (name, key) in reads:
            for ent in self._entries(name, key):
                if ent[0] is not None:
                    deps.append(ent[0])
                if name[0] == "p" and name[1].isupper():
                    deps.extend(o_ for o_ in ent[1] if o_.eng != eng)
        for (name, key) in writes:
            for ent in self._entries(name, key):
                if ent[0] is not None:
                    deps.append(ent[0])
                deps.extend(ent[1])
        for (name, key) in reads:
            d = self.state.setdefault(name, {})
            if key is None:
                if not d:
                    d[None] = [None, []]
                for ent in d.values():
                    ent[1].append(op)
            else:
                if key not in d:
                    d[key] = [None, []]
                d[key][1].append(op)
        for (name, key) in writes:
            d = self.state.setdefault(name, {})
            if key is None:
                d.clear()
                d[None] = [op, []]
            else:
                d[key] = [op, []]
        op.idx = len(self.ops[eng])
        best = {}
        dl = []
        for dop in deps:
            if dop is op:
                continue
            if dop.is_dma:
                if dop not in dl:
                    dl.append(dop)
            else:
                if dop.eng == "tensor" and eng == "tensor" and not dma:
                    continue
                b = best.get(dop.eng)
                if b is None or dop.idx > b.idx:
                    best[dop.eng] = dop
        op.deps = dl + list(best.values())
        for dop in op.deps:
            dop.signal = True
        self.ops[eng].append(op)
        return op

    def emit(self, final_wait_ops=()):
        nc = self.nc
        for eng in ENGINES:
            cnt = 0
            sem = None
            for op in self.ops[eng]:
                if op.is_dma:
                    continue
                if op.signal:
                    if sem is None or cnt >= SEM_CAP:
                        sem = self.stack.enter_context(nc.semaphore(f"s_{eng}_{op.idx}"))
                        cnt = 0
                    cnt += 1
                    op.sem = sem
                    op.val = cnt
        for eng in ENGINES:
            qpool = []
            k = 0
            for op in self.ops[eng]:
                if not op.is_dma:
                    continue
                if len(qpool) < self.n_dma_sems:
                    s = self.stack.enter_context(nc.semaphore(f"d_{eng}_{len(qpool)}"))
                    qpool.append([s, 0])
                    ent = qpool[-1]
                else:
                    ent = qpool[k % self.n_dma_sems]
                    if ent[1] + 16 > SEM_CAP:
                        ent[0] = self.stack.enter_context(nc.semaphore(f"d_{eng}_x{k}"))
                        ent[1] = 0
                if ent[1] > 0:
                    op.pre_wait = (ent[0], ent[1])
                ent[1] += 16
                op.sem = ent[0]
                op.val = ent[1]
                k += 1
        sched = self

        def run(eng_name, e):
            seen = {}
            for op in sched.ops[eng_name]:
                waits = []
                if op.pre_wait is not None:
                    waits.append(op.pre_wait)
                for dop in op.deps:
                    waits.append((dop.sem, dop.val))
                for (s, v) in waits:
                    key = id(s)
                    if seen.get(key, 0) >= v:
                        continue
                    seen[key] = v
                    e.wait_ge(s, v)
                ins = op.fn(e)
                if op.is_dma:
                    ins.then_inc(op.sem, 16)
                elif op.signal:
                    ins.then_inc(op.sem, 1)
            if eng_name == "sync":
                for fop in final_wait_ops:
                    e.wait_ge(fop.sem, fop.val)

        with nc.Block() as block:
            @block.sync
            def _(e):
                run("sync", e)

            @block.tensor
            def _(e):
                run("tensor", e)

            @block.vector
            def _(e):
                run("vector", e)

            @block.scalar
            def _(e):
                run("scalar", e)

            @block.gpsimd
            def _(e):
                run("gpsimd", e)


D_MODEL = 1024
FM_GROUPS = ["Au", "Az", "Bz", "Cq", "Ck", "Co", "Cz", "Dr", "Dk", "Dv", "Dz", "Mz",
             "Bf", "Ci", "Cf0", "Cf1", "Dw", "Da"]
FM_W = {g: 128 for g in FM_GROUPS}
FM_W["Dw"] = 16
FM_W["Da"] = 16
FM_OFF = {}
_o = 0
for _g in FM_GROUPS:
    FM_OFF[_g] = _o
    _o += FM_W[_g]
NFM = _o
TM_GROUPS = ["Av", "Bq", "Bk", "Bv", "Cv", "Mq"]
NTM = 768
NW = NFM + NTM
PC = {n: i for i, n in enumerate([
    "cq0", "cq1", "cq2", "cq3", "ck0", "ck1", "ck2", "ck3", "cqb", "ckb",
    "mu_r", "mu_k", "mu_v", "mu_z", "mu_w", "mu_a",
    "k_k", "k_a", "a0", "w0", "r_k", "ln_g", "out_g",
    "fb_B", "ib_C", "fb_C0", "fb_C1"])}
NPC = len(PC)
PR_SGU_G, PR_BQG, PR_BKG, PR_MQG, PR_MKG, PR_SGUB = 0, 128, 256, 384, 512, 640
NPR = 768

A_OFF = 0
B_OFF = 768
C_OFF = 768 + 1028
D_OFF = C_OFF + 1288
M_OFF = D_OFF + 1056


def host_layer_params(inp, l, hh):
    f32 = np.float32
    w_in = inp["w_in"][l]
    hs = [2 * hh, 2 * hh + 1]

    def hcols(base):
        return np.concatenate([np.arange(base + h * 64, base + h * 64 + 64) for h in hs])

    cols = {}
    cols["Au"] = hcols(A_OFF)
    cols["Av"] = hcols(A_OFF + 256)
    cols["Az"] = hcols(A_OFF + 512)
    cols["Bq"] = hcols(B_OFF)
    cols["Bk"] = hcols(B_OFF + 256)
    cols["Bv"] = hcols(B_OFF + 512)
    bf = B_OFF + 768
    cols["Bf"] = np.concatenate([np.full(64, bf + hs[1]), np.full(64, bf + hs[0])])
    cols["Bz"] = hcols(B_OFF + 772)
    cols["Cq"] = hcols(C_OFF)
    cols["Ck"] = hcols(C_OFF + 256)
    cols["Cv"] = hcols(C_OFF + 512)
    ci = C_OFF + 768
    cols["Ci"] = np.concatenate([np.full(64, ci + hs[0]), np.full(64, ci + hs[1])])
    cols["Cf0"] = np.full(128, ci + 4 + hs[0])
    cols["Cf1"] = np.full(128, ci + 4 + hs[1])
    cols["Co"] = hcols(C_OFF + 776)
    cols["Cz"] = hcols(C_OFF + 1032)
    cols["Dr"] = hcols(D_OFF)
    cols["Dw"] = np.arange(D_OFF + 256, D_OFF + 272)
    cols["Dk"] = hcols(D_OFF + 272)
    cols["Dv"] = hcols(D_OFF + 528)
    cols["Da"] = np.arange(D_OFF + 784, D_OFF + 800)
    cols["Dz"] = hcols(D_OFF + 800)
    cols["Mq"] = hcols(M_OFF)
    cols["Mz"] = hcols(M_OFF + 256)
    allc = np.concatenate([cols[g] for g in FM_GROUPS] + [cols[g] for g in TM_GROUPS])
    wcat = np.ascontiguousarray(w_in[:, allc])

    hc = hcols(0)
    pc = np.zeros((128, NPC), f32)
    cw = inp["mlstm_conv_w"][l]
    cb = inp["mlstm_conv_b"][l]
    for j in range(4):
        pc[:, PC["cq%d" % j]] = cw[j, hc]
        pc[:, PC["ck%d" % j]] = cw[j, 256 + hc]
    pc[:, PC["cqb"]] = cb[hc]
    pc[:, PC["ckb"]] = cb[256 + hc]
    mu = inp["rwkv_mu"][l]
    pc[:, PC["mu_r"]] = mu[hc]
    pc[:16, PC["mu_w"]] = mu[256:272]
    pc[:, PC["mu_k"]] = mu[272 + hc]
    pc[:, PC["mu_v"]] = mu[528 + hc]
    pc[:16, PC["mu_a"]] = mu[784:800]
    pc[:, PC["mu_z"]] = mu[800 + hc]
    pc[:, PC["k_k"]] = inp["rwkv_k_k"][l][hc]
    pc[:, PC["k_a"]] = inp["rwkv_k_a"][l][hc]
    pc[:, PC["a0"]] = inp["rwkv_a0"][l][hc]
    pc[:, PC["w0"]] = inp["rwkv_w0"][l][hc]
    pc[:, PC["r_k"]] = inp["rwkv_r_k"][l].reshape(-1)[hc]
    pc[:, PC["ln_g"]] = inp["rwkv_ln_g"][l][hc]
    pc[:, PC["out_g"]] = inp["mlstm_out_g"][l][hc]
    fb = inp["fox_f_b"][l]
    pc[:, PC["fb_B"]] = np.concatenate([np.full(64, fb[hs[1]]), np.full(64, fb[hs[0]])])
    ib = inp["mlstm_i_b"][l]
    pc[:, PC["ib_C"]] = np.concatenate([np.full(64, ib[hs[0]]), np.full(64, ib[hs[1]])])
    fbc = inp["mlstm_f_b"][l]
    pc[:, PC["fb_C0"]] = fbc[hs[0]]
    pc[:, PC["fb_C1"]] = fbc[hs[1]]

    pr = np.zeros((128, NPR), f32)
    pr[:, PR_SGU_G:PR_SGU_G + 128] = inp["sgu_norm_g"][l][hc][None, :]
    pr[:, PR_BQG:PR_BQG + 128] = np.tile(inp["fox_q_g"][l], 2)[None, :]
    pr[:, PR_BKG:PR_BKG + 128] = np.tile(inp["fox_k_g"][l], 2)[None, :]
    pr[:, PR_MQG:PR_MQG + 128] = np.tile(inp["mem_q_g"][l], 2)[None, :]
    pr[:, PR_MKG:PR_MKG + 128] = np.tile(inp["mem_k_g"][l], 2)[None, :]
    sb_ = inp["sgu_b"][l]
    pr[:64, PR_SGUB:PR_SGUB + 128] = sb_[hs[0]][None, :]
    pr[64:, PR_SGUB:PR_SGUB + 128] = sb_[hs[1]][None, :]

    d = {
        "wcat": wcat,
        "pc": pc,
        "pr": pr,
        "ng": np.ascontiguousarray(inp["norm_g"][l].reshape(8, 128).T),
        "memg": np.ascontiguousarray(inp["mem_norm_g"][l].reshape(8, 128).T),
        "wkv": np.ascontiguousarray(np.concatenate(
            [inp["mem_w_kv"][l][:, hc], inp["mem_w_kv"][l][:, 256 + hc]], axis=1)),
        "w2": np.ascontiguousarray(inp["rwkv_w2"][l][:, hc]),
        "a2": np.ascontiguousarray(inp["rwkv_a2"][l][:, hc]),
        "sguw": np.ascontiguousarray(inp["sgu_w"][l][hs]),
    }
    return d


class Builder:
    def __init__(self, SEQ, has_prev, branches="ABCDM"):
        self.SEQ = SEQ
        self.has_prev = has_prev
        self.branches = branches
        self.nc = bass.Bass("TRN2", target_bir_lowering=False)
        self.st = ExitStack()
        self.S = Sched(self.nc, self.st)
        self.ps_rr = 0

    def dram(self, name, shape, dt, kind):
        return self.nc.dram_tensor(name, shape, dt, kind=kind).ap()

    def sb(self, name, shape, dt=F32):
        if not hasattr(self, "_tiles"):
            self._tiles = {}
        if name not in self._tiles:
            self._tiles[name] = Tl(name, self.st.enter_context(self.nc.sbuf_tensor(name, shape, dt)))
        return self._tiles[name]

    def psum(self, name, shape, dt=F32):
        if not hasattr(self, "_tiles"):
            self._tiles = {}
        if name not in self._tiles:
            self._tiles[name] = Tl(name, self.st.enter_context(self.nc.psum_tensor(name, shape, dt)))
        return self._tiles[name]

    def gps(self):
        p = self.gp[self.ps_rr % len(self.gp)]
        self.ps_rr += 1
        return p

    def op(self, eng, fn, r=(), w=()):
        return self.S.add(eng, fn, reads=r, writes=w)

    def dma(self, eng, out, in_, r=(), w=()):
        return self.S.add(eng, lambda e: e.dma_start(out=out, in_=in_), reads=r, writes=w, dma=True)

    def mm(self, out, lhsT, rhs, start, stop, r, w):
        return self.S.add("tensor", lambda e: e.matmul(out, lhsT=lhsT, rhs=rhs, start=start, stop=stop),
                          reads=r, writes=w)

    def tr(self, out, in_, ident, r, w):
        return self.S.add("tensor", lambda e: e.transpose(out, in_, ident), reads=r, writes=w)

    def act(self, out, in_, func, r, w, bias=None, scale=None, accum_out=None, eng="scalar"):
        kw = {}
        if bias is not None:
            kw["bias"] = bias
        if scale is not None:
            kw["scale"] = scale
        if accum_out is not None:
            kw["accum_out"] = accum_out
        return self.S.add("scalar", lambda e: e.activation(out=out, in_=in_, func=func, **kw), reads=r, writes=w)

    def tt(self, eng, out, in0, in1, op, r, w):
        return self.S.add(eng, lambda e: e.tensor_tensor(out=out, in0=in0, in1=in1, op=op), reads=r, writes=w)

    def ts(self, eng, out, in0, s1, op0, r, w, s2=None, op1=None):
        if op1 is None:
            return self.S.add(eng, lambda e: e.tensor_scalar(out=out, in0=in0, scalar1=s1, scalar2=None, op0=op0),
                              reads=r, writes=w)
        return self.S.add(eng, lambda e: e.tensor_scalar(out=out, in0=in0, scalar1=s1, scalar2=s2, op0=op0, op1=op1),
                          reads=r, writes=w)

    def stt(self, out, in0, scalar, in1, op0, op1, r, w):
        return self.S.add("vector", lambda e: e.scalar_tensor_tensor(out=out, in0=in0, scalar=scalar, in1=in1,
                                                                      op0=op0, op1=op1), reads=r, writes=w)

    def cp(self, eng, out, in_, r, w):
        if eng == "scalar":
            return self.S.add("scalar", lambda e: e.copy(out=out, in_=in_), reads=r, writes=w)
        return self.S.add(eng, lambda e: e.tensor_copy(out, in_), reads=r, writes=w)

    def recip(self, out, in_, r, w):
        return self.S.add("vector", lambda e: e.reciprocal(out, in_), reads=r, writes=w)

    def memset(self, eng, ap, val, w):
        return self.S.add(eng, lambda e: e.memset(ap, val), writes=w)

    def asel(self, out, in_, cmp, w, r=(), fill=0.0, base=0, cm=-1, pat=None):
        pat = pat or [[1, 128]]
        return self.S.add("gpsimd", lambda e: e.affine_select(out=out, in_=in_, pattern=pat, compare_op=cmp,
                                                              fill=fill, base=base, channel_multiplier=cm),
                          reads=r, writes=w)

    def build(self):
        cfg = dict(tag="", xmode="outproj" if self.has_prev else "ext")
        self.last_out = []
        self.run_pass(cfg)
        self.S.emit(final_wait_ops=self.last_out)
        return self.nc

    def build_fused(self):
        SEQ = self.SEQ
        self.last_out = []
        self.x_ext = self.dram("xin", [SEQ, 1024], F32, "ExternalInput")
        self.mem_ext = self.dram("mem", [256, 1024], F32, "ExternalInput")
        self.mixs = [self.dram("mixs%d" % l, [1280, SEQ], BF16, "Internal") for l in range(2)]
        self.x1s = self.dram("x1s", [SEQ, 1024], F32, "Internal")
        self.wouts = [self.dram("wout%d" % l, [1280, 1024], F32, "ExternalInput") for l in range(2)]
        for l in range(2):
            if l == 1:
                self.begin("oproj")
                self.outproj_pass("ext", 0, self.x1s, "x1s", False)
            for hh in range(2):
                xmode = "ext" if l == 0 else "x1"
                self.run_pass(dict(tag="_%d%d" % (l, hh), xmode=xmode, fused=True, l=l, hh=hh))
        out_d = self.dram("out", [SEQ, 1024], F32, "ExternalOutput")
        self.begin("oproj")
        self.outproj_pass("x1", 1, out_d, "out_d", True)
        self.S.emit(final_wait_ops=self.last_out)
        return self.nc

    def outproj_pass(self, src_kind, l, dst, dst_name, final):
        SEQ = self.SEQ
        pP = [self.psum("pP0", [128, 512]), self.psum("pP1", [128, 512])]
        wb = self.sb("wb", [128, 8, NW], BF16)
        wo = Tl("wb", wb.t[:, :, :].rearrange("p a b -> p (a b)")[:, 0:10240].rearrange("p (c n) -> p c n", c=10))
        wov = self.wouts[l].rearrange("(c p) n -> p c n", p=128)
        for c in range(10):
            self.dma("gpsimd", wo[:, c, :], wov[:, c, :], w=[wo])
        xts = [self.sb("xt%d" % i, [128, 1024]) for i in range(2)]
        tm_ = self.sb("tm", [128, 4, 768], BF16)
        flat = tm_.t[:, :, :].rearrange("p a b -> p (a b)")
        mpvs = [Tl("tm", flat[:, 0:1280].rearrange("p (c t) -> p c t", c=10)),
                Tl("tm", flat[:, 1280:2560].rearrange("p (c t) -> p c t", c=10))]
        for ti in range(SEQ // 128):
            xt = xts[ti % 2]
            mpv = mpvs[ti % 2]
            r0 = ti * 128
            if src_kind == "ext":
                self.dma("sync", xt[:], self.x_ext[r0:r0 + 128, :], w=[xt])
            else:
                self.dma("sync", xt[:], self.x1s[r0:r0 + 128, :], r=[("x1s", ti)], w=[xt])
            self.dma("sync", mpv[:], self.mixs[l][:, r0:r0 + 128].rearrange("(c p) t -> p c t", p=128),
                     r=[("mixs%d" % l, (0, ti // 4)), ("mixs%d" % l, (1, ti // 4))], w=[mpv])
            for hf in range(2):
                p = pP[hf]
                for c in range(10):
                    self.mm(p[:], mpv[:, c, :], wo[:, c, hf * 512:(hf + 1) * 512], c == 0, c == 9,
                            [mpv, wo], [p])
                eng = "vector" if hf == 0 else "gpsimd"
                if eng == "gpsimd":
                    tmpo = self.scr("op_tmp", [128, 512])
                    self.cp("scalar", tmpo[:], p[:], [p], [tmpo])
                    self.tt("gpsimd", xt[:, hf * 512:(hf + 1) * 512], xt[:, hf * 512:(hf + 1) * 512], tmpo[:],
                            ALU.add, [xt, tmpo], [xt])
                else:
                    self.tt("vector", xt[:, hf * 512:(hf + 1) * 512], xt[:, hf * 512:(hf + 1) * 512], p[:],
                            ALU.add, [xt, p], [xt])
            o = self.dma("sync", dst[r0:r0 + 128, :], xt[:], r=[xt], w=[(dst_name, ti)])
            if final:
                self.last_out.append(o)

    def run_pass(self, cfg):
        nc = self.nc
        SEQ = self.SEQ
        NMC = SEQ // 512
        NCH = SEQ // 128
        tag = cfg["tag"]
        xmode = cfg["xmode"]
        fused = cfg.get("fused", False)
        has_prev = xmode == "outproj"
        wcat = self.dram("wcat" + tag, [1024, NW], F32, "ExternalInput")
        pc_d = self.dram("pc" + tag, [128, NPC], F32, "ExternalInput")
        pr_d = self.dram("pr" + tag, [128, NPR], F32, "ExternalInput")
        ng_d = self.dram("ng" + tag, [128, 8], F32, "ExternalInput")
        memg_d = self.dram("memg" + tag, [128, 8], F32, "ExternalInput")
        wkv_d = self.dram("wkv" + tag, [1024, 256], F32, "ExternalInput")
        w2_d = self.dram("w2" + tag, [16, 128], F32, "ExternalInput")
        a2_d = self.dram("a2" + tag, [16, 128], F32, "ExternalInput")
        sguw_d = self.dram("sguw" + tag, [2, 128, 128], F32, "ExternalInput")
        if fused:
            l, hh = cfg["l"], cfg["hh"]
            xin = self.x_ext
            mem_d = self.mem_ext
            mixed_d = self.mixs[l].rearrange("(g two p) t -> two p g t", two=2, p=128)[hh]
            mix_name = "mixs%d" % l
            mix_key = lambda mc: (hh, mc)
            if has_prev:
                mprev_d = self.mixs[0]
                wout_d = self.wouts[0]
                x1_d = self.x1s
        else:
            xin = self.dram("xin", [SEQ, 1024], F32, "ExternalInput")
            mem_d = self.dram("mem", [256, 1024], F32, "ExternalInput")
            mixed_d = self.dram("mixed", [640, SEQ], BF16, "ExternalOutput").rearrange("(g p) t -> p g t", p=128)
            mix_name = "mixed_d"
            mix_key = lambda mc: mc
            if has_prev:
                mprev_d = self.dram("mprev", [1280, SEQ], BF16, "ExternalInput")
                wout_d = self.dram("wout", [1280, 1024], F32, "ExternalInput")
                x1_d = self.dram("x1out", [SEQ, 1024], F32, "ExternalOutput")

        pT = self.psum("pT", [128, 1024], BF16)
        pP = [self.psum("pP0", [128, 512]), self.psum("pP1", [128, 512])]
        self.gp = [self.psum("pG%d" % i, [128, 512]) for i in range(3)]
        self.pacc = pP
        pNum = self.psum("pNum", [128, 512])
        pDen = self.psum("pDen", [128, 512])

        identf = self.sb("identf", [128, 128])
        ident = self.sb("ident", [128, 128], BF16)
        ones_bf = self.sb("ones_bf", [128, 128], BF16)
        onesf = self.sb("onesf", [128, 128])
        c64f = self.sb("c64f", [128, 128])
        c64b = self.sb("c64b", [128, 128], BF16)
        blk = self.sb("blk", [128, 128], BF16)
        m_ge = self.sb("m_ge", [128, 128], BF16)
        self.memset("vector", identf[:], 1.0, [identf])
        self.asel(identf[:], identf[:], ALU.is_equal, [identf], [identf])
        self.cp("vector", ident[:], identf[:], [identf], [ident])
        self.memset("vector", ones_bf[:], 1.0, [ones_bf])
        self.memset("vector", onesf[:], 1.0, [onesf])
        self.memset("vector", c64f[:], 1.0 / 64, [c64f])
        self.memset("vector", c64b[:], 1.0 / 64, [c64b])
        self.memset("vector", blk[:], 0.0, [blk])
        self.memset("vector", blk[0:64, 0:64], 1.0, [blk])
        self.memset("vector", blk[64:128, 64:128], 1.0, [blk])
        self.memset("vector", m_ge[:], 1.0, [m_ge])
        self.asel(m_ge[:], m_ge[:], ALU.is_ge, [m_ge], [m_ge])

        pc = self.sb("pcs", [128, NPC])
        pr = self.sb("prs", [128, NPR])
        ng = self.sb("ngs", [128, 8])
        memg = self.sb("memgs", [128, 8])
        self.dma("sync", pc[:], pc_d, w=[pc])
        self.dma("sync", pr[:], pr_d, w=[pr])
        self.dma("sync", ng[:], ng_d, w=[ng])
        self.dma("sync", memg[:], memg_d, w=[memg])
        omm = self.sb("omm", [128, 6])
        self.ts("vector", omm[:], pc[:, PC["mu_r"]:PC["mu_r"] + 6], -1.0, ALU.mult, [pc], [omm], s2=1.0, op1=ALU.add)
        nfbB = self.sb("nfbB", [128, 1])
        self.ts("vector", nfbB[:], pc[:, PC["fb_B"]:PC["fb_B"] + 1], -1.0, ALU.mult, [pc], [nfbB])
        nfbC = self.sb("nfbC", [128, 2])
        self.ts("vector", nfbC[:], pc[:, PC["fb_C0"]:PC["fb_C0"] + 2], -1.0, ALU.mult, [pc], [nfbC])
        gq8 = self.sb("gq8", [128, 128])
        self.ts("vector", gq8[:], pr[:, PR_BQG:PR_BQG + 128], 0.125, ALU.mult, [pr], [gq8])
        gmq8 = self.sb("gmq8", [128, 128])
        self.ts("vector", gmq8[:], pr[:, PR_MQG:PR_MQG + 128], 0.125, ALU.mult, [pr], [gmq8])

        wb = self.sb("wb", [128, 8, NW], BF16)
        wv = wcat.rearrange("(c p) n -> p c n", p=128)
        for c in range(8):
            self.dma("gpsimd", wb[:, c, :], wv[:, c, :], w=[(wb, c)])
        for c in range(8):
            eng = "vector" if c % 2 == 0 else "gpsimd"
            self.ts(eng, wb[:, c, :], wb[:, c, :], ng[:, c:c + 1], ALU.mult, [(wb, c), ng], [(wb, c)])
        if has_prev:
            wo = self.sb("wo", [128, 10, 1024], BF16)
            wov = wout_d.rearrange("(c p) n -> p c n", p=128)
            for c in range(10):
                self.dma("gpsimd", wo[:, c, :], wov[:, c, :], w=[(wo, c)])
        w2s = self.sb("w2s", [16, 128])
        a2s = self.sb("a2s", [16, 128])
        self.dma("sync", w2s[:], w2_d, w=[w2s])
        self.dma("sync", a2s[:], a2_d, w=[a2s])

        wsT = self.sb("wsT", [128, 2, 128], BF16)
        self.begin("setupA")
        if "A" in self.branches:
            sgw = self.scr("sgw", [128, 2, 128])
            self.dma("sync", sgw[:], sguw_d.rearrange("h t s -> t h s"), w=[sgw])
            sgwT = self.scr("sgwT", [128, 2, 128])
            for h in range(2):
                p = self.gps()
                self.tr(p[:, 0:128], sgw[:, h, :], identf[:], [sgw, identf], [p])
                self.cp("vector", sgwT[:, h, :], p[:, 0:128], [p], [sgwT])
                self.asel(sgwT[:, h, :], sgwT[:, h, :], ALU.is_ge, [sgwT], [sgwT])
            self.cp("vector", wsT[:], sgwT[:], [sgwT], [wsT])

        kTm = self.sb("kTm", [128, 256], BF16)
        vm = self.sb("vm", [128, 2, 128], BF16)

        self.alloc_state(NCH)

        xts = [self.sb("xt0", [128, 1024]), self.sb("xt1", [128, 1024])]
        hb = self.sb("hb", [128, 1024], BF16)
        hT = self.sb("hT", [128, 8, 512], BF16)
        ss = self.sb("ss", [128, 2])
        GATED = {"Az": AF.Silu, "Bz": AF.Silu, "Cz": AF.Silu, "Mz": AF.Silu, "Co": AF.Sigmoid}
        fm = {}
        for g in FM_GROUPS:
            if g in ("Cq", "Ck"):
                fm[g] = self.sb("fm_" + g, [128, 4 + 512], BF16)
            elif g in ("Dr", "Dk", "Dv", "Dz"):
                fm[g] = self.sb("fm_" + g, [128, 2 + 512], BF16)
            elif g in ("Dw", "Da"):
                fm[g] = self.sb("fm_" + g, [16, 1 + 512])
            elif g in GATED:
                fm[g] = self.sb("fm_" + g, [128, 512], BF16)
            else:
                fm[g] = self.sb("fm_" + g, [128, 512])
        tm = self.sb("tm", [128, 4, 768], BF16)
        mixo = self.sb("mixo", [128, 5, 512], BF16)
        if "M" in self.branches:
            self.setup_mem(mem_d, wkv_d, memg, pr, ident, identf, pT, kTm, vm, xts, hT, tm, hb)
        mpvs = [Tl("tm", tm.t[:, :, :].rearrange("p a b -> p (a b)")[:, 0:1280].rearrange("p (c t) -> p c t", c=10))] * 2
        for g in ("Cq", "Ck"):
            self.memset("vector", fm[g][:, 0:3], 0.0, [(fm[g], "hist")])
        for g in ("Dr", "Dk", "Dv", "Dz"):
            self.memset("vector", fm[g][:, 0:1], 0.0, [(fm[g], "hist")])
        for g in ("Dw", "Da"):
            self.memset("vector", fm[g][:, 0:1], 0.0, [(fm[g], "hist")])

        self.consts = dict(ident=ident, identf=identf, ones_bf=ones_bf, onesf=onesf, c64f=c64f, c64b=c64b,
                           blk=blk, m_ge=m_ge, pc=pc, pr=pr, omm=omm, nfbB=nfbB, nfbC=nfbC,
                           gq8=gq8, gmq8=gmq8, wsT=wsT, kTm=kTm, vm=vm, w2s=w2s, a2s=a2s, pT=pT,
                           pNum=pNum, pDen=pDen)
        last_out = self.last_out
        xi = 0
        xstate = {"xi": 0}

        def xprep(mc):
            t0 = mc * 512
            for tt in range(4):
                xi = xstate["xi"]
                xt = xts[xi % 2]
                r0 = t0 + tt * 128
                ti = mc * 4 + tt
                if xmode == "x1":
                    self.dma("sync", xt[:], self.x1s[r0:r0 + 128, :], r=[("x1s", ti)], w=[xt])
                else:
                    self.dma("sync", xt[:], xin[r0:r0 + 128, :], w=[xt])
                if has_prev:
                    mpv = mpvs[xi % 2]
                    rr = [("mixs0", (0, mc)), ("mixs0", (1, mc))] if fused else []
                    self.dma("sync", mpv[:], mprev_d[:, r0:r0 + 128].rearrange("(c p) t -> p c t", p=128),
                             r=rr, w=[mpv])
                    for hf in range(2):
                        p = pP[hf]
                        for c in range(10):
                            self.mm(p[:], mpv[:, c, :], wo[:, c, hf * 512:(hf + 1) * 512],
                                    c == 0, c == 9, [mpv, (wo, c)], [p])
                        self.tt("vector", xt[:, hf * 512:(hf + 1) * 512], xt[:, hf * 512:(hf + 1) * 512],
                                p[:], ALU.add, [xt, p], [xt])
                    o = self.dma("sync", x1_d[r0:r0 + 128, :], xt[:], r=[xt], w=[("x1s", ti)])
                    if not fused:
                        last_out.append(o)
                xstate["xi"] = xi + 1
                self.act(hb[:], xt[:], AF.Square, [xt], [hb, ss], accum_out=ss[:, 0:1])
                self.act(ss[:, 1:2], ss[:, 0:1], AF.Ln, [ss], [ss], scale=1.0 / 1024, bias=EPS)
                self.act(ss[:, 1:2], ss[:, 1:2], AF.Exp, [ss], [ss], scale=-0.5)
                self.ts("vector", hb[:], xt[:], ss[:, 1:2], ALU.mult, [xt, ss], [hb])
                yield
                for c in range(8):
                    self.tr(pT[:, c * 128:(c + 1) * 128], hb[:, c * 128:(c + 1) * 128], ident[:],
                            [hb, ident], [pT])
                eng = "vector" if tt % 2 == 0 else "scalar"
                self.cp(eng, hT[:, :, tt * 128:(tt + 1) * 128],
                        pT[:, :].rearrange("p (c t) -> p c t", c=8), [pT], [hT])
                yield

        for _ in xprep(0):
            pass
        for mc in range(NMC):
            t0 = mc * 512
            for gi, g in enumerate(FM_GROUPS):
                p = pP[gi % 2]
                wdt = FM_W[g]
                for c in range(8):
                    self.mm(p[0:wdt, :], wb[:, c, FM_OFF[g]:FM_OFF[g] + wdt], hT[:, c, :], c == 0, c == 7,
                            [(wb, c), hT], [p])
                hist = {"Cq": 3, "Ck": 3, "Dr": 1, "Dk": 1, "Dv": 1, "Dz": 1, "Dw": 1, "Da": 1}.get(g, 0)
                dst = fm[g]
                if g in GATED:
                    self.act(dst[:, :], p[:, :], GATED[g], [p], [(dst, "cur")])
                    continue
                if hist and mc > 0:
                    self.cp("vector", dst[0:wdt, 0:hist], dst[0:wdt, 512:512 + hist], [(dst, "cur")], [(dst, "hist")])
                eng = "scalar" if gi % 2 == 0 else "vector"
                self.cp(eng, dst[0:wdt, hist:hist + 512], p[0:wdt, :], [p, (dst, "hist")], [(dst, "cur")])
            for tt in range(4):
                for hf in range(2):
                    p = pP[hf]
                    for c in range(8):
                        self.mm(p[:, 0:384], hT[:, c, tt * 128:(tt + 1) * 128],
                                wb[:, c, NFM + hf * 384:NFM + (hf + 1) * 384], c == 0, c == 7, [hT, (wb, c)], [p])
                    eng = "scalar" if hf == 0 else "vector"
                    self.cp(eng, tm[:, tt, hf * 384:(hf + 1) * 384], p[:, 0:384], [p], [(tm, tt)])
            self.zero_mix = []
            if "A" in self.branches:
                self.branch_A(mc, fm, tm, mixo)
            else:
                self.memset("gpsimd", mixo[:, 0, :], 0.0, [(mixo, 0)])
            if "B" in self.branches:
                self.branch_B(mc, fm, tm, mixo)
            else:
                self.memset("gpsimd", mixo[:, 1, :], 0.0, [(mixo, 1)])
            gens = []
            if mc + 1 < NMC:
                gens.append(("X", xprep(mc + 1)))
            if "M" in self.branches:
                gens.append(("M", self.branch_M(mc, fm, tm, mixo)))
            else:
                self.memset("gpsimd", mixo[:, 4, :], 0.0, [(mixo, 4)])
            if "D" in self.branches:
                gens.append(("D", self.branch_D(mc, fm, tm, mixo)))
            else:
                self.memset("gpsimd", mixo[:, 3, :], 0.0, [(mixo, 3)])
            if "C" in self.branches:
                gens.append(("C", self.branch_C(mc, fm, tm, mixo)))
            else:
                self.memset("gpsimd", mixo[:, 2, :], 0.0, [(mixo, 2)])
            while gens:
                for item in list(gens):
                    for rep in range(3 if item[0] == "D" else 1):
                        self._br = item[0]
                        try:
                            next(item[1])
                        except StopIteration:
                            gens.remove(item)
                            break
            o = self.dma("sync", mixed_d[:, :, t0:t0 + 512], mixo[:], r=[mixo], w=[(mix_name, mix_key(mc))])
            if not fused:
                last_out.append(o)

    def head_rms_tm(self, src, dst, gain, tag, nt=4):
        sq = self.scr("hr_sq", [128, 4, 128])
        ssq = self.scr("hr_ssq", [128, 8])
        src_ap, src_r = src
        dst_ap, dst_w = dst
        g_ap, g_r = gain
        self.tt("gpsimd", sq[:, 0:nt, :], src_ap, src_ap, ALU.mult, src_r, [sq])
        ssq3 = ssq[:, 0:2 * nt].rearrange("p (t h) -> p t h", h=2)
        self.op("vector", lambda e: e.tensor_reduce(out=ssq3,
                                                     in_=sq[:, 0:nt, :].rearrange("p t (h j) -> p t h j", h=2),
                                                     axis=AX.X, op=ALU.add), [sq], [ssq])
        self.act(ssq[:, 0:2 * nt], ssq[:, 0:2 * nt], AF.Ln, [ssq], [ssq], scale=1.0 / 64, bias=EPS)
        self.act(ssq[:, 0:2 * nt], ssq[:, 0:2 * nt], AF.Exp, [ssq], [ssq], scale=-0.5)
        self.tt("vector", sq[:, 0:nt, :].rearrange("p t (h j) -> p t h j", h=2),
                src_ap.rearrange("p t (h j) -> p t h j", h=2),
                ssq3[:, :, :, None].broadcast_to([128, nt, 2, 64]), ALU.mult, src_r + [ssq], [sq])
        self.tt("vector", dst_ap, sq[:, 0:nt, :], g_ap[:, None, :].broadcast_to([128, nt, 128]), ALU.mult,
                [sq] + g_r, dst_w)

    def begin(self, br):
        self._br = br
        if not hasattr(self, "_brcount"):
            self._brcount = {}
        self._brcount.setdefault(br, {})

    def scr(self, name, shape, dt=F32):
        if not hasattr(self, "_scrmap"):
            self._scrmap = {}
            self._pools = {}
        br = getattr(self, "_br", "x")
        key = (br, name)
        if key in self._scrmap:
            return self._scrmap[key]
        esz = 4 if dt == F32 else 2
        n = 1
        for d_ in shape[1:]:
            n *= d_
        nbytes = n * esz
        cls = 256
        while cls < nbytes:
            cls *= 2
        cnt = self._brcount.setdefault(br, {})
        k = cnt.get(cls, 0)
        cnt[cls] = k + 1
        fam = br if br in ("C", "M") else ""
        pool = self._pools.setdefault((fam, cls), [])
        if k >= len(pool):
            pname = "scr%s%d_%d" % (fam, cls, k)
            pool.append((pname, self.st.enter_context(self.nc.sbuf_tensor(pname, [128, cls // 4], F32))))
        pname, raw = pool[k]
        h = raw if dt == F32 else raw.bitcast(dt)
        ap = h[0:shape[0], 0:n]
        if len(shape) == 3:
            ap = ap.rearrange("p (a b) -> p a b", a=shape[1])
        elif len(shape) == 4:
            ap = ap.rearrange("p (a b c) -> p a b c", a=shape[1], b=shape[2])
        t = Tl(pname, ap)
        self._scrmap[key] = t
        return t

    def gelu(self, dst_ap, dst_w, src_ap, src_r, shape, tag):
        t1 = self.scr("gl_t1" + tag, shape)
        t2 = self.scr("gl_t2" + tag, shape)
        self.tt("gpsimd", t1[:], src_ap, src_ap, ALU.mult, src_r, [t1])
        self.ts("vector", t1[:], t1[:], 0.044715, ALU.mult, [t1], [t1], s2=1.0, op1=ALU.add)
        self.tt("gpsimd", t2[:], t1[:], src_ap, ALU.mult, [t1] + src_r, [t2])
        self.act(t2[:], t2[:], AF.Sigmoid, [t2], [t2], scale=1.5957691216)
        self.tt("vector", dst_ap, t2[:], src_ap, ALU.mult, [t2] + src_r, dst_w)

    def setup_mem(self, mem_d, wkv_d, memg, pr, ident, identf, pT, kTm, vm, xts, hT, tm, hb):
        self.begin("setup")
        for c in range(2):
            self.dma("sync", xts[c][:], mem_d[c * 128:(c + 1) * 128, :], w=[xts[c]])
        wk = Tl("hT", hT[:, :, 0:256])
        mhT = Tl("hT", hT[:, :, 256:512])
        mh = Tl("tm", tm.t[:, :, :].rearrange("p a b -> p (a b)")[:, 0:2048].rearrange("p (c d) -> p c d", c=2))
        wkv_v = wkv_d.rearrange("(c p) n -> p c n", p=128)
        for c in range(8):
            self.dma("gpsimd", wk[:, c, :], wkv_v[:, c, :], w=[wk])
        for c in range(8):
            self.ts("vector", wk[:, c, :], wk[:, c, :], memg[:, c:c + 1], ALU.mult, [wk, memg], [wk])
        mss = self.scr("m_ss", [128, 2])
        for c in range(2):
            self.act(hb[:], xts[c][:], AF.Square, [xts[c]], [hb, mss], accum_out=mss[:, c:c + 1])
        self.act(mss[:], mss[:], AF.Ln, [mss], [mss], scale=1.0 / 1024, bias=EPS)
        self.act(mss[:], mss[:], AF.Exp, [mss], [mss], scale=-0.5)
        for c in range(2):
            self.ts("vector", mh[:, c, :], xts[c][:], mss[:, c:c + 1], ALU.mult, [xts[c], mss], [mh])
        for c2 in range(2):
            for c in range(8):
                self.tr(pT[:, c * 128:(c + 1) * 128], mh[:, c2, c * 128:(c + 1) * 128], ident[:], [mh, ident], [pT])
            self.cp("vector", mhT[:, :, c2 * 128:(c2 + 1) * 128], pT[:, :].rearrange("p (c t) -> p c t", c=8),
                    [pT], [mhT])
        kvt = self.scr("m_kvt", [128, 2, 256])
        for c2 in range(2):
            p = self.gps()
            for c in range(8):
                self.mm(p[:, 0:256], mhT[:, c, c2 * 128:(c2 + 1) * 128], wk[:, c, :], c == 0, c == 7, [mhT, wk], [p])
            self.cp("vector", kvt[:, c2, :], p[:, 0:256], [p], [kvt])
        self.cp("vector", vm[:], kvt[:, :, 128:256], [kvt], [vm])
        kn = self.scr("m_kn", [128, 2, 128], BF16)
        self.head_rms_tm((kvt[:, :, 0:128], [kvt]), (kn[:], [kn]), (pr[:, PR_MKG:PR_MKG + 128], [pr]), "mk", nt=2)
        for c2 in range(2):
            self.tr(pT[:, c2 * 128:(c2 + 1) * 128], kn[:, c2, :], ident[:], [kn, ident], [pT])
        self.cp("vector", kTm[:], pT[:, 0:256], [pT], [kTm])

    def alloc_state(self, NCH):
        SEQ = self.SEQ
        if "B" in self.branches:
            self.KT = self.sb("KT", [128, SEQ], BF16)
            self.VB = self.sb("VB", [128, NCH, 128], BF16)
            self.Fcol = self.sb("Fcol", [128, 2, NCH])
            self.Fprev = self.sb("Fprev", [128, 1])
            self.memset("vector", self.Fprev[:], 0.0, [self.Fprev])
            self.c64h = [self.sb("c64h%d" % h, [128, 128], BF16) for h in range(2)]
            self.hm = self.sb("hm", [128, 2])
            self.memset("vector", self.hm[:], 0.0, [self.hm])
            for h in range(2):
                hs = slice(h * 64, (h + 1) * 64)
                fs = slice(64, 128) if h == 0 else slice(0, 64)
                self.memset("vector", self.c64h[h][:], 0.0, [self.c64h[h]])
                self.memset("vector", self.c64h[h][fs, :], 1.0 / 64, [self.c64h[h]])
                self.memset("vector", self.hm[hs, h:h + 1], 1.0, [self.hm])
        if "C" in self.branches:
            self.Cst = self.sb("Cst", [128, 128])
            self.Cstb = self.sb("Cstb", [128, 128], BF16)
            self.memset("vector", self.Cst[:], 0.0, [self.Cst])
            self.memset("vector", self.Cstb[:], 0.0, [self.Cstb])
        if "D" in self.branches:
            self.Sst = self.sb("Sst", [128, 64])
            self.Sstb = self.sb("Sstb", [128, 64], BF16)
            self.memset("vector", self.Sst[:], 0.0, [self.Sst])
            self.memset("vector", self.Sstb[:], 0.0, [self.Sstb])

    def branch_A(self, mc, fm, tm, mixo):
        self.begin("A")
        c = self.consts
        pr = c["pr"]
        gu = self.scr("A_gu", [128, 512])
        self.gelu(gu[:], [gu], fm["Au"][:, :], [(fm["Au"], "cur")], [128, 512], "u")
        gv = self.scr("A_gv", [128, 4, 128])
        self.gelu(gv[:], [gv], tm[:, :, 0:128], [tm], [128, 4, 128], "v")
        vn = self.scr("A_vn", [128, 4, 128], BF16)
        self.head_rms_tm((gv[:], [gv]), (vn[:], [vn]), (pr[:, PR_SGU_G:PR_SGU_G + 128], [pr]), "av")
        p = self.gps()
        for tt in range(4):
            for h in range(2):
                self.mm(p[h * 64:(h + 1) * 64, tt * 128:(tt + 1) * 128], vn[:, tt, h * 64:(h + 1) * 64],
                        c["wsT"][:, h, :], True, True, [vn, c["wsT"]], [p])
        ya = self.scr("A_ya", [128, 512])
        self.tt("vector", ya[:].rearrange("p (a t) -> p a t", a=4), p[:].rearrange("p (a t) -> p a t", a=4),
                pr[:, None, PR_SGUB:PR_SGUB + 128].broadcast_to([128, 4, 128]), ALU.add, [p, pr], [ya])
        self.tt("gpsimd", ya[:], ya[:], gu[:], ALU.mult, [ya, gu], [ya])
        self.tt("vector", mixo[:, 0, :], ya[:], fm["Az"][:, :], ALU.mult, [ya, fm["Az"]], [(mixo, 0)])

    def branch_M(self, mc, fm, tm, mixo):
        self.begin("M")
        c = self.consts
        pT = c["pT"]
        qn = self.scr("M_qn", [128, 4, 128], BF16)
        self.head_rms_tm((tm[:, :, 640:768], [tm]), (qn[:], [qn]), (c["gmq8"][:], [c["gmq8"]]), "mq")
        yield
        qT = self.scr("M_qT", [128, 512], BF16)
        for tt in range(4):
            self.tr(pT[:, tt * 128:(tt + 1) * 128], qn[:, tt, :], c["ident"][:], [qn, c["ident"]], [pT])
        self.cp("vector", qT[:], pT[:, 0:512], [pT], [qT])
        yield
        pn = self.pacc[0]
        pd = self.pacc[1]
        E = self.scr("M_E", [128, 512], BF16)
        for h in range(2):
            hs = slice(h * 64, (h + 1) * 64)
            for mcx in range(2):
                ps_ = self.gps()
                self.mm(ps_[:], c["kTm"][hs, mcx * 128:(mcx + 1) * 128], qT[hs, :], True, True, [c["kTm"], qT], [ps_])
                self.act(E[:], ps_[:], AF.Exp, [ps_], [E])
                yield
                self.mm(pn[hs, :], c["vm"][:, mcx, hs], E[:], mcx == 0, mcx == 1, [c["vm"], E], [(pn, h)])
                self.mm(pd[hs, :], c["ones_bf"][:, 0:64], E[:], mcx == 0, mcx == 1, [c["ones_bf"], E], [(pd, h)])
                yield
        rd = self.scr("M_rd", [128, 512])
        self.recip(rd[:], pd[:], [pd], [rd])
        self.tt("vector", rd[:], pn[:], rd[:], ALU.mult, [pn, rd], [rd])
        yield
        self.tt("gpsimd", mixo[:, 4, :], rd[:], fm["Mz"][:, :], ALU.mult, [rd, fm["Mz"]], [(mixo, 4)])

    def branch_B(self, mc, fm, tm, mixo):
        self.begin("B")
        c = self.consts
        pT = c["pT"]
        pr = c["pr"]
        G = mc
        if getattr(self, "bstage", 9) < 1:
            return
        qn = self.scr("B_qn", [128, 4, 128], BF16)
        kn = self.scr("B_kn", [128, 4, 128], BF16)
        self.head_rms_tm((tm[:, :, 128:256], [tm]), (qn[:], [qn]), (c["gq8"][:], [c["gq8"]]), "bq")
        self.head_rms_tm((tm[:, :, 256:384], [tm]), (kn[:], [kn]), (pr[:, PR_BKG:PR_BKG + 128], [pr]), "bk")
        QT = self.scr("B_QT", [128, 512], BF16)
        if getattr(self, "bstage", 9) < 0.3:
            return
        for tt in range(4):
            self.tr(pT[:, tt * 128:(tt + 1) * 128], qn[:, tt, :], c["ident"][:], [qn, c["ident"]], [pT])
        for tt in range(4):
            self.tr(pT[:, 512 + tt * 128:512 + (tt + 1) * 128], kn[:, tt, :], c["ident"][:], [kn, c["ident"]], [pT])
        self.cp("vector", QT[:], pT[:, 0:512], [pT], [QT])
        self.cp("scalar", self.KT[:, G * 512:(G + 1) * 512], pT[:, 512:1024], [pT], [(self.KT, G)])
        QTm = [self.scr("B_QTm%d" % h, [128, 512], BF16) for h in range(2)]
        for h in range(2):
            self.ts("gpsimd", QTm[h][:], QT[:], self.hm[:, h:h + 1], ALU.mult, [QT, self.hm], [QTm[h]])
        for tt in range(4):
            self.cp("gpsimd", self.VB[:, G * 4 + tt, :], tm[:, tt, 384:512], [tm], [(self.VB, G * 4 + tt)])
        e1 = self.scr("B_e1", [128, 512])
        self.act(e1[:], fm["Bf"][:, :], AF.Exp, [(fm["Bf"], "cur")], [e1], scale=-1.0, bias=c["nfbB"][:, 0:1])
        self.act(e1[:], e1[:], AF.Ln, [e1], [e1], bias=1.0)
        Lc = self.scr("B_Lc", [128, 512])
        for q4 in range(4):
            qs = slice(q4 * 128, (q4 + 1) * 128)
            ini = self.Fprev[:, 0:1] if q4 == 0 else Lc[:, q4 * 128 - 1:q4 * 128]
            self.op("vector", lambda e, qs=qs, ini=ini: e.tensor_tensor_scan(
                out=Lc[:, qs], data0=c["onesf"][:, 0:128], data1=e1[:, qs], initial=ini,
                op0=ALU.mult, op1=ALU.add), [c["onesf"], e1, self.Fprev, Lc], [Lc])
        pcg = self.gps()
        for h in range(2):
            fs = slice(64, 128) if h == 0 else slice(0, 64)
            for j in range(4):
                self.mm(pcg[:, h * 8 + j:h * 8 + j + 1], Lc[fs, j * 128:(j + 1) * 128], c["c64f"][fs, 0:1],
                        True, True, [Lc, c["c64f"]], [pcg])
            for hf in range(2):
                col = hf * 256 + 127
                self.mm(pcg[:, 16 + h * 2 + hf:17 + h * 2 + hf], c["c64f"][fs, :], Lc[fs, col:col + 1], True, True,
                        [c["c64f"], Lc], [pcg])
        cg = self.scr("B_cg", [128, 4])
        for h in range(2):
            self.cp("vector", self.Fcol[:, h, G * 4:G * 4 + 4], pcg[:, h * 8:h * 8 + 4], [pcg], [(self.Fcol, G)])
        self.cp("vector", cg[:], pcg[:, 16:20], [pcg], [cg])
        self.cp("vector", self.Fprev[:], Lc[:, 511:512], [Lc], [self.Fprev])
        nkb = 4 * G + 4
        bias = self.scr("B_bias", [128, 4, self.SEQ // 128])
        for h in range(2):
            for hf in range(2):
                self.ts("vector", bias[:, h * 2 + hf, 0:nkb], self.Fcol[:, h, 0:nkb],
                        cg[:, h * 2 + hf:h * 2 + hf + 1], ALU.subtract, [self.Fcol, cg], [bias])
        pOff, pDg = c["pNum"], c["pDen"]
        Es = [self.scr("B_E%d" % i, [128, 512], BF16) for i in range(4)]
        EaccO = self.scr("B_EaccO", [128, 512])
        EaccD = self.scr("B_EaccD", [128, 512])
        ndg = self.scr("B_ndg", [128, 512])
        rd = self.scr("B_rd", [128, 512])
        yb = self.scr("B_yb", [128, 512])
        sc = self.scr("B_sc", [128, 2])
        for h in range(2):
            self.tt("vector", sc[:, h:h + 1], cg[:, h * 2:h * 2 + 1], cg[:, h * 2 + 1:h * 2 + 2], ALU.subtract,
                    [cg], [sc])
        self.act(sc[:], sc[:], AF.Exp, [sc], [sc])
        noff = 4 * G
        pairs = [(h, kb) for h in range(2) for kb in range(nkb)]
        psl = {}

        def score(i):
            h, kb = pairs[i]
            j = kb - 4 * G
            c0 = 0 if j <= 0 else j * 128
            ps_ = self.gps()
            self.mm(ps_[:, c0:512], self.KT[:, kb * 128:(kb + 1) * 128], QTm[h][:, c0:512], True, True,
                    [self.KT, QTm[h]], [ps_])
            psl[i] = ps_
        score(0)
        for i, (h, kb) in enumerate(pairs):
            if i + 1 < len(pairs):
                score(i + 1)
            hs = slice(h * 64, (h + 1) * 64)
            j = kb - 4 * G
            c0 = 0 if j <= 0 else j * 128
            ps_ = psl.pop(i)
            E = Es[i % 4]
            if j < 0:
                self.act(E[:, :], ps_[:, :], AF.Exp, [ps_, bias], [E], bias=bias[:, h * 2, kb:kb + 1])
                self.mm(pOff[:, :], self.VB[:, kb, :], E[:, :], kb == 0, kb == noff - 1, [self.VB, E], [pOff])
                if kb == 0:
                    self.cp("vector", EaccO[:], E[:], [E], [EaccO])
                else:
                    self.tt("vector", EaccO[:], EaccO[:], E[:], ALU.add, [EaccO, E], [EaccO])
            else:
                for hf in range(2):
                    a_, b_ = max(c0, hf * 256), (hf + 1) * 256
                    if a_ >= b_:
                        continue
                    self.act(E[:, a_:b_], ps_[:, a_:b_], AF.Exp, [ps_, bias], [E],
                             bias=bias[:, h * 2 + hf, kb:kb + 1])
                self.tt("gpsimd", E[:, c0:c0 + 128], E[:, c0:c0 + 128], c["m_ge"][:], ALU.mult,
                        [E, c["m_ge"]], [E])
                self.mm(pDg[:, c0:512], self.VB[:, kb, :], E[:, c0:512], j == 0, kb == nkb - 1,
                        [self.VB, E], [pDg])
                if j == 0:
                    self.cp("vector", EaccD[:], E[:], [E], [EaccD])
                else:
                    self.tt("vector", EaccD[:, c0:512], EaccD[:, c0:512], E[:, c0:512], ALU.add,
                            [EaccD, E], [EaccD])
            if kb == nkb - 1:
                if noff > 0:
                    self.tt("vector", EaccD[:, 0:256], EaccD[:, 0:256], EaccO[:, 0:256], ALU.add,
                            [EaccD, EaccO], [EaccD])
                    self.stt(EaccD[:, 256:512], EaccO[:, 256:512], sc[:, h:h + 1], EaccD[:, 256:512], ALU.mult,
                             ALU.add, [EaccO, sc, EaccD], [EaccD])
                pd_ = self.gps()
                self.mm(pd_[hs, :], c["onesf"][:, 0:64], EaccD[:], True, True, [c["onesf"], EaccD], [pd_])
                self.recip(rd[hs, :], pd_[hs, :], [pd_], [rd])
                self.cp("scalar", ndg[hs, :], pDg[hs, :], [pDg], [ndg])
                if noff > 0:
                    self.tt("vector", ndg[hs, 0:256], ndg[hs, 0:256], pOff[hs, 0:256], ALU.add, [ndg, pOff], [ndg])
                    self.stt(ndg[hs, 256:512], pOff[hs, 256:512], sc[hs, h:h + 1], ndg[hs, 256:512], ALU.mult,
                             ALU.add, [pOff, sc, ndg], [ndg])
                self.tt("vector", yb[hs, :], ndg[hs, :], rd[hs, :], ALU.mult, [ndg, rd], [yb])
        self.tt("gpsimd", mixo[:, 1, :], yb[:], fm["Bz"][:, :], ALU.mult, [yb, fm["Bz"]], [(mixo, 1)])

    def branch_C(self, mc, fm, tm, mixo):
        self.begin("C")
        c = self.consts
        pc = c["pc"]
        pT = c["pT"]
        qc = self.scr("C_qc", [128, 512], BF16)
        kc = self.scr("C_kc", [128, 512], BF16)
        for g, dst, wn, bn in (("Cq", qc, "cq", "cqb"), ("Ck", kc, "ck", "ckb")):
            x = fm[g]
            acc = self.scr("C_acc" + g, [128, 512])
            self.ts("vector", acc[:], x[:, 0:512], pc[:, PC[wn + "0"]:PC[wn + "0"] + 1], ALU.mult, [x, pc], [acc],
                    s2=pc[:, PC[bn]:PC[bn] + 1], op1=ALU.add)
            for j in range(1, 4):
                self.stt(acc[:], x[:, j:j + 512], pc[:, PC[wn + str(j)]:PC[wn + str(j)] + 1], acc[:], ALU.mult,
                         ALU.add, [x, pc, acc], [acc])
            if g == "Cq":
                self.act(dst[:], acc[:], AF.Silu, [acc], [dst])
            else:
                self.act(acc[:], acc[:], AF.Silu, [acc], [acc])
                self.ts("vector", dst[:], acc[:], 0.125, ALU.mult, [acc], [dst])
        yield
        Lb = []
        for h in range(2):
            e1 = fm["Cf%d" % h]
            self.act(e1[:], fm["Cf%d" % h][:, :], AF.Exp, [(fm["Cf%d" % h], "cur")], [e1], scale=-1.0,
                     bias=c["nfbC"][:, h:h + 1])
            self.act(e1[:], e1[:], AF.Ln, [e1], [e1], bias=1.0)
            L = self.scr("C_Lb%d" % h, [128, 512])
            for ch in range(4):
                cs = slice(ch * 128, (ch + 1) * 128)
                self.op("vector", lambda e, L=L, e1=e1, cs=cs: e.tensor_tensor_scan(
                    out=L[:, cs], data0=c["onesf"][:, 0:128], data1=e1[:, cs], initial=0.0,
                    op0=ALU.mult, op1=ALU.add), [c["onesf"], e1], [L])
            Lb.append(L)
        ilog = fm["Ci"]
        self.ts("vector", ilog[:], fm["Ci"][:, :], pc[:, PC["ib_C"]:PC["ib_C"] + 1], ALU.add,
                [(fm["Ci"], "cur"), pc], [ilog])
        yield
        so = fm["Co"]
        sz = fm["Cz"]
        if not hasattr(self, "vaug"):
            self.vaug = [self.sb("C_vaug%d" % h, [128, 128], BF16) for h in range(2)]
            for h in range(2):
                self.memset("vector", self.vaug[h][:], 1.0, [self.vaug[h]])
        vaug = self.vaug
        for ch in range(4):
            cs = slice(ch * 128, (ch + 1) * 128)
            pcol = self.gps()
            for h in range(2):
                hs = slice(h * 64, (h + 1) * 64)
                self.mm(pcol[:, h:h + 1], Lb[h][0:64, cs], c["c64f"][0:64, 0:1], True, True, [Lb[h], c["c64f"]], [pcol])
                self.mm(pcol[:, 2 + h:3 + h], ilog[hs, cs], c["c64f"][hs, 0:1], True, True, [ilog, c["c64f"]], [pcol])
            col = self.scr("C_col", [128, 8])
            self.cp("vector", col[:, 0:4], pcol[:, 0:4], [pcol], [col])
            yield
            self.tt("vector", col[:, 4:6], col[:, 0:2], col[:, 2:4], ALU.add, [col], [col])
            for h in range(2):
                self.ts("vector", col[:, 6 + h:7 + h], col[:, 4 + h:5 + h], Lb[h][:, ch * 128 + 127:ch * 128 + 128],
                        ALU.subtract, [col, Lb[h]], [col])
            wcol = self.scr("C_wcol", [128, 2])
            self.act(wcol[:], col[:, 6:8], AF.Exp, [col], [wcol])
            yield
            eg = self.scr("C_eg", [128, 2])
            for h in range(2):
                self.act(eg[:, h:h + 1], Lb[h][:, ch * 128 + 127:ch * 128 + 128], AF.Exp, [Lb[h]], [eg], scale=-1.0)
            ebq = self.scr("C_ebq", [128, 128])
            for h in range(2):
                hs = slice(h * 64, (h + 1) * 64)
                self.act(ebq[hs, :], Lb[h][hs, cs], AF.Exp, [Lb[h]], [ebq], scale=-1.0)
            qp = self.scr("C_qp", [128, 128], BF16)
            self.tt("vector", qp[:], qc[:, cs], ebq[:], ALU.mult, [qc, ebq], [qp])
            yield
            for h in range(2):
                hs = slice(h * 64, (h + 1) * 64)
                self.cp("gpsimd", vaug[h][:, 0:64], tm[:, ch, 512 + h * 64:512 + (h + 1) * 64], [tm], [vaug[h]])
            pS = self.gps()
            P = []
            for h in range(2):
                hs = slice(h * 64, (h + 1) * 64)
                self.mm(pS[:, h * 128:(h + 1) * 128], kc[hs, cs], qc[hs, cs], True, True, [kc, qc], [pS])
                D = self.scr("C_D%d" % h, [128, 128])
                self.ts("vector", D[:], Lb[h][:, cs], col[:, h:h + 1], ALU.subtract, [Lb[h], col], [D], s2=0.0,
                        op1=ALU.max)
                self.act(D[:], D[:], AF.Exp, [D, col], [D], scale=-1.0, bias=col[:, 2 + h:3 + h])
                self.tt("gpsimd", D[:], D[:], c["m_ge"][:], ALU.mult, [D, c["m_ge"]], [D])
                Ph = self.scr("C_P%d" % h, [128, 128], BF16)
                self.tt("vector", Ph[:], pS[:, h * 128:(h + 1) * 128], D[:], ALU.mult, [pS, D], [Ph])
                P.append(Ph)
            yield
            pnd = self.gps()
            for h in range(2):
                hs = slice(h * 64, (h + 1) * 64)
                self.mm(pnd[hs, 0:128], vaug[h][:, 0:64], P[h][:], True, False, [vaug[h], P[h]], [pnd])
                self.mm(pnd[hs, 0:128], self.Cstb[hs, 0:64], qp[hs, :], False, True, [self.Cstb, qp], [pnd])
            for h in range(2):
                hs = slice(h * 64, (h + 1) * 64)
                self.mm(pnd[hs, 128:256], c["ones_bf"][:, 0:64], P[h][:], True, False, [c["ones_bf"], P[h]], [pnd])
                self.mm(pnd[hs, 128:256], self.Cstb[hs, 64:128], qp[hs, :], False, True, [self.Cstb, qp], [pnd])
            dn = self.scr("C_dn", [128, 128])
            self.act(dn[:], pnd[:, 128:256], AF.Abs, [pnd], [dn])
            self.ts("vector", dn[:], dn[:], 1.0, ALU.max, [dn], [dn])
            self.recip(dn[:], dn[:], [dn], [dn])
            ho = self.scr("C_ho", [128, 128])
            self.tt("vector", ho[:], pnd[:, 0:128], dn[:], ALU.mult, [pnd, dn], [ho])
            yield
            self.tt("gpsimd", ho[:], ho[:], so[:, cs], ALU.mult, [ho, so], [ho])
            yield
            sq = self.scr("C_sq", [128, 128], BF16)
            self.tt("gpsimd", sq[:], ho[:], ho[:], ALU.mult, [ho], [sq])
            pq = self.gps()
            self.mm(pq[:, 0:128], c["blk"][:], sq[:], True, True, [c["blk"], sq], [pq])
            rs = self.scr("C_rs", [128, 128])
            self.act(rs[:], pq[:, 0:128], AF.Ln, [pq], [rs], scale=1.0 / 64, bias=EPS)
            self.act(rs[:], rs[:], AF.Exp, [rs], [rs], scale=-0.5)
            self.stt(ho[:], ho[:], pc[:, PC["out_g"]:PC["out_g"] + 1], rs[:], ALU.mult, ALU.mult, [ho, pc, rs], [ho])
            self.tt("vector", mixo[:, 2, cs], ho[:], sz[:, cs], ALU.mult, [ho, sz], [(mixo, 2)])
            yield
            self.tr(pT[:, 0:128], kc[:, cs], c["ident"][:], [kc, c["ident"]], [pT])
            khat = self.scr("C_khat", [128, 128], BF16)
            for h in range(2):
                hs = slice(h * 64, (h + 1) * 64)
                self.ts("vector", khat[:, hs], pT[:, h * 64:(h + 1) * 64], wcol[:, h:h + 1], ALU.mult, [pT, wcol],
                        [khat])
            yield
            pC = self.gps()
            for h in range(2):
                hs = slice(h * 64, (h + 1) * 64)
                self.mm(pC[hs, 0:128], khat[:, hs], vaug[h][:], True, True, [khat, vaug[h]], [pC])
            for h in range(2):
                hs = slice(h * 64, (h + 1) * 64)
                self.stt(self.Cst[hs, :], self.Cst[hs, :], eg[hs, h:h + 1], pC[hs, 0:128], ALU.mult, ALU.add,
                         [self.Cst, eg, pC], [self.Cst])
            self.cp("vector", self.Cstb[:], self.Cst[:], [self.Cst], [self.Cstb])
            yield

    def branch_D(self, mc, fm, tm, mixo):
        self.begin("D")
        c = self.consts
        pc = c["pc"]
        pT = c["pT"]
        omm = c["omm"]
        if not hasattr(self, "m_gt"):
            self.m_gt = self.sb("m_gt", [128, 128], BF16)
            self.mN_gt = self.sb("mN_gt", [128, 128], BF16)
            self.memset("vector", self.m_gt[:], 1.0, [self.m_gt])
            self.asel(self.m_gt[:], self.m_gt[:], ALU.is_gt, [self.m_gt], [self.m_gt])
            self.memset("vector", self.mN_gt[:], 1.0, [self.mN_gt])
            self.asel(self.mN_gt[:], self.mN_gt[:], ALU.is_gt, [self.mN_gt], [self.mN_gt], cm=1, pat=[[-1, 128]])
        m_gt, mN_gt, m_ge = self.m_gt, self.mN_gt, c["m_ge"]

        def shift(g, idx, rows, name):
            x = fm[g]
            t = self.scr("D_sh_" + name, [128, 512])
            self.ts("vector", t[0:rows, :], x[0:rows, 1:513], omm[0:rows, idx:idx + 1], ALU.mult, [x, omm], [t])
            self.stt(t[0:rows, :], x[0:rows, 0:512], pc[0:rows, PC["mu_r"] + idx:PC["mu_r"] + idx + 1], t[0:rows, :],
                     ALU.mult, ALU.add, [x, pc, t], [t])
            return t
        rs = shift("Dr", 0, 128, "r")
        ks = shift("Dk", 1, 128, "k")
        vs = shift("Dv", 2, 128, "v")
        zs = shift("Dz", 3, 128, "z")
        ws = shift("Dw", 4, 16, "w")
        as_ = shift("Da", 5, 16, "a")
        self.act(ws[0:16, :], ws[0:16, :], AF.Tanh, [ws], [ws])
        pw = self.gps()
        self.mm(pw[:], c["w2s"][:], ws[0:16, :], True, True, [c["w2s"], ws], [pw])
        lw = ws
        self.act(lw[:], pw[:], AF.Sigmoid, [pw, pc], [lw], bias=pc[:, PC["w0"]:PC["w0"] + 1])
        self.ts("gpsimd", lw[:], lw[:], -0.6065306597126334, ALU.mult, [lw], [lw])
        pa = self.gps()
        self.mm(pa[:], c["a2s"][:], as_[0:16, :], True, True, [c["a2s"], as_], [pa])
        aa = as_
        self.act(aa[:], pa[:], AF.Sigmoid, [pa, pc], [aa], bias=pc[:, PC["a0"]:PC["a0"] + 1])
        sz = zs
        self.act(sz[:], zs[:], AF.Silu, [zs], [sz])
        yield
        vb = self.scr("D_vb", [128, 512], BF16)
        self.cp("gpsimd", vb[:], vs[:], [vs], [vb])
        kx = self.scr("D_kx", [128, 512])
        self.ts("vector", kx[:], ks[:], pc[:, PC["k_k"]:PC["k_k"] + 1], ALU.mult, [ks, pc], [kx])
        sqb = self.scr("D_sqb", [128, 512], BF16)
        self.tt("gpsimd", sqb[:], kx[:], kx[:], ALU.mult, [kx], [sqb])
        pq = self.gps()
        self.mm(pq[:], c["blk"][:], sqb[:], True, True, [c["blk"], sqb], [pq])
        rn = self.scr("D_rn", [128, 512])
        self.ts("vector", rn[:], pq[:], 1e-18, ALU.max, [pq], [rn])
        self.act(rn[:], rn[:], AF.Ln, [rn], [rn])
        self.act(rn[:], rn[:], AF.Exp, [rn], [rn], scale=-0.5)
        kk = kx
        self.tt("vector", kk[:], kx[:], rn[:], ALU.mult, [kx, rn], [kk])
        k2 = rn
        self.ts("vector", k2[:], aa[:], -1.0, ALU.add, [aa, pc], [k2], s2=pc[:, PC["k_a"]:PC["k_a"] + 1], op1=ALU.mult)
        self.stt(k2[:], k2[:], 1.0, ks[:], ALU.add, ALU.mult, [k2, ks], [k2])
        bv = ks
        self.tt("gpsimd", bv[:], kk[:], aa[:], ALU.mult, [kk, aa], [bv])
        yield
        rk = self.scr("D_rk", [128, 512], BF16)
        self.stt(rk[:], rs[:], pc[:, PC["r_k"]:PC["r_k"] + 1], k2[:], ALU.mult, ALU.mult, [rs, pc, k2], [rk])
        pb = self.gps()
        self.mm(pb[:], c["blk"][:], rk[:], True, True, [c["blk"], rk], [pb])
        bon = vs
        self.tt("vector", bon[:], pb[:], vs[:], ALU.mult, [pb, vs], [bon])
        cl = self.scr("D_cl", [128, 512])
        for ch in range(4):
            cs = slice(ch * 128, (ch + 1) * 128)
            self.op("vector", lambda e, cs=cs: e.tensor_tensor_scan(
                out=cl[:, cs], data0=c["onesf"][:, 0:128], data1=lw[:, cs], initial=0.0,
                op0=ALU.mult, op1=ALU.add), [c["onesf"], lw], [cl])
        Ecl = self.scr("D_Ecl", [128, 512])
        Encl = self.scr("D_Encl", [128, 512])
        self.act(Ecl[:], cl[:], AF.Exp, [cl], [Ecl])
        self.act(Encl[:], cl[:], AF.Exp, [cl], [Encl], scale=-1.0)
        Ecx = lw
        self.tt("gpsimd", lw[:], cl[:], lw[:], ALU.subtract, [cl, lw], [lw])
        self.act(Ecx[:], lw[:], AF.Exp, [lw], [Ecx])
        yield
        clT = self.scr("D_clT", [128, 4])
        self.cp("vector", clT[:], cl[:].rearrange("p (a t) -> p a t", a=4)[:, :, 127], [cl], [clT])
        Eh = cl
        for ch in range(4):
            cs = slice(ch * 128, (ch + 1) * 128)
            self.act(Eh[:, cs], cl[:, cs], AF.Exp, [cl, clT], [Eh], scale=-1.0, bias=clT[:, ch:ch + 1])
        KR = self.scr("D_KR", [128, 4, 256], BF16)
        kt = self.scr("D_kt", [128, 512], BF16)
        bt = self.scr("D_bt", [128, 512], BF16)
        khat = self.scr("D_khat", [128, 512], BF16)
        nbh = self.scr("D_nbh", [128, 512], BF16)
        self.tt("vector", KR[:, :, 0:128], kk[:].rearrange("p (a t) -> p a t", a=4),
                Ecx[:].rearrange("p (a t) -> p a t", a=4), ALU.mult, [kk, Ecx], [KR])
        self.tt("gpsimd", KR[:, :, 128:256], rs[:].rearrange("p (a t) -> p a t", a=4),
                Ecl[:].rearrange("p (a t) -> p a t", a=4), ALU.mult, [rs, Ecl], [KR])
        self.tt("vector", kt[:], k2[:], Encl[:], ALU.mult, [k2, Encl], [kt])
        self.tt("gpsimd", bt[:], bv[:], Encl[:], ALU.mult, [bv, Encl], [bt])
        self.tt("vector", khat[:], k2[:], Eh[:], ALU.mult, [k2, Eh], [khat])
        self.stt(nbh[:], bv[:], -1.0, Eh[:], ALU.mult, ALU.mult, [bv, Eh], [nbh])
        yraw = kx
        yield
        for ch in range(4):
            cs = slice(ch * 128, (ch + 1) * 128)
            self.tr(pT[:, 0:128], KR[:, ch, 0:128], c["ident"][:], [KR, c["ident"]], [pT])
            self.tr(pT[:, 128:256], vb[:, cs], c["ident"][:], [vb, c["ident"]], [pT])
            self.tr(pT[:, 256:384], khat[:, cs], c["ident"][:], [khat, c["ident"]], [pT])
            self.tr(pT[:, 384:512], nbh[:, cs], c["ident"][:], [nbh, c["ident"]], [pT])
            TMs = self.scr("D_TMs", [128, 4, 128], BF16)
            self.cp("scalar", TMs[:], pT[:, 0:512].rearrange("p (a t) -> p a t", a=4), [pT], [TMs])
            yield
            rpT = self.scr("D_rpT", [128, 128], BF16)
            GT = self.scr("D_GT", [128, 64], BF16)
            Hs = self.scr("D_Hs", [128, 64])
            py = c["pNum"]
            pHS = c["pDen"]
            HS = [slice(0, 64), slice(64, 128)]
            LT, nQbT, LkT, QkT, LN, R, R2 = {}, {}, {}, {}, {}, {}, {}
            for h in range(2):
                hs = HS[h]
                p1 = self.gps()
                self.mm(p1[:, 0:256], bt[hs, cs], KR[hs, ch, :], True, True, [bt, KR], [p1])
                self.mm(p1[:, 256:512], kt[hs, cs], KR[hs, ch, :], True, True, [kt, KR], [p1])
                LT[h] = self.scr("D_LT%d" % h, [128, 128], BF16)
                nQbT[h] = self.scr("D_nQbT%d" % h, [128, 128], BF16)
                LkT[h] = self.scr("D_LkT%d" % h, [128, 128], BF16)
                QkT[h] = self.scr("D_QkT%d" % h, [128, 128], BF16)
                self.tt("vector", LT[h][:], p1[:, 0:128], m_gt[:], ALU.mult, [p1, m_gt], [LT[h]])
                self.tt("vector", LkT[h][:], p1[:, 256:384], m_gt[:], ALU.mult, [p1, m_gt], [LkT[h]])
                self.stt(nQbT[h][:], p1[:, 128:256], -1.0, m_ge[:], ALU.mult, ALU.mult, [p1, m_ge], [nQbT[h]])
                self.tt("vector", QkT[h][:], p1[:, 384:512], m_ge[:], ALU.mult, [p1, m_ge], [QkT[h]])
                yield
            for h in range(2):
                hs = HS[h]
                p2 = self.gps()
                self.mm(p2[:, 0:128], KR[hs, ch, 0:128], bt[hs, cs], True, True, [KR, bt], [p2])
                self.mm(p2[:, 128:192], LkT[h][:], TMs[:, 1, hs], True, True, [LkT[h], TMs], [p2])
                LN[h] = self.scr("D_LN%d" % h, [128, 128], BF16)
                self.tt("vector", LN[h][:], p2[:, 0:128], mN_gt[:], ALU.mult, [p2, mN_gt], [LN[h]])
                R[h] = self.scr("D_R%d" % h, [128, 128], BF16)
                self.cp("gpsimd", R[h][:, 0:64], TMs[:, 0, hs], [TMs], [R[h]])
                self.cp("scalar", R[h][:, 64:128], p2[:, 128:192], [p2], [R[h]])
                yield
            Rc, Rn, PTc, PNc = {}, {}, {}, {}
            for h in range(2):
                pr_ = self.gps()
                self.mm(pr_[:, 0:128], LT[h][:], R[h][:], True, True, [LT[h], R[h]], [pr_])
                R2[h] = self.scr("D_Rb%d" % h, [128, 128], BF16)
                self.tt("vector", R2[h][:], R[h][:], pr_[:, 0:128], ALU.subtract, [R[h], pr_], [R2[h]])
                Rc[h], Rn[h] = R2[h], R[h]
                PTc[h], PNc[h] = LT[h], LN[h]
            yield
            for j in range(1, 7):
                PTn, PNn = {}, {}
                for h in range(2):
                    PTn[h] = self.scr("D_PT%d_%d" % (h, j % 2), [128, 128], BF16)
                    PNn[h] = self.scr("D_PN%d_%d" % (h, j % 2), [128, 128], BF16)
                    pp = self.gps()
                    self.mm(pp[:, 0:128], PNc[h][:], PTc[h][:], True, True, [PNc[h], PTc[h]], [pp])
                    if j < 6:
                        self.mm(pp[:, 128:256], PTc[h][:], PNc[h][:], True, True, [PNc[h], PTc[h]], [pp])
                        self.cp("scalar", PTn[h][:], pp[:, 0:128], [pp], [PTn[h]])
                        self.cp("vector", PNn[h][:], pp[:, 128:256], [pp], [PNn[h]])
                    else:
                        self.cp("scalar", PTn[h][:], pp[:, 0:128], [pp], [PTn[h]])
                yield
                for h in range(2):
                    pr_ = self.gps()
                    self.mm(pr_[:, 0:128], PTn[h][:], Rc[h][:], True, True, [PTn[h], Rc[h]], [pr_])
                    self.tt("vector", Rn[h][:], Rc[h][:], pr_[:, 0:128], ALU.add, [Rc[h], pr_], [Rn[h]])
                    Rc[h], Rn[h] = Rn[h], Rc[h]
                    PTc[h], PNc[h] = PTn[h], PNn[h]
                yield
            for h in range(2):
                hs = HS[h]
                Rch = Rc[h]
                pr_ = self.gps()
                self.mm(pr_[hs, 0:128], Rch[:, 0:64], nQbT[h][:], True, True, [Rch, nQbT[h]], [pr_])
                self.tt("vector", rpT[hs, :], pr_[hs, 0:128], KR[hs, ch, 128:256], ALU.add, [pr_, KR], [rpT])
                self.mm(py[hs, 0:128], TMs[:, 1, hs], QkT[h][:], True, False, [TMs, QkT[h]], [(py, h)])
                self.mm(py[hs, 0:128], Rch[:, 64:128], nQbT[h][:], False, False, [Rch, nQbT[h]], [(py, h)])
                self.mm(py[hs, 0:128], self.Sstb[hs, :], rpT[hs, :], False, True, [self.Sstb, rpT], [(py, h)])
                pg = self.gps()
                self.mm(pg[hs, 0:64], Rch[:, 0:64], TMs[:, 3, hs], True, True, [Rch, TMs], [pg])
                self.stt(GT[hs, :], c["identf"][hs, h * 64:(h + 1) * 64], Ecl[hs, ch * 128 + 127:ch * 128 + 128],
                         pg[hs, 0:64], ALU.mult, ALU.add, [c["identf"], Ecl, pg], [GT])
                self.mm(pHS[hs, 0:64], TMs[:, 2, hs], TMs[:, 1, hs], True, False, [TMs], [(pHS, h)])
                self.mm(pHS[hs, 0:64], TMs[:, 3, hs], Rch[:, 64:128], False, True, [TMs, Rch], [(pHS, h)])
                self.cp("scalar", Hs[hs, :], pHS[hs, 0:64], [(pHS, h)], [Hs])
                self.mm(pHS[hs, 64:128], GT[hs, :], self.Sstb[hs, :], True, True, [GT, self.Sstb], [(pHS, h)])
                self.tt("vector", self.Sst[hs, :], pHS[hs, 64:128], Hs[hs, :], ALU.add, [(pHS, h), Hs], [self.Sst])
                yield
            self.cp("vector", self.Sstb[:], self.Sst[:], [self.Sst], [self.Sstb])
            self.cp("scalar", yraw[:, cs], py[:, 0:128], [py], [yraw])
        ysq = sqb
        self.tt("gpsimd", ysq[:], yraw[:], yraw[:], ALU.mult, [yraw], [ysq])
        pq2 = self.gps()
        self.mm(pq2[:], c["blk"][:], ysq[:], True, True, [c["blk"], ysq], [pq2])
        rs2 = rn
        self.act(rs2[:], pq2[:], AF.Ln, [pq2], [rs2], scale=1.0 / 64, bias=EPS)
        self.act(rs2[:], rs2[:], AF.Exp, [rs2], [rs2], scale=-0.5)
        self.stt(yraw[:], yraw[:], pc[:, PC["ln_g"]:PC["ln_g"] + 1], rs2[:], ALU.mult, ALU.mult, [yraw, pc, rs2],
                 [yraw])
        self.tt("vector", yraw[:], yraw[:], bon[:], ALU.add, [yraw, bon], [yraw])
        self.tt("gpsimd", mixo[:, 3, :], yraw[:], sz[:], ALU.mult, [yraw, sz], [(mixo, 3)])


def build_final(SEQ):
    Bd = Builder(SEQ, False)
    nc = Bd.nc
    x1 = Bd.dram("x1", [SEQ, 1024], F32, "ExternalInput")
    mp = Bd.dram("mprev", [1280, SEQ], BF16, "ExternalInput")
    wout_d = Bd.dram("wout", [1280, 1024], F32, "ExternalInput")
    out = Bd.dram("out", [SEQ, 1024], F32, "ExternalOutput")
    pP = [Bd.psum("pP0", [128, 512]), Bd.psum("pP1", [128, 512])]
    wo = Bd.sb("wo", [128, 10, 1024], BF16)
    wov = wout_d.rearrange("(c p) n -> p c n", p=128)
    for c in range(10):
        Bd.dma("gpsimd", wo[:, c, :], wov[:, c, :], w=[(wo, c)])
    xts = [Bd.sb("xt%d" % i, [128, 4, 1024]) for i in range(2)]
    mpvs = [Bd.sb("mpv%d" % i, [128, 10, 512], BF16) for i in range(2)]
    outs = []
    for mc in range(SEQ // 512):
        t0 = mc * 512
        xt = xts[mc % 2]
        mpv = mpvs[mc % 2]
        Bd.dma("sync", xt[:], x1[t0:t0 + 512, :].rearrange("(tt p) d -> p tt d", p=128), w=[xt])
        Bd.dma("sync", mpv[:], mp[:, t0:t0 + 512].rearrange("(c p) t -> p c t", p=128), w=[mpv])
        for tt in range(4):
            for hf in range(2):
                p = pP[hf]
                for c in range(10):
                    Bd.mm(p[:], mpv[:, c, tt * 128:(tt + 1) * 128], wo[:, c, hf * 512:(hf + 1) * 512],
                          c == 0, c == 9, [mpv, (wo, c)], [p])
                Bd.tt("vector", xt[:, tt, hf * 512:(hf + 1) * 512], xt[:, tt, hf * 512:(hf + 1) * 512],
                      p[:], ALU.add, [xt, p], [xt])
        o = Bd.dma("sync", out[t0:t0 + 512, :].rearrange("(tt p) d -> p tt d", p=128), xt[:], r=[xt], w=["out_d"])
        outs.append(o)
    Bd.S.emit(final_wait_ops=outs)
    return nc


_CACHE = {}


def _get_prog(key, fn):
    if key not in _CACHE:
        _CACHE[key] = fn()
    return _CACHE[key]


def kernel_unfused(**inputs):
    inp = {k: np.asarray(v) for k, v in inputs.items()}
    x = inp["x"]
    BATCH, SEQ, _ = x.shape
    n = 8
    cores = [(b, hh) for b in range(BATCH) for hh in range(2)]
    xcur = [np.ascontiguousarray(x[b]) for b in range(BATCH)]
    mprev = None
    for l in range(2):
        has_prev = l > 0
        nc = _get_prog(("layer", SEQ, has_prev), lambda: Builder(SEQ, has_prev).build())
        in_maps = []
        for (b, hh) in cores:
            d = host_layer_params(inp, l, hh)
            d["xin"] = xcur[b]
            d["mem"] = np.ascontiguousarray(inp["mem"][b])
            if has_prev:
                d["mprev"] = mprev[b]
                d["wout"] = np.ascontiguousarray(inp["w_out"][l - 1])
            in_maps.append(d)
        res = run_bass_kernel_spmd(nc, in_maps, core_ids=list(range(n)))
        new_m = []
        for b in range(BATCH):
            full = np.zeros((1280, SEQ), dtype=ml_dtypes.bfloat16)
            for hh in range(2):
                m = np.asarray(res.results[b * 2 + hh]["mixed"])
                for g in range(5):
                    full[g * 256 + hh * 128:g * 256 + hh * 128 + 128] = m[g * 128:(g + 1) * 128]
            new_m.append(full)
            if has_prev:
                xcur[b] = np.asarray(res.results[b * 2]["x1out"])
        mprev = new_m
    ncf = _get_prog(("final", SEQ), lambda: build_final(SEQ // 2))
    in_maps = []
    H = SEQ // 2
    for (b, hh) in cores:
        in_maps.append({"x1": np.ascontiguousarray(xcur[b][hh * H:(hh + 1) * H]),
                        "mprev": np.ascontiguousarray(mprev[b][:, hh * H:(hh + 1) * H]),
                        "wout": np.ascontiguousarray(inp["w_out"][1])})
    res = run_bass_kernel_spmd(ncf, in_maps, core_ids=list(range(n)))
    out = np.zeros((BATCH, SEQ, 1024), np.float32)
    for i, (b, hh) in enumerate(cores):
        out[b, hh * H:(hh + 1) * H] = np.asarray(res.results[i]["out"])
    return out


def kernel(**inputs):
    inp = {k: np.asarray(v) for k, v in inputs.items()}
    x = inp["x"]
    BATCH, SEQ, _ = x.shape
    nc = _get_prog(("fused", SEQ), lambda: Builder(SEQ, False).build_fused())
    per = {}
    for l in range(2):
        for hh in range(2):
            d = host_layer_params(inp, l, hh)
            for k, v in d.items():
                per["%s_%d%d" % (k, l, hh)] = v
    in_maps = []
    for b in range(BATCH):
        d = dict(per)
        d["xin"] = np.ascontiguousarray(x[b])
        d["mem"] = np.ascontiguousarray(inp["mem"][b])
        d["wout0"] = np.ascontiguousarray(inp["w_out"][0])
        d["wout1"] = np.ascontiguousarray(inp["w_out"][1])
        in_maps.append(d)
    res = run_bass_kernel_spmd(nc, in_maps, core_ids=list(range(BATCH)))
    out = np.stack([np.asarray(res.results[b]["out"]) for b in range(BATCH)], axis=0)
    return out.astype(np.float32)
```

```python
from contextlib import ExitStack
import numpy as np
import ml_dtypes
import concourse.bass as bass
import concourse.mybir as mybir
from concourse.bass_utils import run_bass_kernel_spmd

F32 = mybir.dt.float32
BF16 = mybir.dt.bfloat16
ALU = mybir.AluOpType
AF = mybir.ActivationFunctionType
AX = mybir.AxisListType

ENGINES = ("tensor", "vector", "scalar", "gpsimd", "sync")
SEM_CAP = 30000
EPS = 1e-6


class _Op:
    __slots__ = ("eng", "fn", "idx", "deps", "signal", "is_dma", "sem", "val", "pre_wait")

    def __init__(self, eng, fn, is_dma):
        self.eng = eng
        self.fn = fn
        self.is_dma = is_dma
        self.deps = []
        self.signal = False
        self.sem = None
        self.val = None
        self.pre_wait = None


class Tl:
    def __init__(self, name, t):
        self.name = name
        self.t = t

    def __getitem__(self, idx):
        return self.t[idx]


def _norm(rs):
    out = []
    for r in rs:
        if isinstance(r, tuple):
            a, k = r
        else:
            a, k = r, None
        if isinstance(a, Tl):
            a = a.name
        if a.startswith("scr"):
            k = None
        out.append((a, k))
    return out


class Sched:
    def __init__(self, nc, stack, n_dma_sems=16):
        self.nc = nc
        self.stack = stack
        self.ops = {e: [] for e in ENGINES}
        self.state = {}
        self.n_dma_sems = n_dma_sems

    def _entries(self, name, key):
        d = self.state.setdefault(name, {})
        if key is None:
            return list(d.values())
        res = []
        if key in d:
            res.append(d[key])
        if None in d:
            res.append(d[None])
        return res

    def add(self, eng, fn, reads=(), writes=(), dma=False):
        reads = _norm(reads)
        writes = _norm(writes)
        op = _Op(eng, fn, dma)
        deps = []
        for (name, key) in reads:
            for ent in self._entries(name, key):
                if ent[0] is not None:
                    deps.append(ent[0])
                if name[0] == "p" and name[1].isupper():
                    deps.extend(o_ for o_ in ent[1] if o_.eng != eng)
        for (name, key) in writes:
            for ent in self._entries(name, key):
                if ent[0] is not None:
                    deps.append(ent[0])
                deps.extend(ent[1])
        for (name, key) in reads:
            d = self.state.setdefault(name, {})
            if key is None:
                if not d:
                    d[None] = [None, []]
                for ent in d.values():
                    ent[1].append(op)
            else:
                if key not in d:
                    d[key] = [None, []]
                d[key][1].append(op)
        for (name, key) in writes:
            d = self.state.setdefault(name, {})
            if key is None:
                d.clear()
                d[None] = [op, []]
            else:
                d[key] = [op, []]
        op.idx = len(self.ops[eng])
        best = {}
        dl = []
        for dop in deps:
            if dop is op:
                continue
            if dop.is_dma:
                if dop not in dl:
                    dl.append(dop)
            else:
                if dop.eng == "tensor" and eng == "tensor" and not dma:
                    continue
                b = best.get(dop.eng)
                if b is None or dop.idx > b.idx:
                    best[dop.eng] = dop
        op.deps = dl + list(best.values())
        for dop in op.deps:
            dop.signal = True
        self.ops[eng].append(op)
        return op

    def emit(self, final_wait_ops=()):
        nc = self.nc
        for eng in ENGINES:
            cnt = 0
            sem = None
            for op in self.ops[eng]:
                if op.is_dma:
                    continue
                if op.signal:
                    if sem is None or cnt >= SEM_CAP:
                        sem = self.stack.enter_context(nc.semaphore(f"s_{eng}_{op.idx}"))
                        cnt = 0
                    cnt += 1
                    op.sem = sem
                    op.val = cnt
        for eng in ENGINES:
            qpool = []
            k = 0
            for op in self.ops[eng]:
                if not op.is_dma:
                    continue
                if len(qpool) < self.n_dma_sems:
                    s = self.stack.enter_context(nc.semaphore(f"d_{eng}_{len(qpool)}"))
                    qpool.append([s, 0])
                    ent = qpool[-1]
                else:
                    ent = qpool[k % self.n_dma_sems]
                    if ent[1] + 16 > SEM_CAP:
                        ent[0] = self.stack.enter_context(nc.semaphore(f"d_{eng}_x{k}"))
                        ent[1] = 0
                if ent[1] > 0:
                    op.pre_wait = (ent[0], ent[1])
                ent[1] += 16
                op.sem = ent[0]
                op.val = ent[1]
                k += 1
        sched = self

        def run(eng_name, e):
            seen = {}
            for op in sched.ops[eng_name]:
                waits = []
                if op.pre_wait is not None:
                    waits.append(op.pre_wait)
                for dop in op.deps:
                    waits.append((dop.sem, dop.val))
                for (s, v) in waits:
                    key = id(s)
                    if seen.get(key, 0) >= v:
                        continue
                    seen[key] = v
                    e.wait_ge(s, v)
                ins = op.fn(e)
                if op.is_dma:
                    ins.then_inc(op.sem, 16)
                elif op.signal:
                    ins.then_inc(op.sem, 1)
            if eng_name == "sync":
                for fop in final_wait_ops:
                    e.wait_ge(fop.sem, fop.val)

        with nc.Block() as block:
            @block.sync
            def _(e):
                run("sync", e)

            @block.tensor
            def _(e):
                run("tensor", e)

            @block.vector
            def _(e):
                run("vector", e)

            @block.scalar
            def _(e):
                run("scalar", e)

            @block.gpsimd
            def _(e):
                run("gpsimd", e)


D_MODEL = 1024
FM_GROUPS = ["Au", "Az", "Bz", "Cq", "Ck", "Co", "Cz", "Dr", "Dk", "Dv", "Dz", "Mz",
             "Bf", "Ci", "Cf0", "Cf1", "Dw", "Da"]
FM_W = {g: 128 for g in FM_GROUPS}
FM_W["Dw"] = 16
FM_W["Da"] = 16
FM_OFF = {}
_o = 0
for _g in FM_GROUPS:
    FM_OFF[_g] = _o
    _o += FM_W[_g]
NFM = _o
TM_GROUPS = ["Av", "Bq", "Bk", "Bv", "Cv", "Mq"]
NTM = 768
NW = NFM + NTM
PC = {n: i for i, n in enumerate([
    "cq0", "cq1", "cq2", "cq3", "ck0", "ck1", "ck2", "ck3", "cqb", "ckb",
    "mu_r", "mu_k", "mu_v", "mu_z", "mu_w", "mu_a",
    "k_k", "k_a", "a0", "w0", "r_k", "ln_g", "out_g",
    "fb_B", "ib_C", "fb_C0", "fb_C1"])}
NPC = len(PC)
PR_SGU_G, PR_BQG, PR_BKG, PR_MQG, PR_MKG, PR_SGUB = 0, 128, 256, 384, 512, 640
NPR = 768

A_OFF = 0
B_OFF = 768
C_OFF = 768 + 1028
D_OFF = C_OFF + 1288
M_OFF = D_OFF + 1056


def host_layer_params(inp, l, hh):
    f32 = np.float32
    w_in = inp["w_in"][l]
    hs = [2 * hh, 2 * hh + 1]

    def hcols(base):
        return np.concatenate([np.arange(base + h * 64, base + h * 64 + 64) for h in hs])

    cols = {}
    cols["Au"] = hcols(A_OFF)
    cols["Av"] = hcols(A_OFF + 256)
    cols["Az"] = hcols(A_OFF + 512)
    cols["Bq"] = hcols(B_OFF)
    cols["Bk"] = hcols(B_OFF + 256)
    cols["Bv"] = hcols(B_OFF + 512)
    bf = B_OFF + 768
    cols["Bf"] = np.concatenate([np.full(64, bf + hs[1]), np.full(64, bf + hs[0])])
    cols["Bz"] = hcols(B_OFF + 772)
    cols["Cq"] = hcols(C_OFF)
    cols["Ck"] = hcols(C_OFF + 256)
    cols["Cv"] = hcols(C_OFF + 512)
    ci = C_OFF + 768
    cols["Ci"] = np.concatenate([np.full(64, ci + hs[0]), np.full(64, ci + hs[1])])
    cols["Cf0"] = np.full(128, ci + 4 + hs[0])
    cols["Cf1"] = np.full(128, ci + 4 + hs[1])
    cols["Co"] = hcols(C_OFF + 776)
    cols["Cz"] = hcols(C_OFF + 1032)
    cols["Dr"] = hcols(D_OFF)
    cols["Dw"] = np.arange(D_OFF + 256, D_OFF + 272)
    cols["Dk"] = hcols(D_OFF + 272)
    cols["Dv"] = hcols(D_OFF + 528)
    cols["Da"] = np.arange(D_OFF + 784, D_OFF + 800)
    cols["Dz"] = hcols(D_OFF + 800)
    cols["Mq"] = hcols(M_OFF)
    cols["Mz"] = hcols(M_OFF + 256)
    allc = np.concatenate([cols[g] for g in FM_GROUPS] + [cols[g] for g in TM_GROUPS])
    wcat = np.ascontiguousarray(w_in[:, allc])

    hc = hcols(0)
    pc = np.zeros((128, NPC), f32)
    cw = inp["mlstm_conv_w"][l]
    cb = inp["mlstm_conv_b"][l]
    for j in range(4):
        pc[:, PC["cq%d" % j]] = cw[j, hc]
        pc[:, PC["ck%d" % j]] = cw[j, 256 + hc]
    pc[:, PC["cqb"]] = cb[hc]
    pc[:, PC["ckb"]] = cb[256 + hc]
    mu = inp["rwkv_mu"][l]
    pc[:, PC["mu_r"]] = mu[hc]
    pc[:16, PC["mu_w"]] = mu[256:272]
    pc[:, PC["mu_k"]] = mu[272 + hc]
    pc[:, PC["mu_v"]] = mu[528 + hc]
    pc[:16, PC["mu_a"]] = mu[784:800]
    pc[:, PC["mu_z"]] = mu[800 + hc]
    pc[:, PC["k_k"]] = inp["rwkv_k_k"][l][hc]
    pc[:, PC["k_a"]] = inp["rwkv_k_a"][l][hc]
    pc[:, PC["a0"]] = inp["rwkv_a0"][l][hc]
    pc[:, PC["w0"]] = inp["rwkv_w0"][l][hc]
    pc[:, PC["r_k"]] = inp["rwkv_r_k"][l].reshape(-1)[hc]
    pc[:, PC["ln_g"]] = inp["rwkv_ln_g"][l][hc]
    pc[:, PC["out_g"]] = inp["mlstm_out_g"][l][hc]
    fb = inp["fox_f_b"][l]
    pc[:, PC["fb_B"]] = np.concatenate([np.full(64, fb[hs[1]]), np.full(64, fb[hs[0]])])
    ib = inp["mlstm_i_b"][l]
    pc[:, PC["ib_C"]] = np.concatenate([np.full(64, ib[hs[0]]), np.full(64, ib[hs[1]])])
    fbc = inp["mlstm_f_b"][l]
    pc[:, PC["fb_C0"]] = fbc[hs[0]]
    pc[:, PC["fb_C1"]] = fbc[hs[1]]

    pr = np.zeros((128, NPR), f32)
    pr[:, PR_SGU_G:PR_SGU_G + 128] = inp["sgu_norm_g"][l][hc][None, :]
    pr[:, PR_BQG:PR_BQG + 128] = np.tile(inp["fox_q_g"][l], 2)[None, :]
    pr[:, PR_BKG:PR_BKG + 128] = np.tile(inp["fox_k_g"][l], 2)[None, :]
    pr[:, PR_MQG:PR_MQG + 128] = np.tile(inp["mem_q_g"][l], 2)[None, :]
    pr[:, PR_MKG:PR_MKG + 128] = np.tile(inp["mem_k_g"][l], 2)[None, :]
    sb_ = inp["sgu_b"][l]
    pr[:64, PR_SGUB:PR_SGUB + 128] = sb_[hs[0]][None, :]
    pr[64:, PR_SGUB:PR_SGUB + 128] = sb_[hs[1]][None, :]

    d = {
        "wcat": wcat,
        "pc": pc,
        "pr": pr,
        "ng": np.ascontiguousarray(inp["norm_g"][l].reshape(8, 128).T),
        "memg": np.ascontiguousarray(inp["mem_norm_g"][l].reshape(8, 128).T),
        "wkv": np.ascontiguousarray(np.concatenate(
            [inp["mem_w_kv"][l][:, hc], inp["mem_w_kv"][l][:, 256 + hc]], axis=1)),
        "w2": np.ascontiguousarray(inp["rwkv_w2"][l][:, hc]),
        "a2": np.ascontiguousarray(inp["rwkv_a2"][l][:, hc]),
        "sguw": np.ascontiguousarray(inp["sgu_w"][l][hs]),
    }
    return d


class Builder:
    def __init__(self, SEQ, has_prev, branches="ABCDM"):
        self.SEQ = SEQ
        self.has_prev = has_prev
        self.branches = branches
        self.nc = bass.Bass("TRN2", target_bir_lowering=False)
        self.st = ExitStack()
        self.S = Sched(self.nc, self.st)
        self.ps_rr = 0

    def dram(self, name, shape, dt, kind):
        return self.nc.dram_tensor(name, shape, dt, kind=kind).ap()

    def sb(self, name, shape, dt=F32):
        if not hasattr(self, "_tiles"):
            self._tiles = {}
        if name not in self._tiles:
            self._tiles[name] = Tl(name, self.st.enter_context(self.nc.sbuf_tensor(name, shape, dt)))
        return self._tiles[name]

    def psum(self, name, shape, dt=F32):
        if not hasattr(self, "_tiles"):
            self._tiles = {}
        if name not in self._tiles:
            self._tiles[name] = Tl(name, self.st.enter_context(self.nc.psum_tensor(name, shape, dt)))
        return self._tiles[name]

    def gps(self):
        p = self.gp[self.ps_rr % len(self.gp)]
        self.ps_rr += 1
        return p

    def op(self, eng, fn, r=(), w=()):
        return self.S.add(eng, fn, reads=r, writes=w)

    def dma(self, eng, out, in_, r=(), w=()):
        return self.S.add(eng, lambda e: e.dma_start(out=out, in_=in_), reads=r, writes=w, dma=True)

    def mm(self, out, lhsT, rhs, start, stop, r, w):
        return self.S.add("tensor", lambda e: e.matmul(out, lhsT=lhsT, rhs=rhs, start=start, stop=stop),
                          reads=r, writes=w)

    def tr(self, out, in_, ident, r, w):
        return self.S.add("tensor", lambda e: e.transpose(out, in_, ident), reads=r, writes=w)

    def act(self, out, in_, func, r, w, bias=None, scale=None, accum_out=None, eng="scalar"):
        kw = {}
        if bias is not None:
            kw["bias"] = bias
        if scale is not None:
            kw["scale"] = scale
        if accum_out is not None:
            kw["accum_out"] = accum_out
        return self.S.add("scalar", lambda e: e.activation(out=out, in_=in_, func=func, **kw), reads=r, writes=w)

    def tt(self, eng, out, in0, in1, op, r, w):
        return self.S.add(eng, lambda e: e.tensor_tensor(out=out, in0=in0, in1=in1, op=op), reads=r, writes=w)

    def ts(self, eng, out, in0, s1, op0, r, w, s2=None, op1=None):
        if op1 is None:
            return self.S.add(eng, lambda e: e.tensor_scalar(out=out, in0=in0, scalar1=s1, scalar2=None, op0=op0),
                              reads=r, writes=w)
        return self.S.add(eng, lambda e: e.tensor_scalar(out=out, in0=in0, scalar1=s1, scalar2=s2, op0=op0, op1=op1),
                          reads=r, writes=w)

    def stt(self, out, in0, scalar, in1, op0, op1, r, w):
        return self.S.add("vector", lambda e: e.scalar_tensor_tensor(out=out, in0=in0, scalar=scalar, in1=in1,
                                                                      op0=op0, op1=op1), reads=r, writes=w)

    def cp(self, eng, out, in_, r, w):
        if eng == "scalar":
            return self.S.add("scalar", lambda e: e.copy(out=out, in_=in_), reads=r, writes=w)
        return self.S.add(eng, lambda e: e.tensor_copy(out, in_), reads=r, writes=w)

    def recip(self, out, in_, r, w):
        return self.S.add("vector", lambda e: e.reciprocal(out, in_), reads=r, writes=w)

    def memset(self, eng, ap, val, w):
        return self.S.add(eng, lambda e: e.memset(ap, val), writes=w)

    def asel(self, out, in_, cmp, w, r=(), fill=0.0, base=0, cm=-1, pat=None):
        pat = pat or [[1, 128]]
        return self.S.add("gpsimd", lambda e: e.affine_select(out=out, in_=in_, pattern=pat, compare_op=cmp,
                                                              fill=fill, base=base, channel_multiplier=cm),
                          reads=r, writes=w)

    def build(self):
        cfg = dict(tag="", xmode="outproj" if self.has_prev else "ext")
        self.last_out = []
        self.run_pass(cfg)
        self.S.emit(final_wait_ops=self.last_out)
        return self.nc

    def build_fused(self):
        SEQ = self.SEQ
        self.last_out = []
        self.x_ext = self.dram("xin", [SEQ, 1024], F32, "ExternalInput")
        self.mem_ext = self.dram("mem", [256, 1024], F32, "ExternalInput")
        self.mixs = [self.dram("mixs%d" % l, [1280, SEQ], BF16, "Internal") for l in range(2)]
        self.x1s = self.dram("x1s", [SEQ, 1024], F32, "Internal")
        self.wouts = [self.dram("wout%d" % l, [1280, 1024], F32, "ExternalInput") for l in range(2)]
        for l in range(2):
            if l == 1:
                self.begin("oproj")
                self.outproj_pass("ext", 0, self.x1s, "x1s", False)
            for hh in range(2):
                xmode = "ext" if l == 0 else "x1"
                self.run_pass(dict(tag="_%d%d" % (l, hh), xmode=xmode, fused=True, l=l, hh=hh))
        out_d = self.dram("out", [SEQ, 1024], F32, "ExternalOutput")
        self.begin("oproj")
        self.outproj_pass("x1", 1, out_d, "out_d", True)
        self.S.emit(final_wait_ops=self.last_out)
        return self.nc

    def outproj_pass(self, src_kind, l, dst, dst_name, final):
        SEQ = self.SEQ
        pP = [self.psum("pP0", [128, 512]), self.psum("pP1", [128, 512])]
        wb = self.sb("wb", [128, 8, NW], BF16)
        wo = Tl("wb", wb.t[:, :, :].rearrange("p a b -> p (a b)")[:, 0:10240].rearrange("p (c n) -> p c n", c=10))
        wov = self.wouts[l].rearrange("(c p) n -> p c n", p=128)
        for c in range(10):
            self.dma("gpsimd", wo[:, c, :], wov[:, c, :], w=[wo])
        xts = [self.sb("xt%d" % i, [128, 1024]) for i in range(2)]
        tm_ = self.sb("tm", [128, 4, 768], BF16)
        flat = tm_.t[:, :, :].rearrange("p a b -> p (a b)")
        mpvs = [Tl("tm", flat[:, 0:1280].rearrange("p (c t) -> p c t", c=10)),
                Tl("tm", flat[:, 1280:2560].rearrange("p (c t) -> p c t", c=10))]
        for ti in range(SEQ // 128):
            xt = xts[ti % 2]
            mpv = mpvs[ti % 2]
            r0 = ti * 128
            if src_kind == "ext":
                self.dma("sync", xt[:], self.x_ext[r0:r0 + 128, :], w=[xt])
            else:
                self.dma("sync", xt[:], self.x1s[r0:r0 + 128, :], r=[("x1s", ti)], w=[xt])
            self.dma("sync", mpv[:], self.mixs[l][:, r0:r0 + 128].rearrange("(c p) t -> p c t", p=128),
                     r=[("mixs%d" % l, (0, ti // 4)), ("mixs%d" % l, (1, ti // 4))], w=[mpv])
            for hf in range(2):
                p = pP[hf]
                for c in range(10):
                    self.mm(p[:], mpv[:, c, :], wo[:, c, hf * 512:(hf + 1) * 512], c == 0, c == 9,
                            [mpv, wo], [p])
                eng = "vector" if hf == 0 else "gpsimd"
                if eng == "gpsimd":
                    tmpo = self.scr("op_tmp", [128, 512])
                    self.cp("scalar", tmpo[:], p[:], [p], [tmpo])
                    self.tt("gpsimd", xt[:, hf * 512:(hf + 1) * 512], xt[:, hf * 512:(hf + 1) * 512], tmpo[:],
                            ALU.add, [xt, tmpo], [xt])
                else:
                    self.tt("vector", xt[:, hf * 512:(hf + 1) * 512], xt[:, hf * 512:(hf + 1) * 512], p[:],
                            ALU.add, [xt, p], [xt])
            o = self.dma("sync", dst[r0:r0 + 128, :], xt[:], r=[xt], w=[(dst_name, ti)])
            if final:
                self.last_out.append(o)

    def run_pass(self, cfg):
        nc = self.nc
        SEQ = self.SEQ
        NMC = SEQ // 512
        NCH = SEQ // 128
        tag = cfg["tag"]
        xmode = cfg["xmode"]
        fused = cfg.get("fused", False)
        has_prev = xmode == "outproj"
        wcat = self.dram("wcat" + tag, [1024, NW], F32, "ExternalInput")
        pc_d = self.dram("pc" + tag, [128, NPC], F32, "ExternalInput")
        pr_d = self.dram("pr" + tag, [128, NPR], F32, "ExternalInput")
        ng_d = self.dram("ng" + tag, [128, 8], F32, "ExternalInput")
        memg_d = self.dram("memg" + tag, [128, 8], F32, "ExternalInput")
        wkv_d = self.dram("wkv" + tag, [1024, 256], F32, "ExternalInput")
        w2_d = self.dram("w2" + tag, [16, 128], F32, "ExternalInput")
        a2_d = self.dram("a2" + tag, [16, 128], F32, "ExternalInput")
        sguw_d = self.dram("sguw" + tag, [2, 128, 128], F32, "ExternalInput")
        if fused:
            l, hh = cfg["l"], cfg["hh"]
            xin = self.x_ext
            mem_d = self.mem_ext
            mixed_d = self.mixs[l].rearrange("(g two p) t -> two p g t", two=2, p=128)[hh]
            mix_name = "mixs%d" % l
            mix_key = lambda mc: (hh, mc)
            if has_prev:
                mprev_d = self.mixs[0]
                wout_d = self.wouts[0]
                x1_d = self.x1s
        else:
            xin = self.dram("xin", [SEQ, 1024], F32, "ExternalInput")
            mem_d = self.dram("mem", [256, 1024], F32, "ExternalInput")
            mixed_d = self.dram("mixed", [640, SEQ], BF16, "ExternalOutput").rearrange("(g p) t -> p g t", p=128)
            mix_name = "mixed_d"
            mix_key = lambda mc: mc
            if has_prev:
                mprev_d = self.dram("mprev", [1280, SEQ], BF16, "ExternalInput")
                wout_d = self.dram("wout", [1280, 1024], F32, "ExternalInput")
                x1_d = self.dram("x1out", [SEQ, 1024], F32, "ExternalOutput")

        pT = self.psum("pT", [128, 1024], BF16)
        pP = [self.psum("pP0", [128, 512]), self.psum("pP1", [128, 512])]
        self.gp = [self.psum("pG%d" % i, [128, 512]) for i in range(3)]
        self.pacc = pP
        pNum = self.psum("pNum", [128, 512])
        pDen = self.psum("pDen", [128, 512])

        identf = self.sb("identf", [128, 128])
        ident = self.sb("ident", [128, 128], BF16)
        ones_bf = self.sb("ones_bf", [128, 128], BF16)
        onesf = self.sb("onesf", [128, 128])
        c64f = self.sb("c64f", [128, 128])
        c64b = self.sb("c64b", [128, 128], BF16)
        blk = self.sb("blk", [128, 128], BF16)
        m_ge = self.sb("m_ge", [128, 128], BF16)
        self.memset("vector", identf[:], 1.0, [identf])
        self.asel(identf[:], identf[:], ALU.is_equal, [identf], [identf])
        self.cp("vector", ident[:], identf[:], [identf], [ident])
        self.memset("vector", ones_bf[:], 1.0, [ones_bf])
        self.memset("vector", onesf[:], 1.0, [onesf])
        self.memset("vector", c64f[:], 1.0 / 64, [c64f])
        self.memset("vector", c64b[:], 1.0 / 64, [c64b])
        self.memset("vector", blk[:], 0.0, [blk])
        self.memset("vector", blk[0:64, 0:64], 1.0, [blk])
        self.memset("vector", blk[64:128, 64:128], 1.0, [blk])
        self.memset("vector", m_ge[:], 1.0, [m_ge])
        self.asel(m_ge[:], m_ge[:], ALU.is_ge, [m_ge], [m_ge])

        pc = self.sb("pcs", [128, NPC])
        pr = self.sb("prs", [128, NPR])
        ng = self.sb("ngs", [128, 8])
        memg = self.sb("memgs", [128, 8])
        self.dma("sync", pc[:], pc_d, w=[pc])
        self.dma("sync", pr[:], pr_d, w=[pr])
        self.dma("sync", ng[:], ng_d, w=[ng])
        self.dma("sync", memg[:], memg_d, w=[memg])
        omm = self.sb("omm", [128, 6])
        self.ts("vector", omm[:], pc[:, PC["mu_r"]:PC["mu_r"] + 6], -1.0, ALU.mult, [pc], [omm], s2=1.0, op1=ALU.add)
        nfbB = self.sb("nfbB", [128, 1])
        self.ts("vector", nfbB[:], pc[:, PC["fb_B"]:PC["fb_B"] + 1], -1.0, ALU.mult, [pc], [nfbB])
        nfbC = self.sb("nfbC", [128, 2])
        self.ts("vector", nfbC[:], pc[:, PC["fb_C0"]:PC["fb_C0"] + 2], -1.0, ALU.mult, [pc], [nfbC])
        gq8 = self.sb("gq8", [128, 128])
        self.ts("vector", gq8[:], pr[:, PR_BQG:PR_BQG + 128], 0.125, ALU.mult, [pr], [gq8])
        gmq8 = self.sb("gmq8", [128, 128])
        self.ts("vector", gmq8[:], pr[:, PR_MQG:PR_MQG + 128], 0.125, ALU.mult, [pr], [gmq8])

        wb = self.sb("wb", [128, 8, NW], BF16)
        wv = wcat.rearrange("(c p) n -> p c n", p=128)
        for c in range(8):
            self.dma("gpsimd", wb[:, c, :], wv[:, c, :], w=[(wb, c)])
        for c in range(8):
            eng = "vector" if c % 2 == 0 else "gpsimd"
            self.ts(eng, wb[:, c, :], wb[:, c, :], ng[:, c:c + 1], ALU.mult, [(wb, c), ng], [(wb, c)])
        if has_prev:
            wo = self.sb("wo", [128, 10, 1024], BF16)
            wov = wout_d.rearrange("(c p) n -> p c n", p=128)
            for c in range(10):
                self.dma("gpsimd", wo[:, c, :], wov[:, c, :], w=[(wo, c)])
        w2s = self.sb("w2s", [16, 128])
        a2s = self.sb("a2s", [16, 128])
        self.dma("sync", w2s[:], w2_d, w=[w2s])
        self.dma("sync", a2s[:], a2_d, w=[a2s])

        wsT = self.sb("wsT", [128, 2, 128], BF16)
        self.begin("setupA")
        if "A" in self.branches:
            sgw = self.scr("sgw", [128, 2, 128])
            self.dma("sync", sgw[:], sguw_d.rearrange("h t s -> t h s"), w=[sgw])
            sgwT = self.scr("sgwT", [128, 2, 128])
            for h in range(2):
                p = self.gps()
                self.tr(p[:, 0:128], sgw[:, h, :], identf[:], [sgw, identf], [p])
                self.cp("vector", sgwT[:, h, :], p[:, 0:128], [p], [sgwT])
                self.asel(sgwT[:, h, :], sgwT[:, h, :], ALU.is_ge, [sgwT], [sgwT])
            self.cp("vector", wsT[:], sgwT[:], [sgwT], [wsT])

        kTm = self.sb("kTm", [128, 256], BF16)
        vm = self.sb("vm", [128, 2, 128], BF16)

        self.alloc_state(NCH)

        xts = [self.sb("xt0", [128, 1024]), self.sb("xt1", [128, 1024])]
        hb = self.sb("hb", [128, 1024], BF16)
        hT = self.sb("hT", [128, 8, 512], BF16)
        ss = self.sb("ss", [128, 2])
        GATED = {"Az": AF.Silu, "Bz": AF.Silu, "Cz": AF.Silu, "Mz": AF.Silu, "Co": AF.Sigmoid}
        fm = {}
        for g in FM_GROUPS:
            if g in ("Cq", "Ck"):
                fm[g] = self.sb("fm_" + g, [128, 4 + 512], BF16)
            elif g in ("Dr", "Dk", "Dv", "Dz"):
                fm[g] = self.sb("fm_" + g, [128, 2 + 512], BF16)
            elif g in ("Dw", "Da"):
                fm[g] = self.sb("fm_" + g, [16, 1 + 512])
            elif g in GATED:
                fm[g] = self.sb("fm_" + g, [128, 512], BF16)
            else:
                fm[g] = self.sb("fm_" + g, [128, 512])
        tm = self.sb("tm", [128, 4, 768], BF16)
        mixo = self.sb("mixo", [128, 5, 512], BF16)
        if "M" in self.branches:
            self.setup_mem(mem_d, wkv_d, memg, pr, ident, identf, pT, kTm, vm, xts, hT, tm, hb)
        mpvs = [Tl("tm", tm.t[:, :, :].rearrange("p a b -> p (a b)")[:, 0:1280].rearrange("p (c t) -> p c t", c=10))] * 2
        for g in ("Cq", "Ck"):
            self.memset("vector", fm[g][:, 0:3], 0.0, [(fm[g], "hist")])
        for g in ("Dr", "Dk", "Dv", "Dz"):
            self.memset("vector", fm[g][:, 0:1], 0.0, [(fm[g], "hist")])
        for g in ("Dw", "Da"):
            self.memset("vector", fm[g][:, 0:1], 0.0, [(fm[g], "hist")])

        self.consts = dict(ident=ident, identf=identf, ones_bf=ones_bf, onesf=onesf, c64f=c64f, c64b=c64b,
                           blk=blk, m_ge=m_ge, pc=pc, pr=pr, omm=omm, nfbB=nfbB, nfbC=nfbC,
                           gq8=gq8, gmq8=gmq8, wsT=wsT, kTm=kTm, vm=vm, w2s=w2s, a2s=a2s, pT=pT,
                           pNum=pNum, pDen=pDen)
        last_out = self.last_out
        xi = 0
        xstate = {"xi": 0}

        def xprep(mc):
            t0 = mc * 512
            for tt in range(4):
                xi = xstate["xi"]
                xt = xts[xi % 2]
                r0 = t0 + tt * 128
                ti = mc * 4 + tt
                if xmode == "x1":
                    self.dma("sync", xt[:], self.x1s[r0:r0 + 128, :], r=[("x1s", ti)], w=[xt])
                else:
                    self.dma("sync", xt[:], xin[r0:r0 + 128, :], w=[xt])
                if has_prev:
                    mpv = mpvs[xi % 2]
                    rr = [("mixs0", (0, mc)), ("mixs0", (1, mc))] if fused else []
                    self.dma("sync", mpv[:], mprev_d[:, r0:r0 + 128].rearrange("(c p) t -> p c t", p=128),
                             r=rr, w=[mpv])
                    for hf in range(2):
                        p = pP[hf]
                        for c in range(10):
                            self.mm(p[:], mpv[:, c, :], wo[:, c, hf * 512:(hf + 1) * 512],
                                    c == 0, c == 9, [mpv, (wo, c)], [p])
                        self.tt("vector", xt[:, hf * 512:(hf + 1) * 512], xt[:, hf * 512:(hf + 1) * 512],
                                p[:], ALU.add, [xt, p], [xt])
                    o = self.dma("sync", x1_d[r0:r0 + 128, :], xt[:], r=[xt], w=[("x1s", ti)])
                    if not fused:
                        last_out.append(o)
                xstate["xi"] = xi + 1
                self.act(hb[:], xt[:], AF.Square, [xt], [hb, ss], accum_out=ss[:, 0:1])
                self.act(ss[:, 1:2], ss[:, 0:1], AF.Ln, [ss], [ss], scale=1.0 / 1024, bias=EPS)
                self.act(ss[:, 1:2], ss[:, 1:2], AF.Exp, [ss], [ss], scale=-0.5)
                self.ts("vector", hb[:], xt[:], ss[:, 1:2], ALU.mult, [xt, ss], [hb])
                yield
                for c in range(8):
                    self.tr(pT[:, c * 128:(c + 1) * 128], hb[:, c * 128:(c + 1) * 128], ident[:],
                            [hb, ident], [pT])
                eng = "vector" if tt % 2 == 0 else "scalar"
                self.cp(eng, hT[:, :, tt * 128:(tt + 1) * 128],
                        pT[:, :].rearrange("p (c t) -> p c t", c=8), [pT], [hT])
                yield

        for _ in xprep(0):
            pass
        for mc in range(NMC):
            t0 = mc * 512
            for gi, g in enumerate(FM_GROUPS):
                p = pP[gi % 2]
                wdt = FM_W[g]
                for c in range(8):
                    self.mm(p[0:wdt, :], wb[:, c, FM_OFF[g]:FM_OFF[g] + wdt], hT[:, c, :], c == 0, c == 7,
                            [(wb, c), hT], [p])
                hist = {"Cq": 3, "Ck": 3, "Dr": 1, "Dk": 1, "Dv": 1, "Dz": 1, "Dw": 1, "Da": 1}.get(g, 0)
                dst = fm[g]
                if g in GATED:
                    self.act(dst[:, :], p[:, :], GATED[g], [p], [(dst, "cur")])
                    continue
                if hist and mc > 0:
                    self.cp("vector", dst[0:wdt, 0:hist], dst[0:wdt, 512:512 + hist], [(dst, "cur")], [(dst, "hist")])
                eng = "scalar" if gi % 2 == 0 else "vector"
                self.cp(eng, dst[0:wdt, hist:hist + 512], p[0:wdt, :], [p, (dst, "hist")], [(dst, "cur")])
            for tt in range(4):
                for hf in range(2):
                    p = pP[hf]
                    for c in range(8):
                        self.mm(p[:, 0:384], hT[:, c, tt * 128:(tt + 1) * 128],
                                wb[:, c, NFM + hf * 384:NFM + (hf + 1) * 384], c == 0, c == 7, [hT, (wb, c)], [p])
                    eng = "scalar" if hf == 0 else "vector"
                    self.cp(eng, tm[:, tt, hf * 384:(hf + 1) * 384], p[:, 0:384], [p], [(tm, tt)])
            self.zero_mix = []
            if "A" in self.branches:
                self.branch_A(mc, fm, tm, mixo)
            else:
                self.memset("gpsimd", mixo[:, 0, :], 0.0, [(mixo, 0)])
            if "B" in self.branches:
                self.branch_B(mc, fm, tm, mixo)
            else:
                self.memset("gpsimd", mixo[:, 1, :], 0.0, [(mixo, 1)])
            gens = []
            if mc + 1 < NMC:
                gens.append(("X", xprep(mc + 1)))
            if "M" in self.branches:
                gens.append(("M", self.branch_M(mc, fm, tm, mixo)))
            else:
                self.memset("gpsimd", mixo[:, 4, :], 0.0, [(mixo, 4)])
            if "D" in self.branches:
                gens.append(("D", self.branch_D(mc, fm, tm, mixo)))
            else:
                self.memset("gpsimd", mixo[:, 3, :], 0.0, [(mixo, 3)])
            if "C" in self.branches:
                gens.append(("C", self.branch_C(mc, fm, tm, mixo)))
            else:
                self.memset("gpsimd", mixo[:, 2, :], 0.0, [(mixo, 2)])
            while gens:
                for item in list(gens):
                    for rep in range({"D": 3, "C": 2}.get(item[0], 1)):
                        self._br = item[0]
                        try:
                            next(item[1])
                        except StopIteration:
                            gens.remove(item)
                            break
            o = self.dma("sync", mixed_d[:, :, t0:t0 + 512], mixo[:], r=[mixo], w=[(mix_name, mix_key(mc))])
            if not fused:
                last_out.append(o)

    def head_rms_tm(self, src, dst, gain, tag, nt=4):
        sq = self.scr("hr_sq", [128, 4, 128])
        ssq = self.scr("hr_ssq", [128, 8])
        src_ap, src_r = src
        dst_ap, dst_w = dst
        g_ap, g_r = gain
        self.tt("gpsimd", sq[:, 0:nt, :], src_ap, src_ap, ALU.mult, src_r, [sq])
        ssq3 = ssq[:, 0:2 * nt].rearrange("p (t h) -> p t h", h=2)
        self.op("vector", lambda e: e.tensor_reduce(out=ssq3,
                                                     in_=sq[:, 0:nt, :].rearrange("p t (h j) -> p t h j", h=2),
                                                     axis=AX.X, op=ALU.add), [sq], [ssq])
        self.act(ssq[:, 0:2 * nt], ssq[:, 0:2 * nt], AF.Ln, [ssq], [ssq], scale=1.0 / 64, bias=EPS)
        self.act(ssq[:, 0:2 * nt], ssq[:, 0:2 * nt], AF.Exp, [ssq], [ssq], scale=-0.5)
        self.tt("vector", sq[:, 0:nt, :].rearrange("p t (h j) -> p t h j", h=2),
                src_ap.rearrange("p t (h j) -> p t h j", h=2),
                ssq3[:, :, :, None].broadcast_to([128, nt, 2, 64]), ALU.mult, src_r + [ssq], [sq])
        self.tt("vector", dst_ap, sq[:, 0:nt, :], g_ap[:, None, :].broadcast_to([128, nt, 128]), ALU.mult,
                [sq] + g_r, dst_w)

    def begin(self, br):
        self._br = br
        if not hasattr(self, "_brcount"):
            self._brcount = {}
        self._brcount.setdefault(br, {})

    def scr(self, name, shape, dt=F32):
        if not hasattr(self, "_scrmap"):
            self._scrmap = {}
            self._pools = {}
        br = getattr(self, "_br", "x")
        key = (br, name)
        if key in self._scrmap:
            return self._scrmap[key]
        esz = 4 if dt == F32 else 2
        n = 1
        for d_ in shape[1:]:
            n *= d_
        nbytes = n * esz
        cls = 256
        while cls < nbytes:
            cls *= 2
        cnt = self._brcount.setdefault(br, {})
        k = cnt.get(cls, 0)
        cnt[cls] = k + 1
        fam = br if br in ("C", "M") else ""
        pool = self._pools.setdefault((fam, cls), [])
        if k >= len(pool):
            pname = "scr%s%d_%d" % (fam, cls, k)
            pool.append((pname, self.st.enter_context(self.nc.sbuf_tensor(pname, [128, cls // 4], F32))))
        pname, raw = pool[k]
        h = raw if dt == F32 else raw.bitcast(dt)
        ap = h[0:shape[0], 0:n]
        if len(shape) == 3:
            ap = ap.rearrange("p (a b) -> p a b", a=shape[1])
        elif len(shape) == 4:
            ap = ap.rearrange("p (a b c) -> p a b c", a=shape[1], b=shape[2])
        t = Tl(pname, ap)
        self._scrmap[key] = t
        return t

    def gelu(self, dst_ap, dst_w, src_ap, src_r, shape, tag):
        t1 = self.scr("gl_t1" + tag, shape)
        t2 = self.scr("gl_t2" + tag, shape)
        self.tt("gpsimd", t1[:], src_ap, src_ap, ALU.mult, src_r, [t1])
        self.ts("vector", t1[:], t1[:], 0.044715, ALU.mult, [t1], [t1], s2=1.0, op1=ALU.add)
        self.tt("gpsimd", t2[:], t1[:], src_ap, ALU.mult, [t1] + src_r, [t2])
        self.act(t2[:], t2[:], AF.Sigmoid, [t2], [t2], scale=1.5957691216)
        self.tt("vector", dst_ap, t2[:], src_ap, ALU.mult, [t2] + src_r, dst_w)

    def setup_mem(self, mem_d, wkv_d, memg, pr, ident, identf, pT, kTm, vm, xts, hT, tm, hb):
        self.begin("setup")
        for c in range(2):
            self.dma("sync", xts[c][:], mem_d[c * 128:(c + 1) * 128, :], w=[xts[c]])
        wk = Tl("hT", hT[:, :, 0:256])
        mhT = Tl("hT", hT[:, :, 256:512])
        mh = Tl("tm", tm.t[:, :, :].rearrange("p a b -> p (a b)")[:, 0:2048].rearrange("p (c d) -> p c d", c=2))
        wkv_v = wkv_d.rearrange("(c p) n -> p c n", p=128)
        for c in range(8):
            self.dma("gpsimd", wk[:, c, :], wkv_v[:, c, :], w=[wk])
        for c in range(8):
            self.ts("vector", wk[:, c, :], wk[:, c, :], memg[:, c:c + 1], ALU.mult, [wk, memg], [wk])
        mss = self.scr("m_ss", [128, 2])
        for c in range(2):
            self.act(hb[:], xts[c][:], AF.Square, [xts[c]], [hb, mss], accum_out=mss[:, c:c + 1])
        self.act(mss[:], mss[:], AF.Ln, [mss], [mss], scale=1.0 / 1024, bias=EPS)
        self.act(mss[:], mss[:], AF.Exp, [mss], [mss], scale=-0.5)
        for c in range(2):
            self.ts("vector", mh[:, c, :], xts[c][:], mss[:, c:c + 1], ALU.mult, [xts[c], mss], [mh])
        for c2 in range(2):
            for c in range(8):
                self.tr(pT[:, c * 128:(c + 1) * 128], mh[:, c2, c * 128:(c + 1) * 128], ident[:], [mh, ident], [pT])
            self.cp("vector", mhT[:, :, c2 * 128:(c2 + 1) * 128], pT[:, :].rearrange("p (c t) -> p c t", c=8),
                    [pT], [mhT])
        kvt = self.scr("m_kvt", [128, 2, 256])
        for c2 in range(2):
            p = self.gps()
            for c in range(8):
                self.mm(p[:, 0:256], mhT[:, c, c2 * 128:(c2 + 1) * 128], wk[:, c, :], c == 0, c == 7, [mhT, wk], [p])
            self.cp("vector", kvt[:, c2, :], p[:, 0:256], [p], [kvt])
        self.cp("vector", vm[:], kvt[:, :, 128:256], [kvt], [vm])
        kn = self.scr("m_kn", [128, 2, 128], BF16)
        self.head_rms_tm((kvt[:, :, 0:128], [kvt]), (kn[:], [kn]), (pr[:, PR_MKG:PR_MKG + 128], [pr]), "mk", nt=2)
        for c2 in range(2):
            self.tr(pT[:, c2 * 128:(c2 + 1) * 128], kn[:, c2, :], ident[:], [kn, ident], [pT])
        self.cp("vector", kTm[:], pT[:, 0:256], [pT], [kTm])

    def alloc_state(self, NCH):
        SEQ = self.SEQ
        if "B" in self.branches:
            self.KT = self.sb("KT", [128, SEQ], BF16)
            self.VB = self.sb("VB", [128, NCH, 128], BF16)
            self.Fcol = self.sb("Fcol", [128, 2, NCH])
            self.Fprev = self.sb("Fprev", [128, 1])
            self.memset("vector", self.Fprev[:], 0.0, [self.Fprev])
            self.c64h = [self.sb("c64h%d" % h, [128, 128], BF16) for h in range(2)]
            self.hm = self.sb("hm", [128, 2])
            self.memset("vector", self.hm[:], 0.0, [self.hm])
            for h in range(2):
                hs = slice(h * 64, (h + 1) * 64)
                fs = slice(64, 128) if h == 0 else slice(0, 64)
                self.memset("vector", self.c64h[h][:], 0.0, [self.c64h[h]])
                self.memset("vector", self.c64h[h][fs, :], 1.0 / 64, [self.c64h[h]])
                self.memset("vector", self.hm[hs, h:h + 1], 1.0, [self.hm])
        if "C" in self.branches:
            self.Cst = self.sb("Cst", [128, 128])
            self.Cstb = self.sb("Cstb", [128, 128], BF16)
            self.memset("vector", self.Cst[:], 0.0, [self.Cst])
            self.memset("vector", self.Cstb[:], 0.0, [self.Cstb])
        if "D" in self.branches:
            self.Sst = self.sb("Sst", [128, 64])
            self.Sstb = self.sb("Sstb", [128, 64], BF16)
            self.memset("vector", self.Sst[:], 0.0, [self.Sst])
            self.memset("vector", self.Sstb[:], 0.0, [self.Sstb])

    def branch_A(self, mc, fm, tm, mixo):
        self.begin("A")
        c = self.consts
        pr = c["pr"]
        gu = self.scr("A_gu", [128, 512])
        self.gelu(gu[:], [gu], fm["Au"][:, :], [(fm["Au"], "cur")], [128, 512], "u")
        gv = self.scr("A_gv", [128, 4, 128])
        self.gelu(gv[:], [gv], tm[:, :, 0:128], [tm], [128, 4, 128], "v")
        vn = self.scr("A_vn", [128, 4, 128], BF16)
        self.head_rms_tm((gv[:], [gv]), (vn[:], [vn]), (pr[:, PR_SGU_G:PR_SGU_G + 128], [pr]), "av")
        p = self.gps()
        for tt in range(4):
            for h in range(2):
                self.mm(p[h * 64:(h + 1) * 64, tt * 128:(tt + 1) * 128], vn[:, tt, h * 64:(h + 1) * 64],
                        c["wsT"][:, h, :], True, True, [vn, c["wsT"]], [p])
        ya = self.scr("A_ya", [128, 512])
        self.tt("vector", ya[:].rearrange("p (a t) -> p a t", a=4), p[:].rearrange("p (a t) -> p a t", a=4),
                pr[:, None, PR_SGUB:PR_SGUB + 128].broadcast_to([128, 4, 128]), ALU.add, [p, pr], [ya])
        self.tt("gpsimd", ya[:], ya[:], gu[:], ALU.mult, [ya, gu], [ya])
        self.tt("vector", mixo[:, 0, :], ya[:], fm["Az"][:, :], ALU.mult, [ya, fm["Az"]], [(mixo, 0)])

    def branch_M(self, mc, fm, tm, mixo):
        self.begin("M")
        c = self.consts
        pT = c["pT"]
        qn = self.scr("M_qn", [128, 4, 128], BF16)
        self.head_rms_tm((tm[:, :, 640:768], [tm]), (qn[:], [qn]), (c["gmq8"][:], [c["gmq8"]]), "mq")
        yield
        qT = self.scr("M_qT", [128, 512], BF16)
        for tt in range(4):
            self.tr(pT[:, tt * 128:(tt + 1) * 128], qn[:, tt, :], c["ident"][:], [qn, c["ident"]], [pT])
        self.cp("vector", qT[:], pT[:, 0:512], [pT], [qT])
        yield
        pn = self.pacc[0]
        pd = self.pacc[1]
        E = self.scr("M_E", [128, 512], BF16)
        for h in range(2):
            hs = slice(h * 64, (h + 1) * 64)
            for mcx in range(2):
                ps_ = self.gps()
                self.mm(ps_[:], c["kTm"][hs, mcx * 128:(mcx + 1) * 128], qT[hs, :], True, True, [c["kTm"], qT], [ps_])
                self.act(E[:], ps_[:], AF.Exp, [ps_], [E])
                yield
                self.mm(pn[hs, :], c["vm"][:, mcx, hs], E[:], mcx == 0, mcx == 1, [c["vm"], E], [(pn, h)])
                self.mm(pd[hs, :], c["ones_bf"][:, 0:64], E[:], mcx == 0, mcx == 1, [c["ones_bf"], E], [(pd, h)])
                yield
        rd = self.scr("M_rd", [128, 512])
        self.recip(rd[:], pd[:], [pd], [rd])
        self.tt("vector", rd[:], pn[:], rd[:], ALU.mult, [pn, rd], [rd])
        yield
        self.tt("gpsimd", mixo[:, 4, :], rd[:], fm["Mz"][:, :], ALU.mult, [rd, fm["Mz"]], [(mixo, 4)])

    def branch_B(self, mc, fm, tm, mixo):
        self.begin("B")
        c = self.consts
        pT = c["pT"]
        pr = c["pr"]
        G = mc
        if getattr(self, "bstage", 9) < 1:
            return
        qn = self.scr("B_qn", [128, 4, 128], BF16)
        kn = self.scr("B_kn", [128, 4, 128], BF16)
        self.head_rms_tm((tm[:, :, 128:256], [tm]), (qn[:], [qn]), (c["gq8"][:], [c["gq8"]]), "bq")
        self.head_rms_tm((tm[:, :, 256:384], [tm]), (kn[:], [kn]), (pr[:, PR_BKG:PR_BKG + 128], [pr]), "bk")
        QT = self.scr("B_QT", [128, 512], BF16)
        if getattr(self, "bstage", 9) < 0.3:
            return
        for tt in range(4):
            self.tr(pT[:, tt * 128:(tt + 1) * 128], qn[:, tt, :], c["ident"][:], [qn, c["ident"]], [pT])
        for tt in range(4):
            self.tr(pT[:, 512 + tt * 128:512 + (tt + 1) * 128], kn[:, tt, :], c["ident"][:], [kn, c["ident"]], [pT])
        self.cp("vector", QT[:], pT[:, 0:512], [pT], [QT])
        self.cp("scalar", self.KT[:, G * 512:(G + 1) * 512], pT[:, 512:1024], [pT], [(self.KT, G)])
        QTm = [self.scr("B_QTm%d" % h, [128, 512], BF16) for h in range(2)]
        for h in range(2):
            self.ts("gpsimd", QTm[h][:], QT[:], self.hm[:, h:h + 1], ALU.mult, [QT, self.hm], [QTm[h]])
        for tt in range(4):
            self.cp("gpsimd", self.VB[:, G * 4 + tt, :], tm[:, tt, 384:512], [tm], [(self.VB, G * 4 + tt)])
        e1 = self.scr("B_e1", [128, 512])
        self.act(e1[:], fm["Bf"][:, :], AF.Exp, [(fm["Bf"], "cur")], [e1], scale=-1.0, bias=c["nfbB"][:, 0:1])
        self.act(e1[:], e1[:], AF.Ln, [e1], [e1], bias=1.0)
        Lc = self.scr("B_Lc", [128, 512])
        for q4 in range(4):
            qs = slice(q4 * 128, (q4 + 1) * 128)
            ini = self.Fprev[:, 0:1] if q4 == 0 else Lc[:, q4 * 128 - 1:q4 * 128]
            self.op("vector", lambda e, qs=qs, ini=ini: e.tensor_tensor_scan(
                out=Lc[:, qs], data0=c["onesf"][:, 0:128], data1=e1[:, qs], initial=ini,
                op0=ALU.mult, op1=ALU.add), [c["onesf"], e1, self.Fprev, Lc], [Lc])
        pcg = self.gps()
        for h in range(2):
            fs = slice(64, 128) if h == 0 else slice(0, 64)
            for j in range(4):
                self.mm(pcg[:, h * 8 + j:h * 8 + j + 1], Lc[fs, j * 128:(j + 1) * 128], c["c64f"][fs, 0:1],
                        True, True, [Lc, c["c64f"]], [pcg])
            for hf in range(2):
                col = hf * 256 + 127
                self.mm(pcg[:, 16 + h * 2 + hf:17 + h * 2 + hf], c["c64f"][fs, :], Lc[fs, col:col + 1], True, True,
                        [c["c64f"], Lc], [pcg])
        cg = self.scr("B_cg", [128, 4])
        for h in range(2):
            self.cp("vector", self.Fcol[:, h, G * 4:G * 4 + 4], pcg[:, h * 8:h * 8 + 4], [pcg], [(self.Fcol, G)])
        self.cp("vector", cg[:], pcg[:, 16:20], [pcg], [cg])
        self.cp("vector", self.Fprev[:], Lc[:, 511:512], [Lc], [self.Fprev])
        nkb = 4 * G + 4
        bias = self.scr("B_bias", [128, 4, self.SEQ // 128])
        for h in range(2):
            for hf in range(2):
                self.ts("vector", bias[:, h * 2 + hf, 0:nkb], self.Fcol[:, h, 0:nkb],
                        cg[:, h * 2 + hf:h * 2 + hf + 1], ALU.subtract, [self.Fcol, cg], [bias])
        pOff, pDg = c["pNum"], c["pDen"]
        Es = [self.scr("B_E%d" % i, [128, 512], BF16) for i in range(4)]
        EaccO = self.scr("B_EaccO", [128, 512])
        EaccD = self.scr("B_EaccD", [128, 512])
        EaccP = self.scr("B_EaccP", [128, 512])
        ndg = self.scr("B_ndg", [128, 512])
        rd = self.scr("B_rd", [128, 512])
        yb = self.scr("B_yb", [128, 512])
        sc = self.scr("B_sc", [128, 2])
        for h in range(2):
            self.tt("vector", sc[:, h:h + 1], cg[:, h * 2:h * 2 + 1], cg[:, h * 2 + 1:h * 2 + 2], ALU.subtract,
                    [cg], [sc])
        self.act(sc[:], sc[:], AF.Exp, [sc], [sc])
        noff = 4 * G
        pairs = [(h, kb) for h in range(2) for kb in range(nkb)]
        psl = {}

        def score(i):
            h, kb = pairs[i]
            j = kb - 4 * G
            c0 = 0 if j <= 0 else j * 128
            ps_ = self.gps()
            self.mm(ps_[:, c0:512], self.KT[:, kb * 128:(kb + 1) * 128], QTm[h][:, c0:512], True, True,
                    [self.KT, QTm[h]], [ps_])
            psl[i] = ps_
        score(0)
        for i, (h, kb) in enumerate(pairs):
            if i + 1 < len(pairs):
                score(i + 1)
            hs = slice(h * 64, (h + 1) * 64)
            j = kb - 4 * G
            c0 = 0 if j <= 0 else j * 128
            ps_ = psl.pop(i)
            E = Es[i % 4]
            if j < 0:
                self.act(E[:, :], ps_[:, :], AF.Exp, [ps_, bias], [E], bias=bias[:, h * 2, kb:kb + 1])
                self.mm(pOff[:, :], self.VB[:, kb, :], E[:, :], kb == 0, kb == noff - 1, [self.VB, E], [pOff])
                if kb == 0:
                    self.cp("vector", EaccO[:], E[:], [E], [EaccO])
                elif kb == 1:
                    self.cp("gpsimd", EaccP[:], E[:], [E], [EaccP])
                elif kb % 2 == 0:
                    self.tt("vector", EaccO[:], EaccO[:], E[:], ALU.add, [EaccO, E], [EaccO])
                else:
                    self.tt("gpsimd", EaccP[:], EaccP[:], E[:], ALU.add, [EaccP, E], [EaccP])
            else:
                for hf in range(2):
                    a_, b_ = max(c0, hf * 256), (hf + 1) * 256
                    if a_ >= b_:
                        continue
                    self.act(E[:, a_:b_], ps_[:, a_:b_], AF.Exp, [ps_, bias], [E],
                             bias=bias[:, h * 2 + hf, kb:kb + 1])
                self.tt("gpsimd", E[:, c0:c0 + 128], E[:, c0:c0 + 128], c["m_ge"][:], ALU.mult,
                        [E, c["m_ge"]], [E])
                self.mm(pDg[:, c0:512], self.VB[:, kb, :], E[:, c0:512], j == 0, kb == nkb - 1,
                        [self.VB, E], [pDg])
                if j == 0:
                    self.cp("vector", EaccD[:], E[:], [E], [EaccD])
                else:
                    self.tt("vector", EaccD[:, c0:512], EaccD[:, c0:512], E[:, c0:512], ALU.add,
                            [EaccD, E], [EaccD])
            if kb == nkb - 1:
                if noff > 0:
                    self.tt("gpsimd", EaccO[:], EaccO[:], EaccP[:], ALU.add, [EaccO, EaccP], [EaccO])
                    self.tt("vector", EaccD[:, 0:256], EaccD[:, 0:256], EaccO[:, 0:256], ALU.add,
                            [EaccD, EaccO], [EaccD])
                    self.stt(EaccD[:, 256:512], EaccO[:, 256:512], sc[:, h:h + 1], EaccD[:, 256:512], ALU.mult,
                             ALU.add, [EaccO, sc, EaccD], [EaccD])
                pd_ = self.gps()
                self.mm(pd_[hs, :], c["onesf"][:, 0:64], EaccD[:], True, True, [c["onesf"], EaccD], [pd_])
                self.recip(rd[hs, :], pd_[hs, :], [pd_], [rd])
                self.cp("scalar", ndg[hs, :], pDg[hs, :], [pDg], [ndg])
                if noff > 0:
                    self.tt("vector", ndg[hs, 0:256], ndg[hs, 0:256], pOff[hs, 0:256], ALU.add, [ndg, pOff], [ndg])
                    self.stt(ndg[hs, 256:512], pOff[hs, 256:512], sc[hs, h:h + 1], ndg[hs, 256:512], ALU.mult,
                             ALU.add, [pOff, sc, ndg], [ndg])
                self.tt("vector", yb[hs, :], ndg[hs, :], rd[hs, :], ALU.mult, [ndg, rd], [yb])
        self.tt("gpsimd", mixo[:, 1, :], yb[:], fm["Bz"][:, :], ALU.mult, [yb, fm["Bz"]], [(mixo, 1)])

    def branch_C(self, mc, fm, tm, mixo):
        self.begin("C")
        c = self.consts
        pc = c["pc"]
        pT = c["pT"]
        qc = self.scr("C_qc", [128, 512], BF16)
        kc = self.scr("C_kc", [128, 512], BF16)
        for g, dst, wn, bn in (("Cq", qc, "cq", "cqb"), ("Ck", kc, "ck", "ckb")):
            x = fm[g]
            acc = self.scr("C_acc" + g, [128, 512])
            self.ts("vector", acc[:], x[:, 0:512], pc[:, PC[wn + "0"]:PC[wn + "0"] + 1], ALU.mult, [x, pc], [acc],
                    s2=pc[:, PC[bn]:PC[bn] + 1], op1=ALU.add)
            for j in range(1, 4):
                self.stt(acc[:], x[:, j:j + 512], pc[:, PC[wn + str(j)]:PC[wn + str(j)] + 1], acc[:], ALU.mult,
                         ALU.add, [x, pc, acc], [acc])
            if g == "Cq":
                self.act(dst[:], acc[:], AF.Silu, [acc], [dst])
            else:
                self.act(acc[:], acc[:], AF.Silu, [acc], [acc])
                self.ts("vector", dst[:], acc[:], 0.125, ALU.mult, [acc], [dst])
        yield
        Lb = []
        for h in range(2):
            e1 = fm["Cf%d" % h]
            self.act(e1[:], fm["Cf%d" % h][:, :], AF.Exp, [(fm["Cf%d" % h], "cur")], [e1], scale=-1.0,
                     bias=c["nfbC"][:, h:h + 1])
            self.act(e1[:], e1[:], AF.Ln, [e1], [e1], bias=1.0)
            L = self.scr("C_Lb%d" % h, [128, 512])
            for ch in range(4):
                cs = slice(ch * 128, (ch + 1) * 128)
                self.op("vector", lambda e, L=L, e1=e1, cs=cs: e.tensor_tensor_scan(
                    out=L[:, cs], data0=c["onesf"][:, 0:128], data1=e1[:, cs], initial=0.0,
                    op0=ALU.mult, op1=ALU.add), [c["onesf"], e1], [L])
            Lb.append(L)
        ilog = fm["Ci"]
        self.ts("vector", ilog[:], fm["Ci"][:, :], pc[:, PC["ib_C"]:PC["ib_C"] + 1], ALU.add,
                [(fm["Ci"], "cur"), pc], [ilog])
        yield
        so = fm["Co"]
        sz = fm["Cz"]
        if not hasattr(self, "vaug"):
            self.vaug = [self.sb("C_vaug%d" % h, [128, 128], BF16) for h in range(2)]
            for h in range(2):
                self.memset("vector", self.vaug[h][:], 1.0, [self.vaug[h]])
        vaug = self.vaug
        for ch in range(4):
            cs = slice(ch * 128, (ch + 1) * 128)
            pcol = self.gps()
            for h in range(2):
                hs = slice(h * 64, (h + 1) * 64)
                self.mm(pcol[:, h:h + 1], Lb[h][0:64, cs], c["c64f"][0:64, 0:1], True, True, [Lb[h], c["c64f"]], [pcol])
                self.mm(pcol[:, 2 + h:3 + h], ilog[hs, cs], c["c64f"][hs, 0:1], True, True, [ilog, c["c64f"]], [pcol])
            col = self.scr("C_col", [128, 8])
            self.cp("vector", col[:, 0:4], pcol[:, 0:4], [pcol], [col])
            yield
            self.tt("vector", col[:, 4:6], col[:, 0:2], col[:, 2:4], ALU.add, [col], [col])
            for h in range(2):
                self.ts("vector", col[:, 6 + h:7 + h], col[:, 4 + h:5 + h], Lb[h][:, ch * 128 + 127:ch * 128 + 128],
                        ALU.subtract, [col, Lb[h]], [col])
            wcol = self.scr("C_wcol", [128, 2])
            self.act(wcol[:], col[:, 6:8], AF.Exp, [col], [wcol])
            yield
            eg = self.scr("C_eg", [128, 2])
            for h in range(2):
                self.act(eg[:, h:h + 1], Lb[h][:, ch * 128 + 127:ch * 128 + 128], AF.Exp, [Lb[h]], [eg], scale=-1.0)
            ebq = self.scr("C_ebq", [128, 128])
            for h in range(2):
                hs = slice(h * 64, (h + 1) * 64)
                self.act(ebq[hs, :], Lb[h][hs, cs], AF.Exp, [Lb[h]], [ebq], scale=-1.0)
            qp = self.scr("C_qp", [128, 128], BF16)
            self.tt("vector", qp[:], qc[:, cs], ebq[:], ALU.mult, [qc, ebq], [qp])
            yield
            for h in range(2):
                hs = slice(h * 64, (h + 1) * 64)
                self.cp("gpsimd", vaug[h][:, 0:64], tm[:, ch, 512 + h * 64:512 + (h + 1) * 64], [tm], [vaug[h]])
            pS = self.gps()
            P = []
            for h in range(2):
                hs = slice(h * 64, (h + 1) * 64)
                self.mm(pS[:, h * 128:(h + 1) * 128], kc[hs, cs], qc[hs, cs], True, True, [kc, qc], [pS])
                D = self.scr("C_D%d" % h, [128, 128])
                self.ts("vector", D[:], Lb[h][:, cs], col[:, h:h + 1], ALU.subtract, [Lb[h], col], [D], s2=0.0,
                        op1=ALU.max)
                self.act(D[:], D[:], AF.Exp, [D, col], [D], scale=-1.0, bias=col[:, 2 + h:3 + h])
                self.tt("gpsimd", D[:], D[:], c["m_ge"][:], ALU.mult, [D, c["m_ge"]], [D])
                Ph = self.scr("C_P%d" % h, [128, 128], BF16)
                self.tt("vector", Ph[:], pS[:, h * 128:(h + 1) * 128], D[:], ALU.mult, [pS, D], [Ph])
                P.append(Ph)
            yield
            pnd = self.gps()
            for h in range(2):
                hs = slice(h * 64, (h + 1) * 64)
                self.mm(pnd[hs, 0:128], vaug[h][:, 0:64], P[h][:], True, False, [vaug[h], P[h]], [pnd])
                self.mm(pnd[hs, 0:128], self.Cstb[hs, 0:64], qp[hs, :], False, True, [self.Cstb, qp], [pnd])
            for h in range(2):
                hs = slice(h * 64, (h + 1) * 64)
                self.mm(pnd[hs, 128:256], c["ones_bf"][:, 0:64], P[h][:], True, False, [c["ones_bf"], P[h]], [pnd])
                self.mm(pnd[hs, 128:256], self.Cstb[hs, 64:128], qp[hs, :], False, True, [self.Cstb, qp], [pnd])
            dn = self.scr("C_dn", [128, 128])
            self.act(dn[:], pnd[:, 128:256], AF.Abs, [pnd], [dn])
            self.ts("vector", dn[:], dn[:], 1.0, ALU.max, [dn], [dn])
            self.recip(dn[:], dn[:], [dn], [dn])
            ho = self.scr("C_ho", [128, 128])
            self.tt("vector", ho[:], pnd[:, 0:128], dn[:], ALU.mult, [pnd, dn], [ho])
            yield
            self.tt("gpsimd", ho[:], ho[:], so[:, cs], ALU.mult, [ho, so], [ho])
            yield
            sq = self.scr("C_sq", [128, 128], BF16)
            self.tt("gpsimd", sq[:], ho[:], ho[:], ALU.mult, [ho], [sq])
            pq = self.gps()
            self.mm(pq[:, 0:128], c["blk"][:], sq[:], True, True, [c["blk"], sq], [pq])
            rs = self.scr("C_rs", [128, 128])
            self.act(rs[:], pq[:, 0:128], AF.Ln, [pq], [rs], scale=1.0 / 64, bias=EPS)
            self.act(rs[:], rs[:], AF.Exp, [rs], [rs], scale=-0.5)
            self.stt(ho[:], ho[:], pc[:, PC["out_g"]:PC["out_g"] + 1], rs[:], ALU.mult, ALU.mult, [ho, pc, rs], [ho])
            self.tt("vector", mixo[:, 2, cs], ho[:], sz[:, cs], ALU.mult, [ho, sz], [(mixo, 2)])
            yield
            self.tr(pT[:, 0:128], kc[:, cs], c["ident"][:], [kc, c["ident"]], [pT])
            khat = self.scr("C_khat", [128, 128], BF16)
            for h in range(2):
                hs = slice(h * 64, (h + 1) * 64)
                self.ts("vector", khat[:, hs], pT[:, h * 64:(h + 1) * 64], wcol[:, h:h + 1], ALU.mult, [pT, wcol],
                        [khat])
            yield
            pC = self.gps()
            for h in range(2):
                hs = slice(h * 64, (h + 1) * 64)
                self.mm(pC[hs, 0:128], khat[:, hs], vaug[h][:], True, True, [khat, vaug[h]], [pC])
            for h in range(2):
                hs = slice(h * 64, (h + 1) * 64)
                self.stt(self.Cst[hs, :], self.Cst[hs, :], eg[hs, h:h + 1], pC[hs, 0:128], ALU.mult, ALU.add,
                         [self.Cst, eg, pC], [self.Cst])
            self.cp("vector", self.Cstb[:], self.Cst[:], [self.Cst], [self.Cstb])
            yield

    def branch_D(self, mc, fm, tm, mixo):
        self.begin("D")
        c = self.consts
        pc = c["pc"]
        pT = c["pT"]
        omm = c["omm"]
        if not hasattr(self, "m_gt"):
            self.m_gt = self.sb("m_gt", [128, 128], BF16)
            self.mN_gt = self.sb("mN_gt", [128, 128], BF16)
            self.memset("vector", self.m_gt[:], 1.0, [self.m_gt])
            self.asel(self.m_gt[:], self.m_gt[:], ALU.is_gt, [self.m_gt], [self.m_gt])
            self.memset("vector", self.mN_gt[:], 1.0, [self.mN_gt])
            self.asel(self.mN_gt[:], self.mN_gt[:], ALU.is_gt, [self.mN_gt], [self.mN_gt], cm=1, pat=[[-1, 128]])
        m_gt, mN_gt, m_ge = self.m_gt, self.mN_gt, c["m_ge"]

        def shift(g, idx, rows, name):
            x = fm[g]
            t = self.scr("D_sh_" + name, [128, 512])
            self.ts("vector", t[0:rows, :], x[0:rows, 1:513], omm[0:rows, idx:idx + 1], ALU.mult, [x, omm], [t])
            self.stt(t[0:rows, :], x[0:rows, 0:512], pc[0:rows, PC["mu_r"] + idx:PC["mu_r"] + idx + 1], t[0:rows, :],
                     ALU.mult, ALU.add, [x, pc, t], [t])
            return t
        rs = shift("Dr", 0, 128, "r")
        ks = shift("Dk", 1, 128, "k")
        vs = shift("Dv", 2, 128, "v")
        zs = shift("Dz", 3, 128, "z")
        ws = shift("Dw", 4, 16, "w")
        as_ = shift("Da", 5, 16, "a")
        self.act(ws[0:16, :], ws[0:16, :], AF.Tanh, [ws], [ws])
        pw = self.gps()
        self.mm(pw[:], c["w2s"][:], ws[0:16, :], True, True, [c["w2s"], ws], [pw])
        lw = ws
        self.act(lw[:], pw[:], AF.Sigmoid, [pw, pc], [lw], bias=pc[:, PC["w0"]:PC["w0"] + 1])
        self.ts("gpsimd", lw[:], lw[:], -0.6065306597126334, ALU.mult, [lw], [lw])
        pa = self.gps()
        self.mm(pa[:], c["a2s"][:], as_[0:16, :], True, True, [c["a2s"], as_], [pa])
        aa = as_
        self.act(aa[:], pa[:], AF.Sigmoid, [pa, pc], [aa], bias=pc[:, PC["a0"]:PC["a0"] + 1])
        sz = zs
        self.act(sz[:], zs[:], AF.Silu, [zs], [sz])
        yield
        vb = self.scr("D_vb", [128, 512], BF16)
        self.cp("gpsimd", vb[:], vs[:], [vs], [vb])
        kx = self.scr("D_kx", [128, 512])
        self.ts("vector", kx[:], ks[:], pc[:, PC["k_k"]:PC["k_k"] + 1], ALU.mult, [ks, pc], [kx])
        sqb = self.scr("D_sqb", [128, 512], BF16)
        self.tt("gpsimd", sqb[:], kx[:], kx[:], ALU.mult, [kx], [sqb])
        pq = self.gps()
        self.mm(pq[:], c["blk"][:], sqb[:], True, True, [c["blk"], sqb], [pq])
        rn = self.scr("D_rn", [128, 512])
        self.ts("vector", rn[:], pq[:], 1e-18, ALU.max, [pq], [rn])
        self.act(rn[:], rn[:], AF.Ln, [rn], [rn])
        self.act(rn[:], rn[:], AF.Exp, [rn], [rn], scale=-0.5)
        kk = kx
        self.tt("vector", kk[:], kx[:], rn[:], ALU.mult, [kx, rn], [kk])
        k2 = rn
        self.ts("vector", k2[:], aa[:], -1.0, ALU.add, [aa, pc], [k2], s2=pc[:, PC["k_a"]:PC["k_a"] + 1], op1=ALU.mult)
        self.stt(k2[:], k2[:], 1.0, ks[:], ALU.add, ALU.mult, [k2, ks], [k2])
        bv = ks
        self.tt("gpsimd", bv[:], kk[:], aa[:], ALU.mult, [kk, aa], [bv])
        yield
        rk = self.scr("D_rk", [128, 512], BF16)
        self.stt(rk[:], rs[:], pc[:, PC["r_k"]:PC["r_k"] + 1], k2[:], ALU.mult, ALU.mult, [rs, pc, k2], [rk])
        pb = self.gps()
        self.mm(pb[:], c["blk"][:], rk[:], True, True, [c["blk"], rk], [pb])
        bon = vs
        self.tt("vector", bon[:], pb[:], vs[:], ALU.mult, [pb, vs], [bon])
        cl = self.scr("D_cl", [128, 512])
        for ch in range(4):
            cs = slice(ch * 128, (ch + 1) * 128)
            self.op("vector", lambda e, cs=cs: e.tensor_tensor_scan(
                out=cl[:, cs], data0=c["onesf"][:, 0:128], data1=lw[:, cs], initial=0.0,
                op0=ALU.mult, op1=ALU.add), [c["onesf"], lw], [cl])
        Ecl = self.scr("D_Ecl", [128, 512])
        Encl = self.scr("D_Encl", [128, 512])
        self.act(Ecl[:], cl[:], AF.Exp, [cl], [Ecl])
        self.act(Encl[:], cl[:], AF.Exp, [cl], [Encl], scale=-1.0)
        Ecx = lw
        self.tt("gpsimd", lw[:], cl[:], lw[:], ALU.subtract, [cl, lw], [lw])
        self.act(Ecx[:], lw[:], AF.Exp, [lw], [Ecx])
        yield
        clT = self.scr("D_clT", [128, 4])
        self.cp("vector", clT[:], cl[:].rearrange("p (a t) -> p a t", a=4)[:, :, 127], [cl], [clT])
        Eh = cl
        for ch in range(4):
            cs = slice(ch * 128, (ch + 1) * 128)
            self.act(Eh[:, cs], cl[:, cs], AF.Exp, [cl, clT], [Eh], scale=-1.0, bias=clT[:, ch:ch + 1])
        KR = self.scr("D_KR", [128, 4, 256], BF16)
        kt = self.scr("D_kt", [128, 512], BF16)
        bt = self.scr("D_bt", [128, 512], BF16)
        khat = self.scr("D_khat", [128, 512], BF16)
        nbh = self.scr("D_nbh", [128, 512], BF16)
        self.tt("vector", KR[:, :, 0:128], kk[:].rearrange("p (a t) -> p a t", a=4),
                Ecx[:].rearrange("p (a t) -> p a t", a=4), ALU.mult, [kk, Ecx], [KR])
        self.tt("gpsimd", KR[:, :, 128:256], rs[:].rearrange("p (a t) -> p a t", a=4),
                Ecl[:].rearrange("p (a t) -> p a t", a=4), ALU.mult, [rs, Ecl], [KR])
        self.tt("vector", kt[:], k2[:], Encl[:], ALU.mult, [k2, Encl], [kt])
        self.tt("gpsimd", bt[:], bv[:], Encl[:], ALU.mult, [bv, Encl], [bt])
        self.tt("vector", khat[:], k2[:], Eh[:], ALU.mult, [k2, Eh], [khat])
        self.stt(nbh[:], bv[:], -1.0, Eh[:], ALU.mult, ALU.mult, [bv, Eh], [nbh])
        yraw = kx
        yield
        for ch in range(4):
            cs = slice(ch * 128, (ch + 1) * 128)
            self.tr(pT[:, 0:128], KR[:, ch, 0:128], c["ident"][:], [KR, c["ident"]], [pT])
            self.tr(pT[:, 128:256], vb[:, cs], c["ident"][:], [vb, c["ident"]], [pT])
            self.tr(pT[:, 256:384], khat[:, cs], c["ident"][:], [khat, c["ident"]], [pT])
            self.tr(pT[:, 384:512], nbh[:, cs], c["ident"][:], [nbh, c["ident"]], [pT])
            TMs = self.scr("D_TMs", [128, 4, 128], BF16)
            self.cp("scalar", TMs[:], pT[:, 0:512].rearrange("p (a t) -> p a t", a=4), [pT], [TMs])
            yield
            rpT = self.scr("D_rpT", [128, 128], BF16)
            GT = self.scr("D_GT", [128, 64], BF16)
            Hs = self.scr("D_Hs", [128, 64])
            py = c["pNum"]
            pHS = c["pDen"]
            HS = [slice(0, 64), slice(64, 128)]
            LT, nQbT, LkT, QkT, LN, R, R2 = {}, {}, {}, {}, {}, {}, {}
            for h in range(2):
                hs = HS[h]
                p1 = self.gps()
                self.mm(p1[:, 0:256], bt[hs, cs], KR[hs, ch, :], True, True, [bt, KR], [p1])
                self.mm(p1[:, 256:512], kt[hs, cs], KR[hs, ch, :], True, True, [kt, KR], [p1])
                LT[h] = self.scr("D_LT%d" % h, [128, 128], BF16)
                nQbT[h] = self.scr("D_nQbT%d" % h, [128, 128], BF16)
                LkT[h] = self.scr("D_LkT%d" % h, [128, 128], BF16)
                QkT[h] = self.scr("D_QkT%d" % h, [128, 128], BF16)
                self.tt("vector", LT[h][:], p1[:, 0:128], m_gt[:], ALU.mult, [p1, m_gt], [LT[h]])
                self.tt("vector", LkT[h][:], p1[:, 256:384], m_gt[:], ALU.mult, [p1, m_gt], [LkT[h]])
                self.stt(nQbT[h][:], p1[:, 128:256], -1.0, m_ge[:], ALU.mult, ALU.mult, [p1, m_ge], [nQbT[h]])
                self.tt("vector", QkT[h][:], p1[:, 384:512], m_ge[:], ALU.mult, [p1, m_ge], [QkT[h]])
                yield
            for h in range(2):
                hs = HS[h]
                p2 = self.gps()
                self.mm(p2[:, 0:128], KR[hs, ch, 0:128], bt[hs, cs], True, True, [KR, bt], [p2])
                self.mm(p2[:, 128:192], LkT[h][:], TMs[:, 1, hs], True, True, [LkT[h], TMs], [p2])
                LN[h] = self.scr("D_LN%d" % h, [128, 128], BF16)
                self.tt("vector", LN[h][:], p2[:, 0:128], mN_gt[:], ALU.mult, [p2, mN_gt], [LN[h]])
                R[h] = self.scr("D_R%d" % h, [128, 128], BF16)
                self.cp("gpsimd", R[h][:, 0:64], TMs[:, 0, hs], [TMs], [R[h]])
                self.cp("scalar", R[h][:, 64:128], p2[:, 128:192], [p2], [R[h]])
                yield
            Rc, Rn, PTc, PNc = {}, {}, {}, {}
            for h in range(2):
                pr_ = self.gps()
                self.mm(pr_[:, 0:128], LT[h][:], R[h][:], True, True, [LT[h], R[h]], [pr_])
                R2[h] = self.scr("D_Rb%d" % h, [128, 128], BF16)
                self.tt("vector", R2[h][:], R[h][:], pr_[:, 0:128], ALU.subtract, [R[h], pr_], [R2[h]])
                Rc[h], Rn[h] = R2[h], R[h]
                PTc[h], PNc[h] = LT[h], LN[h]
            yield
            for j in range(1, 7):
                PTn, PNn = {}, {}
                for h in range(2):
                    PTn[h] = self.scr("D_PT%d_%d" % (h, j % 2), [128, 128], BF16)
                    PNn[h] = self.scr("D_PN%d_%d" % (h, j % 2), [128, 128], BF16)
                    pp = self.gps()
                    self.mm(pp[:, 0:128], PNc[h][:], PTc[h][:], True, True, [PNc[h], PTc[h]], [pp])
                    if j < 6:
                        self.mm(pp[:, 128:256], PTc[h][:], PNc[h][:], True, True, [PNc[h], PTc[h]], [pp])
                        self.cp("scalar", PTn[h][:], pp[:, 0:128], [pp], [PTn[h]])
                        self.cp("vector", PNn[h][:], pp[:, 128:256], [pp], [PNn[h]])
                    else:
                        self.cp("scalar", PTn[h][:], pp[:, 0:128], [pp], [PTn[h]])
                yield
                for h in range(2):
                    pr_ = self.gps()
                    self.mm(pr_[:, 0:128], PTn[h][:], Rc[h][:], True, True, [PTn[h], Rc[h]], [pr_])
                    self.tt("vector", Rn[h][:], Rc[h][:], pr_[:, 0:128], ALU.add, [Rc[h], pr_], [Rn[h]])
                    Rc[h], Rn[h] = Rn[h], Rc[h]
                    PTc[h], PNc[h] = PTn[h], PNn[h]
                yield
            for h in range(2):
                hs = HS[h]
                Rch = Rc[h]
                pr_ = self.gps()
                self.mm(pr_[hs, 0:128], Rch[:, 0:64], nQbT[h][:], True, True, [Rch, nQbT[h]], [pr_])
                self.tt("vector", rpT[hs, :], pr_[hs, 0:128], KR[hs, ch, 128:256], ALU.add, [pr_, KR], [rpT])
                self.mm(py[hs, 0:128], TMs[:, 1, hs], QkT[h][:], True, False, [TMs, QkT[h]], [(py, h)])
                self.mm(py[hs, 0:128], Rch[:, 64:128], nQbT[h][:], False, False, [Rch, nQbT[h]], [(py, h)])
                self.mm(py[hs, 0:128], self.Sstb[hs, :], rpT[hs, :], False, True, [self.Sstb, rpT], [(py, h)])
                pg = self.gps()
                self.mm(pg[hs, 0:64], Rch[:, 0:64], TMs[:, 3, hs], True, True, [Rch, TMs], [pg])
                self.stt(GT[hs, :], c["identf"][hs, h * 64:(h + 1) * 64], Ecl[hs, ch * 128 + 127:ch * 128 + 128],
                         pg[hs, 0:64], ALU.mult, ALU.add, [c["identf"], Ecl, pg], [GT])
                self.mm(pHS[hs, 0:64], TMs[:, 2, hs], TMs[:, 1, hs], True, False, [TMs], [(pHS, h)])
                self.mm(pHS[hs, 0:64], TMs[:, 3, hs], Rch[:, 64:128], False, True, [TMs, Rch], [(pHS, h)])
                self.cp("scalar", Hs[hs, :], pHS[hs, 0:64], [(pHS, h)], [Hs])
                self.mm(pHS[hs, 64:128], GT[hs, :], self.Sstb[hs, :], True, True, [GT, self.Sstb], [(pHS, h)])
                self.tt("vector", self.Sst[hs, :], pHS[hs, 64:128], Hs[hs, :], ALU.add, [(pHS, h), Hs], [self.Sst])
                yield
            self.cp("vector", self.Sstb[:], self.Sst[:], [self.Sst], [self.Sstb])
            self.cp("scalar", yraw[:, cs], py[:, 0:128], [py], [yraw])
        ysq = sqb
        self.tt("gpsimd", ysq[:], yraw[:], yraw[:], ALU.mult, [yraw], [ysq])
        pq2 = self.gps()
        self.mm(pq2[:], c["blk"][:], ysq[:], True, True, [c["blk"], ysq], [pq2])
        rs2 = rn
        self.act(rs2[:], pq2[:], AF.Ln, [pq2], [rs2], scale=1.0 / 64, bias=EPS)
        self.act(rs2[:], rs2[:], AF.Exp, [rs2], [rs2], scale=-0.5)
        self.stt(yraw[:], yraw[:], pc[:, PC["ln_g"]:PC["ln_g"] + 1], rs2[:], ALU.mult, ALU.mult, [yraw, pc, rs2],
                 [yraw])
        self.tt("vector", yraw[:], yraw[:], bon[:], ALU.add, [yraw, bon], [yraw])
        self.tt("gpsimd", mixo[:, 3, :], yraw[:], sz[:], ALU.mult, [yraw, sz], [(mixo, 3)])


def build_final(SEQ):
    Bd = Builder(SEQ, False)
    nc = Bd.nc
    x1 = Bd.dram("x1", [SEQ, 1024], F32, "ExternalInput")
    mp = Bd.dram("mprev", [1280, SEQ], BF16, "ExternalInput")
    wout_d = Bd.dram("wout", [1280, 1024], F32, "ExternalInput")
    out = Bd.dram("out", [SEQ, 1024], F32, "ExternalOutput")
    pP = [Bd.psum("pP0", [128, 512]), Bd.psum("pP1", [128, 512])]
    wo = Bd.sb("wo", [128, 10, 1024], BF16)
    wov = wout_d.rearrange("(c p) n -> p c n", p=128)
    for c in range(10):
        Bd.dma("gpsimd", wo[:, c, :], wov[:, c, :], w=[(wo, c)])
    xts = [Bd.sb("xt%d" % i, [128, 4, 1024]) for i in range(2)]
    mpvs = [Bd.sb("mpv%d" % i, [128, 10, 512], BF16) for i in range(2)]
    outs = []
    for mc in range(SEQ // 512):
        t0 = mc * 512
        xt = xts[mc % 2]
        mpv = mpvs[mc % 2]
        Bd.dma("sync", xt[:], x1[t0:t0 + 512, :].rearrange("(tt p) d -> p tt d", p=128), w=[xt])
        Bd.dma("sync", mpv[:], mp[:, t0:t0 + 512].rearrange("(c p) t -> p c t", p=128), w=[mpv])
        for tt in range(4):
            for hf in range(2):
                p = pP[hf]
                for c in range(10):
                    Bd.mm(p[:], mpv[:, c, tt * 128:(tt + 1) * 128], wo[:, c, hf * 512:(hf + 1) * 512],
                          c == 0, c == 9, [mpv, (wo, c)], [p])
                Bd.tt("vector", xt[:, tt, hf * 512:(hf + 1) * 512], xt[:, tt, hf * 512:(hf + 1) * 512],
                      p[:], ALU.add, [xt, p], [xt])
        o = Bd.dma("sync", out[t0:t0 + 512, :].rearrange("(tt p) d -> p tt d", p=128), xt[:], r=[xt], w=["out_d"])
        outs.append(o)
    Bd.S.emit(final_wait_ops=outs)
    return nc


_CACHE = {}


def _get_prog(key, fn):
    if key not in _CACHE:
        _CACHE[key] = fn()
    return _CACHE[key]


def kernel_unfused(**inputs):
    inp = {k: np.asarray(v) for k, v in inputs.items()}
    x = inp["x"]
    BATCH, SEQ, _ = x.shape
    n = 8
    cores = [(b, hh) for b in range(BATCH) for hh in range(2)]
    xcur = [np.ascontiguousarray(x[b]) for b in range(BATCH)]
    mprev = None
    for l in range(2):
        has_prev = l > 0
        nc = _get_prog(("layer", SEQ, has_prev), lambda: Builder(SEQ, has_prev).build())
        in_maps = []
        for (b, hh) in cores:
            d = host_layer_params(inp, l, hh)
            d["xin"] = xcur[b]
            d["mem"] = np.ascontiguousarray(inp["mem"][b])
            if has_prev:
                d["mprev"] = mprev[b]
                d["wout"] = np.ascontiguousarray(inp["w_out"][l - 1])
            in_maps.append(d)
        res = run_bass_kernel_spmd(nc, in_maps, core_ids=list(range(n)))
        new_m = []
        for b in range(BATCH):
            full = np.zeros((1280, SEQ), dtype=ml_dtypes.bfloat16)
            for hh in range(2):
                m = np.asarray(res.results[b * 2 + hh]["mixed"])
                for g in range(5):
                    full[g * 256 + hh * 128:g * 256 + hh * 128 + 128] = m[g * 128:(g + 1) * 128]
            new_m.append(full)
            if has_prev:
                xcur[b] = np.asarray(res.results[b * 2]["x1out"])
        mprev = new_m
    ncf = _get_prog(("final", SEQ), lambda: build_final(SEQ // 2))
    in_maps = []
    H = SEQ // 2
    for (b, hh) in cores:
        in_maps.append({"x1": np.ascontiguousarray(xcur[b][hh * H:(hh + 1) * H]),
                        "mprev": np.ascontiguousarray(mprev[b][:, hh * H:(hh + 1) * H]),
                        "wout": np.ascontiguousarray(inp["w_out"][1])})
    res = run_bass_kernel_spmd(ncf, in_maps, core_ids=list(range(n)))
    out = np.zeros((BATCH, SEQ, 1024), np.float32)
    for i, (b, hh) in enumerate(cores):
        out[b, hh * H:(hh + 1) * H] = np.asarray(res.results[i]["out"])
    return out


def kernel(**inputs):
    inp = {k: np.asarray(v) for k, v in inputs.items()}
    x = inp["x"]
    BATCH, SEQ, _ = x.shape
    nc = _get_prog(("fused", SEQ), lambda: Builder(SEQ, False).build_fused())
    per = {}
    for l in range(2):
        for hh in range(2):
            d = host_layer_params(inp, l, hh)
            for k, v in d.items():
                per["%s_%d%d" % (k, l, hh)] = v
    in_maps = []
    for b in range(BATCH):
        d = dict(per)
        d["xin"] = np.ascontiguousarray(x[b])
        d["mem"] = np.ascontiguousarray(inp["mem"][b])
        d["wout0"] = np.ascontiguousarray(inp["w_out"][0])
        d["wout1"] = np.ascontiguousarray(inp["w_out"][1])
        in_maps.append(d)
    res = run_bass_kernel_spmd(nc, in_maps, core_ids=list(range(BATCH)))
    out = np.stack([np.asarray(res.results[b]["out"]) for b in range(BATCH)], axis=0)
    return out.astype(np.float32)
```

```python
from contextlib import ExitStack
import numpy as np
import ml_dtypes
import concourse.bass as bass
import concourse.mybir as mybir
from concourse.bass_utils import run_bass_kernel_spmd

F32 = mybir.dt.float32
BF16 = mybir.dt.bfloat16
ALU = mybir.AluOpType
AF = mybir.ActivationFunctionType
AX = mybir.AxisListType

ENGINES = ("tensor", "vector", "scalar", "gpsimd", "sync")
SEM_CAP = 30000
EPS = 1e-6


class _Op:
    __slots__ = ("eng", "fn", "idx", "deps", "signal", "is_dma", "sem", "val", "pre_wait")

    def __init__(self, eng, fn, is_dma):
        self.eng = eng
        self.fn = fn
        self.is_dma = is_dma
        self.deps = []
        self.signal = False
        self.sem = None
        self.val = None
        self.pre_wait = None


class Tl:
    def __init__(self, name, t):
        self.name = name
        self.t = t

    def __getitem__(self, idx):
        return self.t[idx]


def _norm(rs):
    out = []
    for r in rs:
        if isinstance(r, tuple):
            a, k = r
        else:
            a, k = r, None
        if isinstance(a, Tl):
            a = a.name
        if a.startswith("scr"):
            k = None
        out.append((a, k))
    return out


class Sched:
    def __init__(self, nc, stack, n_dma_sems=16):
        self.nc = nc
        self.stack = stack
        self.ops = {e: [] for e in ENGINES}
        self.state = {}
        self.n_dma_sems = n_dma_sems

    def _entries(self, name, key):
        d = self.state.setdefault(name, {})
        if key is None:
            return list(d.values())
        res = []
        if key in d:
            res.append(d[key])
        if None in d:
            res.append(d[None])
        return res

    def add(self, eng, fn, reads=(), writes=(), dma=False):
        reads = _norm(reads)
        writes = _norm(writes)
        op = _Op(eng, fn, dma)
        deps = []
        for (name, key) in reads:
            for ent in self._entries(name, key):
                if ent[0] is not None:
                    deps.append(ent[0])
                if name[0] == "p" and name[1].isupper():
                    deps.extend(o_ for o_ in ent[1] if o_.eng != eng)
        for (name, key) in writes:
            for ent in self._entries(name, key):
                if ent[0] is not None:
                    deps.append(ent[0])
                deps.extend(ent[1])
        for (name, key) in reads:
            d = self.state.setdefault(name, {})
            if key is None:
                if not d:
                    d[None] = [None, []]
                for ent in d.values():
                    ent[1].append(op)
            else:
                if key not in d:
                    d[key] = [None, []]
                d[key][1].append(op)
        for (name, key) in writes:
            d = self.state.setdefault(name, {})
            if key is None:
                d.clear()
                d[None] = [op, []]
            else:
                d[key] = [op, []]
        op.idx = len(self.ops[eng])
        best = {}
        dl = []
        for dop in deps:
            if dop is op:
                continue
            if dop.is_dma:
                if dop not in dl:
                    dl.append(dop)
            else:
                if dop.eng == "tensor" and eng == "tensor" and not dma:
                    continue
                b = best.get(dop.eng)
                if b is None or dop.idx > b.idx:
                    best[dop.eng] = dop
        op.deps = dl + list(best.values())
        for dop in op.deps:
            dop.signal = True
        self.ops[eng].append(op)
        return op

    def emit(self, final_wait_ops=()):
        nc = self.nc
        for eng in ENGINES:
            cnt = 0
            sem = None
            for op in self.ops[eng]:
                if op.is_dma:
                    continue
                if op.signal:
                    if sem is None or cnt >= SEM_CAP:
                        sem = self.stack.enter_context(nc.semaphore(f"s_{eng}_{op.idx}"))
                        cnt = 0
                    cnt += 1
                    op.sem = sem
                    op.val = cnt
        for eng in ENGINES:
            qpool = []
            k = 0
            for op in self.ops[eng]:
                if not op.is_dma:
                    continue
                if len(qpool) < self.n_dma_sems:
                    s = self.stack.enter_context(nc.semaphore(f"d_{eng}_{len(qpool)}"))
                    qpool.append([s, 0])
                    ent = qpool[-1]
                else:
                    ent = qpool[k % self.n_dma_sems]
                    if ent[1] + 16 > SEM_CAP:
                        ent[0] = self.stack.enter_context(nc.semaphore(f"d_{eng}_x{k}"))
                        ent[1] = 0
                if ent[1] > 0:
                    op.pre_wait = (ent[0], ent[1])
                ent[1] += 16
                op.sem = ent[0]
                op.val = ent[1]
                k += 1
        sched = self

        def run(eng_name, e):
            seen = {}
            for op in sched.ops[eng_name]:
                waits = []
                if op.pre_wait is not None:
                    waits.append(op.pre_wait)
                for dop in op.deps:
                    waits.append((dop.sem, dop.val))
                for (s, v) in waits:
                    key = id(s)
                    if seen.get(key, 0) >= v:
                        continue
                    seen[key] = v
                    e.wait_ge(s, v)
                ins = op.fn(e)
                if op.is_dma:
                    ins.then_inc(op.sem, 16)
                elif op.signal:
                    ins.then_inc(op.sem, 1)
            if eng_name == "sync":
                for fop in final_wait_ops:
                    e.wait_ge(fop.sem, fop.val)

        with nc.Block() as block:
            @block.sync
            def _(e):
                run("sync", e)

            @block.tensor
            def _(e):
                run("tensor", e)

            @block.vector
            def _(e):
                run("vector", e)

            @block.scalar
            def _(e):
                run("scalar", e)

            @block.gpsimd
            def _(e):
                run("gpsimd", e)


D_MODEL = 1024
FM_GROUPS = ["Au", "Az", "Bz", "Cq", "Ck", "Co", "Cz", "Dr", "Dk", "Dv", "Dz", "Mz",
             "Bf", "Ci", "Cf0", "Cf1", "Dw", "Da"]
FM_W = {g: 128 for g in FM_GROUPS}
FM_W["Dw"] = 16
FM_W["Da"] = 16
FM_OFF = {}
_o = 0
for _g in FM_GROUPS:
    FM_OFF[_g] = _o
    _o += FM_W[_g]
NFM = _o
TM_GROUPS = ["Av", "Bq", "Bk", "Bv", "Cv", "Mq"]
NTM = 768
NW = NFM + NTM
PC = {n: i for i, n in enumerate([
    "cq0", "cq1", "cq2", "cq3", "ck0", "ck1", "ck2", "ck3", "cqb", "ckb",
    "mu_r", "mu_k", "mu_v", "mu_z", "mu_w", "mu_a",
    "k_k", "k_a", "a0", "w0", "r_k", "ln_g", "out_g",
    "fb_B", "ib_C", "fb_C0", "fb_C1"])}
NPC = len(PC)
PR_SGU_G, PR_BQG, PR_BKG, PR_MQG, PR_MKG, PR_SGUB = 0, 128, 256, 384, 512, 640
NPR = 768

A_OFF = 0
B_OFF = 768
C_OFF = 768 + 1028
D_OFF = C_OFF + 1288
M_OFF = D_OFF + 1056


def host_layer_params(inp, l, hh):
    f32 = np.float32
    w_in = inp["w_in"][l]
    hs = [2 * hh, 2 * hh + 1]

    def hcols(base):
        return np.concatenate([np.arange(base + h * 64, base + h * 64 + 64) for h in hs])

    cols = {}
    cols["Au"] = hcols(A_OFF)
    cols["Av"] = hcols(A_OFF + 256)
    cols["Az"] = hcols(A_OFF + 512)
    cols["Bq"] = hcols(B_OFF)
    cols["Bk"] = hcols(B_OFF + 256)
    cols["Bv"] = hcols(B_OFF + 512)
    bf = B_OFF + 768
    cols["Bf"] = np.concatenate([np.full(64, bf + hs[1]), np.full(64, bf + hs[0])])
    cols["Bz"] = hcols(B_OFF + 772)
    cols["Cq"] = hcols(C_OFF)
    cols["Ck"] = hcols(C_OFF + 256)
    cols["Cv"] = hcols(C_OFF + 512)
    ci = C_OFF + 768
    cols["Ci"] = np.concatenate([np.full(64, ci + hs[0]), np.full(64, ci + hs[1])])
    cols["Cf0"] = np.full(128, ci + 4 + hs[0])
    cols["Cf1"] = np.full(128, ci + 4 + hs[1])
    cols["Co"] = hcols(C_OFF + 776)
    cols["Cz"] = hcols(C_OFF + 1032)
    cols["Dr"] = hcols(D_OFF)
    cols["Dw"] = np.arange(D_OFF + 256, D_OFF + 272)
    cols["Dk"] = hcols(D_OFF + 272)
    cols["Dv"] = hcols(D_OFF + 528)
    cols["Da"] = np.arange(D_OFF + 784, D_OFF + 800)
    cols["Dz"] = hcols(D_OFF + 800)
    cols["Mq"] = hcols(M_OFF)
    cols["Mz"] = hcols(M_OFF + 256)
    allc = np.concatenate([cols[g] for g in FM_GROUPS] + [cols[g] for g in TM_GROUPS])
    wcat = np.ascontiguousarray(w_in[:, allc])

    hc = hcols(0)
    pc = np.zeros((128, NPC), f32)
    cw = inp["mlstm_conv_w"][l]
    cb = inp["mlstm_conv_b"][l]
    for j in range(4):
        pc[:, PC["cq%d" % j]] = cw[j, hc]
        pc[:, PC["ck%d" % j]] = cw[j, 256 + hc]
    pc[:, PC["cqb"]] = cb[hc]
    pc[:, PC["ckb"]] = cb[256 + hc]
    mu = inp["rwkv_mu"][l]
    pc[:, PC["mu_r"]] = mu[hc]
    pc[:16, PC["mu_w"]] = mu[256:272]
    pc[:, PC["mu_k"]] = mu[272 + hc]
    pc[:, PC["mu_v"]] = mu[528 + hc]
    pc[:16, PC["mu_a"]] = mu[784:800]
    pc[:, PC["mu_z"]] = mu[800 + hc]
    pc[:, PC["k_k"]] = inp["rwkv_k_k"][l][hc]
    pc[:, PC["k_a"]] = inp["rwkv_k_a"][l][hc]
    pc[:, PC["a0"]] = inp["rwkv_a0"][l][hc]
    pc[:, PC["w0"]] = inp["rwkv_w0"][l][hc]
    pc[:, PC["r_k"]] = inp["rwkv_r_k"][l].reshape(-1)[hc]
    pc[:, PC["ln_g"]] = inp["rwkv_ln_g"][l][hc]
    pc[:, PC["out_g"]] = inp["mlstm_out_g"][l][hc]
    fb = inp["fox_f_b"][l]
    pc[:, PC["fb_B"]] = np.concatenate([np.full(64, fb[hs[1]]), np.full(64, fb[hs[0]])])
    ib = inp["mlstm_i_b"][l]
    pc[:, PC["ib_C"]] = np.concatenate([np.full(64, ib[hs[0]]), np.full(64, ib[hs[1]])])
    fbc = inp["mlstm_f_b"][l]
    pc[:, PC["fb_C0"]] = fbc[hs[0]]
    pc[:, PC["fb_C1"]] = fbc[hs[1]]

    pr = np.zeros((128, NPR), f32)
    pr[:, PR_SGU_G:PR_SGU_G + 128] = inp["sgu_norm_g"][l][hc][None, :]
    pr[:, PR_BQG:PR_BQG + 128] = np.tile(inp["fox_q_g"][l], 2)[None, :]
    pr[:, PR_BKG:PR_BKG + 128] = np.tile(inp["fox_k_g"][l], 2)[None, :]
    pr[:, PR_MQG:PR_MQG + 128] = np.tile(inp["mem_q_g"][l], 2)[None, :]
    pr[:, PR_MKG:PR_MKG + 128] = np.tile(inp["mem_k_g"][l], 2)[None, :]
    sb_ = inp["sgu_b"][l]
    pr[:64, PR_SGUB:PR_SGUB + 128] = sb_[hs[0]][None, :]
    pr[64:, PR_SGUB:PR_SGUB + 128] = sb_[hs[1]][None, :]

    d = {
        "wcat": wcat,
        "pc": pc,
        "pr": pr,
        "ng": np.ascontiguousarray(inp["norm_g"][l].reshape(8, 128).T),
        "memg": np.ascontiguousarray(inp["mem_norm_g"][l].reshape(8, 128).T),
        "wkv": np.ascontiguousarray(np.concatenate(
            [inp["mem_w_kv"][l][:, hc], inp["mem_w_kv"][l][:, 256 + hc]], axis=1)),
        "w2": np.ascontiguousarray(inp["rwkv_w2"][l][:, hc]),
        "a2": np.ascontiguousarray(inp["rwkv_a2"][l][:, hc]),
        "sguw": np.ascontiguousarray(inp["sgu_w"][l][hs]),
    }
    return d


class Builder:
    def __init__(self, SEQ, has_prev, branches="ABCDM"):
        self.SEQ = SEQ
        self.has_prev = has_prev
        self.branches = branches
        self.nc = bass.Bass("TRN2", target_bir_lowering=False)
        self.st = ExitStack()
        self.S = Sched(self.nc, self.st)
        self.ps_rr = 0

    def dram(self, name, shape, dt, kind):
        return self.nc.dram_tensor(name, shape, dt, kind=kind).ap()

    def sb(self, name, shape, dt=F32):
        if not hasattr(self, "_tiles"):
            self._tiles = {}
        if name not in self._tiles:
            self._tiles[name] = Tl(name, self.st.enter_context(self.nc.sbuf_tensor(name, shape, dt)))
        return self._tiles[name]

    def psum(self, name, shape, dt=F32):
        if not hasattr(self, "_tiles"):
            self._tiles = {}
        if name not in self._tiles:
            self._tiles[name] = Tl(name, self.st.enter_context(self.nc.psum_tensor(name, shape, dt)))
        return self._tiles[name]

    def gps(self):
        p = self.gp[self.ps_rr % len(self.gp)]
        self.ps_rr += 1
        return p

    def op(self, eng, fn, r=(), w=()):
        return self.S.add(eng, fn, reads=r, writes=w)

    def dma(self, eng, out, in_, r=(), w=()):
        return self.S.add(eng, lambda e: e.dma_start(out=out, in_=in_), reads=r, writes=w, dma=True)

    def mm(self, out, lhsT, rhs, start, stop, r, w):
        return self.S.add("tensor", lambda e: e.matmul(out, lhsT=lhsT, rhs=rhs, start=start, stop=stop),
                          reads=r, writes=w)

    def tr(self, out, in_, ident, r, w):
        return self.S.add("tensor", lambda e: e.transpose(out, in_, ident), reads=r, writes=w)

    def act(self, out, in_, func, r, w, bias=None, scale=None, accum_out=None, eng="scalar"):
        kw = {}
        if bias is not None:
            kw["bias"] = bias
        if scale is not None:
            kw["scale"] = scale
        if accum_out is not None:
            kw["accum_out"] = accum_out
        return self.S.add("scalar", lambda e: e.activation(out=out, in_=in_, func=func, **kw), reads=r, writes=w)

    def tt(self, eng, out, in0, in1, op, r, w):
        return self.S.add(eng, lambda e: e.tensor_tensor(out=out, in0=in0, in1=in1, op=op), reads=r, writes=w)

    def ts(self, eng, out, in0, s1, op0, r, w, s2=None, op1=None):
        if op1 is None:
            return self.S.add(eng, lambda e: e.tensor_scalar(out=out, in0=in0, scalar1=s1, scalar2=None, op0=op0),
                              reads=r, writes=w)
        return self.S.add(eng, lambda e: e.tensor_scalar(out=out, in0=in0, scalar1=s1, scalar2=s2, op0=op0, op1=op1),
                          reads=r, writes=w)

    def stt(self, out, in0, scalar, in1, op0, op1, r, w):
        return self.S.add("vector", lambda e: e.scalar_tensor_tensor(out=out, in0=in0, scalar=scalar, in1=in1,
                                                                      op0=op0, op1=op1), reads=r, writes=w)

    def cp(self, eng, out, in_, r, w):
        if eng == "scalar":
            return self.S.add("scalar", lambda e: e.copy(out=out, in_=in_), reads=r, writes=w)
        return self.S.add(eng, lambda e: e.tensor_copy(out, in_), reads=r, writes=w)

    def recip(self, out, in_, r, w):
        return self.S.add("vector", lambda e: e.reciprocal(out, in_), reads=r, writes=w)

    def memset(self, eng, ap, val, w):
        return self.S.add(eng, lambda e: e.memset(ap, val), writes=w)

    def asel(self, out, in_, cmp, w, r=(), fill=0.0, base=0, cm=-1, pat=None):
        pat = pat or [[1, 128]]
        return self.S.add("gpsimd", lambda e: e.affine_select(out=out, in_=in_, pattern=pat, compare_op=cmp,
                                                              fill=fill, base=base, channel_multiplier=cm),
                          reads=r, writes=w)

    def build(self):
        cfg = dict(tag="", xmode="outproj" if self.has_prev else "ext")
        self.last_out = []
        self.run_pass(cfg)
        self.S.emit(final_wait_ops=self.last_out)
        return self.nc

    def build_fused(self):
        SEQ = self.SEQ
        self.last_out = []
        self.x_ext = self.dram("xin", [SEQ, 1024], F32, "ExternalInput")
        self.mem_ext = self.dram("mem", [256, 1024], F32, "ExternalInput")
        self.mixs = [self.dram("mixs%d" % l, [1280, SEQ], BF16, "Internal") for l in range(2)]
        self.x1s = self.dram("x1s", [SEQ, 1024], F32, "Internal")
        self.wouts = [self.dram("wout%d" % l, [1280, 1024], F32, "ExternalInput") for l in range(2)]
        self.wcats = {"_%d%d" % (l, hh): self.dram("wcat_%d%d" % (l, hh), [1024, NW], F32, "ExternalInput")
                      for l in range(2) for hh in range(2)}
        self.seq = ["_00", "_01", "W0", "_10", "_11", "W1"]
        self.prefetched = None
        for l in range(2):
            if l == 1:
                self.begin("oproj")
                self.outproj_pass("ext", 0, self.x1s, "x1s", False)
            for hh in range(2):
                xmode = "ext" if l == 0 else "x1"
                self.run_pass(dict(tag="_%d%d" % (l, hh), xmode=xmode, fused=True, l=l, hh=hh))
        out_d = self.dram("out", [SEQ, 1024], F32, "ExternalOutput")
        self.begin("oproj")
        self.outproj_pass("x1", 1, out_d, "out_d", True)
        self.S.emit(final_wait_ops=self.last_out)
        return self.nc

    def load_wb(self, tag):
        wb = self.sb("wb", [128, 8, NW], BF16)
        wv = self.wcats[tag].rearrange("(c p) n -> p c n", p=128)
        for c in range(8):
            self.dma("gpsimd", wb[:, c, :], wv[:, c, :], w=[(wb, c)])
        self.prefetched = ("wb", tag)

    def load_wo(self, l):
        wb = self.sb("wb", [128, 8, NW], BF16)
        wo = Tl("wb", wb.t[:, :, :].rearrange("p a b -> p (a b)")[:, 0:10240].rearrange("p (c n) -> p c n", c=10))
        wov = self.wouts[l].rearrange("(c p) n -> p c n", p=128)
        for c in range(10):
            self.dma("gpsimd", wo[:, c, :], wov[:, c, :], w=[wo])
        self.prefetched = ("wo", l)

    def prefetch_after(self, cur):
        seq = getattr(self, "seq", None)
        if not seq or cur not in seq:
            return
        i = seq.index(cur)
        if i + 1 >= len(seq):
            return
        nxt = seq[i + 1]
        if nxt.startswith("W"):
            self.load_wo(int(nxt[1]))
        else:
            self.load_wb(nxt)

    def outproj_pass(self, src_kind, l, dst, dst_name, final):
        SEQ = self.SEQ
        pP = [self.psum("pP0", [128, 512]), self.psum("pP1", [128, 512])]
        wb = self.sb("wb", [128, 8, NW], BF16)
        wo = Tl("wb", wb.t[:, :, :].rearrange("p a b -> p (a b)")[:, 0:10240].rearrange("p (c n) -> p c n", c=10))
        if getattr(self, "prefetched", None) != ("wo", l):
            self.load_wo(l)
        xts = [self.sb("xt%d" % i, [128, 1024]) for i in range(2)]
        tm_ = self.sb("tm", [128, 4, 768], BF16)
        flat = tm_.t[:, :, :].rearrange("p a b -> p (a b)")
        mpvs = [Tl("tm", flat[:, 0:1280].rearrange("p (c t) -> p c t", c=10)),
                Tl("tm", flat[:, 1280:2560].rearrange("p (c t) -> p c t", c=10))]
        for ti in range(SEQ // 128):
            xt = xts[ti % 2]
            mpv = mpvs[ti % 2]
            r0 = ti * 128
            if src_kind == "ext":
                self.dma("sync", xt[:], self.x_ext[r0:r0 + 128, :], w=[xt])
            else:
                self.dma("sync", xt[:], self.x1s[r0:r0 + 128, :], r=[("x1s", ti)], w=[xt])
            self.dma("sync", mpv[:], self.mixs[l][:, r0:r0 + 128].rearrange("(c p) t -> p c t", p=128),
                     r=[("mixs%d" % l, (0, ti // 4)), ("mixs%d" % l, (1, ti // 4))], w=[mpv])
            for hf in range(2):
                p = pP[hf]
                for c in range(10):
                    self.mm(p[:], mpv[:, c, :], wo[:, c, hf * 512:(hf + 1) * 512], c == 0, c == 9,
                            [mpv, wo], [p])
                eng = "vector" if hf == 0 else "gpsimd"
                if eng == "gpsimd":
                    tmpo = self.scr("op_tmp", [128, 512])
                    self.cp("scalar", tmpo[:], p[:], [p], [tmpo])
                    self.tt("gpsimd", xt[:, hf * 512:(hf + 1) * 512], xt[:, hf * 512:(hf + 1) * 512], tmpo[:],
                            ALU.add, [xt, tmpo], [xt])
                else:
                    self.tt("vector", xt[:, hf * 512:(hf + 1) * 512], xt[:, hf * 512:(hf + 1) * 512], p[:],
                            ALU.add, [xt, p], [xt])
            o = self.dma("sync", dst[r0:r0 + 128, :], xt[:], r=[xt], w=[(dst_name, ti)])
            if final:
                self.last_out.append(o)
            if ti == SEQ // 128 - 1:
                self.prefetch_after("W%d" % l)

    def run_pass(self, cfg):
        nc = self.nc
        SEQ = self.SEQ
        NMC = SEQ // 512
        NCH = SEQ // 128
        tag = cfg["tag"]
        xmode = cfg["xmode"]
        fused = cfg.get("fused", False)
        has_prev = xmode == "outproj"
        wcat = None if fused else self.dram("wcat" + tag, [1024, NW], F32, "ExternalInput")
        pc_d = self.dram("pc" + tag, [128, NPC], F32, "ExternalInput")
        pr_d = self.dram("pr" + tag, [128, NPR], F32, "ExternalInput")
        ng_d = self.dram("ng" + tag, [128, 8], F32, "ExternalInput")
        memg_d = self.dram("memg" + tag, [128, 8], F32, "ExternalInput")
        wkv_d = self.dram("wkv" + tag, [1024, 256], F32, "ExternalInput")
        w2_d = self.dram("w2" + tag, [16, 128], F32, "ExternalInput")
        a2_d = self.dram("a2" + tag, [16, 128], F32, "ExternalInput")
        sguw_d = self.dram("sguw" + tag, [2, 128, 128], F32, "ExternalInput")
        if fused:
            l, hh = cfg["l"], cfg["hh"]
            xin = self.x_ext
            mem_d = self.mem_ext
            mixed_d = self.mixs[l].rearrange("(g two p) t -> two p g t", two=2, p=128)[hh]
            mix_name = "mixs%d" % l
            mix_key = lambda mc: (hh, mc)
            if has_prev:
                mprev_d = self.mixs[0]
                wout_d = self.wouts[0]
                x1_d = self.x1s
        else:
            xin = self.dram("xin", [SEQ, 1024], F32, "ExternalInput")
            mem_d = self.dram("mem", [256, 1024], F32, "ExternalInput")
            mixed_d = self.dram("mixed", [640, SEQ], BF16, "ExternalOutput").rearrange("(g p) t -> p g t", p=128)
            mix_name = "mixed_d"
            mix_key = lambda mc: mc
            if has_prev:
                mprev_d = self.dram("mprev", [1280, SEQ], BF16, "ExternalInput")
                wout_d = self.dram("wout", [1280, 1024], F32, "ExternalInput")
                x1_d = self.dram("x1out", [SEQ, 1024], F32, "ExternalOutput")

        pT = self.psum("pT", [128, 1024], BF16)
        pP = [self.psum("pP0", [128, 512]), self.psum("pP1", [128, 512])]
        self.gp = [self.psum("pG%d" % i, [128, 512]) for i in range(3)]
        self.pacc = pP
        pNum = self.psum("pNum", [128, 512])
        pDen = self.psum("pDen", [128, 512])

        identf = self.sb("identf", [128, 128])
        ident = self.sb("ident", [128, 128], BF16)
        ones_bf = self.sb("ones_bf", [128, 128], BF16)
        onesf = self.sb("onesf", [128, 128])
        c64f = self.sb("c64f", [128, 128])
        c64b = self.sb("c64b", [128, 128], BF16)
        blk = self.sb("blk", [128, 128], BF16)
        m_ge = self.sb("m_ge", [128, 128], BF16)
        self.memset("vector", identf[:], 1.0, [identf])
        self.asel(identf[:], identf[:], ALU.is_equal, [identf], [identf])
        self.cp("vector", ident[:], identf[:], [identf], [ident])
        self.memset("vector", ones_bf[:], 1.0, [ones_bf])
        self.memset("vector", onesf[:], 1.0, [onesf])
        self.memset("vector", c64f[:], 1.0 / 64, [c64f])
        self.memset("vector", c64b[:], 1.0 / 64, [c64b])
        self.memset("vector", blk[:], 0.0, [blk])
        self.memset("vector", blk[0:64, 0:64], 1.0, [blk])
        self.memset("vector", blk[64:128, 64:128], 1.0, [blk])
        self.memset("vector", m_ge[:], 1.0, [m_ge])
        self.asel(m_ge[:], m_ge[:], ALU.is_ge, [m_ge], [m_ge])

        pc = self.sb("pcs", [128, NPC])
        pr = self.sb("prs", [128, NPR])
        ng = self.sb("ngs", [128, 8])
        memg = self.sb("memgs", [128, 8])
        self.dma("sync", pc[:], pc_d, w=[pc])
        self.dma("sync", pr[:], pr_d, w=[pr])
        self.dma("sync", ng[:], ng_d, w=[ng])
        self.dma("sync", memg[:], memg_d, w=[memg])
        omm = self.sb("omm", [128, 6])
        self.ts("vector", omm[:], pc[:, PC["mu_r"]:PC["mu_r"] + 6], -1.0, ALU.mult, [pc], [omm], s2=1.0, op1=ALU.add)
        nfbB = self.sb("nfbB", [128, 1])
        self.ts("vector", nfbB[:], pc[:, PC["fb_B"]:PC["fb_B"] + 1], -1.0, ALU.mult, [pc], [nfbB])
        nfbC = self.sb("nfbC", [128, 2])
        self.ts("vector", nfbC[:], pc[:, PC["fb_C0"]:PC["fb_C0"] + 2], -1.0, ALU.mult, [pc], [nfbC])
        gq8 = self.sb("gq8", [128, 128])
        self.ts("vector", gq8[:], pr[:, PR_BQG:PR_BQG + 128], 0.125, ALU.mult, [pr], [gq8])
        gmq8 = self.sb("gmq8", [128, 128])
        self.ts("vector", gmq8[:], pr[:, PR_MQG:PR_MQG + 128], 0.125, ALU.mult, [pr], [gmq8])

        wb = self.sb("wb", [128, 8, NW], BF16)
        if fused:
            if getattr(self, "prefetched", None) != ("wb", tag):
                self.load_wb(tag)
        else:
            wv = wcat.rearrange("(c p) n -> p c n", p=128)
            for c in range(8):
                self.dma("gpsimd", wb[:, c, :], wv[:, c, :], w=[(wb, c)])
        for c in range(8):
            eng = "vector" if c % 2 == 0 else "gpsimd"
            self.ts(eng, wb[:, c, :], wb[:, c, :], ng[:, c:c + 1], ALU.mult, [(wb, c), ng], [(wb, c)])
        if has_prev:
            wo = self.sb("wo", [128, 10, 1024], BF16)
            wov = wout_d.rearrange("(c p) n -> p c n", p=128)
            for c in range(10):
                self.dma("gpsimd", wo[:, c, :], wov[:, c, :], w=[(wo, c)])
        w2s = self.sb("w2s", [16, 128])
        a2s = self.sb("a2s", [16, 128])
        self.dma("sync", w2s[:], w2_d, w=[w2s])
        self.dma("sync", a2s[:], a2_d, w=[a2s])

        wsT = self.sb("wsT", [128, 2, 128], BF16)
        self.begin("setupA")
        if "A" in self.branches:
            sgw = self.scr("sgw", [128, 2, 128])
            self.dma("sync", sgw[:], sguw_d.rearrange("h t s -> t h s"), w=[sgw])
            sgwT = self.scr("sgwT", [128, 2, 128])
            for h in range(2):
                p = self.gps()
                self.tr(p[:, 0:128], sgw[:, h, :], identf[:], [sgw, identf], [p])
                self.cp("vector", sgwT[:, h, :], p[:, 0:128], [p], [sgwT])
                self.asel(sgwT[:, h, :], sgwT[:, h, :], ALU.is_ge, [sgwT], [sgwT])
            self.cp("vector", wsT[:], sgwT[:], [sgwT], [wsT])

        kTm = self.sb("kTm", [128, 256], BF16)
        vm = self.sb("vm", [128, 2, 128], BF16)

        self.alloc_state(NCH)

        xts = [self.sb("xt0", [128, 1024]), self.sb("xt1", [128, 1024])]
        hb = self.sb("hb", [128, 1024], BF16)
        hT = self.sb("hT", [128, 8, 512], BF16)
        ss = self.sb("ss", [128, 2])
        GATED = {"Az": AF.Silu, "Bz": AF.Silu, "Cz": AF.Silu, "Mz": AF.Silu, "Co": AF.Sigmoid}
        fm = {}
        for g in FM_GROUPS:
            if g in ("Cq", "Ck"):
                fm[g] = self.sb("fm_" + g, [128, 4 + 512], BF16)
            elif g in ("Dr", "Dk", "Dv", "Dz"):
                fm[g] = self.sb("fm_" + g, [128, 2 + 512], BF16)
            elif g in ("Dw", "Da"):
                fm[g] = self.sb("fm_" + g, [16, 1 + 512])
            elif g in GATED:
                fm[g] = self.sb("fm_" + g, [128, 512], BF16)
            else:
                fm[g] = self.sb("fm_" + g, [128, 512])
        tm = self.sb("tm", [128, 4, 768], BF16)
        mixo = self.sb("mixo", [128, 5, 512], BF16)
        if "M" in self.branches:
            self.setup_mem(mem_d, wkv_d, memg, pr, ident, identf, pT, kTm, vm, xts, hT, tm, hb)
        mpvs = [Tl("tm", tm.t[:, :, :].rearrange("p a b -> p (a b)")[:, 0:1280].rearrange("p (c t) -> p c t", c=10))] * 2
        for g in ("Cq", "Ck"):
            self.memset("vector", fm[g][:, 0:3], 0.0, [(fm[g], "hist")])
        for g in ("Dr", "Dk", "Dv", "Dz"):
            self.memset("vector", fm[g][:, 0:1], 0.0, [(fm[g], "hist")])
        for g in ("Dw", "Da"):
            self.memset("vector", fm[g][:, 0:1], 0.0, [(fm[g], "hist")])

        self.consts = dict(ident=ident, identf=identf, ones_bf=ones_bf, onesf=onesf, c64f=c64f, c64b=c64b,
                           blk=blk, m_ge=m_ge, pc=pc, pr=pr, omm=omm, nfbB=nfbB, nfbC=nfbC,
                           gq8=gq8, gmq8=gmq8, wsT=wsT, kTm=kTm, vm=vm, w2s=w2s, a2s=a2s, pT=pT,
                           pNum=pNum, pDen=pDen)
        last_out = self.last_out
        xi = 0
        xstate = {"xi": 0}

        def xprep(mc):
            t0 = mc * 512
            for tt in range(4):
                xi = xstate["xi"]
                xt = xts[xi % 2]
                r0 = t0 + tt * 128
                ti = mc * 4 + tt
                if xmode == "x1":
                    self.dma("sync", xt[:], self.x1s[r0:r0 + 128, :], r=[("x1s", ti)], w=[xt])
                else:
                    self.dma("sync", xt[:], xin[r0:r0 + 128, :], w=[xt])
                if has_prev:
                    mpv = mpvs[xi % 2]
                    rr = [("mixs0", (0, mc)), ("mixs0", (1, mc))] if fused else []
                    self.dma("sync", mpv[:], mprev_d[:, r0:r0 + 128].rearrange("(c p) t -> p c t", p=128),
                             r=rr, w=[mpv])
                    for hf in range(2):
                        p = pP[hf]
                        for c in range(10):
                            self.mm(p[:], mpv[:, c, :], wo[:, c, hf * 512:(hf + 1) * 512],
                                    c == 0, c == 9, [mpv, (wo, c)], [p])
                        self.tt("vector", xt[:, hf * 512:(hf + 1) * 512], xt[:, hf * 512:(hf + 1) * 512],
                                p[:], ALU.add, [xt, p], [xt])
                    o = self.dma("sync", x1_d[r0:r0 + 128, :], xt[:], r=[xt], w=[("x1s", ti)])
                    if not fused:
                        last_out.append(o)
                xstate["xi"] = xi + 1
                self.act(hb[:], xt[:], AF.Square, [xt], [hb, ss], accum_out=ss[:, 0:1])
                self.act(ss[:, 1:2], ss[:, 0:1], AF.Ln, [ss], [ss], scale=1.0 / 1024, bias=EPS)
                self.act(ss[:, 1:2], ss[:, 1:2], AF.Exp, [ss], [ss], scale=-0.5)
                self.ts("vector", hb[:], xt[:], ss[:, 1:2], ALU.mult, [xt, ss], [hb])
                yield
                for c in range(8):
                    self.tr(pT[:, c * 128:(c + 1) * 128], hb[:, c * 128:(c + 1) * 128], ident[:],
                            [hb, ident], [pT])
                eng = "vector" if tt % 2 == 0 else "scalar"
                self.cp(eng, hT[:, :, tt * 128:(tt + 1) * 128],
                        pT[:, :].rearrange("p (c t) -> p c t", c=8), [pT], [hT])
                yield

        for _ in xprep(0):
            pass
        for mc in range(NMC):
            t0 = mc * 512
            for gi, g in enumerate(FM_GROUPS):
                p = pP[gi % 2]
                wdt = FM_W[g]
                for c in range(8):
                    self.mm(p[0:wdt, :], wb[:, c, FM_OFF[g]:FM_OFF[g] + wdt], hT[:, c, :], c == 0, c == 7,
                            [(wb, c), hT], [p])
                hist = {"Cq": 3, "Ck": 3, "Dr": 1, "Dk": 1, "Dv": 1, "Dz": 1, "Dw": 1, "Da": 1}.get(g, 0)
                dst = fm[g]
                if g in GATED:
                    self.act(dst[:, :], p[:, :], GATED[g], [p], [(dst, "cur")])
                    continue
                if hist and mc > 0:
                    self.cp("vector", dst[0:wdt, 0:hist], dst[0:wdt, 512:512 + hist], [(dst, "cur")], [(dst, "hist")])
                eng = "scalar" if gi % 2 == 0 else "vector"
                self.cp(eng, dst[0:wdt, hist:hist + 512], p[0:wdt, :], [p, (dst, "hist")], [(dst, "cur")])
            for tt in range(4):
                for hf in range(2):
                    p = pP[hf]
                    for c in range(8):
                        self.mm(p[:, 0:384], hT[:, c, tt * 128:(tt + 1) * 128],
                                wb[:, c, NFM + hf * 384:NFM + (hf + 1) * 384], c == 0, c == 7, [hT, (wb, c)], [p])
                    eng = "scalar" if hf == 0 else "vector"
                    self.cp(eng, tm[:, tt, hf * 384:(hf + 1) * 384], p[:, 0:384], [p], [(tm, tt)])
            if fused and mc == NMC - 1:
                self.prefetch_after(tag)
            self.zero_mix = []
            if "A" in self.branches:
                self.branch_A(mc, fm, tm, mixo)
            else:
                self.memset("gpsimd", mixo[:, 0, :], 0.0, [(mixo, 0)])
            if "B" in self.branches:
                self.branch_B(mc, fm, tm, mixo)
            else:
                self.memset("gpsimd", mixo[:, 1, :], 0.0, [(mixo, 1)])
            gens = []
            if mc + 1 < NMC:
                gens.append(("X", xprep(mc + 1)))
            if "M" in self.branches:
                gens.append(("M", self.branch_M(mc, fm, tm, mixo)))
            else:
                self.memset("gpsimd", mixo[:, 4, :], 0.0, [(mixo, 4)])
            if "D" in self.branches:
                gens.append(("D", self.branch_D(mc, fm, tm, mixo)))
            else:
                self.memset("gpsimd", mixo[:, 3, :], 0.0, [(mixo, 3)])
            if "C" in self.branches:
                gens.append(("C", self.branch_C(mc, fm, tm, mixo)))
            else:
                self.memset("gpsimd", mixo[:, 2, :], 0.0, [(mixo, 2)])
            while gens:
                for item in list(gens):
                    for rep in range(3 if item[0] == "D" else 1):
                        self._br = item[0]
                        try:
                            next(item[1])
                        except StopIteration:
                            gens.remove(item)
                            break
            o = self.dma("sync", mixed_d[:, :, t0:t0 + 512], mixo[:], r=[mixo], w=[(mix_name, mix_key(mc))])
            if not fused:
                last_out.append(o)

    def head_rms_tm(self, src, dst, gain, tag, nt=4):
        sq = self.scr("hr_sq", [128, 4, 128])
        ssq = self.scr("hr_ssq", [128, 8])
        src_ap, src_r = src
        dst_ap, dst_w = dst
        g_ap, g_r = gain
        self.tt("gpsimd", sq[:, 0:nt, :], src_ap, src_ap, ALU.mult, src_r, [sq])
        ssq3 = ssq[:, 0:2 * nt].rearrange("p (t h) -> p t h", h=2)
        self.op("vector", lambda e: e.tensor_reduce(out=ssq3,
                                                     in_=sq[:, 0:nt, :].rearrange("p t (h j) -> p t h j", h=2),
                                                     axis=AX.X, op=ALU.add), [sq], [ssq])
        self.act(ssq[:, 0:2 * nt], ssq[:, 0:2 * nt], AF.Ln, [ssq], [ssq], scale=1.0 / 64, bias=EPS)
        self.act(ssq[:, 0:2 * nt], ssq[:, 0:2 * nt], AF.Exp, [ssq], [ssq], scale=-0.5)
        self.tt("vector", sq[:, 0:nt, :].rearrange("p t (h j) -> p t h j", h=2),
                src_ap.rearrange("p t (h j) -> p t h j", h=2),
                ssq3[:, :, :, None].broadcast_to([128, nt, 2, 64]), ALU.mult, src_r + [ssq], [sq])
        self.tt("vector", dst_ap, sq[:, 0:nt, :], g_ap[:, None, :].broadcast_to([128, nt, 128]), ALU.mult,
                [sq] + g_r, dst_w)

    def begin(self, br):
        self._br = br
        if not hasattr(self, "_brcount"):
            self._brcount = {}
        self._brcount.setdefault(br, {})

    def scr(self, name, shape, dt=F32):
        if not hasattr(self, "_scrmap"):
            self._scrmap = {}
            self._pools = {}
        br = getattr(self, "_br", "x")
        key = (br, name)
        if key in self._scrmap:
            return self._scrmap[key]
        esz = 4 if dt == F32 else 2
        n = 1
        for d_ in shape[1:]:
            n *= d_
        nbytes = n * esz
        cls = 256
        while cls < nbytes:
            cls *= 2
        cnt = self._brcount.setdefault(br, {})
        k = cnt.get(cls, 0)
        cnt[cls] = k + 1
        fam = br if br in ("C", "M") else ""
        pool = self._pools.setdefault((fam, cls), [])
        if k >= len(pool):
            pname = "scr%s%d_%d" % (fam, cls, k)
            pool.append((pname, self.st.enter_context(self.nc.sbuf_tensor(pname, [128, cls // 4], F32))))
        pname, raw = pool[k]
        h = raw if dt == F32 else raw.bitcast(dt)
        ap = h[0:shape[0], 0:n]
        if len(shape) == 3:
            ap = ap.rearrange("p (a b) -> p a b", a=shape[1])
        elif len(shape) == 4:
            ap = ap.rearrange("p (a b c) -> p a b c", a=shape[1], b=shape[2])
        t = Tl(pname, ap)
        self._scrmap[key] = t
        return t

    def gelu(self, dst_ap, dst_w, src_ap, src_r, shape, tag):
        self.act(dst_ap, src_ap, AF.Gelu_apprx_tanh, src_r, dst_w)

    def setup_mem(self, mem_d, wkv_d, memg, pr, ident, identf, pT, kTm, vm, xts, hT, tm, hb):
        self.begin("setup")
        for c in range(2):
            self.dma("sync", xts[c][:], mem_d[c * 128:(c + 1) * 128, :], w=[xts[c]])
        wk = Tl("hT", hT[:, :, 0:256])
        mhT = Tl("hT", hT[:, :, 256:512])
        mh = Tl("tm", tm.t[:, :, :].rearrange("p a b -> p (a b)")[:, 0:2048].rearrange("p (c d) -> p c d", c=2))
        wkv_v = wkv_d.rearrange("(c p) n -> p c n", p=128)
        for c in range(8):
            self.dma("gpsimd", wk[:, c, :], wkv_v[:, c, :], w=[wk])
        for c in range(8):
            self.ts("vector", wk[:, c, :], wk[:, c, :], memg[:, c:c + 1], ALU.mult, [wk, memg], [wk])
        mss = self.scr("m_ss", [128, 2])
        for c in range(2):
            self.act(hb[:], xts[c][:], AF.Square, [xts[c]], [hb, mss], accum_out=mss[:, c:c + 1])
        self.act(mss[:], mss[:], AF.Ln, [mss], [mss], scale=1.0 / 1024, bias=EPS)
        self.act(mss[:], mss[:], AF.Exp, [mss], [mss], scale=-0.5)
        for c in range(2):
            self.ts("vector", mh[:, c, :], xts[c][:], mss[:, c:c + 1], ALU.mult, [xts[c], mss], [mh])
        for c2 in range(2):
            for c in range(8):
                self.tr(pT[:, c * 128:(c + 1) * 128], mh[:, c2, c * 128:(c + 1) * 128], ident[:], [mh, ident], [pT])
            self.cp("vector", mhT[:, :, c2 * 128:(c2 + 1) * 128], pT[:, :].rearrange("p (c t) -> p c t", c=8),
                    [pT], [mhT])
        kvt = self.scr("m_kvt", [128, 2, 256])
        for c2 in range(2):
            p = self.gps()
            for c in range(8):
                self.mm(p[:, 0:256], mhT[:, c, c2 * 128:(c2 + 1) * 128], wk[:, c, :], c == 0, c == 7, [mhT, wk], [p])
            self.cp("vector", kvt[:, c2, :], p[:, 0:256], [p], [kvt])
        self.cp("vector", vm[:], kvt[:, :, 128:256], [kvt], [vm])
        kn = self.scr("m_kn", [128, 2, 128], BF16)
        self.head_rms_tm((kvt[:, :, 0:128], [kvt]), (kn[:], [kn]), (pr[:, PR_MKG:PR_MKG + 128], [pr]), "mk", nt=2)
        for c2 in range(2):
            self.tr(pT[:, c2 * 128:(c2 + 1) * 128], kn[:, c2, :], ident[:], [kn, ident], [pT])
        self.cp("vector", kTm[:], pT[:, 0:256], [pT], [kTm])

    def alloc_state(self, NCH):
        SEQ = self.SEQ
        if "B" in self.branches:
            self.KT = self.sb("KT", [128, SEQ], BF16)
            self.VB = self.sb("VB", [128, NCH, 128], BF16)
            self.Fcol = self.sb("Fcol", [128, 2, NCH])
            self.Fprev = self.sb("Fprev", [128, 1])
            self.memset("vector", self.Fprev[:], 0.0, [self.Fprev])
            self.c64h = [self.sb("c64h%d" % h, [128, 128], BF16) for h in range(2)]
            self.hm = self.sb("hm", [128, 2])
            self.memset("vector", self.hm[:], 0.0, [self.hm])
            for h in range(2):
                hs = slice(h * 64, (h + 1) * 64)
                fs = slice(64, 128) if h == 0 else slice(0, 64)
                self.memset("vector", self.c64h[h][:], 0.0, [self.c64h[h]])
                self.memset("vector", self.c64h[h][fs, :], 1.0 / 64, [self.c64h[h]])
                self.memset("vector", self.hm[hs, h:h + 1], 1.0, [self.hm])
        if "C" in self.branches:
            self.Cst = self.sb("Cst", [128, 128])
            self.Cstb = self.sb("Cstb", [128, 128], BF16)
            self.memset("vector", self.Cst[:], 0.0, [self.Cst])
            self.memset("vector", self.Cstb[:], 0.0, [self.Cstb])
        if "D" in self.branches:
            self.Sst = self.sb("Sst", [128, 64])
            self.Sstb = self.sb("Sstb", [128, 64], BF16)
            self.memset("vector", self.Sst[:], 0.0, [self.Sst])
            self.memset("vector", self.Sstb[:], 0.0, [self.Sstb])

    def branch_A(self, mc, fm, tm, mixo):
        self.begin("A")
        c = self.consts
        pr = c["pr"]
        gu = self.scr("A_gu", [128, 512])
        self.gelu(gu[:], [gu], fm["Au"][:, :], [(fm["Au"], "cur")], [128, 512], "u")
        gv = self.scr("A_gv", [128, 4, 128])
        self.gelu(gv[:], [gv], tm[:, :, 0:128], [tm], [128, 4, 128], "v")
        vn = self.scr("A_vn", [128, 4, 128], BF16)
        self.head_rms_tm((gv[:], [gv]), (vn[:], [vn]), (pr[:, PR_SGU_G:PR_SGU_G + 128], [pr]), "av")
        p = self.gps()
        for tt in range(4):
            for h in range(2):
                self.mm(p[h * 64:(h + 1) * 64, tt * 128:(tt + 1) * 128], vn[:, tt, h * 64:(h + 1) * 64],
                        c["wsT"][:, h, :], True, True, [vn, c["wsT"]], [p])
        ya = self.scr("A_ya", [128, 512])
        self.tt("vector", ya[:].rearrange("p (a t) -> p a t", a=4), p[:].rearrange("p (a t) -> p a t", a=4),
                pr[:, None, PR_SGUB:PR_SGUB + 128].broadcast_to([128, 4, 128]), ALU.add, [p, pr], [ya])
        self.tt("gpsimd", ya[:], ya[:], gu[:], ALU.mult, [ya, gu], [ya])
        self.tt("vector", mixo[:, 0, :], ya[:], fm["Az"][:, :], ALU.mult, [ya, fm["Az"]], [(mixo, 0)])

    def branch_M(self, mc, fm, tm, mixo):
        self.begin("M")
        c = self.consts
        pT = c["pT"]
        qn = self.scr("M_qn", [128, 4, 128], BF16)
        self.head_rms_tm((tm[:, :, 640:768], [tm]), (qn[:], [qn]), (c["gmq8"][:], [c["gmq8"]]), "mq")
        yield
        qT = self.scr("M_qT", [128, 512], BF16)
        for tt in range(4):
            self.tr(pT[:, tt * 128:(tt + 1) * 128], qn[:, tt, :], c["ident"][:], [qn, c["ident"]], [pT])
        self.cp("vector", qT[:], pT[:, 0:512], [pT], [qT])
        yield
        pn = self.pacc[0]
        pd = self.pacc[1]
        E = self.scr("M_E", [128, 512], BF16)
        for h in range(2):
            hs = slice(h * 64, (h + 1) * 64)
            for mcx in range(2):
                ps_ = self.gps()
                self.mm(ps_[:], c["kTm"][hs, mcx * 128:(mcx + 1) * 128], qT[hs, :], True, True, [c["kTm"], qT], [ps_])
                self.act(E[:], ps_[:], AF.Exp, [ps_], [E])
                yield
                self.mm(pn[hs, :], c["vm"][:, mcx, hs], E[:], mcx == 0, mcx == 1, [c["vm"], E], [(pn, h)])
                self.mm(pd[hs, :], c["ones_bf"][:, 0:64], E[:], mcx == 0, mcx == 1, [c["ones_bf"], E], [(pd, h)])
                yield
        rd = self.scr("M_rd", [128, 512])
        self.recip(rd[:], pd[:], [pd], [rd])
        self.tt("vector", rd[:], pn[:], rd[:], ALU.mult, [pn, rd], [rd])
        yield
        self.tt("gpsimd", mixo[:, 4, :], rd[:], fm["Mz"][:, :], ALU.mult, [rd, fm["Mz"]], [(mixo, 4)])

    def branch_B(self, mc, fm, tm, mixo):
        self.begin("B")
        c = self.consts
        pT = c["pT"]
        pr = c["pr"]
        G = mc
        if getattr(self, "bstage", 9) < 1:
            return
        qn = self.scr("B_qn", [128, 4, 128], BF16)
        kn = self.scr("B_kn", [128, 4, 128], BF16)
        self.head_rms_tm((tm[:, :, 128:256], [tm]), (qn[:], [qn]), (c["gq8"][:], [c["gq8"]]), "bq")
        self.head_rms_tm((tm[:, :, 256:384], [tm]), (kn[:], [kn]), (pr[:, PR_BKG:PR_BKG + 128], [pr]), "bk")
        QT = self.scr("B_QT", [128, 512], BF16)
        if getattr(self, "bstage", 9) < 0.3:
            return
        for tt in range(4):
            self.tr(pT[:, tt * 128:(tt + 1) * 128], qn[:, tt, :], c["ident"][:], [qn, c["ident"]], [pT])
        for tt in range(4):
            self.tr(pT[:, 512 + tt * 128:512 + (tt + 1) * 128], kn[:, tt, :], c["ident"][:], [kn, c["ident"]], [pT])
        self.cp("vector", QT[:], pT[:, 0:512], [pT], [QT])
        self.cp("scalar", self.KT[:, G * 512:(G + 1) * 512], pT[:, 512:1024], [pT], [(self.KT, G)])
        QTm = [self.scr("B_QTm%d" % h, [128, 512], BF16) for h in range(2)]
        for h in range(2):
            self.ts("gpsimd", QTm[h][:], QT[:], self.hm[:, h:h + 1], ALU.mult, [QT, self.hm], [QTm[h]])
        for tt in range(4):
            self.cp("gpsimd", self.VB[:, G * 4 + tt, :], tm[:, tt, 384:512], [tm], [(self.VB, G * 4 + tt)])
        e1 = self.scr("B_e1", [128, 512])
        self.act(e1[:], fm["Bf"][:, :], AF.Exp, [(fm["Bf"], "cur")], [e1], scale=-1.0, bias=c["nfbB"][:, 0:1])
        self.act(e1[:], e1[:], AF.Ln, [e1], [e1], bias=1.0)
        Lc = self.scr("B_Lc", [128, 512])
        for q4 in range(4):
            qs = slice(q4 * 128, (q4 + 1) * 128)
            ini = self.Fprev[:, 0:1] if q4 == 0 else Lc[:, q4 * 128 - 1:q4 * 128]
            self.op("vector", lambda e, qs=qs, ini=ini: e.tensor_tensor_scan(
                out=Lc[:, qs], data0=c["onesf"][:, 0:128], data1=e1[:, qs], initial=ini,
                op0=ALU.mult, op1=ALU.add), [c["onesf"], e1, self.Fprev, Lc], [Lc])
        pcg = self.gps()
        for h in range(2):
            fs = slice(64, 128) if h == 0 else slice(0, 64)
            for j in range(4):
                self.mm(pcg[:, h * 8 + j:h * 8 + j + 1], Lc[fs, j * 128:(j + 1) * 128], c["c64f"][fs, 0:1],
                        True, True, [Lc, c["c64f"]], [pcg])
            for hf in range(2):
                col = hf * 256 + 127
                self.mm(pcg[:, 16 + h * 2 + hf:17 + h * 2 + hf], c["c64f"][fs, :], Lc[fs, col:col + 1], True, True,
                        [c["c64f"], Lc], [pcg])
        cg = self.scr("B_cg", [128, 4])
        for h in range(2):
            self.cp("vector", self.Fcol[:, h, G * 4:G * 4 + 4], pcg[:, h * 8:h * 8 + 4], [pcg], [(self.Fcol, G)])
        self.cp("vector", cg[:], pcg[:, 16:20], [pcg], [cg])
        self.cp("vector", self.Fprev[:], Lc[:, 511:512], [Lc], [self.Fprev])
        nkb = 4 * G + 4
        bias = self.scr("B_bias", [128, 4, self.SEQ // 128])
        for h in range(2):
            for hf in range(2):
                self.ts("vector", bias[:, h * 2 + hf, 0:nkb], self.Fcol[:, h, 0:nkb],
                        cg[:, h * 2 + hf:h * 2 + hf + 1], ALU.subtract, [self.Fcol, cg], [bias])
        pOff, pDg = c["pNum"], c["pDen"]
        Es = [self.scr("B_E%d" % i, [128, 512], BF16) for i in range(4)]
        EaccO = self.scr("B_EaccO", [128, 512])
        EaccD = self.scr("B_EaccD", [128, 512])
        EaccP = self.scr("B_EaccP", [128, 512])
        ndg = self.scr("B_ndg", [128, 512])
        rd = self.scr("B_rd", [128, 512])
        yb = self.scr("B_yb", [128, 512])
        sc = self.scr("B_sc", [128, 2])
        for h in range(2):
            self.tt("vector", sc[:, h:h + 1], cg[:, h * 2:h * 2 + 1], cg[:, h * 2 + 1:h * 2 + 2], ALU.subtract,
                    [cg], [sc])
        self.act(sc[:], sc[:], AF.Exp, [sc], [sc])
        noff = 4 * G
        pairs = [(h, kb) for h in range(2) for kb in range(nkb)]
        psl = {}

        def score(i):
            h, kb = pairs[i]
            j = kb - 4 * G
            c0 = 0 if j <= 0 else j * 128
            ps_ = self.gps()
            self.mm(ps_[:, c0:512], self.KT[:, kb * 128:(kb + 1) * 128], QTm[h][:, c0:512], True, True,
                    [self.KT, QTm[h]], [ps_])
            psl[i] = ps_
        score(0)
        for i, (h, kb) in enumerate(pairs):
            if i + 1 < len(pairs):
                score(i + 1)
            hs = slice(h * 64, (h + 1) * 64)
            j = kb - 4 * G
            c0 = 0 if j <= 0 else j * 128
            ps_ = psl.pop(i)
            E = Es[i % 4]
            if j < 0:
                self.act(E[:, :], ps_[:, :], AF.Exp, [ps_, bias], [E], bias=bias[:, h * 2, kb:kb + 1])
                self.mm(pOff[:, :], self.VB[:, kb, :], E[:, :], kb == 0, kb == noff - 1, [self.VB, E], [pOff])
                if kb == 0:
                    self.cp("vector", EaccO[:], E[:], [E], [EaccO])
                elif kb == 1:
                    self.cp("gpsimd", EaccP[:], E[:], [E], [EaccP])
                elif kb % 2 == 0:
                    self.tt("vector", EaccO[:], EaccO[:], E[:], ALU.add, [EaccO, E], [EaccO])
                else:
                    self.tt("gpsimd", EaccP[:], EaccP[:], E[:], ALU.add, [EaccP, E], [EaccP])
            else:
                for hf in range(2):
                    a_, b_ = max(c0, hf * 256), (hf + 1) * 256
                    if a_ >= b_:
                        continue
                    self.act(E[:, a_:b_], ps_[:, a_:b_], AF.Exp, [ps_, bias], [E],
                             bias=bias[:, h * 2 + hf, kb:kb + 1])
                self.tt("gpsimd", E[:, c0:c0 + 128], E[:, c0:c0 + 128], c["m_ge"][:], ALU.mult,
                        [E, c["m_ge"]], [E])
                self.mm(pDg[:, c0:512], self.VB[:, kb, :], E[:, c0:512], j == 0, kb == nkb - 1,
                        [self.VB, E], [pDg])
                if j == 0:
                    self.cp("vector", EaccD[:], E[:], [E], [EaccD])
                else:
                    self.tt("vector", EaccD[:, c0:512], EaccD[:, c0:512], E[:, c0:512], ALU.add,
                            [EaccD, E], [EaccD])
            if kb == nkb - 1:
                if noff > 0:
                    self.tt("gpsimd", EaccO[:], EaccO[:], EaccP[:], ALU.add, [EaccO, EaccP], [EaccO])
                    self.tt("vector", EaccD[:, 0:256], EaccD[:, 0:256], EaccO[:, 0:256], ALU.add,
                            [EaccD, EaccO], [EaccD])
                    self.stt(EaccD[:, 256:512], EaccO[:, 256:512], sc[:, h:h + 1], EaccD[:, 256:512], ALU.mult,
                             ALU.add, [EaccO, sc, EaccD], [EaccD])
                pd_ = self.gps()
                self.mm(pd_[hs, :], c["onesf"][:, 0:64], EaccD[:], True, True, [c["onesf"], EaccD], [pd_])
                self.recip(rd[hs, :], pd_[hs, :], [pd_], [rd])
                self.cp("scalar", ndg[hs, :], pDg[hs, :], [pDg], [ndg])
                if noff > 0:
                    self.tt("vector", ndg[hs, 0:256], ndg[hs, 0:256], pOff[hs, 0:256], ALU.add, [ndg, pOff], [ndg])
                    self.stt(ndg[hs, 256:512], pOff[hs, 256:512], sc[hs, h:h + 1], ndg[hs, 256:512], ALU.mult,
                             ALU.add, [pOff, sc, ndg], [ndg])
                self.tt("vector", yb[hs, :], ndg[hs, :], rd[hs, :], ALU.mult, [ndg, rd], [yb])
        self.tt("gpsimd", mixo[:, 1, :], yb[:], fm["Bz"][:, :], ALU.mult, [yb, fm["Bz"]], [(mixo, 1)])

    def branch_C(self, mc, fm, tm, mixo):
        self.begin("C")
        c = self.consts
        pc = c["pc"]
        pT = c["pT"]
        qc = self.scr("C_qc", [128, 512], BF16)
        kc = self.scr("C_kc", [128, 512], BF16)
        for g, dst, wn, bn in (("Cq", qc, "cq", "cqb"), ("Ck", kc, "ck", "ckb")):
            x = fm[g]
            acc = self.scr("C_acc" + g, [128, 512])
            self.ts("vector", acc[:], x[:, 0:512], pc[:, PC[wn + "0"]:PC[wn + "0"] + 1], ALU.mult, [x, pc], [acc],
                    s2=pc[:, PC[bn]:PC[bn] + 1], op1=ALU.add)
            for j in range(1, 4):
                self.stt(acc[:], x[:, j:j + 512], pc[:, PC[wn + str(j)]:PC[wn + str(j)] + 1], acc[:], ALU.mult,
                         ALU.add, [x, pc, acc], [acc])
            if g == "Cq":
                self.act(dst[:], acc[:], AF.Silu, [acc], [dst])
            else:
                self.act(acc[:], acc[:], AF.Silu, [acc], [acc])
                self.ts("vector", dst[:], acc[:], 0.125, ALU.mult, [acc], [dst])
        yield
        Lb = []
        for h in range(2):
            e1 = fm["Cf%d" % h]
            self.act(e1[:], fm["Cf%d" % h][:, :], AF.Exp, [(fm["Cf%d" % h], "cur")], [e1], scale=-1.0,
                     bias=c["nfbC"][:, h:h + 1])
            self.act(e1[:], e1[:], AF.Ln, [e1], [e1], bias=1.0)
            L = self.scr("C_Lb%d" % h, [128, 512])
            for ch in range(4):
                cs = slice(ch * 128, (ch + 1) * 128)
                self.op("vector", lambda e, L=L, e1=e1, cs=cs: e.tensor_tensor_scan(
                    out=L[:, cs], data0=c["onesf"][:, 0:128], data1=e1[:, cs], initial=0.0,
                    op0=ALU.mult, op1=ALU.add), [c["onesf"], e1], [L])
            Lb.append(L)
        ilog = fm["Ci"]
        self.ts("vector", ilog[:], fm["Ci"][:, :], pc[:, PC["ib_C"]:PC["ib_C"] + 1], ALU.add,
                [(fm["Ci"], "cur"), pc], [ilog])
        yield
        so = fm["Co"]
        sz = fm["Cz"]
        if not hasattr(self, "vaug"):
            self.vaug = [self.sb("C_vaug%d" % h, [128, 128], BF16) for h in range(2)]
            for h in range(2):
                self.memset("vector", self.vaug[h][:], 1.0, [self.vaug[h]])
        vaug = self.vaug
        for ch in range(4):
            cs = slice(ch * 128, (ch + 1) * 128)
            pcol = self.gps()
            for h in range(2):
                hs = slice(h * 64, (h + 1) * 64)
                self.mm(pcol[:, h:h + 1], Lb[h][0:64, cs], c["c64f"][0:64, 0:1], True, True, [Lb[h], c["c64f"]], [pcol])
                self.mm(pcol[:, 2 + h:3 + h], ilog[hs, cs], c["c64f"][hs, 0:1], True, True, [ilog, c["c64f"]], [pcol])
            col = self.scr("C_col", [128, 8])
            self.cp("vector", col[:, 0:4], pcol[:, 0:4], [pcol], [col])
            yield
            self.tt("vector", col[:, 4:6], col[:, 0:2], col[:, 2:4], ALU.add, [col], [col])
            for h in range(2):
                self.ts("vector", col[:, 6 + h:7 + h], col[:, 4 + h:5 + h], Lb[h][:, ch * 128 + 127:ch * 128 + 128],
                        ALU.subtract, [col, Lb[h]], [col])
            wcol = self.scr("C_wcol", [128, 2])
            self.act(wcol[:], col[:, 6:8], AF.Exp, [col], [wcol])
            yield
            eg = self.scr("C_eg", [128, 2])
            for h in range(2):
                self.act(eg[:, h:h + 1], Lb[h][:, ch * 128 + 127:ch * 128 + 128], AF.Exp, [Lb[h]], [eg], scale=-1.0)
            ebq = self.scr("C_ebq", [128, 128])
            for h in range(2):
                hs = slice(h * 64, (h + 1) * 64)
                self.act(ebq[hs, :], Lb[h][hs, cs], AF.Exp, [Lb[h]], [ebq], scale=-1.0)
            qp = self.scr("C_qp", [128, 128], BF16)
            self.tt("vector", qp[:], qc[:, cs], ebq[:], ALU.mult, [qc, ebq], [qp])
            yield
            for h in range(2):
                hs = slice(h * 64, (h + 1) * 64)
                self.cp("gpsimd", vaug[h][:, 0:64], tm[:, ch, 512 + h * 64:512 + (h + 1) * 64], [tm], [vaug[h]])
            pS = self.gps()
            P = []
            for h in range(2):
                hs = slice(h * 64, (h + 1) * 64)
                self.mm(pS[:, h * 128:(h + 1) * 128], kc[hs, cs], qc[hs, cs], True, True, [kc, qc], [pS])
                D = self.scr("C_D%d" % h, [128, 128])
                self.ts("vector", D[:], Lb[h][:, cs], col[:, h:h + 1], ALU.subtract, [Lb[h], col], [D], s2=0.0,
                        op1=ALU.max)
                self.act(D[:], D[:], AF.Exp, [D, col], [D], scale=-1.0, bias=col[:, 2 + h:3 + h])
                self.tt("gpsimd", D[:], D[:], c["m_ge"][:], ALU.mult, [D, c["m_ge"]], [D])
                Ph = self.scr("C_P%d" % h, [128, 128], BF16)
                self.tt("vector", Ph[:], pS[:, h * 128:(h + 1) * 128], D[:], ALU.mult, [pS, D], [Ph])
                P.append(Ph)
            yield
            pnd = self.gps()
            for h in range(2):
                hs = slice(h * 64, (h + 1) * 64)
                self.mm(pnd[hs, 0:128], vaug[h][:, 0:64], P[h][:], True, False, [vaug[h], P[h]], [pnd])
                self.mm(pnd[hs, 0:128], self.Cstb[hs, 0:64], qp[hs, :], False, True, [self.Cstb, qp], [pnd])
            for h in range(2):
                hs = slice(h * 64, (h + 1) * 64)
                self.mm(pnd[hs, 128:256], c["ones_bf"][:, 0:64], P[h][:], True, False, [c["ones_bf"], P[h]], [pnd])
                self.mm(pnd[hs, 128:256], self.Cstb[hs, 64:128], qp[hs, :], False, True, [self.Cstb, qp], [pnd])
            dn = self.scr("C_dn", [128, 128])
            self.act(dn[:], pnd[:, 128:256], AF.Abs, [pnd], [dn])
            self.ts("vector", dn[:], dn[:], 1.0, ALU.max, [dn], [dn])
            self.recip(dn[:], dn[:], [dn], [dn])
            ho = self.scr("C_ho", [128, 128])
            self.tt("vector", ho[:], pnd[:, 0:128], dn[:], ALU.mult, [pnd, dn], [ho])
            yield
            self.tt("gpsimd", ho[:], ho[:], so[:, cs], ALU.mult, [ho, so], [ho])
            yield
            sq = self.scr("C_sq", [128, 128], BF16)
            self.tt("gpsimd", sq[:], ho[:], ho[:], ALU.mult, [ho], [sq])
            pq = self.gps()
            self.mm(pq[:, 0:128], c["blk"][:], sq[:], True, True, [c["blk"], sq], [pq])
            rs = self.scr("C_rs", [128, 128])
            self.act(rs[:], pq[:, 0:128], AF.Ln, [pq], [rs], scale=1.0 / 64, bias=EPS)
            self.act(rs[:], rs[:], AF.Exp, [rs], [rs], scale=-0.5)
            self.stt(ho[:], ho[:], pc[:, PC["out_g"]:PC["out_g"] + 1], rs[:], ALU.mult, ALU.mult, [ho, pc, rs], [ho])
            self.tt("vector", mixo[:, 2, cs], ho[:], sz[:, cs], ALU.mult, [ho, sz], [(mixo, 2)])
            yield
            self.tr(pT[:, 0:128], kc[:, cs], c["ident"][:], [kc, c["ident"]], [pT])
            khat = self.scr("C_khat", [128, 128], BF16)
            for h in range(2):
                hs = slice(h * 64, (h + 1) * 64)
                self.ts("vector", khat[:, hs], pT[:, h * 64:(h + 1) * 64], wcol[:, h:h + 1], ALU.mult, [pT, wcol],
                        [khat])
            yield
            pC = self.gps()
            for h in range(2):
                hs = slice(h * 64, (h + 1) * 64)
                self.mm(pC[hs, 0:128], khat[:, hs], vaug[h][:], True, True, [khat, vaug[h]], [pC])
            for h in range(2):
                hs = slice(h * 64, (h + 1) * 64)
                self.stt(self.Cst[hs, :], self.Cst[hs, :], eg[hs, h:h + 1], pC[hs, 0:128], ALU.mult, ALU.add,
                         [self.Cst, eg, pC], [self.Cst])
            self.cp("vector", self.Cstb[:], self.Cst[:], [self.Cst], [self.Cstb])
            yield

    def branch_D(self, mc, fm, tm, mixo):
        self.begin("D")
        c = self.consts
        pc = c["pc"]
        pT = c["pT"]
        omm = c["omm"]
        if not hasattr(self, "m_gt"):
            self.m_gt = self.sb("m_gt", [128, 128], BF16)
            self.mN_gt = self.sb("mN_gt", [128, 128], BF16)
            self.memset("vector", self.m_gt[:], 1.0, [self.m_gt])
            self.asel(self.m_gt[:], self.m_gt[:], ALU.is_gt, [self.m_gt], [self.m_gt])
            self.memset("vector", self.mN_gt[:], 1.0, [self.mN_gt])
            self.asel(self.mN_gt[:], self.mN_gt[:], ALU.is_gt, [self.mN_gt], [self.mN_gt], cm=1, pat=[[-1, 128]])
        m_gt, mN_gt, m_ge = self.m_gt, self.mN_gt, c["m_ge"]

        def shift(g, idx, rows, name):
            x = fm[g]
            t = self.scr("D_sh_" + name, [128, 512])
            self.ts("vector", t[0:rows, :], x[0:rows, 1:513], omm[0:rows, idx:idx + 1], ALU.mult, [x, omm], [t])
            self.stt(t[0:rows, :], x[0:rows, 0:512], pc[0:rows, PC["mu_r"] + idx:PC["mu_r"] + idx + 1], t[0:rows, :],
                     ALU.mult, ALU.add, [x, pc, t], [t])
            return t
        rs = shift("Dr", 0, 128, "r")
        ks = shift("Dk", 1, 128, "k")
        vs = shift("Dv", 2, 128, "v")
        zs = shift("Dz", 3, 128, "z")
        ws = shift("Dw", 4, 16, "w")
        as_ = shift("Da", 5, 16, "a")
        self.act(ws[0:16, :], ws[0:16, :], AF.Tanh, [ws], [ws])
        pw = self.gps()
        self.mm(pw[:], c["w2s"][:], ws[0:16, :], True, True, [c["w2s"], ws], [pw])
        lw = ws
        self.act(lw[:], pw[:], AF.Sigmoid, [pw, pc], [lw], bias=pc[:, PC["w0"]:PC["w0"] + 1])
        self.ts("gpsimd", lw[:], lw[:], -0.6065306597126334, ALU.mult, [lw], [lw])
        pa = self.gps()
        self.mm(pa[:], c["a2s"][:], as_[0:16, :], True, True, [c["a2s"], as_], [pa])
        aa = as_
        self.act(aa[:], pa[:], AF.Sigmoid, [pa, pc], [aa], bias=pc[:, PC["a0"]:PC["a0"] + 1])
        sz = zs
        self.act(sz[:], zs[:], AF.Silu, [zs], [sz])
        yield
        vb = self.scr("D_vb", [128, 512], BF16)
        self.cp("gpsimd", vb[:], vs[:], [vs], [vb])
        kx = self.scr("D_kx", [128, 512])
        self.ts("vector", kx[:], ks[:], pc[:, PC["k_k"]:PC["k_k"] + 1], ALU.mult, [ks, pc], [kx])
        sqb = self.scr("D_sqb", [128, 512], BF16)
        self.tt("gpsimd", sqb[:], kx[:], kx[:], ALU.mult, [kx], [sqb])
        pq = self.gps()
        self.mm(pq[:], c["blk"][:], sqb[:], True, True, [c["blk"], sqb], [pq])
        rn = self.scr("D_rn", [128, 512])
        self.ts("vector", rn[:], pq[:], 1e-18, ALU.max, [pq], [rn])
        self.act(rn[:], rn[:], AF.Ln, [rn], [rn])
        self.act(rn[:], rn[:], AF.Exp, [rn], [rn], scale=-0.5)
        kk = kx
        self.tt("vector", kk[:], kx[:], rn[:], ALU.mult, [kx, rn], [kk])
        k2 = rn
        self.ts("vector", k2[:], aa[:], -1.0, ALU.add, [aa, pc], [k2], s2=pc[:, PC["k_a"]:PC["k_a"] + 1], op1=ALU.mult)
        self.stt(k2[:], k2[:], 1.0, ks[:], ALU.add, ALU.mult, [k2, ks], [k2])
        bv = ks
        self.tt("gpsimd", bv[:], kk[:], aa[:], ALU.mult, [kk, aa], [bv])
        yield
        rk = self.scr("D_rk", [128, 512], BF16)
        self.stt(rk[:], rs[:], pc[:, PC["r_k"]:PC["r_k"] + 1], k2[:], ALU.mult, ALU.mult, [rs, pc, k2], [rk])
        pb = self.gps()
        self.mm(pb[:], c["blk"][:], rk[:], True, True, [c["blk"], rk], [pb])
        bon = vs
        self.tt("vector", bon[:], pb[:], vs[:], ALU.mult, [pb, vs], [bon])
        cl = self.scr("D_cl", [128, 512])
        for ch in range(4):
            cs = slice(ch * 128, (ch + 1) * 128)
            self.op("vector", lambda e, cs=cs: e.tensor_tensor_scan(
                out=cl[:, cs], data0=c["onesf"][:, 0:128], data1=lw[:, cs], initial=0.0,
                op0=ALU.mult, op1=ALU.add), [c["onesf"], lw], [cl])
        Ecl = self.scr("D_Ecl", [128, 512])
        Encl = self.scr("D_Encl", [128, 512])
        self.act(Ecl[:], cl[:], AF.Exp, [cl], [Ecl])
        self.act(Encl[:], cl[:], AF.Exp, [cl], [Encl], scale=-1.0)
        Ecx = lw
        self.tt("gpsimd", lw[:], cl[:], lw[:], ALU.subtract, [cl, lw], [lw])
        self.act(Ecx[:], lw[:], AF.Exp, [lw], [Ecx])
        yield
        clT = self.scr("D_clT", [128, 4])
        self.cp("vector", clT[:], cl[:].rearrange("p (a t) -> p a t", a=4)[:, :, 127], [cl], [clT])
        Eh = cl
        for ch in range(4):
            cs = slice(ch * 128, (ch + 1) * 128)
            self.act(Eh[:, cs], cl[:, cs], AF.Exp, [cl, clT], [Eh], scale=-1.0, bias=clT[:, ch:ch + 1])
        KR = self.scr("D_KR", [128, 4, 256], BF16)
        kt = self.scr("D_kt", [128, 512], BF16)
        bt = self.scr("D_bt", [128, 512], BF16)
        khat = self.scr("D_khat", [128, 512], BF16)
        nbh = self.scr("D_nbh", [128, 512], BF16)
        self.tt("vector", KR[:, :, 0:128], kk[:].rearrange("p (a t) -> p a t", a=4),
                Ecx[:].rearrange("p (a t) -> p a t", a=4), ALU.mult, [kk, Ecx], [KR])
        self.tt("gpsimd", KR[:, :, 128:256], rs[:].rearrange("p (a t) -> p a t", a=4),
                Ecl[:].rearrange("p (a t) -> p a t", a=4), ALU.mult, [rs, Ecl], [KR])
        self.tt("vector", kt[:], k2[:], Encl[:], ALU.mult, [k2, Encl], [kt])
        self.tt("gpsimd", bt[:], bv[:], Encl[:], ALU.mult, [bv, Encl], [bt])
        self.tt("vector", khat[:], k2[:], Eh[:], ALU.mult, [k2, Eh], [khat])
        self.stt(nbh[:], bv[:], -1.0, Eh[:], ALU.mult, ALU.mult, [bv, Eh], [nbh])
        yraw = kx
        yield
        for ch in range(4):
            cs = slice(ch * 128, (ch + 1) * 128)
            self.tr(pT[:, 0:128], KR[:, ch, 0:128], c["ident"][:], [KR, c["ident"]], [pT])
            self.tr(pT[:, 128:256], vb[:, cs], c["ident"][:], [vb, c["ident"]], [pT])
            self.tr(pT[:, 256:384], khat[:, cs], c["ident"][:], [khat, c["ident"]], [pT])
            self.tr(pT[:, 384:512], nbh[:, cs], c["ident"][:], [nbh, c["ident"]], [pT])
            TMs = self.scr("D_TMs", [128, 4, 128], BF16)
            self.cp("scalar", TMs[:], pT[:, 0:512].rearrange("p (a t) -> p a t", a=4), [pT], [TMs])
            yield
            rpT = self.scr("D_rpT", [128, 128], BF16)
            GT = self.scr("D_GT", [128, 64], BF16)
            Hs = self.scr("D_Hs", [128, 64])
            py = c["pNum"]
            pHS = c["pDen"]
            HS = [slice(0, 64), slice(64, 128)]
            LT, nQbT, LkT, QkT, LN, R, R2 = {}, {}, {}, {}, {}, {}, {}
            for h in range(2):
                hs = HS[h]
                p1 = self.gps()
                self.mm(p1[:, 0:256], bt[hs, cs], KR[hs, ch, :], True, True, [bt, KR], [p1])
                self.mm(p1[:, 256:512], kt[hs, cs], KR[hs, ch, :], True, True, [kt, KR], [p1])
                LT[h] = self.scr("D_LT%d" % h, [128, 128], BF16)
                nQbT[h] = self.scr("D_nQbT%d" % h, [128, 128], BF16)
                LkT[h] = self.scr("D_LkT%d" % h, [128, 128], BF16)
                QkT[h] = self.scr("D_QkT%d" % h, [128, 128], BF16)
                self.tt("vector", LT[h][:], p1[:, 0:128], m_gt[:], ALU.mult, [p1, m_gt], [LT[h]])
                self.tt("vector", LkT[h][:], p1[:, 256:384], m_gt[:], ALU.mult, [p1, m_gt], [LkT[h]])
                self.stt(nQbT[h][:], p1[:, 128:256], -1.0, m_ge[:], ALU.mult, ALU.mult, [p1, m_ge], [nQbT[h]])
                self.tt("vector", QkT[h][:], p1[:, 384:512], m_ge[:], ALU.mult, [p1, m_ge], [QkT[h]])
                yield
            for h in range(2):
                hs = HS[h]
                p2 = self.gps()
                self.mm(p2[:, 0:128], KR[hs, ch, 0:128], bt[hs, cs], True, True, [KR, bt], [p2])
                self.mm(p2[:, 128:192], LkT[h][:], TMs[:, 1, hs], True, True, [LkT[h], TMs], [p2])
                LN[h] = self.scr("D_LN%d" % h, [128, 128], BF16)
                self.tt("vector", LN[h][:], p2[:, 0:128], mN_gt[:], ALU.mult, [p2, mN_gt], [LN[h]])
                R[h] = self.scr("D_R%d" % h, [128, 128], BF16)
                self.cp("gpsimd", R[h][:, 0:64], TMs[:, 0, hs], [TMs], [R[h]])
                self.cp("scalar", R[h][:, 64:128], p2[:, 128:192], [p2], [R[h]])
                yield
            Rc, Rn, PTc, PNc = {}, {}, {}, {}
            for h in range(2):
                pr_ = self.gps()
                self.mm(pr_[:, 0:128], LT[h][:], R[h][:], True, True, [LT[h], R[h]], [pr_])
                R2[h] = self.scr("D_Rb%d" % h, [128, 128], BF16)
                self.tt("vector", R2[h][:], R[h][:], pr_[:, 0:128], ALU.subtract, [R[h], pr_], [R2[h]])
                Rc[h], Rn[h] = R2[h], R[h]
                PTc[h], PNc[h] = LT[h], LN[h]
            yield
            for j in range(1, 7):
                PTn, PNn = {}, {}
                for h in range(2):
                    PTn[h] = self.scr("D_PT%d_%d" % (h, j % 2), [128, 128], BF16)
                    PNn[h] = self.scr("D_PN%d_%d" % (h, j % 2), [128, 128], BF16)
                    pp = self.gps()
                    self.mm(pp[:, 0:128], PNc[h][:], PTc[h][:], True, True, [PNc[h], PTc[h]], [pp])
                    if j < 6:
                        self.mm(pp[:, 128:256], PTc[h][:], PNc[h][:], True, True, [PNc[h], PTc[h]], [pp])
                        self.cp("scalar", PTn[h][:], pp[:, 0:128], [pp], [PTn[h]])
                        self.cp("vector", PNn[h][:], pp[:, 128:256], [pp], [PNn[h]])
                    else:
                        self.cp("scalar", PTn[h][:], pp[:, 0:128], [pp], [PTn[h]])
                yield
                for h in range(2):
                    pr_ = self.gps()
                    self.mm(pr_[:, 0:128], PTn[h][:], Rc[h][:], True, True, [PTn[h], Rc[h]], [pr_])
                    self.tt("vector", Rn[h][:], Rc[h][:], pr_[:, 0:128], ALU.add, [Rc[h], pr_], [Rn[h]])
                    Rc[h], Rn[h] = Rn[h], Rc[h]
                    PTc[h], PNc[h] = PTn[h], PNn[h]
                yield
            for h in range(2):
                hs = HS[h]
                Rch = Rc[h]
                pr_ = self.gps()
                self.mm(pr_[hs, 0:128], Rch[:, 0:64], nQbT[h][:], True, True, [Rch, nQbT[h]], [pr_])
                self.tt("vector", rpT[hs, :], pr_[hs, 0:128], KR[hs, ch, 128:256], ALU.add, [pr_, KR], [rpT])
                self.mm(py[hs, 0:128], TMs[:, 1, hs], QkT[h][:], True, False, [TMs, QkT[h]], [(py, h)])
                self.mm(py[hs, 0:128], Rch[:, 64:128], nQbT[h][:], False, False, [Rch, nQbT[h]], [(py, h)])
                self.mm(py[hs, 0:128], self.Sstb[hs, :], rpT[hs, :], False, True, [self.Sstb, rpT], [(py, h)])
                pg = self.gps()
                self.mm(pg[hs, 0:64], Rch[:, 0:64], TMs[:, 3, hs], True, True, [Rch, TMs], [pg])
                self.stt(GT[hs, :], c["identf"][hs, h * 64:(h + 1) * 64], Ecl[hs, ch * 128 + 127:ch * 128 + 128],
                         pg[hs, 0:64], ALU.mult, ALU.add, [c["identf"], Ecl, pg], [GT])
                self.mm(pHS[hs, 0:64], TMs[:, 2, hs], TMs[:, 1, hs], True, False, [TMs], [(pHS, h)])
                self.mm(pHS[hs, 0:64], TMs[:, 3, hs], Rch[:, 64:128], False, True, [TMs, Rch], [(pHS, h)])
                self.cp("scalar", Hs[hs, :], pHS[hs, 0:64], [(pHS, h)], [Hs])
                self.mm(pHS[hs, 64:128], GT[hs, :], self.Sstb[hs, :], True, True, [GT, self.Sstb], [(pHS, h)])
                self.tt("vector", self.Sst[hs, :], pHS[hs, 64:128], Hs[hs, :], ALU.add, [(pHS, h), Hs], [self.Sst])
                yield
            self.cp("vector", self.Sstb[:], self.Sst[:], [self.Sst], [self.Sstb])
            self.cp("scalar", yraw[:, cs], py[:, 0:128], [py], [yraw])
        ysq = sqb
        self.tt("gpsimd", ysq[:], yraw[:], yraw[:], ALU.mult, [yraw], [ysq])
        pq2 = self.gps()
        self.mm(pq2[:], c["blk"][:], ysq[:], True, True, [c["blk"], ysq], [pq2])
        rs2 = rn
        self.act(rs2[:], pq2[:], AF.Ln, [pq2], [rs2], scale=1.0 / 64, bias=EPS)
        self.act(rs2[:], rs2[:], AF.Exp, [rs2], [rs2], scale=-0.5)
        self.stt(yraw[:], yraw[:], pc[:, PC["ln_g"]:PC["ln_g"] + 1], rs2[:], ALU.mult, ALU.mult, [yraw, pc, rs2],
                 [yraw])
        self.tt("vector", yraw[:], yraw[:], bon[:], ALU.add, [yraw, bon], [yraw])
        self.tt("gpsimd", mixo[:, 3, :], yraw[:], sz[:], ALU.mult, [yraw, sz], [(mixo, 3)])


def build_final(SEQ):
    Bd = Builder(SEQ, False)
    nc = Bd.nc
    x1 = Bd.dram("x1", [SEQ, 1024], F32, "ExternalInput")
    mp = Bd.dram("mprev", [1280, SEQ], BF16, "ExternalInput")
    wout_d = Bd.dram("wout", [1280, 1024], F32, "ExternalInput")
    out = Bd.dram("out", [SEQ, 1024], F32, "ExternalOutput")
    pP = [Bd.psum("pP0", [128, 512]), Bd.psum("pP1", [128, 512])]
    wo = Bd.sb("wo", [128, 10, 1024], BF16)
    wov = wout_d.rearrange("(c p) n -> p c n", p=128)
    for c in range(10):
        Bd.dma("gpsimd", wo[:, c, :], wov[:, c, :], w=[(wo, c)])
    xts = [Bd.sb("xt%d" % i, [128, 4, 1024]) for i in range(2)]
    mpvs = [Bd.sb("mpv%d" % i, [128, 10, 512], BF16) for i in range(2)]
    outs = []
    for mc in range(SEQ // 512):
        t0 = mc * 512
        xt = xts[mc % 2]
        mpv = mpvs[mc % 2]
        Bd.dma("sync", xt[:], x1[t0:t0 + 512, :].rearrange("(tt p) d -> p tt d", p=128), w=[xt])
        Bd.dma("sync", mpv[:], mp[:, t0:t0 + 512].rearrange("(c p) t -> p c t", p=128), w=[mpv])
        for tt in range(4):
            for hf in range(2):
                p = pP[hf]
                for c in range(10):
                    Bd.mm(p[:], mpv[:, c, tt * 128:(tt + 1) * 128], wo[:, c, hf * 512:(hf + 1) * 512],
                          c == 0, c == 9, [mpv, (wo, c)], [p])
                Bd.tt("vector", xt[:, tt, hf * 512:(hf + 1) * 512], xt[:, tt, hf * 512:(hf + 1) * 512],
                      p[:], ALU.add, [xt, p], [xt])
        o = Bd.dma("sync", out[t0:t0 + 512, :].rearrange("(tt p) d -> p tt d", p=128), xt[:], r=[xt], w=["out_d"])
        outs.append(o)
    Bd.S.emit(final_wait_ops=outs)
    return nc


_CACHE = {}


def _get_prog(key, fn):
    if key not in _CACHE:
        _CACHE[key] = fn()
    return _CACHE[key]


def kernel_unfused(**inputs):
    inp = {k: np.asarray(v) for k, v in inputs.items()}
    x = inp["x"]
    BATCH, SEQ, _ = x.shape
    n = 8
    cores = [(b, hh) for b in range(BATCH) for hh in range(2)]
    xcur = [np.ascontiguousarray(x[b]) for b in range(BATCH)]
    mprev = None
    for l in range(2):
        has_prev = l > 0
        nc = _get_prog(("layer", SEQ, has_prev), lambda: Builder(SEQ, has_prev).build())
        in_maps = []
        for (b, hh) in cores:
            d = host_layer_params(inp, l, hh)
            d["xin"] = xcur[b]
            d["mem"] = np.ascontiguousarray(inp["mem"][b])
            if has_prev:
                d["mprev"] = mprev[b]
                d["wout"] = np.ascontiguousarray(inp["w_out"][l - 1])
            in_maps.append(d)
        res = run_bass_kernel_spmd(nc, in_maps, core_ids=list(range(n)))
        new_m = []
        for b in range(BATCH):
            full = np.zeros((1280, SEQ), dtype=ml_dtypes.bfloat16)
            for hh in range(2):
                m = np.asarray(res.results[b * 2 + hh]["mixed"])
                for g in range(5):
                    full[g * 256 + hh * 128:g * 256 + hh * 128 + 128] = m[g * 128:(g + 1) * 128]
            new_m.append(full)
            if has_prev:
                xcur[b] = np.asarray(res.results[b * 2]["x1out"])
        mprev = new_m
    ncf = _get_prog(("final", SEQ), lambda: build_final(SEQ // 2))
    in_maps = []
    H = SEQ // 2
    for (b, hh) in cores:
        in_maps.append({"x1": np.ascontiguousarray(xcur[b][hh * H:(hh + 1) * H]),
                        "mprev": np.ascontiguousarray(mprev[b][:, hh * H:(hh + 1) * H]),
                        "wout": np.ascontiguousarray(inp["w_out"][1])})
    res = run_bass_kernel_spmd(ncf, in_maps, core_ids=list(range(n)))
    out = np.zeros((BATCH, SEQ, 1024), np.float32)
    for i, (b, hh) in enumerate(cores):
        out[b, hh * H:(hh + 1) * H] = np.asarray(res.results[i]["out"])
    return out


def kernel(**inputs):
    inp = {k: np.asarray(v) for k, v in inputs.items()}
    x = inp["x"]
    BATCH, SEQ, _ = x.shape
    nc = _get_prog(("fused", SEQ), lambda: Builder(SEQ, False).build_fused())
    per = {}
    for l in range(2):
        for hh in range(2):
            d = host_layer_params(inp, l, hh)
            for k, v in d.items():
                per["%s_%d%d" % (k, l, hh)] = v
    in_maps = []
    for b in range(BATCH):
        d = dict(per)
        d["xin"] = np.ascontiguousarray(x[b])
        d["mem"] = np.ascontiguousarray(inp["mem"][b])
        d["wout0"] = np.ascontiguousarray(inp["w_out"][0])
        d["wout1"] = np.ascontiguousarray(inp["w_out"][1])
        in_maps.append(d)
    res = run_bass_kernel_spmd(nc, in_maps, core_ids=list(range(BATCH)))
    out = np.stack([np.asarray(res.results[b]["out"]) for b in range(BATCH)], axis=0)
    return out.astype(np.float32)
```

```python
from contextlib import ExitStack
import numpy as np
import ml_dtypes
import concourse.bass as bass
import concourse.mybir as mybir
from concourse.bass_utils import run_bass_kernel_spmd

F32 = mybir.dt.float32
BF16 = mybir.dt.bfloat16
ALU = mybir.AluOpType
AF = mybir.ActivationFunctionType
AX = mybir.AxisListType

ENGINES = ("tensor", "vector", "scalar", "gpsimd", "sync")
SEM_CAP = 30000
EPS = 1e-6


class _Op:
    __slots__ = ("eng", "fn", "idx", "deps", "signal", "is_dma", "sem", "val", "pre_wait")

    def __init__(self, eng, fn, is_dma):
        self.eng = eng
        self.fn = fn
        self.is_dma = is_dma
        self.deps = []
        self.signal = False
        self.sem = None
        self.val = None
        self.pre_wait = None


class Tl:
    def __init__(self, name, t):
        self.name = name
        self.t = t

    def __getitem__(self, idx):
        return self.t[idx]


def _norm(rs):
    out = []
    for r in rs:
        if isinstance(r, tuple):
            a, k = r
        else:
            a, k = r, None
        if isinstance(a, Tl):
            a = a.name
        if a.startswith("scr"):
            k = None
        out.append((a, k))
    return out


class Sched:
    def __init__(self, nc, stack, n_dma_sems=16):
        self.nc = nc
        self.stack = stack
        self.ops = {e: [] for e in ENGINES}
        self.state = {}
        self.n_dma_sems = n_dma_sems

    def _entries(self, name, key):
        d = self.state.setdefault(name, {})
        if key is None:
            return list(d.values())
        res = []
        if key in d:
            res.append(d[key])
        if None in d:
            res.append(d[None])
        return res

    def add(self, eng, fn, reads=(), writes=(), dma=False):
        reads = _norm(reads)
        writes = _norm(writes)
        op = _Op(eng, fn, dma)
        deps = []
        for (name, key) in reads:
            for ent in self._entries(name, key):
                if ent[0] is not None:
                    deps.append(ent[0])
                if name[0] == "p" and name[1].isupper():
                    deps.extend(o_ for o_ in ent[1] if o_.eng != eng)
        for (name, key) in writes:
            for ent in self._entries(name, key):
                if ent[0] is not None:
                    deps.append(ent[0])
                deps.extend(ent[1])
        for (name, key) in reads:
            d = self.state.setdefault(name, {})
            if key is None:
                if not d:
                    d[None] = [None, []]
                for ent in d.values():
                    ent[1].append(op)
            else:
                if key not in d:
                    d[key] = [None, []]
                d[key][1].append(op)
        for (name, key) in writes:
            d = self.state.setdefault(name, {})
            if key is None:
                d.clear()
                d[None] = [op, []]
            else:
                d[key] = [op, []]
        op.idx = len(self.ops[eng])
        best = {}
        dl = []
        for dop in deps:
            if dop is op:
                continue
            if dop.is_dma:
                if dop not in dl:
                    dl.append(dop)
            else:
                if dop.eng == "tensor" and eng == "tensor" and not dma:
                    continue
                b = best.get(dop.eng)
                if b is None or dop.idx > b.idx:
                    best[dop.eng] = dop
        op.deps = dl + list(best.values())
        for dop in op.deps:
            dop.signal = True
        self.ops[eng].append(op)
        return op

    def emit(self, final_wait_ops=()):
        nc = self.nc
        for eng in ENGINES:
            cnt = 0
            sem = None
            for op in self.ops[eng]:
                if op.is_dma:
                    continue
                if op.signal:
                    if sem is None or cnt >= SEM_CAP:
                        sem = self.stack.enter_context(nc.semaphore(f"s_{eng}_{op.idx}"))
                        cnt = 0
                    cnt += 1
                    op.sem = sem
                    op.val = cnt
        for eng in ENGINES:
            qpool = []
            k = 0
            for op in self.ops[eng]:
                if not op.is_dma:
                    continue
                if len(qpool) < self.n_dma_sems:
                    s = self.stack.enter_context(nc.semaphore(f"d_{eng}_{len(qpool)}"))
                    qpool.append([s, 0])
                    ent = qpool[-1]
                else:
                    ent = qpool[k % self.n_dma_sems]
                    if ent[1] + 16 > SEM_CAP:
                        ent[0] = self.stack.enter_context(nc.semaphore(f"d_{eng}_x{k}"))
                        ent[1] = 0
                if ent[1] > 0:
                    op.pre_wait = (ent[0], ent[1])
                ent[1] += 16
                op.sem = ent[0]
                op.val = ent[1]
                k += 1
        sched = self

        def run(eng_name, e):
            seen = {}
            for op in sched.ops[eng_name]:
                waits = []
                if op.pre_wait is not None:
                    waits.append(op.pre_wait)
                for dop in op.deps:
                    waits.append((dop.sem, dop.val))
                for (s, v) in waits:
                    key = id(s)
                    if seen.get(key, 0) >= v:
                        continue
                    seen[key] = v
                    e.wait_ge(s, v)
                ins = op.fn(e)
                if op.is_dma:
                    ins.then_inc(op.sem, 16)
                elif op.signal:
                    ins.then_inc(op.sem, 1)
            if eng_name == "sync":
                for fop in final_wait_ops:
                    e.wait_ge(fop.sem, fop.val)

        with nc.Block() as block:
            @block.sync
            def _(e):
                run("sync", e)

            @block.tensor
            def _(e):
                run("tensor", e)

            @block.vector
            def _(e):
                run("vector", e)

            @block.scalar
            def _(e):
                run("scalar", e)

            @block.gpsimd
            def _(e):
                run("gpsimd", e)


D_MODEL = 1024
FM_GROUPS = ["Au", "Az", "Bz", "Cq", "Ck", "Co", "Cz", "Dr", "Dk", "Dv", "Dz", "Mz",
             "Bf", "Ci", "Cf0", "Cf1", "Dw", "Da"]
FM_W = {g: 128 for g in FM_GROUPS}
FM_W["Dw"] = 16
FM_W["Da"] = 16
FM_OFF = {}
_o = 0
for _g in FM_GROUPS:
    FM_OFF[_g] = _o
    _o += FM_W[_g]
NFM = _o
TM_GROUPS = ["Av", "Bq", "Bk", "Bv", "Cv", "Mq"]
NTM = 768
NW = NFM + NTM
PC = {n: i for i, n in enumerate([
    "cq0", "cq1", "cq2", "cq3", "ck0", "ck1", "ck2", "ck3", "cqb", "ckb",
    "mu_r", "mu_k", "mu_v", "mu_z", "mu_w", "mu_a",
    "k_k", "k_a", "a0", "w0", "r_k", "ln_g", "out_g",
    "fb_B", "ib_C", "fb_C0", "fb_C1"])}
NPC = len(PC)
PR_SGU_G, PR_BQG, PR_BKG, PR_MQG, PR_MKG, PR_SGUB = 0, 128, 256, 384, 512, 640
NPR = 768

A_OFF = 0
B_OFF = 768
C_OFF = 768 + 1028
D_OFF = C_OFF + 1288
M_OFF = D_OFF + 1056


def host_layer_params(inp, l, hh):
    f32 = np.float32
    w_in = inp["w_in"][l]
    hs = [2 * hh, 2 * hh + 1]

    def hcols(base):
        return np.concatenate([np.arange(base + h * 64, base + h * 64 + 64) for h in hs])

    cols = {}
    cols["Au"] = hcols(A_OFF)
    cols["Av"] = hcols(A_OFF + 256)
    cols["Az"] = hcols(A_OFF + 512)
    cols["Bq"] = hcols(B_OFF)
    cols["Bk"] = hcols(B_OFF + 256)
    cols["Bv"] = hcols(B_OFF + 512)
    bf = B_OFF + 768
    cols["Bf"] = np.concatenate([np.full(64, bf + hs[1]), np.full(64, bf + hs[0])])
    cols["Bz"] = hcols(B_OFF + 772)
    cols["Cq"] = hcols(C_OFF)
    cols["Ck"] = hcols(C_OFF + 256)
    cols["Cv"] = hcols(C_OFF + 512)
    ci = C_OFF + 768
    cols["Ci"] = np.concatenate([np.full(64, ci + hs[0]), np.full(64, ci + hs[1])])
    cols["Cf0"] = np.full(128, ci + 4 + hs[0])
    cols["Cf1"] = np.full(128, ci + 4 + hs[1])
    cols["Co"] = hcols(C_OFF + 776)
    cols["Cz"] = hcols(C_OFF + 1032)
    cols["Dr"] = hcols(D_OFF)
    cols["Dw"] = np.arange(D_OFF + 256, D_OFF + 272)
    cols["Dk"] = hcols(D_OFF + 272)
    cols["Dv"] = hcols(D_OFF + 528)
    cols["Da"] = np.arange(D_OFF + 784, D_OFF + 800)
    cols["Dz"] = hcols(D_OFF + 800)
    cols["Mq"] = hcols(M_OFF)
    cols["Mz"] = hcols(M_OFF + 256)
    allc = np.concatenate([cols[g] for g in FM_GROUPS] + [cols[g] for g in TM_GROUPS])
    wcat = np.ascontiguousarray(w_in[:, allc])

    hc = hcols(0)
    pc = np.zeros((128, NPC), f32)
    cw = inp["mlstm_conv_w"][l]
    cb = inp["mlstm_conv_b"][l]
    for j in range(4):
        pc[:, PC["cq%d" % j]] = cw[j, hc]
        pc[:, PC["ck%d" % j]] = cw[j, 256 + hc]
    pc[:, PC["cqb"]] = cb[hc]
    pc[:, PC["ckb"]] = cb[256 + hc]
    mu = inp["rwkv_mu"][l]
    pc[:, PC["mu_r"]] = mu[hc]
    pc[:16, PC["mu_w"]] = mu[256:272]
    pc[:, PC["mu_k"]] = mu[272 + hc]
    pc[:, PC["mu_v"]] = mu[528 + hc]
    pc[:16, PC["mu_a"]] = mu[784:800]
    pc[:, PC["mu_z"]] = mu[800 + hc]
    pc[:, PC["k_k"]] = inp["rwkv_k_k"][l][hc]
    pc[:, PC["k_a"]] = inp["rwkv_k_a"][l][hc]
    pc[:, PC["a0"]] = inp["rwkv_a0"][l][hc]
    pc[:, PC["w0"]] = inp["rwkv_w0"][l][hc]
    pc[:, PC["r_k"]] = inp["rwkv_r_k"][l].reshape(-1)[hc]
    pc[:, PC["ln_g"]] = inp["rwkv_ln_g"][l][hc]
    pc[:, PC["out_g"]] = inp["mlstm_out_g"][l][hc]
    fb = inp["fox_f_b"][l]
    pc[:, PC["fb_B"]] = np.concatenate([np.full(64, fb[hs[1]]), np.full(64, fb[hs[0]])])
    ib = inp["mlstm_i_b"][l]
    pc[:, PC["ib_C"]] = np.concatenate([np.full(64, ib[hs[0]]), np.full(64, ib[hs[1]])])
    fbc = inp["mlstm_f_b"][l]
    pc[:, PC["fb_C0"]] = fbc[hs[0]]
    pc[:, PC["fb_C1"]] = fbc[hs[1]]

    pr = np.zeros((128, NPR), f32)
    pr[:, PR_SGU_G:PR_SGU_G + 128] = inp["sgu_norm_g"][l][hc][None, :]
    pr[:, PR_BQG:PR_BQG + 128] = np.tile(inp["fox_q_g"][l], 2)[None, :]
    pr[:, PR_BKG:PR_BKG + 128] = np.tile(inp["fox_k_g"][l], 2)[None, :]
    pr[:, PR_MQG:PR_MQG + 128] = np.tile(inp["mem_q_g"][l], 2)[None, :]
    pr[:, PR_MKG:PR_MKG + 128] = np.tile(inp["mem_k_g"][l], 2)[None, :]
    sb_ = inp["sgu_b"][l]
    pr[:64, PR_SGUB:PR_SGUB + 128] = sb_[hs[0]][None, :]
    pr[64:, PR_SGUB:PR_SGUB + 128] = sb_[hs[1]][None, :]

    d = {
        "wcat": wcat,
        "pc": pc,
        "pr": pr,
        "ng": np.ascontiguousarray(inp["norm_g"][l].reshape(8, 128).T),
        "memg": np.ascontiguousarray(inp["mem_norm_g"][l].reshape(8, 128).T),
        "wkv": np.ascontiguousarray(np.concatenate(
            [inp["mem_w_kv"][l][:, hc], inp["mem_w_kv"][l][:, 256 + hc]], axis=1)),
        "w2": np.ascontiguousarray(inp["rwkv_w2"][l][:, hc]),
        "a2": np.ascontiguousarray(inp["rwkv_a2"][l][:, hc]),
        "sguw": np.ascontiguousarray(inp["sgu_w"][l][hs]),
    }
    return d


class Builder:
    def __init__(self, SEQ, has_prev, branches="ABCDM"):
        self.SEQ = SEQ
        self.has_prev = has_prev
        self.branches = branches
        self.nc = bass.Bass("TRN2", target_bir_lowering=False)
        self.st = ExitStack()
        self.S = Sched(self.nc, self.st)
        self.ps_rr = 0

    def dram(self, name, shape, dt, kind):
        return self.nc.dram_tensor(name, shape, dt, kind=kind).ap()

    def sb(self, name, shape, dt=F32):
        if not hasattr(self, "_tiles"):
            self._tiles = {}
        if name not in self._tiles:
            self._tiles[name] = Tl(name, self.st.enter_context(self.nc.sbuf_tensor(name, shape, dt)))
        return self._tiles[name]

    def psum(self, name, shape, dt=F32):
        if not hasattr(self, "_tiles"):
            self._tiles = {}
        if name not in self._tiles:
            self._tiles[name] = Tl(name, self.st.enter_context(self.nc.psum_tensor(name, shape, dt)))
        return self._tiles[name]

    def gps(self):
        p = self.gp[self.ps_rr % len(self.gp)]
        self.ps_rr += 1
        return p

    def op(self, eng, fn, r=(), w=()):
        return self.S.add(eng, fn, reads=r, writes=w)

    def dma(self, eng, out, in_, r=(), w=()):
        return self.S.add(eng, lambda e: e.dma_start(out=out, in_=in_), reads=r, writes=w, dma=True)

    def mm(self, out, lhsT, rhs, start, stop, r, w):
        return self.S.add("tensor", lambda e: e.matmul(out, lhsT=lhsT, rhs=rhs, start=start, stop=stop),
                          reads=r, writes=w)

    def tr(self, out, in_, ident, r, w):
        return self.S.add("tensor", lambda e: e.transpose(out, in_, ident), reads=r, writes=w)

    def act(self, out, in_, func, r, w, bias=None, scale=None, accum_out=None, eng="scalar"):
        kw = {}
        if bias is not None:
            kw["bias"] = bias
        if scale is not None:
            kw["scale"] = scale
        if accum_out is not None:
            kw["accum_out"] = accum_out
        return self.S.add("scalar", lambda e: e.activation(out=out, in_=in_, func=func, **kw), reads=r, writes=w)

    def tt(self, eng, out, in0, in1, op, r, w):
        return self.S.add(eng, lambda e: e.tensor_tensor(out=out, in0=in0, in1=in1, op=op), reads=r, writes=w)

    def ts(self, eng, out, in0, s1, op0, r, w, s2=None, op1=None):
        if op1 is None:
            return self.S.add(eng, lambda e: e.tensor_scalar(out=out, in0=in0, scalar1=s1, scalar2=None, op0=op0),
                              reads=r, writes=w)
        return self.S.add(eng, lambda e: e.tensor_scalar(out=out, in0=in0, scalar1=s1, scalar2=s2, op0=op0, op1=op1),
                          reads=r, writes=w)

    def stt(self, out, in0, scalar, in1, op0, op1, r, w):
        return self.S.add("vector", lambda e: e.scalar_tensor_tensor(out=out, in0=in0, scalar=scalar, in1=in1,
                                                                      op0=op0, op1=op1), reads=r, writes=w)

    def cp(self, eng, out, in_, r, w):
        if eng == "scalar":
            return self.S.add("scalar", lambda e: e.copy(out=out, in_=in_), reads=r, writes=w)
        return self.S.add(eng, lambda e: e.tensor_copy(out, in_), reads=r, writes=w)

    def recip(self, out, in_, r, w):
        return self.S.add("vector", lambda e: e.reciprocal(out, in_), reads=r, writes=w)

    def memset(self, eng, ap, val, w):
        return self.S.add(eng, lambda e: e.memset(ap, val), writes=w)

    def asel(self, out, in_, cmp, w, r=(), fill=0.0, base=0, cm=-1, pat=None):
        pat = pat or [[1, 128]]
        return self.S.add("gpsimd", lambda e: e.affine_select(out=out, in_=in_, pattern=pat, compare_op=cmp,
                                                              fill=fill, base=base, channel_multiplier=cm),
                          reads=r, writes=w)

    def build(self):
        cfg = dict(tag="", xmode="outproj" if self.has_prev else "ext")
        self.last_out = []
        self.run_pass(cfg)
        self.S.emit(final_wait_ops=self.last_out)
        return self.nc

    def build_fused(self):
        SEQ = self.SEQ
        self.last_out = []
        self.x_ext = self.dram("xin", [SEQ, 1024], F32, "ExternalInput")
        self.mem_ext = self.dram("mem", [256, 1024], F32, "ExternalInput")
        self.mixs = [self.dram("mixs%d" % l, [1280, SEQ], BF16, "Internal") for l in range(2)]
        self.x1s = self.dram("x1s", [SEQ, 1024], F32, "Internal")
        self.wouts = [self.dram("wout%d" % l, [1280, 1024], F32, "ExternalInput") for l in range(2)]
        for l in range(2):
            if l == 1:
                self.begin("oproj")
                self.outproj_pass("ext", 0, self.x1s, "x1s", False)
            for hh in range(2):
                xmode = "ext" if l == 0 else "x1"
                self.run_pass(dict(tag="_%d%d" % (l, hh), xmode=xmode, fused=True, l=l, hh=hh))
        out_d = self.dram("out", [SEQ, 1024], F32, "ExternalOutput")
        self.begin("oproj")
        self.outproj_pass("x1", 1, out_d, "out_d", True)
        self.S.emit(final_wait_ops=self.last_out)
        return self.nc

    def outproj_pass(self, src_kind, l, dst, dst_name, final):
        SEQ = self.SEQ
        pP = [self.psum("pP0", [128, 512]), self.psum("pP1", [128, 512])]
        wb = self.sb("wb", [128, 8, NW], BF16)
        wo = Tl("wb", wb.t[:, :, :].rearrange("p a b -> p (a b)")[:, 0:10240].rearrange("p (c n) -> p c n", c=10))
        wov = self.wouts[l].rearrange("(c p) n -> p c n", p=128)
        for c in range(10):
            self.dma("gpsimd", wo[:, c, :], wov[:, c, :], w=[wo])
        xts = [self.sb("xt%d" % i, [128, 1024]) for i in range(2)]
        tm_ = self.sb("tm", [128, 4, 768], BF16)
        flat = tm_.t[:, :, :].rearrange("p a b -> p (a b)")
        mpvs = [Tl("tm", flat[:, 0:1280].rearrange("p (c t) -> p c t", c=10)),
                Tl("tm", flat[:, 1280:2560].rearrange("p (c t) -> p c t", c=10))]
        for ti in range(SEQ // 128):
            xt = xts[ti % 2]
            mpv = mpvs[ti % 2]
            r0 = ti * 128
            if src_kind == "ext":
                self.dma("sync", xt[:], self.x_ext[r0:r0 + 128, :], w=[xt])
            else:
                self.dma("sync", xt[:], self.x1s[r0:r0 + 128, :], r=[("x1s", ti)], w=[xt])
            self.dma("sync", mpv[:], self.mixs[l][:, r0:r0 + 128].rearrange("(c p) t -> p c t", p=128),
                     r=[("mixs%d" % l, (0, ti // 4)), ("mixs%d" % l, (1, ti // 4))], w=[mpv])
            for hf in range(2):
                p = pP[hf]
                for c in range(10):
                    self.mm(p[:], mpv[:, c, :], wo[:, c, hf * 512:(hf + 1) * 512], c == 0, c == 9,
                            [mpv, wo], [p])
                eng = "vector"
                if eng == "gpsimd":
                    tmpo = self.scr("op_tmp", [128, 512])
                    self.cp("scalar", tmpo[:], p[:], [p], [tmpo])
                    self.tt("gpsimd", xt[:, hf * 512:(hf + 1) * 512], xt[:, hf * 512:(hf + 1) * 512], tmpo[:],
                            ALU.add, [xt, tmpo], [xt])
                else:
                    self.tt("vector", xt[:, hf * 512:(hf + 1) * 512], xt[:, hf * 512:(hf + 1) * 512], p[:],
                            ALU.add, [xt, p], [xt])
            o = self.dma("sync", dst[r0:r0 + 128, :], xt[:], r=[xt], w=[(dst_name, ti)])
            if final:
                self.last_out.append(o)

    def run_pass(self, cfg):
        nc = self.nc
        SEQ = self.SEQ
        NMC = SEQ // 512
        NCH = SEQ // 128
        tag = cfg["tag"]
        xmode = cfg["xmode"]
        fused = cfg.get("fused", False)
        has_prev = xmode == "outproj"
        wcat = self.dram("wcat" + tag, [1024, NW], F32, "ExternalInput")
        pc_d = self.dram("pc" + tag, [128, NPC], F32, "ExternalInput")
        pr_d = self.dram("pr" + tag, [128, NPR], F32, "ExternalInput")
        ng_d = self.dram("ng" + tag, [128, 8], F32, "ExternalInput")
        memg_d = self.dram("memg" + tag, [128, 8], F32, "ExternalInput")
        wkv_d = self.dram("wkv" + tag, [1024, 256], F32, "ExternalInput")
        w2_d = self.dram("w2" + tag, [16, 128], F32, "ExternalInput")
        a2_d = self.dram("a2" + tag, [16, 128], F32, "ExternalInput")
        sguw_d = self.dram("sguw" + tag, [2, 128, 128], F32, "ExternalInput")
        if fused:
            l, hh = cfg["l"], cfg["hh"]
            xin = self.x_ext
            mem_d = self.mem_ext
            mixed_d = self.mixs[l].rearrange("(g two p) t -> two p g t", two=2, p=128)[hh]
            mix_name = "mixs%d" % l
            mix_key = lambda mc: (hh, mc)
            if has_prev:
                mprev_d = self.mixs[0]
                wout_d = self.wouts[0]
                x1_d = self.x1s
        else:
            xin = self.dram("xin", [SEQ, 1024], F32, "ExternalInput")
            mem_d = self.dram("mem", [256, 1024], F32, "ExternalInput")
            mixed_d = self.dram("mixed", [640, SEQ], BF16, "ExternalOutput").rearrange("(g p) t -> p g t", p=128)
            mix_name = "mixed_d"
            mix_key = lambda mc: mc
            if has_prev:
                mprev_d = self.dram("mprev", [1280, SEQ], BF16, "ExternalInput")
                wout_d = self.dram("wout", [1280, 1024], F32, "ExternalInput")
                x1_d = self.dram("x1out", [SEQ, 1024], F32, "ExternalOutput")

        pT = self.psum("pT", [128, 1024], BF16)
        pP = [self.psum("pP0", [128, 512]), self.psum("pP1", [128, 512])]
        self.gp = [self.psum("pG%d" % i, [128, 512]) for i in range(3)]
        self.pacc = pP
        pNum = self.psum("pNum", [128, 512])
        pDen = self.psum("pDen", [128, 512])

        identf = self.sb("identf", [128, 128])
        ident = self.sb("ident", [128, 128], BF16)
        ones_bf = self.sb("ones_bf", [128, 128], BF16)
        onesf = self.sb("onesf", [128, 128])
        c64f = self.sb("c64f", [128, 128])
        c64b = self.sb("c64b", [128, 128], BF16)
        blk = self.sb("blk", [128, 128], BF16)
        m_ge = self.sb("m_ge", [128, 128], BF16)
        self.memset("vector", identf[:], 1.0, [identf])
        self.asel(identf[:], identf[:], ALU.is_equal, [identf], [identf])
        self.cp("vector", ident[:], identf[:], [identf], [ident])
        self.memset("vector", ones_bf[:], 1.0, [ones_bf])
        self.memset("vector", onesf[:], 1.0, [onesf])
        self.memset("vector", c64f[:], 1.0 / 64, [c64f])
        self.memset("vector", c64b[:], 1.0 / 64, [c64b])
        self.memset("vector", blk[:], 0.0, [blk])
        self.memset("vector", blk[0:64, 0:64], 1.0, [blk])
        self.memset("vector", blk[64:128, 64:128], 1.0, [blk])
        self.memset("vector", m_ge[:], 1.0, [m_ge])
        self.asel(m_ge[:], m_ge[:], ALU.is_ge, [m_ge], [m_ge])

        pc = self.sb("pcs", [128, NPC])
        pr = self.sb("prs", [128, NPR])
        ng = self.sb("ngs", [128, 8])
        memg = self.sb("memgs", [128, 8])
        self.dma("sync", pc[:], pc_d, w=[pc])
        self.dma("sync", pr[:], pr_d, w=[pr])
        self.dma("sync", ng[:], ng_d, w=[ng])
        self.dma("sync", memg[:], memg_d, w=[memg])
        omm = self.sb("omm", [128, 6])
        self.ts("vector", omm[:], pc[:, PC["mu_r"]:PC["mu_r"] + 6], -1.0, ALU.mult, [pc], [omm], s2=1.0, op1=ALU.add)
        nfbB = self.sb("nfbB", [128, 1])
        self.ts("vector", nfbB[:], pc[:, PC["fb_B"]:PC["fb_B"] + 1], -1.0, ALU.mult, [pc], [nfbB])
        nfbC = self.sb("nfbC", [128, 2])
        self.ts("vector", nfbC[:], pc[:, PC["fb_C0"]:PC["fb_C0"] + 2], -1.0, ALU.mult, [pc], [nfbC])
        gq8 = self.sb("gq8", [128, 128])
        self.ts("vector", gq8[:], pr[:, PR_BQG:PR_BQG + 128], 0.125, ALU.mult, [pr], [gq8])
        gmq8 = self.sb("gmq8", [128, 128])
        self.ts("vector", gmq8[:], pr[:, PR_MQG:PR_MQG + 128], 0.125, ALU.mult, [pr], [gmq8])

        wb = self.sb("wb", [128, 8, NW], BF16)
        wv = wcat.rearrange("(c p) n -> p c n", p=128)
        for c in range(8):
            self.dma("gpsimd", wb[:, c, :], wv[:, c, :], w=[(wb, c)])
        for c in range(8):
            eng = "vector" if c % 2 == 0 else "gpsimd"
            self.ts(eng, wb[:, c, :], wb[:, c, :], ng[:, c:c + 1], ALU.mult, [(wb, c), ng], [(wb, c)])
        if has_prev:
            wo = self.sb("wo", [128, 10, 1024], BF16)
            wov = wout_d.rearrange("(c p) n -> p c n", p=128)
            for c in range(10):
                self.dma("gpsimd", wo[:, c, :], wov[:, c, :], w=[(wo, c)])
        w2s = self.sb("w2s", [16, 128])
        a2s = self.sb("a2s", [16, 128])
        self.dma("sync", w2s[:], w2_d, w=[w2s])
        self.dma("sync", a2s[:], a2_d, w=[a2s])

        wsT = self.sb("wsT", [128, 2, 128], BF16)
        self.begin("setupA")
        if "A" in self.branches:
            sgw = self.scr("sgw", [128, 2, 128])
            self.dma("sync", sgw[:], sguw_d.rearrange("h t s -> t h s"), w=[sgw])
            sgwT = self.scr("sgwT", [128, 2, 128])
            for h in range(2):
                p = self.gps()
                self.tr(p[:, 0:128], sgw[:, h, :], identf[:], [sgw, identf], [p])
                self.cp("vector", sgwT[:, h, :], p[:, 0:128], [p], [sgwT])
                self.asel(sgwT[:, h, :], sgwT[:, h, :], ALU.is_ge, [sgwT], [sgwT])
            self.cp("vector", wsT[:], sgwT[:], [sgwT], [wsT])

        kTm = self.sb("kTm", [128, 256], BF16)
        vm = self.sb("vm", [128, 2, 128], BF16)

        self.alloc_state(NCH)

        xts = [self.sb("xt0", [128, 1024]), self.sb("xt1", [128, 1024])]
        hb = self.sb("hb", [128, 1024], BF16)
        hT = self.sb("hT", [128, 8, 512], BF16)
        ss = self.sb("ss", [128, 2])
        GATED = {"Az": AF.Silu, "Bz": AF.Silu, "Cz": AF.Silu, "Mz": AF.Silu, "Co": AF.Sigmoid}
        fm = {}
        for g in FM_GROUPS:
            if g in ("Cq", "Ck"):
                fm[g] = self.sb("fm_" + g, [128, 4 + 512], BF16)
            elif g in ("Dr", "Dk", "Dv", "Dz"):
                fm[g] = self.sb("fm_" + g, [128, 2 + 512], BF16)
            elif g in ("Dw", "Da"):
                fm[g] = self.sb("fm_" + g, [16, 1 + 512])
            elif g in GATED:
                fm[g] = self.sb("fm_" + g, [128, 512], BF16)
            else:
                fm[g] = self.sb("fm_" + g, [128, 512])
        tm = self.sb("tm", [128, 4, 768], BF16)
        mixo = self.sb("mixo", [128, 5, 512], BF16)
        if "M" in self.branches:
            self.setup_mem(mem_d, wkv_d, memg, pr, ident, identf, pT, kTm, vm, xts, hT, tm, hb)
        mpvs = [Tl("tm", tm.t[:, :, :].rearrange("p a b -> p (a b)")[:, 0:1280].rearrange("p (c t) -> p c t", c=10))] * 2
        for g in ("Cq", "Ck"):
            self.memset("vector", fm[g][:, 0:3], 0.0, [(fm[g], "hist")])
        for g in ("Dr", "Dk", "Dv", "Dz"):
            self.memset("vector", fm[g][:, 0:1], 0.0, [(fm[g], "hist")])
        for g in ("Dw", "Da"):
            self.memset("vector", fm[g][:, 0:1], 0.0, [(fm[g], "hist")])

        self.consts = dict(ident=ident, identf=identf, ones_bf=ones_bf, onesf=onesf, c64f=c64f, c64b=c64b,
                           blk=blk, m_ge=m_ge, pc=pc, pr=pr, omm=omm, nfbB=nfbB, nfbC=nfbC,
                           gq8=gq8, gmq8=gmq8, wsT=wsT, kTm=kTm, vm=vm, w2s=w2s, a2s=a2s, pT=pT,
                           pNum=pNum, pDen=pDen)
        last_out = self.last_out
        xi = 0
        xstate = {"xi": 0}

        def xprep(mc):
            t0 = mc * 512
            for tt in range(4):
                xi = xstate["xi"]
                xt = xts[xi % 2]
                r0 = t0 + tt * 128
                ti = mc * 4 + tt
                if xmode == "x1":
                    self.dma("sync", xt[:], self.x1s[r0:r0 + 128, :], r=[("x1s", ti)], w=[xt])
                else:
                    self.dma("sync", xt[:], xin[r0:r0 + 128, :], w=[xt])
                if has_prev:
                    mpv = mpvs[xi % 2]
                    rr = [("mixs0", (0, mc)), ("mixs0", (1, mc))] if fused else []
                    self.dma("sync", mpv[:], mprev_d[:, r0:r0 + 128].rearrange("(c p) t -> p c t", p=128),
                             r=rr, w=[mpv])
                    for hf in range(2):
                        p = pP[hf]
                        for c in range(10):
                            self.mm(p[:], mpv[:, c, :], wo[:, c, hf * 512:(hf + 1) * 512],
                                    c == 0, c == 9, [mpv, (wo, c)], [p])
                        self.tt("vector", xt[:, hf * 512:(hf + 1) * 512], xt[:, hf * 512:(hf + 1) * 512],
                                p[:], ALU.add, [xt, p], [xt])
                    o = self.dma("sync", x1_d[r0:r0 + 128, :], xt[:], r=[xt], w=[("x1s", ti)])
                    if not fused:
                        last_out.append(o)
                xstate["xi"] = xi + 1
                self.act(hb[:], xt[:], AF.Square, [xt], [hb, ss], accum_out=ss[:, 0:1])
                self.act(ss[:, 1:2], ss[:, 0:1], AF.Ln, [ss], [ss], scale=1.0 / 1024, bias=EPS)
                self.act(ss[:, 1:2], ss[:, 1:2], AF.Exp, [ss], [ss], scale=-0.5)
                self.ts("vector", hb[:], xt[:], ss[:, 1:2], ALU.mult, [xt, ss], [hb])
                yield
                for c in range(8):
                    self.tr(pT[:, c * 128:(c + 1) * 128], hb[:, c * 128:(c + 1) * 128], ident[:],
                            [hb, ident], [pT])
                eng = "vector" if tt % 2 == 0 else "scalar"
                self.cp(eng, hT[:, :, tt * 128:(tt + 1) * 128],
                        pT[:, :].rearrange("p (c t) -> p c t", c=8), [pT], [hT])
                yield

        for _ in xprep(0):
            pass
        for mc in range(NMC):
            t0 = mc * 512
            for gi, g in enumerate(FM_GROUPS):
                p = pP[gi % 2]
                wdt = FM_W[g]
                for c in range(8):
                    self.mm(p[0:wdt, :], wb[:, c, FM_OFF[g]:FM_OFF[g] + wdt], hT[:, c, :], c == 0, c == 7,
                            [(wb, c), hT], [p])
                hist = {"Cq": 3, "Ck": 3, "Dr": 1, "Dk": 1, "Dv": 1, "Dz": 1, "Dw": 1, "Da": 1}.get(g, 0)
                dst = fm[g]
                if g in GATED:
                    self.act(dst[:, :], p[:, :], GATED[g], [p], [(dst, "cur")])
                    continue
                if hist and mc > 0:
                    self.cp("vector", dst[0:wdt, 0:hist], dst[0:wdt, 512:512 + hist], [(dst, "cur")], [(dst, "hist")])
                eng = "scalar" if gi % 2 == 0 else "vector"
                self.cp(eng, dst[0:wdt, hist:hist + 512], p[0:wdt, :], [p, (dst, "hist")], [(dst, "cur")])
            for tt in range(4):
                for hf in range(2):
                    p = pP[hf]
                    for c in range(8):
                        self.mm(p[:, 0:384], hT[:, c, tt * 128:(tt + 1) * 128],
                                wb[:, c, NFM + hf * 384:NFM + (hf + 1) * 384], c == 0, c == 7, [hT, (wb, c)], [p])
                    eng = "scalar" if hf == 0 else "vector"
                    self.cp(eng, tm[:, tt, hf * 384:(hf + 1) * 384], p[:, 0:384], [p], [(tm, tt)])
            self.zero_mix = []
            if "A" in self.branches:
                self.branch_A(mc, fm, tm, mixo)
            else:
                self.memset("gpsimd", mixo[:, 0, :], 0.0, [(mixo, 0)])
            if "B" in self.branches:
                self.branch_B(mc, fm, tm, mixo)
            else:
                self.memset("gpsimd", mixo[:, 1, :], 0.0, [(mixo, 1)])
            gens = []
            if mc + 1 < NMC:
                gens.append(("X", xprep(mc + 1)))
            if "M" in self.branches:
                gens.append(("M", self.branch_M(mc, fm, tm, mixo)))
            else:
                self.memset("gpsimd", mixo[:, 4, :], 0.0, [(mixo, 4)])
            if "D" in self.branches:
                gens.append(("D", self.branch_D(mc, fm, tm, mixo)))
            else:
                self.memset("gpsimd", mixo[:, 3, :], 0.0, [(mixo, 3)])
            if "C" in self.branches:
                gens.append(("C", self.branch_C(mc, fm, tm, mixo)))
            else:
                self.memset("gpsimd", mixo[:, 2, :], 0.0, [(mixo, 2)])
            while gens:
                for item in list(gens):
                    for rep in range(3 if item[0] == "D" else 1):
                        self._br = item[0]
                        try:
                            next(item[1])
                        except StopIteration:
                            gens.remove(item)
                            break
            o = self.dma("sync", mixed_d[:, :, t0:t0 + 512], mixo[:], r=[mixo], w=[(mix_name, mix_key(mc))])
            if not fused:
                last_out.append(o)

    def head_rms_tm(self, src, dst, gain, tag, nt=4):
        sq = self.scr("hr_sq", [128, 4, 128])
        ssq = self.scr("hr_ssq", [128, 8])
        src_ap, src_r = src
        dst_ap, dst_w = dst
        g_ap, g_r = gain
        self.tt("gpsimd", sq[:, 0:nt, :], src_ap, src_ap, ALU.mult, src_r, [sq])
        ssq3 = ssq[:, 0:2 * nt].rearrange("p (t h) -> p t h", h=2)
        self.op("vector", lambda e: e.tensor_reduce(out=ssq3,
                                                     in_=sq[:, 0:nt, :].rearrange("p t (h j) -> p t h j", h=2),
                                                     axis=AX.X, op=ALU.add), [sq], [ssq])
        self.act(ssq[:, 0:2 * nt], ssq[:, 0:2 * nt], AF.Ln, [ssq], [ssq], scale=1.0 / 64, bias=EPS)
        self.act(ssq[:, 0:2 * nt], ssq[:, 0:2 * nt], AF.Exp, [ssq], [ssq], scale=-0.5)
        self.tt("vector", sq[:, 0:nt, :].rearrange("p t (h j) -> p t h j", h=2),
                src_ap.rearrange("p t (h j) -> p t h j", h=2),
                ssq3[:, :, :, None].broadcast_to([128, nt, 2, 64]), ALU.mult, src_r + [ssq], [sq])
        self.tt("vector", dst_ap, sq[:, 0:nt, :], g_ap[:, None, :].broadcast_to([128, nt, 128]), ALU.mult,
                [sq] + g_r, dst_w)

    def begin(self, br):
        self._br = br
        if not hasattr(self, "_brcount"):
            self._brcount = {}
        self._brcount.setdefault(br, {})

    def scr(self, name, shape, dt=F32):
        if not hasattr(self, "_scrmap"):
            self._scrmap = {}
            self._pools = {}
        br = getattr(self, "_br", "x")
        key = (br, name)
        if key in self._scrmap:
            return self._scrmap[key]
        esz = 4 if dt == F32 else 2
        n = 1
        for d_ in shape[1:]:
            n *= d_
        nbytes = n * esz
        cls = 256
        while cls < nbytes:
            cls *= 2
        cnt = self._brcount.setdefault(br, {})
        k = cnt.get(cls, 0)
        cnt[cls] = k + 1
        fam = br if br in ("C", "M") else ""
        pool = self._pools.setdefault((fam, cls), [])
        if k >= len(pool):
            pname = "scr%s%d_%d" % (fam, cls, k)
            pool.append((pname, self.st.enter_context(self.nc.sbuf_tensor(pname, [128, cls // 4], F32))))
        pname, raw = pool[k]
        h = raw if dt == F32 else raw.bitcast(dt)
        ap = h[0:shape[0], 0:n]
        if len(shape) == 3:
            ap = ap.rearrange("p (a b) -> p a b", a=shape[1])
        elif len(shape) == 4:
            ap = ap.rearrange("p (a b c) -> p a b c", a=shape[1], b=shape[2])
        t = Tl(pname, ap)
        self._scrmap[key] = t
        return t

    def gelu(self, dst_ap, dst_w, src_ap, src_r, shape, tag):
        self.act(dst_ap, src_ap, AF.Gelu_apprx_tanh, src_r, dst_w)

    def setup_mem(self, mem_d, wkv_d, memg, pr, ident, identf, pT, kTm, vm, xts, hT, tm, hb):
        self.begin("setup")
        for c in range(2):
            self.dma("sync", xts[c][:], mem_d[c * 128:(c + 1) * 128, :], w=[xts[c]])
        wk = Tl("hT", hT[:, :, 0:256])
        mhT = Tl("hT", hT[:, :, 256:512])
        mh = Tl("tm", tm.t[:, :, :].rearrange("p a b -> p (a b)")[:, 0:2048].rearrange("p (c d) -> p c d", c=2))
        wkv_v = wkv_d.rearrange("(c p) n -> p c n", p=128)
        for c in range(8):
            self.dma("gpsimd", wk[:, c, :], wkv_v[:, c, :], w=[wk])
        for c in range(8):
            self.ts("vector", wk[:, c, :], wk[:, c, :], memg[:, c:c + 1], ALU.mult, [wk, memg], [wk])
        mss = self.scr("m_ss", [128, 2])
        for c in range(2):
            self.act(hb[:], xts[c][:], AF.Square, [xts[c]], [hb, mss], accum_out=mss[:, c:c + 1])
        self.act(mss[:], mss[:], AF.Ln, [mss], [mss], scale=1.0 / 1024, bias=EPS)
        self.act(mss[:], mss[:], AF.Exp, [mss], [mss], scale=-0.5)
        for c in range(2):
            self.ts("vector", mh[:, c, :], xts[c][:], mss[:, c:c + 1], ALU.mult, [xts[c], mss], [mh])
        for c2 in range(2):
            for c in range(8):
                self.tr(pT[:, c * 128:(c + 1) * 128], mh[:, c2, c * 128:(c + 1) * 128], ident[:], [mh, ident], [pT])
            self.cp("vector", mhT[:, :, c2 * 128:(c2 + 1) * 128], pT[:, :].rearrange("p (c t) -> p c t", c=8),
                    [pT], [mhT])
        kvt = self.scr("m_kvt", [128, 2, 256])
        for c2 in range(2):
            p = self.gps()
            for c in range(8):
                self.mm(p[:, 0:256], mhT[:, c, c2 * 128:(c2 + 1) * 128], wk[:, c, :], c == 0, c == 7, [mhT, wk], [p])
            self.cp("vector", kvt[:, c2, :], p[:, 0:256], [p], [kvt])
        self.cp("vector", vm[:], kvt[:, :, 128:256], [kvt], [vm])
        kn = self.scr("m_kn", [128, 2, 128], BF16)
        self.head_rms_tm((kvt[:, :, 0:128], [kvt]), (kn[:], [kn]), (pr[:, PR_MKG:PR_MKG + 128], [pr]), "mk", nt=2)
        for c2 in range(2):
            self.tr(pT[:, c2 * 128:(c2 + 1) * 128], kn[:, c2, :], ident[:], [kn, ident], [pT])
        self.cp("vector", kTm[:], pT[:, 0:256], [pT], [kTm])

    def alloc_state(self, NCH):
        SEQ = self.SEQ
        if "B" in self.branches:
            self.KT = self.sb("KT", [128, SEQ], BF16)
            self.VB = self.sb("VB", [128, NCH, 128], BF16)
            self.Fcol = self.sb("Fcol", [128, 2, NCH])
            self.Fprev = self.sb("Fprev", [128, 1])
            self.memset("vector", self.Fprev[:], 0.0, [self.Fprev])
            self.c64h = [self.sb("c64h%d" % h, [128, 128], BF16) for h in range(2)]
            self.hm = self.sb("hm", [128, 2])
            self.memset("vector", self.hm[:], 0.0, [self.hm])
            for h in range(2):
                hs = slice(h * 64, (h + 1) * 64)
                fs = slice(64, 128) if h == 0 else slice(0, 64)
                self.memset("vector", self.c64h[h][:], 0.0, [self.c64h[h]])
                self.memset("vector", self.c64h[h][fs, :], 1.0 / 64, [self.c64h[h]])
                self.memset("vector", self.hm[hs, h:h + 1], 1.0, [self.hm])
        if "C" in self.branches:
            self.Cst = self.sb("Cst", [128, 128])
            self.Cstb = self.sb("Cstb", [128, 128], BF16)
            self.memset("vector", self.Cst[:], 0.0, [self.Cst])
            self.memset("vector", self.Cstb[:], 0.0, [self.Cstb])
        if "D" in self.branches:
            self.Sst = self.sb("Sst", [128, 64])
            self.Sstb = self.sb("Sstb", [128, 64], BF16)
            self.memset("vector", self.Sst[:], 0.0, [self.Sst])
            self.memset("vector", self.Sstb[:], 0.0, [self.Sstb])

    def branch_A(self, mc, fm, tm, mixo):
        self.begin("A")
        c = self.consts
        pr = c["pr"]
        gu = self.scr("A_gu", [128, 512])
        self.gelu(gu[:], [gu], fm["Au"][:, :], [(fm["Au"], "cur")], [128, 512], "u")
        gv = self.scr("A_gv", [128, 4, 128])
        self.gelu(gv[:], [gv], tm[:, :, 0:128], [tm], [128, 4, 128], "v")
        vn = self.scr("A_vn", [128, 4, 128], BF16)
        self.head_rms_tm((gv[:], [gv]), (vn[:], [vn]), (pr[:, PR_SGU_G:PR_SGU_G + 128], [pr]), "av")
        p = self.gps()
        for tt in range(4):
            for h in range(2):
                self.mm(p[h * 64:(h + 1) * 64, tt * 128:(tt + 1) * 128], vn[:, tt, h * 64:(h + 1) * 64],
                        c["wsT"][:, h, :], True, True, [vn, c["wsT"]], [p])
        ya = self.scr("A_ya", [128, 512])
        self.tt("vector", ya[:].rearrange("p (a t) -> p a t", a=4), p[:].rearrange("p (a t) -> p a t", a=4),
                pr[:, None, PR_SGUB:PR_SGUB + 128].broadcast_to([128, 4, 128]), ALU.add, [p, pr], [ya])
        self.tt("gpsimd", ya[:], ya[:], gu[:], ALU.mult, [ya, gu], [ya])
        self.tt("vector", mixo[:, 0, :], ya[:], fm["Az"][:, :], ALU.mult, [ya, fm["Az"]], [(mixo, 0)])

    def branch_M(self, mc, fm, tm, mixo):
        self.begin("M")
        c = self.consts
        pT = c["pT"]
        qn = self.scr("M_qn", [128, 4, 128], BF16)
        self.head_rms_tm((tm[:, :, 640:768], [tm]), (qn[:], [qn]), (c["gmq8"][:], [c["gmq8"]]), "mq")
        yield
        qT = self.scr("M_qT", [128, 512], BF16)
        for tt in range(4):
            self.tr(pT[:, tt * 128:(tt + 1) * 128], qn[:, tt, :], c["ident"][:], [qn, c["ident"]], [pT])
        self.cp("vector", qT[:], pT[:, 0:512], [pT], [qT])
        yield
        pn = self.pacc[0]
        pd = self.pacc[1]
        E = self.scr("M_E", [128, 512], BF16)
        for h in range(2):
            hs = slice(h * 64, (h + 1) * 64)
            for mcx in range(2):
                ps_ = self.gps()
                self.mm(ps_[:], c["kTm"][hs, mcx * 128:(mcx + 1) * 128], qT[hs, :], True, True, [c["kTm"], qT], [ps_])
                self.act(E[:], ps_[:], AF.Exp, [ps_], [E])
                yield
                self.mm(pn[hs, :], c["vm"][:, mcx, hs], E[:], mcx == 0, mcx == 1, [c["vm"], E], [(pn, h)])
                self.mm(pd[hs, :], c["ones_bf"][:, 0:64], E[:], mcx == 0, mcx == 1, [c["ones_bf"], E], [(pd, h)])
                yield
        rd = self.scr("M_rd", [128, 512])
        self.recip(rd[:], pd[:], [pd], [rd])
        self.tt("vector", rd[:], pn[:], rd[:], ALU.mult, [pn, rd], [rd])
        yield
        self.tt("gpsimd", mixo[:, 4, :], rd[:], fm["Mz"][:, :], ALU.mult, [rd, fm["Mz"]], [(mixo, 4)])

    def branch_B(self, mc, fm, tm, mixo):
        self.begin("B")
        c = self.consts
        pT = c["pT"]
        pr = c["pr"]
        G = mc
        if getattr(self, "bstage", 9) < 1:
            return
        qn = self.scr("B_qn", [128, 4, 128], BF16)
        kn = self.scr("B_kn", [128, 4, 128], BF16)
        self.head_rms_tm((tm[:, :, 128:256], [tm]), (qn[:], [qn]), (c["gq8"][:], [c["gq8"]]), "bq")
        self.head_rms_tm((tm[:, :, 256:384], [tm]), (kn[:], [kn]), (pr[:, PR_BKG:PR_BKG + 128], [pr]), "bk")
        QT = self.scr("B_QT", [128, 512], BF16)
        if getattr(self, "bstage", 9) < 0.3:
            return
        for tt in range(4):
            self.tr(pT[:, tt * 128:(tt + 1) * 128], qn[:, tt, :], c["ident"][:], [qn, c["ident"]], [pT])
        for tt in range(4):
            self.tr(pT[:, 512 + tt * 128:512 + (tt + 1) * 128], kn[:, tt, :], c["ident"][:], [kn, c["ident"]], [pT])
        self.cp("vector", QT[:], pT[:, 0:512], [pT], [QT])
        self.cp("scalar", self.KT[:, G * 512:(G + 1) * 512], pT[:, 512:1024], [pT], [(self.KT, G)])
        QTm = [self.scr("B_QTm%d" % h, [128, 512], BF16) for h in range(2)]
        for h in range(2):
            self.ts("gpsimd", QTm[h][:], QT[:], self.hm[:, h:h + 1], ALU.mult, [QT, self.hm], [QTm[h]])
        for tt in range(4):
            self.cp("gpsimd", self.VB[:, G * 4 + tt, :], tm[:, tt, 384:512], [tm], [(self.VB, G * 4 + tt)])
        e1 = self.scr("B_e1", [128, 512])
        self.act(e1[:], fm["Bf"][:, :], AF.Exp, [(fm["Bf"], "cur")], [e1], scale=-1.0, bias=c["nfbB"][:, 0:1])
        self.act(e1[:], e1[:], AF.Ln, [e1], [e1], bias=1.0)
        Lc = self.scr("B_Lc", [128, 512])
        for q4 in range(4):
            qs = slice(q4 * 128, (q4 + 1) * 128)
            ini = self.Fprev[:, 0:1] if q4 == 0 else Lc[:, q4 * 128 - 1:q4 * 128]
            self.op("vector", lambda e, qs=qs, ini=ini: e.tensor_tensor_scan(
                out=Lc[:, qs], data0=c["onesf"][:, 0:128], data1=e1[:, qs], initial=ini,
                op0=ALU.mult, op1=ALU.add), [c["onesf"], e1, self.Fprev, Lc], [Lc])
        pcg = self.gps()
        for h in range(2):
            fs = slice(64, 128) if h == 0 else slice(0, 64)
            for j in range(4):
                self.mm(pcg[:, h * 8 + j:h * 8 + j + 1], Lc[fs, j * 128:(j + 1) * 128], c["c64f"][fs, 0:1],
                        True, True, [Lc, c["c64f"]], [pcg])
            for hf in range(2):
                col = hf * 256 + 127
                self.mm(pcg[:, 16 + h * 2 + hf:17 + h * 2 + hf], c["c64f"][fs, :], Lc[fs, col:col + 1], True, True,
                        [c["c64f"], Lc], [pcg])
        cg = self.scr("B_cg", [128, 4])
        for h in range(2):
            self.cp("vector", self.Fcol[:, h, G * 4:G * 4 + 4], pcg[:, h * 8:h * 8 + 4], [pcg], [(self.Fcol, G)])
        self.cp("vector", cg[:], pcg[:, 16:20], [pcg], [cg])
        self.cp("vector", self.Fprev[:], Lc[:, 511:512], [Lc], [self.Fprev])
        nkb = 4 * G + 4
        bias = self.scr("B_bias", [128, 4, self.SEQ // 128])
        for h in range(2):
            for hf in range(2):
                self.ts("vector", bias[:, h * 2 + hf, 0:nkb], self.Fcol[:, h, 0:nkb],
                        cg[:, h * 2 + hf:h * 2 + hf + 1], ALU.subtract, [self.Fcol, cg], [bias])
        pOff, pDg = c["pNum"], c["pDen"]
        Es = [self.scr("B_E%d" % i, [128, 512], BF16) for i in range(4)]
        EaccO = self.scr("B_EaccO", [128, 512])
        EaccD = self.scr("B_EaccD", [128, 512])
        EaccP = self.scr("B_EaccP", [128, 512])
        ndg = self.scr("B_ndg", [128, 512])
        rd = self.scr("B_rd", [128, 512])
        yb = self.scr("B_yb", [128, 512])
        sc = self.scr("B_sc", [128, 2])
        for h in range(2):
            self.tt("vector", sc[:, h:h + 1], cg[:, h * 2:h * 2 + 1], cg[:, h * 2 + 1:h * 2 + 2], ALU.subtract,
                    [cg], [sc])
        self.act(sc[:], sc[:], AF.Exp, [sc], [sc])
        noff = 4 * G
        pairs = [(h, kb) for h in range(2) for kb in range(nkb)]
        psl = {}

        def score(i):
            h, kb = pairs[i]
            j = kb - 4 * G
            c0 = 0 if j <= 0 else j * 128
            ps_ = self.gps()
            self.mm(ps_[:, c0:512], self.KT[:, kb * 128:(kb + 1) * 128], QTm[h][:, c0:512], True, True,
                    [self.KT, QTm[h]], [ps_])
            psl[i] = ps_
        score(0)
        for i, (h, kb) in enumerate(pairs):
            if i + 1 < len(pairs):
                score(i + 1)
            hs = slice(h * 64, (h + 1) * 64)
            j = kb - 4 * G
            c0 = 0 if j <= 0 else j * 128
            ps_ = psl.pop(i)
            E = Es[i % 4]
            if j < 0:
                self.act(E[:, :], ps_[:, :], AF.Exp, [ps_, bias], [E], bias=bias[:, h * 2, kb:kb + 1])
                self.mm(pOff[:, :], self.VB[:, kb, :], E[:, :], kb == 0, kb == noff - 1, [self.VB, E], [pOff])
                if kb == 0:
                    self.cp("vector", EaccO[:], E[:], [E], [EaccO])
                elif kb == 1:
                    self.cp("gpsimd", EaccP[:], E[:], [E], [EaccP])
                elif kb % 2 == 0:
                    self.tt("vector", EaccO[:], EaccO[:], E[:], ALU.add, [EaccO, E], [EaccO])
                else:
                    self.tt("gpsimd", EaccP[:], EaccP[:], E[:], ALU.add, [EaccP, E], [EaccP])
            else:
                for hf in range(2):
                    a_, b_ = max(c0, hf * 256), (hf + 1) * 256
                    if a_ >= b_:
                        continue
                    self.act(E[:, a_:b_], ps_[:, a_:b_], AF.Exp, [ps_, bias], [E],
                             bias=bias[:, h * 2 + hf, kb:kb + 1])
                self.tt("gpsimd", E[:, c0:c0 + 128], E[:, c0:c0 + 128], c["m_ge"][:], ALU.mult,
                        [E, c["m_ge"]], [E])
                self.mm(pDg[:, c0:512], self.VB[:, kb, :], E[:, c0:512], j == 0, kb == nkb - 1,
                        [self.VB, E], [pDg])
                if j == 0:
                    self.cp("vector", EaccD[:], E[:], [E], [EaccD])
                else:
                    self.tt("vector", EaccD[:, c0:512], EaccD[:, c0:512], E[:, c0:512], ALU.add,
                            [EaccD, E], [EaccD])
            if kb == nkb - 1:
                if noff > 0:
                    self.tt("gpsimd", EaccO[:], EaccO[:], EaccP[:], ALU.add, [EaccO, EaccP], [EaccO])
                    self.tt("vector", EaccD[:, 0:256], EaccD[:, 0:256], EaccO[:, 0:256], ALU.add,
                            [EaccD, EaccO], [EaccD])
                    self.stt(EaccD[:, 256:512], EaccO[:, 256:512], sc[:, h:h + 1], EaccD[:, 256:512], ALU.mult,
                             ALU.add, [EaccO, sc, EaccD], [EaccD])
                pd_ = self.gps()
                self.mm(pd_[hs, :], c["onesf"][:, 0:64], EaccD[:], True, True, [c["onesf"], EaccD], [pd_])
                self.recip(rd[hs, :], pd_[hs, :], [pd_], [rd])
                self.cp("scalar", ndg[hs, :], pDg[hs, :], [pDg], [ndg])
                if noff > 0:
                    self.tt("vector", ndg[hs, 0:256], ndg[hs, 0:256], pOff[hs, 0:256], ALU.add, [ndg, pOff], [ndg])
                    self.stt(ndg[hs, 256:512], pOff[hs, 256:512], sc[hs, h:h + 1], ndg[hs, 256:512], ALU.mult,
                             ALU.add, [pOff, sc, ndg], [ndg])
                self.tt("vector", yb[hs, :], ndg[hs, :], rd[hs, :], ALU.mult, [ndg, rd], [yb])
        self.tt("gpsimd", mixo[:, 1, :], yb[:], fm["Bz"][:, :], ALU.mult, [yb, fm["Bz"]], [(mixo, 1)])

    def branch_C(self, mc, fm, tm, mixo):
        self.begin("C")
        c = self.consts
        pc = c["pc"]
        pT = c["pT"]
        qc = self.scr("C_qc", [128, 512], BF16)
        kc = self.scr("C_kc", [128, 512], BF16)
        for g, dst, wn, bn in (("Cq", qc, "cq", "cqb"), ("Ck", kc, "ck", "ckb")):
            x = fm[g]
            acc = self.scr("C_acc" + g, [128, 512])
            self.ts("vector", acc[:], x[:, 0:512], pc[:, PC[wn + "0"]:PC[wn + "0"] + 1], ALU.mult, [x, pc], [acc],
                    s2=pc[:, PC[bn]:PC[bn] + 1], op1=ALU.add)
            for j in range(1, 4):
                self.stt(acc[:], x[:, j:j + 512], pc[:, PC[wn + str(j)]:PC[wn + str(j)] + 1], acc[:], ALU.mult,
                         ALU.add, [x, pc, acc], [acc])
            if g == "Cq":
                self.act(dst[:], acc[:], AF.Silu, [acc], [dst])
            else:
                self.act(acc[:], acc[:], AF.Silu, [acc], [acc])
                self.ts("vector", dst[:], acc[:], 0.125, ALU.mult, [acc], [dst])
        yield
        Lb = []
        for h in range(2):
            e1 = fm["Cf%d" % h]
            self.act(e1[:], fm["Cf%d" % h][:, :], AF.Exp, [(fm["Cf%d" % h], "cur")], [e1], scale=-1.0,
                     bias=c["nfbC"][:, h:h + 1])
            self.act(e1[:], e1[:], AF.Ln, [e1], [e1], bias=1.0)
            L = self.scr("C_Lb%d" % h, [128, 512])
            for ch in range(4):
                cs = slice(ch * 128, (ch + 1) * 128)
                self.op("vector", lambda e, L=L, e1=e1, cs=cs: e.tensor_tensor_scan(
                    out=L[:, cs], data0=c["onesf"][:, 0:128], data1=e1[:, cs], initial=0.0,
                    op0=ALU.mult, op1=ALU.add), [c["onesf"], e1], [L])
            Lb.append(L)
        ilog = fm["Ci"]
        self.ts("vector", ilog[:], fm["Ci"][:, :], pc[:, PC["ib_C"]:PC["ib_C"] + 1], ALU.add,
                [(fm["Ci"], "cur"), pc], [ilog])
        yield
        so = fm["Co"]
        sz = fm["Cz"]
        if not hasattr(self, "vaug"):
            self.vaug = [self.sb("C_vaug%d" % h, [128, 128], BF16) for h in range(2)]
            for h in range(2):
                self.memset("vector", self.vaug[h][:], 1.0, [self.vaug[h]])
        vaug = self.vaug
        for ch in range(4):
            cs = slice(ch * 128, (ch + 1) * 128)
            pcol = self.gps()
            for h in range(2):
                hs = slice(h * 64, (h + 1) * 64)
                self.mm(pcol[:, h:h + 1], Lb[h][0:64, cs], c["c64f"][0:64, 0:1], True, True, [Lb[h], c["c64f"]], [pcol])
                self.mm(pcol[:, 2 + h:3 + h], ilog[hs, cs], c["c64f"][hs, 0:1], True, True, [ilog, c["c64f"]], [pcol])
            col = self.scr("C_col", [128, 8])
            self.cp("vector", col[:, 0:4], pcol[:, 0:4], [pcol], [col])
            yield
            self.tt("vector", col[:, 4:6], col[:, 0:2], col[:, 2:4], ALU.add, [col], [col])
            for h in range(2):
                self.ts("vector", col[:, 6 + h:7 + h], col[:, 4 + h:5 + h], Lb[h][:, ch * 128 + 127:ch * 128 + 128],
                        ALU.subtract, [col, Lb[h]], [col])
            wcol = self.scr("C_wcol", [128, 2])
            self.act(wcol[:], col[:, 6:8], AF.Exp, [col], [wcol])
            yield
            eg = self.scr("C_eg", [128, 2])
            for h in range(2):
                self.act(eg[:, h:h + 1], Lb[h][:, ch * 128 + 127:ch * 128 + 128], AF.Exp, [Lb[h]], [eg], scale=-1.0)
            ebq = self.scr("C_ebq", [128, 128])
            for h in range(2):
                hs = slice(h * 64, (h + 1) * 64)
                self.act(ebq[hs, :], Lb[h][hs, cs], AF.Exp, [Lb[h]], [ebq], scale=-1.0)
            qp = self.scr("C_qp", [128, 128], BF16)
            self.tt("vector", qp[:], qc[:, cs], ebq[:], ALU.mult, [qc, ebq], [qp])
            yield
            for h in range(2):
                hs = slice(h * 64, (h + 1) * 64)
                self.cp("gpsimd", vaug[h][:, 0:64], tm[:, ch, 512 + h * 64:512 + (h + 1) * 64], [tm], [vaug[h]])
            pS = self.gps()
            P = []
            for h in range(2):
                hs = slice(h * 64, (h + 1) * 64)
                self.mm(pS[:, h * 128:(h + 1) * 128], kc[hs, cs], qc[hs, cs], True, True, [kc, qc], [pS])
                D = self.scr("C_D%d" % h, [128, 128])
                self.ts("vector", D[:], Lb[h][:, cs], col[:, h:h + 1], ALU.subtract, [Lb[h], col], [D], s2=0.0,
                        op1=ALU.max)
                self.act(D[:], D[:], AF.Exp, [D, col], [D], scale=-1.0, bias=col[:, 2 + h:3 + h])
                self.tt("gpsimd", D[:], D[:], c["m_ge"][:], ALU.mult, [D, c["m_ge"]], [D])
                Ph = self.scr("C_P%d" % h, [128, 128], BF16)
                self.tt("vector", Ph[:], pS[:, h * 128:(h + 1) * 128], D[:], ALU.mult, [pS, D], [Ph])
                P.append(Ph)
            yield
            pnd = self.gps()
            for h in range(2):
                hs = slice(h * 64, (h + 1) * 64)
                self.mm(pnd[hs, 0:128], vaug[h][:, 0:64], P[h][:], True, False, [vaug[h], P[h]], [pnd])
                self.mm(pnd[hs, 0:128], self.Cstb[hs, 0:64], qp[hs, :], False, True, [self.Cstb, qp], [pnd])
            for h in range(2):
                hs = slice(h * 64, (h + 1) * 64)
                self.mm(pnd[hs, 128:256], c["ones_bf"][:, 0:64], P[h][:], True, False, [c["ones_bf"], P[h]], [pnd])
                self.mm(pnd[hs, 128:256], self.Cstb[hs, 64:128], qp[hs, :], False, True, [self.Cstb, qp], [pnd])
            dn = self.scr("C_dn", [128, 128])
            self.act(dn[:], pnd[:, 128:256], AF.Abs, [pnd], [dn])
            self.ts("vector", dn[:], dn[:], 1.0, ALU.max, [dn], [dn])
            self.recip(dn[:], dn[:], [dn], [dn])
            ho = self.scr("C_ho", [128, 128])
            self.tt("vector", ho[:], pnd[:, 0:128], dn[:], ALU.mult, [pnd, dn], [ho])
            yield
            self.tt("gpsimd", ho[:], ho[:], so[:, cs], ALU.mult, [ho, so], [ho])
            yield
            sq = self.scr("C_sq", [128, 128], BF16)
            self.tt("gpsimd", sq[:], ho[:], ho[:], ALU.mult, [ho], [sq])
            pq = self.gps()
            self.mm(pq[:, 0:128], c["blk"][:], sq[:], True, True, [c["blk"], sq], [pq])
            rs = self.scr("C_rs", [128, 128])
            self.act(rs[:], pq[:, 0:128], AF.Ln, [pq], [rs], scale=1.0 / 64, bias=EPS)
            self.act(rs[:], rs[:], AF.Exp, [rs], [rs], scale=-0.5)
            self.stt(ho[:], ho[:], pc[:, PC["out_g"]:PC["out_g"] + 1], rs[:], ALU.mult, ALU.mult, [ho, pc, rs], [ho])
            self.tt("vector", mixo[:, 2, cs], ho[:], sz[:, cs], ALU.mult, [ho, sz], [(mixo, 2)])
            yield
            self.tr(pT[:, 0:128], kc[:, cs], c["ident"][:], [kc, c["ident"]], [pT])
            khat = self.scr("C_khat", [128, 128], BF16)
            for h in range(2):
                hs = slice(h * 64, (h + 1) * 64)
                self.ts("vector", khat[:, hs], pT[:, h * 64:(h + 1) * 64], wcol[:, h:h + 1], ALU.mult, [pT, wcol],
                        [khat])
            yield
            pC = self.gps()
            for h in range(2):
                hs = slice(h * 64, (h + 1) * 64)
                self.mm(pC[hs, 0:128], khat[:, hs], vaug[h][:], True, True, [khat, vaug[h]], [pC])
            for h in range(2):
                hs = slice(h * 64, (h + 1) * 64)
                self.stt(self.Cst[hs, :], self.Cst[hs, :], eg[hs, h:h + 1], pC[hs, 0:128], ALU.mult, ALU.add,
                         [self.Cst, eg, pC], [self.Cst])
            self.cp("vector", self.Cstb[:], self.Cst[:], [self.Cst], [self.Cstb])
            yield

    def branch_D(self, mc, fm, tm, mixo):
        self.begin("D")
        c = self.consts
        pc = c["pc"]
        pT = c["pT"]
        omm = c["omm"]
        if not hasattr(self, "m_gt"):
            self.m_gt = self.sb("m_gt", [128, 128], BF16)
            self.mN_gt = self.sb("mN_gt", [128, 128], BF16)
            self.memset("vector", self.m_gt[:], 1.0, [self.m_gt])
            self.asel(self.m_gt[:], self.m_gt[:], ALU.is_gt, [self.m_gt], [self.m_gt])
            self.memset("vector", self.mN_gt[:], 1.0, [self.mN_gt])
            self.asel(self.mN_gt[:], self.mN_gt[:], ALU.is_gt, [self.mN_gt], [self.mN_gt], cm=1, pat=[[-1, 128]])
        m_gt, mN_gt, m_ge = self.m_gt, self.mN_gt, c["m_ge"]

        def shift(g, idx, rows, name):
            x = fm[g]
            t = self.scr("D_sh_" + name, [128, 512])
            self.ts("vector", t[0:rows, :], x[0:rows, 1:513], omm[0:rows, idx:idx + 1], ALU.mult, [x, omm], [t])
            self.stt(t[0:rows, :], x[0:rows, 0:512], pc[0:rows, PC["mu_r"] + idx:PC["mu_r"] + idx + 1], t[0:rows, :],
                     ALU.mult, ALU.add, [x, pc, t], [t])
            return t
        rs = shift("Dr", 0, 128, "r")
        ks = shift("Dk", 1, 128, "k")
        vs = shift("Dv", 2, 128, "v")
        zs = shift("Dz", 3, 128, "z")
        ws = shift("Dw", 4, 16, "w")
        as_ = shift("Da", 5, 16, "a")
        self.act(ws[0:16, :], ws[0:16, :], AF.Tanh, [ws], [ws])
        pw = self.gps()
        self.mm(pw[:], c["w2s"][:], ws[0:16, :], True, True, [c["w2s"], ws], [pw])
        lw = ws
        self.act(lw[:], pw[:], AF.Sigmoid, [pw, pc], [lw], bias=pc[:, PC["w0"]:PC["w0"] + 1])
        self.ts("gpsimd", lw[:], lw[:], -0.6065306597126334, ALU.mult, [lw], [lw])
        pa = self.gps()
        self.mm(pa[:], c["a2s"][:], as_[0:16, :], True, True, [c["a2s"], as_], [pa])
        aa = as_
        self.act(aa[:], pa[:], AF.Sigmoid, [pa, pc], [aa], bias=pc[:, PC["a0"]:PC["a0"] + 1])
        sz = zs
        self.act(sz[:], zs[:], AF.Silu, [zs], [sz])
        yield
        vb = self.scr("D_vb", [128, 512], BF16)
        self.cp("gpsimd", vb[:], vs[:], [vs], [vb])
        kx = self.scr("D_kx", [128, 512])
        self.ts("vector", kx[:], ks[:], pc[:, PC["k_k"]:PC["k_k"] + 1], ALU.mult, [ks, pc], [kx])
        sqb = self.scr("D_sqb", [128, 512], BF16)
        self.tt("gpsimd", sqb[:], kx[:], kx[:], ALU.mult, [kx], [sqb])
        pq = self.gps()
        self.mm(pq[:], c["blk"][:], sqb[:], True, True, [c["blk"], sqb], [pq])
        rn = self.scr("D_rn", [128, 512])
        self.ts("vector", rn[:], pq[:], 1e-18, ALU.max, [pq], [rn])
        self.act(rn[:], rn[:], AF.Ln, [rn], [rn])
        self.act(rn[:], rn[:], AF.Exp, [rn], [rn], scale=-0.5)
        kk = kx
        self.tt("vector", kk[:], kx[:], rn[:], ALU.mult, [kx, rn], [kk])
        k2 = rn
        self.ts("vector", k2[:], aa[:], -1.0, ALU.add, [aa, pc], [k2], s2=pc[:, PC["k_a"]:PC["k_a"] + 1], op1=ALU.mult)
        self.stt(k2[:], k2[:], 1.0, ks[:], ALU.add, ALU.mult, [k2, ks], [k2])
        bv = ks
        self.tt("gpsimd", bv[:], kk[:], aa[:], ALU.mult, [kk, aa], [bv])
        yield
        rk = self.scr("D_rk", [128, 512], BF16)
        self.stt(rk[:], rs[:], pc[:, PC["r_k"]:PC["r_k"] + 1], k2[:], ALU.mult, ALU.mult, [rs, pc, k2], [rk])
        pb = self.gps()
        self.mm(pb[:], c["blk"][:], rk[:], True, True, [c["blk"], rk], [pb])
        bon = vs
        self.tt("vector", bon[:], pb[:], vs[:], ALU.mult, [pb, vs], [bon])
        cl = self.scr("D_cl", [128, 512])
        for ch in range(4):
            cs = slice(ch * 128, (ch + 1) * 128)
            self.op("vector", lambda e, cs=cs: e.tensor_tensor_scan(
                out=cl[:, cs], data0=c["onesf"][:, 0:128], data1=lw[:, cs], initial=0.0,
                op0=ALU.mult, op1=ALU.add), [c["onesf"], lw], [cl])
        Ecl = self.scr("D_Ecl", [128, 512])
        Encl = self.scr("D_Encl", [128, 512])
        self.act(Ecl[:], cl[:], AF.Exp, [cl], [Ecl])
        self.act(Encl[:], cl[:], AF.Exp, [cl], [Encl], scale=-1.0)
        Ecx = lw
        self.tt("gpsimd", lw[:], cl[:], lw[:], ALU.subtract, [cl, lw], [lw])
        self.act(Ecx[:], lw[:], AF.Exp, [lw], [Ecx])
        yield
        clT = self.scr("D_clT", [128, 4])
        self.cp("vector", clT[:], cl[:].rearrange("p (a t) -> p a t", a=4)[:, :, 127], [cl], [clT])
        Eh = cl
        for ch in range(4):
            cs = slice(ch * 128, (ch + 1) * 128)
            self.act(Eh[:, cs], cl[:, cs], AF.Exp, [cl, clT], [Eh], scale=-1.0, bias=clT[:, ch:ch + 1])
        KR = self.scr("D_KR", [128, 4, 256], BF16)
        kt = self.scr("D_kt", [128, 512], BF16)
        bt = self.scr("D_bt", [128, 512], BF16)
        khat = self.scr("D_khat", [128, 512], BF16)
        nbh = self.scr("D_nbh", [128, 512], BF16)
        self.tt("vector", KR[:, :, 0:128], kk[:].rearrange("p (a t) -> p a t", a=4),
                Ecx[:].rearrange("p (a t) -> p a t", a=4), ALU.mult, [kk, Ecx], [KR])
        self.tt("gpsimd", KR[:, :, 128:256], rs[:].rearrange("p (a t) -> p a t", a=4),
                Ecl[:].rearrange("p (a t) -> p a t", a=4), ALU.mult, [rs, Ecl], [KR])
        self.tt("vector", kt[:], k2[:], Encl[:], ALU.mult, [k2, Encl], [kt])
        self.tt("gpsimd", bt[:], bv[:], Encl[:], ALU.mult, [bv, Encl], [bt])
        self.tt("vector", khat[:], k2[:], Eh[:], ALU.mult, [k2, Eh], [khat])
        self.stt(nbh[:], bv[:], -1.0, Eh[:], ALU.mult, ALU.mult, [bv, Eh], [nbh])
        yraw = kx
        yield
        for ch in range(4):
            cs = slice(ch * 128, (ch + 1) * 128)
            self.tr(pT[:, 0:128], KR[:, ch, 0:128], c["ident"][:], [KR, c["ident"]], [pT])
            self.tr(pT[:, 128:256], vb[:, cs], c["ident"][:], [vb, c["ident"]], [pT])
            self.tr(pT[:, 256:384], khat[:, cs], c["ident"][:], [khat, c["ident"]], [pT])
            self.tr(pT[:, 384:512], nbh[:, cs], c["ident"][:], [nbh, c["ident"]], [pT])
            TMs = self.scr("D_TMs", [128, 4, 128], BF16)
            self.cp("scalar", TMs[:], pT[:, 0:512].rearrange("p (a t) -> p a t", a=4), [pT], [TMs])
            yield
            rpT = self.scr("D_rpT", [128, 128], BF16)
            GT = self.scr("D_GT", [128, 64], BF16)
            Hs = self.scr("D_Hs", [128, 64])
            py = c["pNum"]
            pHS = c["pDen"]
            HS = [slice(0, 64), slice(64, 128)]
            LT, nQbT, LkT, QkT, LN, R, R2 = {}, {}, {}, {}, {}, {}, {}
            for h in range(2):
                hs = HS[h]
                p1 = self.gps()
                self.mm(p1[:, 0:256], bt[hs, cs], KR[hs, ch, :], True, True, [bt, KR], [p1])
                self.mm(p1[:, 256:512], kt[hs, cs], KR[hs, ch, :], True, True, [kt, KR], [p1])
                LT[h] = self.scr("D_LT%d" % h, [128, 128], BF16)
                nQbT[h] = self.scr("D_nQbT%d" % h, [128, 128], BF16)
                LkT[h] = self.scr("D_LkT%d" % h, [128, 128], BF16)
                QkT[h] = self.scr("D_QkT%d" % h, [128, 128], BF16)
                self.tt("vector", LT[h][:], p1[:, 0:128], m_gt[:], ALU.mult, [p1, m_gt], [LT[h]])
                self.tt("vector", LkT[h][:], p1[:, 256:384], m_gt[:], ALU.mult, [p1, m_gt], [LkT[h]])
                self.stt(nQbT[h][:], p1[:, 128:256], -1.0, m_ge[:], ALU.mult, ALU.mult, [p1, m_ge], [nQbT[h]])
                self.tt("vector", QkT[h][:], p1[:, 384:512], m_ge[:], ALU.mult, [p1, m_ge], [QkT[h]])
                yield
            for h in range(2):
                hs = HS[h]
                p2 = self.gps()
                self.mm(p2[:, 0:128], KR[hs, ch, 0:128], bt[hs, cs], True, True, [KR, bt], [p2])
                self.mm(p2[:, 128:192], LkT[h][:], TMs[:, 1, hs], True, True, [LkT[h], TMs], [p2])
                LN[h] = self.scr("D_LN%d" % h, [128, 128], BF16)
                self.tt("vector", LN[h][:], p2[:, 0:128], mN_gt[:], ALU.mult, [p2, mN_gt], [LN[h]])
                R[h] = self.scr("D_R%d" % h, [128, 128], BF16)
                self.cp("gpsimd", R[h][:, 0:64], TMs[:, 0, hs], [TMs], [R[h]])
                self.cp("scalar", R[h][:, 64:128], p2[:, 128:192], [p2], [R[h]])
                yield
            Rc, Rn, PTc, PNc = {}, {}, {}, {}
            for h in range(2):
                pr_ = self.gps()
                self.mm(pr_[:, 0:128], LT[h][:], R[h][:], True, True, [LT[h], R[h]], [pr_])
                R2[h] = self.scr("D_Rb%d" % h, [128, 128], BF16)
                self.tt("vector", R2[h][:], R[h][:], pr_[:, 0:128], ALU.subtract, [R[h], pr_], [R2[h]])
                Rc[h], Rn[h] = R2[h], R[h]
                PTc[h], PNc[h] = LT[h], LN[h]
            yield
            for j in range(1, 7):
                PTn, PNn = {}, {}
                for h in range(2):
                    PTn[h] = self.scr("D_PT%d_%d" % (h, j % 2), [128, 128], BF16)
                    PNn[h] = self.scr("D_PN%d_%d" % (h, j % 2), [128, 128], BF16)
                    pp = self.gps()
                    self.mm(pp[:, 0:128], PNc[h][:], PTc[h][:], True, True, [PNc[h], PTc[h]], [pp])
                    if j < 6:
                        self.mm(pp[:, 128:256], PTc[h][:], PNc[h][:], True, True, [PNc[h], PTc[h]], [pp])
                        self.cp("scalar", PTn[h][:], pp[:, 0:128], [pp], [PTn[h]])
                        self.cp("vector", PNn[h][:], pp[:, 128:256], [pp], [PNn[h]])
                    else:
                        self.cp("scalar", PTn[h][:], pp[:, 0:128], [pp], [PTn[h]])
                yield
                for h in range(2):
                    pr_ = self.gps()
                    self.mm(pr_[:, 0:128], PTn[h][:], Rc[h][:], True, True, [PTn[h], Rc[h]], [pr_])
                    self.tt("vector", Rn[h][:], Rc[h][:], pr_[:, 0:128], ALU.add, [Rc[h], pr_], [Rn[h]])
                    Rc[h], Rn[h] = Rn[h], Rc[h]
                    PTc[h], PNc[h] = PTn[h], PNn[h]
                yield
            for h in range(2):
                hs = HS[h]
                Rch = Rc[h]
                pr_ = self.gps()
                self.mm(pr_[hs, 0:128], Rch[:, 0:64], nQbT[h][:], True, True, [Rch, nQbT[h]], [pr_])
                self.tt("vector", rpT[hs, :], pr_[hs, 0:128], KR[hs, ch, 128:256], ALU.add, [pr_, KR], [rpT])
                self.mm(py[hs, 0:128], TMs[:, 1, hs], QkT[h][:], True, False, [TMs, QkT[h]], [(py, h)])
                self.mm(py[hs, 0:128], Rch[:, 64:128], nQbT[h][:], False, False, [Rch, nQbT[h]], [(py, h)])
                self.mm(py[hs, 0:128], self.Sstb[hs, :], rpT[hs, :], False, True, [self.Sstb, rpT], [(py, h)])
                pg = self.gps()
                self.mm(pg[hs, 0:64], Rch[:, 0:64], TMs[:, 3, hs], True, True, [Rch, TMs], [pg])
                self.stt(GT[hs, :], c["identf"][hs, h * 64:(h + 1) * 64], Ecl[hs, ch * 128 + 127:ch * 128 + 128],
                         pg[hs, 0:64], ALU.mult, ALU.add, [c["identf"], Ecl, pg], [GT])
                self.mm(pHS[hs, 0:64], TMs[:, 2, hs], TMs[:, 1, hs], True, False, [TMs], [(pHS, h)])
                self.mm(pHS[hs, 0:64], TMs[:, 3, hs], Rch[:, 64:128], False, True, [TMs, Rch], [(pHS, h)])
                self.cp("scalar", Hs[hs, :], pHS[hs, 0:64], [(pHS, h)], [Hs])
                self.mm(pHS[hs, 64:128], GT[hs, :], self.Sstb[hs, :], True, True, [GT, self.Sstb], [(pHS, h)])
                self.tt("vector", self.Sst[hs, :], pHS[hs, 64:128], Hs[hs, :], ALU.add, [(pHS, h), Hs], [self.Sst])
                yield
            self.cp("vector", self.Sstb[:], self.Sst[:], [self.Sst], [self.Sstb])
            self.cp("scalar", yraw[:, cs], py[:, 0:128], [py], [yraw])
        ysq = sqb
        self.tt("gpsimd", ysq[:], yraw[:], yraw[:], ALU.mult, [yraw], [ysq])
        pq2 = self.gps()
        self.mm(pq2[:], c["blk"][:], ysq[:], True, True, [c["blk"], ysq], [pq2])
        rs2 = rn
        self.act(rs2[:], pq2[:], AF.Ln, [pq2], [rs2], scale=1.0 / 64, bias=EPS)
        self.act(rs2[:], rs2[:], AF.Exp, [rs2], [rs2], scale=-0.5)
        self.stt(yraw[:], yraw[:], pc[:, PC["ln_g"]:PC["ln_g"] + 1], rs2[:], ALU.mult, ALU.mult, [yraw, pc, rs2],
                 [yraw])
        self.tt("vector", yraw[:], yraw[:], bon[:], ALU.add, [yraw, bon], [yraw])
        self.tt("gpsimd", mixo[:, 3, :], yraw[:], sz[:], ALU.mult, [yraw, sz], [(mixo, 3)])


def build_final(SEQ):
    Bd = Builder(SEQ, False)
    nc = Bd.nc
    x1 = Bd.dram("x1", [SEQ, 1024], F32, "ExternalInput")
    mp = Bd.dram("mprev", [1280, SEQ], BF16, "ExternalInput")
    wout_d = Bd.dram("wout", [1280, 1024], F32, "ExternalInput")
    out = Bd.dram("out", [SEQ, 1024], F32, "ExternalOutput")
    pP = [Bd.psum("pP0", [128, 512]), Bd.psum("pP1", [128, 512])]
    wo = Bd.sb("wo", [128, 10, 1024], BF16)
    wov = wout_d.rearrange("(c p) n -> p c n", p=128)
    for c in range(10):
        Bd.dma("gpsimd", wo[:, c, :], wov[:, c, :], w=[(wo, c)])
    xts = [Bd.sb("xt%d" % i, [128, 4, 1024]) for i in range(2)]
    mpvs = [Bd.sb("mpv%d" % i, [128, 10, 512], BF16) for i in range(2)]
    outs = []
    for mc in range(SEQ // 512):
        t0 = mc * 512
        xt = xts[mc % 2]
        mpv = mpvs[mc % 2]
        Bd.dma("sync", xt[:], x1[t0:t0 + 512, :].rearrange("(tt p) d -> p tt d", p=128), w=[xt])
        Bd.dma("sync", mpv[:], mp[:, t0:t0 + 512].rearrange("(c p) t -> p c t", p=128), w=[mpv])
        for tt in range(4):
            for hf in range(2):
                p = pP[hf]
                for c in range(10):
                    Bd.mm(p[:], mpv[:, c, tt * 128:(tt + 1) * 128], wo[:, c, hf * 512:(hf + 1) * 512],
                          c == 0, c == 9, [mpv, (wo, c)], [p])
                Bd.tt("vector", xt[:, tt, hf * 512:(hf + 1) * 512], xt[:, tt, hf * 512:(hf + 1) * 512],
                      p[:], ALU.add, [xt, p], [xt])
        o = Bd.dma("sync", out[t0:t0 + 512, :].rearrange("(tt p) d -> p tt d", p=128), xt[:], r=[xt], w=["out_d"])
        outs.append(o)
    Bd.S.emit(final_wait_ops=outs)
    return nc


_CACHE = {}


def _get_prog(key, fn):
    if key not in _CACHE:
        _CACHE[key] = fn()
    return _CACHE[key]


def kernel_unfused(**inputs):
    inp = {k: np.asarray(v) for k, v in inputs.items()}
    x = inp["x"]
    BATCH, SEQ, _ = x.shape
    n = 8
    cores = [(b, hh) for b in range(BATCH) for hh in range(2)]
    xcur = [np.ascontiguousarray(x[b]) for b in range(BATCH)]
    mprev = None
    for l in range(2):
        has_prev = l > 0
        nc = _get_prog(("layer", SEQ, has_prev), lambda: Builder(SEQ, has_prev).build())
        in_maps = []
        for (b, hh) in cores:
            d = host_layer_params(inp, l, hh)
            d["xin"] = xcur[b]
            d["mem"] = np.ascontiguousarray(inp["mem"][b])
            if has_prev:
                d["mprev"] = mprev[b]
                d["wout"] = np.ascontiguousarray(inp["w_out"][l - 1])
            in_maps.append(d)
        res = run_bass_kernel_spmd(nc, in_maps, core_ids=list(range(n)))
        new_m = []
        for b in range(BATCH):
            full = np.zeros((1280, SEQ), dtype=ml_dtypes.bfloat16)
            for hh in range(2):
                m = np.asarray(res.results[b * 2 + hh]["mixed"])
                for g in range(5):
                    full[g * 256 + hh * 128:g * 256 + hh * 128 + 128] = m[g * 128:(g + 1) * 128]
            new_m.append(full)
            if has_prev:
                xcur[b] = np.asarray(res.results[b * 2]["x1out"])
        mprev = new_m
    ncf = _get_prog(("final", SEQ), lambda: build_final(SEQ // 2))
    in_maps = []
    H = SEQ // 2
    for (b, hh) in cores:
        in_maps.append({"x1": np.ascontiguousarray(xcur[b][hh * H:(hh + 1) * H]),
                        "mprev": np.ascontiguousarray(mprev[b][:, hh * H:(hh + 1) * H]),
                        "wout": np.ascontiguousarray(inp["w_out"][1])})
    res = run_bass_kernel_spmd(ncf, in_maps, core_ids=list(range(n)))
    out = np.zeros((BATCH, SEQ, 1024), np.float32)
    for i, (b, hh) in enumerate(cores):
        out[b, hh * H:(hh + 1) * H] = np.asarray(res.results[i]["out"])
    return out


def kernel(**inputs):
    inp = {k: np.asarray(v) for k, v in inputs.items()}
    x = inp["x"]
    BATCH, SEQ, _ = x.shape
    nc = _get_prog(("fused", SEQ), lambda: Builder(SEQ, False).build_fused())
    per = {}
    for l in range(2):
        for hh in range(2):
            d = host_layer_params(inp, l, hh)
            for k, v in d.items():
                per["%s_%d%d" % (k, l, hh)] = v
    in_maps = []
    for b in range(BATCH):
        d = dict(per)
        d["xin"] = np.ascontiguousarray(x[b])
        d["mem"] = np.ascontiguousarray(inp["mem"][b])
        d["wout0"] = np.ascontiguousarray(inp["w_out"][0])
        d["wout1"] = np.ascontiguousarray(inp["w_out"][1])
        in_maps.append(d)
    res = run_bass_kernel_spmd(nc, in_maps, core_ids=list(range(BATCH)))
    out = np.stack([np.asarray(res.results[b]["out"]) for b in range(BATCH)], axis=0)
    return out.astype(np.float32)
```

```python
from contextlib import ExitStack
import numpy as np
import ml_dtypes
import concourse.bass as bass
import concourse.mybir as mybir
from concourse.bass_utils import run_bass_kernel_spmd

F32 = mybir.dt.float32
BF16 = mybir.dt.bfloat16
ALU = mybir.AluOpType
AF = mybir.ActivationFunctionType
AX = mybir.AxisListType

ENGINES = ("tensor", "vector", "scalar", "gpsimd", "sync")
SEM_CAP = 30000
EPS = 1e-6


class _Op:
    __slots__ = ("eng", "fn", "idx", "deps", "signal", "is_dma", "sem", "val", "pre_wait")

    def __init__(self, eng, fn, is_dma):
        self.eng = eng
        self.fn = fn
        self.is_dma = is_dma
        self.deps = []
        self.signal = False
        self.sem = None
        self.val = None
        self.pre_wait = None


class Tl:
    def __init__(self, name, t):
        self.name = name
        self.t = t

    def __getitem__(self, idx):
        return self.t[idx]


def _norm(rs):
    out = []
    for r in rs:
        if isinstance(r, tuple):
            a, k = r
        else:
            a, k = r, None
        if isinstance(a, Tl):
            a = a.name
        if a.startswith("scr"):
            k = None
        out.append((a, k))
    return out


class Sched:
    def __init__(self, nc, stack, n_dma_sems=16):
        self.nc = nc
        self.stack = stack
        self.ops = {e: [] for e in ENGINES}
        self.state = {}
        self.n_dma_sems = n_dma_sems

    def _entries(self, name, key):
        d = self.state.setdefault(name, {})
        if key is None:
            return list(d.values())
        res = []
        if key in d:
            res.append(d[key])
        if None in d:
            res.append(d[None])
        return res

    def add(self, eng, fn, reads=(), writes=(), dma=False):
        reads = _norm(reads)
        writes = _norm(writes)
        op = _Op(eng, fn, dma)
        deps = []
        for (name, key) in reads:
            for ent in self._entries(name, key):
                if ent[0] is not None:
                    deps.append(ent[0])
                if name[0] == "p" and name[1].isupper():
                    deps.extend(o_ for o_ in ent[1] if o_.eng != eng)
        for (name, key) in writes:
            for ent in self._entries(name, key):
                if ent[0] is not None:
                    deps.append(ent[0])
                deps.extend(ent[1])
        for (name, key) in reads:
            d = self.state.setdefault(name, {})
            if key is None:
                if not d:
                    d[None] = [None, []]
                for ent in d.values():
                    ent[1].append(op)
            else:
                if key not in d:
                    d[key] = [None, []]
                d[key][1].append(op)
        for (name, key) in writes:
            d = self.state.setdefault(name, {})
            if key is None:
                d.clear()
                d[None] = [op, []]
            else:
                d[key] = [op, []]
        op.idx = len(self.ops[eng])
        best = {}
        dl = []
        for dop in deps:
            if dop is op:
                continue
            if dop.is_dma:
                if dop not in dl:
                    dl.append(dop)
            else:
                if dop.eng == "tensor" and eng == "tensor" and not dma:
                    continue
                b = best.get(dop.eng)
                if b is None or dop.idx > b.idx:
                    best[dop.eng] = dop
        op.deps = dl + list(best.values())
        for dop in op.deps:
            dop.signal = True
        self.ops[eng].append(op)
        return op

    def emit(self, final_wait_ops=()):
        nc = self.nc
        for eng in ENGINES:
            cnt = 0
            sem = None
            for op in self.ops[eng]:
                if op.is_dma:
                    continue
                if op.signal:
                    if sem is None or cnt >= SEM_CAP:
                        sem = self.stack.enter_context(nc.semaphore(f"s_{eng}_{op.idx}"))
                        cnt = 0
                    cnt += 1
                    op.sem = sem
                    op.val = cnt
        for eng in ENGINES:
            qpool = []
            k = 0
            for op in self.ops[eng]:
                if not op.is_dma:
                    continue
                if len(qpool) < self.n_dma_sems:
                    s = self.stack.enter_context(nc.semaphore(f"d_{eng}_{len(qpool)}"))
                    qpool.append([s, 0])
                    ent = qpool[-1]
                else:
                    ent = qpool[k % self.n_dma_sems]
                    if ent[1] + 16 > SEM_CAP:
                        ent[0] = self.stack.enter_context(nc.semaphore(f"d_{eng}_x{k}"))
                        ent[1] = 0
                if ent[1] > 0:
                    op.pre_wait = (ent[0], ent[1])
                ent[1] += 16
                op.sem = ent[0]
                op.val = ent[1]
                k += 1
        sched = self

        def run(eng_name, e):
            seen = {}
            for op in sched.ops[eng_name]:
                waits = []
                if op.pre_wait is not None:
                    waits.append(op.pre_wait)
                for dop in op.deps:
                    waits.append((dop.sem, dop.val))
                for (s, v) in waits:
                    key = id(s)
                    if seen.get(key, 0) >= v:
                        continue
                    seen[key] = v
                    e.wait_ge(s, v)
                ins = op.fn(e)
                if op.is_dma:
                    ins.then_inc(op.sem, 16)
                elif op.signal:
                    ins.then_inc(op.sem, 1)
            if eng_name == "sync":
                for fop in final_wait_ops:
                    e.wait_ge(fop.sem, fop.val)

        with nc.Block() as block:
            @block.sync
            def _(e):
                run("sync", e)

            @block.tensor
            def _(e):
                run("tensor", e)

            @block.vector
            def _(e):
                run("vector", e)

            @block.scalar
            def _(e):
                run("scalar", e)

            @block.gpsimd
            def _(e):
                run("gpsimd", e)


D_MODEL = 1024
FM_GROUPS = ["Au", "Az", "Bz", "Cq", "Ck", "Co", "Cz", "Dr", "Dk", "Dv", "Dz", "Mz",
             "Bf", "Ci", "Cf0", "Cf1", "Dw", "Da"]
FM_W = {g: 128 for g in FM_GROUPS}
FM_W["Dw"] = 16
FM_W["Da"] = 16
FM_OFF = {}
_o = 0
for _g in FM_GROUPS:
    FM_OFF[_g] = _o
    _o += FM_W[_g]
NFM = _o
TM_GROUPS = ["Av", "Bq", "Bk", "Bv", "Cv", "Mq"]
NTM = 768
NW = NFM + NTM
PC = {n: i for i, n in enumerate([
    "cq0", "cq1", "cq2", "cq3", "ck0", "ck1", "ck2", "ck3", "cqb", "ckb",
    "mu_r", "mu_k", "mu_v", "mu_z", "mu_w", "mu_a",
    "k_k", "k_a", "a0", "w0", "r_k", "ln_g", "out_g",
    "fb_B", "ib_C", "fb_C0", "fb_C1"])}
NPC = len(PC)
PR_SGU_G, PR_BQG, PR_BKG, PR_MQG, PR_MKG, PR_SGUB = 0, 128, 256, 384, 512, 640
NPR = 768

A_OFF = 0
B_OFF = 768
C_OFF = 768 + 1028
D_OFF = C_OFF + 1288
M_OFF = D_OFF + 1056


def host_layer_params(inp, l, hh):
    f32 = np.float32
    w_in = inp["w_in"][l]
    hs = [2 * hh, 2 * hh + 1]

    def hcols(base):
        return np.concatenate([np.arange(base + h * 64, base + h * 64 + 64) for h in hs])

    cols = {}
    cols["Au"] = hcols(A_OFF)
    cols["Av"] = hcols(A_OFF + 256)
    cols["Az"] = hcols(A_OFF + 512)
    cols["Bq"] = hcols(B_OFF)
    cols["Bk"] = hcols(B_OFF + 256)
    cols["Bv"] = hcols(B_OFF + 512)
    bf = B_OFF + 768
    cols["Bf"] = np.concatenate([np.full(64, bf + hs[1]), np.full(64, bf + hs[0])])
    cols["Bz"] = hcols(B_OFF + 772)
    cols["Cq"] = hcols(C_OFF)
    cols["Ck"] = hcols(C_OFF + 256)
    cols["Cv"] = hcols(C_OFF + 512)
    ci = C_OFF + 768
    cols["Ci"] = np.concatenate([np.full(64, ci + hs[0]), np.full(64, ci + hs[1])])
    cols["Cf0"] = np.full(128, ci + 4 + hs[0])
    cols["Cf1"] = np.full(128, ci + 4 + hs[1])
    cols["Co"] = hcols(C_OFF + 776)
    cols["Cz"] = hcols(C_OFF + 1032)
    cols["Dr"] = hcols(D_OFF)
    cols["Dw"] = np.arange(D_OFF + 256, D_OFF + 272)
    cols["Dk"] = hcols(D_OFF + 272)
    cols["Dv"] = hcols(D_OFF + 528)
    cols["Da"] = np.arange(D_OFF + 784, D_OFF + 800)
    cols["Dz"] = hcols(D_OFF + 800)
    cols["Mq"] = hcols(M_OFF)
    cols["Mz"] = hcols(M_OFF + 256)
    allc = np.concatenate([cols[g] for g in FM_GROUPS] + [cols[g] for g in TM_GROUPS])
    wcat = np.ascontiguousarray(w_in[:, allc])

    hc = hcols(0)
    pc = np.zeros((128, NPC), f32)
    cw = inp["mlstm_conv_w"][l]
    cb = inp["mlstm_conv_b"][l]
    for j in range(4):
        pc[:, PC["cq%d" % j]] = cw[j, hc]
        pc[:, PC["ck%d" % j]] = cw[j, 256 + hc]
    pc[:, PC["cqb"]] = cb[hc]
    pc[:, PC["ckb"]] = cb[256 + hc]
    mu = inp["rwkv_mu"][l]
    pc[:, PC["mu_r"]] = mu[hc]
    pc[:16, PC["mu_w"]] = mu[256:272]
    pc[:, PC["mu_k"]] = mu[272 + hc]
    pc[:, PC["mu_v"]] = mu[528 + hc]
    pc[:16, PC["mu_a"]] = mu[784:800]
    pc[:, PC["mu_z"]] = mu[800 + hc]
    pc[:, PC["k_k"]] = inp["rwkv_k_k"][l][hc]
    pc[:, PC["k_a"]] = inp["rwkv_k_a"][l][hc]
    pc[:, PC["a0"]] = inp["rwkv_a0"][l][hc]
    pc[:, PC["w0"]] = inp["rwkv_w0"][l][hc]
    pc[:, PC["r_k"]] = inp["rwkv_r_k"][l].reshape(-1)[hc]
    pc[:, PC["ln_g"]] = inp["rwkv_ln_g"][l][hc]
    pc[:, PC["out_g"]] = inp["mlstm_out_g"][l][hc]
    fb = inp["fox_f_b"][l]
    pc[:, PC["fb_B"]] = np.concatenate([np.full(64, fb[hs[1]]), np.full(64, fb[hs[0]])])
    ib = inp["mlstm_i_b"][l]
    pc[:, PC["ib_C"]] = np.concatenate([np.full(64, ib[hs[0]]), np.full(64, ib[hs[1]])])
    fbc = inp["mlstm_f_b"][l]
    pc[:, PC["fb_C0"]] = fbc[hs[0]]
    pc[:, PC["fb_C1"]] = fbc[hs[1]]

    pr = np.zeros((128, NPR), f32)
    pr[:, PR_SGU_G:PR_SGU_G + 128] = inp["sgu_norm_g"][l][hc][None, :]
    pr[:, PR_BQG:PR_BQG + 128] = np.tile(inp["fox_q_g"][l], 2)[None, :]
    pr[:, PR_BKG:PR_BKG + 128] = np.tile(inp["fox_k_g"][l], 2)[None, :]
    pr[:, PR_MQG:PR_MQG + 128] = np.tile(inp["mem_q_g"][l], 2)[None, :]
    pr[:, PR_MKG:PR_MKG + 128] = np.tile(inp["mem_k_g"][l], 2)[None, :]
    sb_ = inp["sgu_b"][l]
    pr[:64, PR_SGUB:PR_SGUB + 128] = sb_[hs[0]][None, :]
    pr[64:, PR_SGUB:PR_SGUB + 128] = sb_[hs[1]][None, :]

    d = {
        "wcat": wcat,
        "pc": pc,
        "pr": pr,
        "ng": np.ascontiguousarray(inp["norm_g"][l].reshape(8, 128).T),
        "memg": np.ascontiguousarray(inp["mem_norm_g"][l].reshape(8, 128).T),
        "wkv": np.ascontiguousarray(np.concatenate(
            [inp["mem_w_kv"][l][:, hc], inp["mem_w_kv"][l][:, 256 + hc]], axis=1)),
        "w2": np.ascontiguousarray(inp["rwkv_w2"][l][:, hc]),
        "a2": np.ascontiguousarray(inp["rwkv_a2"][l][:, hc]),
        "sguw": np.ascontiguousarray(inp["sgu_w"][l][hs]),
    }
    return d


class Builder:
    def __init__(self, SEQ, has_prev, branches="ABCDM"):
        self.SEQ = SEQ
        self.has_prev = has_prev
        self.branches = branches
        self.nc = bass.Bass("TRN2", target_bir_lowering=False)
        self.st = ExitStack()
        self.S = Sched(self.nc, self.st)
        self.ps_rr = 0

    def dram(self, name, shape, dt, kind):
        return self.nc.dram_tensor(name, shape, dt, kind=kind).ap()

    def sb(self, name, shape, dt=F32):
        if not hasattr(self, "_tiles"):
            self._tiles = {}
        if name not in self._tiles:
            self._tiles[name] = Tl(name, self.st.enter_context(self.nc.sbuf_tensor(name, shape, dt)))
        return self._tiles[name]

    def psum(self, name, shape, dt=F32):
        if not hasattr(self, "_tiles"):
            self._tiles = {}
        if name not in self._tiles:
            self._tiles[name] = Tl(name, self.st.enter_context(self.nc.psum_tensor(name, shape, dt)))
        return self._tiles[name]

    def gps(self):
        p = self.gp[self.ps_rr % len(self.gp)]
        self.ps_rr += 1
        return p

    def op(self, eng, fn, r=(), w=()):
        return self.S.add(eng, fn, reads=r, writes=w)

    def dma(self, eng, out, in_, r=(), w=()):
        return self.S.add(eng, lambda e: e.dma_start(out=out, in_=in_), reads=r, writes=w, dma=True)

    def mm(self, out, lhsT, rhs, start, stop, r, w):
        return self.S.add("tensor", lambda e: e.matmul(out, lhsT=lhsT, rhs=rhs, start=start, stop=stop),
                          reads=r, writes=w)

    def tr(self, out, in_, ident, r, w):
        return self.S.add("tensor", lambda e: e.transpose(out, in_, ident), reads=r, writes=w)

    def act(self, out, in_, func, r, w, bias=None, scale=None, accum_out=None, eng="scalar"):
        kw = {}
        if bias is not None:
            kw["bias"] = bias
        if scale is not None:
            kw["scale"] = scale
        if accum_out is not None:
            kw["accum_out"] = accum_out
        return self.S.add("scalar", lambda e: e.activation(out=out, in_=in_, func=func, **kw), reads=r, writes=w)

    def tt(self, eng, out, in0, in1, op, r, w):
        return self.S.add(eng, lambda e: e.tensor_tensor(out=out, in0=in0, in1=in1, op=op), reads=r, writes=w)

    def ts(self, eng, out, in0, s1, op0, r, w, s2=None, op1=None):
        if op1 is None:
            return self.S.add(eng, lambda e: e.tensor_scalar(out=out, in0=in0, scalar1=s1, scalar2=None, op0=op0),
                              reads=r, writes=w)
        return self.S.add(eng, lambda e: e.tensor_scalar(out=out, in0=in0, scalar1=s1, scalar2=s2, op0=op0, op1=op1),
                          reads=r, writes=w)

    def stt(self, out, in0, scalar, in1, op0, op1, r, w):
        return self.S.add("vector", lambda e: e.scalar_tensor_tensor(out=out, in0=in0, scalar=scalar, in1=in1,
                                                                      op0=op0, op1=op1), reads=r, writes=w)

    def cp(self, eng, out, in_, r, w):
        if eng == "scalar":
            return self.S.add("scalar", lambda e: e.copy(out=out, in_=in_), reads=r, writes=w)
        return self.S.add(eng, lambda e: e.tensor_copy(out, in_), reads=r, writes=w)

    def recip(self, out, in_, r, w):
        return self.S.add("vector", lambda e: e.reciprocal(out, in_), reads=r, writes=w)

    def memset(self, eng, ap, val, w):
        return self.S.add(eng, lambda e: e.memset(ap, val), writes=w)

    def asel(self, out, in_, cmp, w, r=(), fill=0.0, base=0, cm=-1, pat=None):
        pat = pat or [[1, 128]]
        return self.S.add("gpsimd", lambda e: e.affine_select(out=out, in_=in_, pattern=pat, compare_op=cmp,
                                                              fill=fill, base=base, channel_multiplier=cm),
                          reads=r, writes=w)

    def build(self):
        cfg = dict(tag="", xmode="outproj" if self.has_prev else "ext")
        self.last_out = []
        self.run_pass(cfg)
        self.S.emit(final_wait_ops=self.last_out)
        return self.nc

    def build_fused(self):
        SEQ = self.SEQ
        self.last_out = []
        self.x_ext = self.dram("xin", [SEQ, 1024], F32, "ExternalInput")
        self.mem_ext = self.dram("mem", [256, 1024], F32, "ExternalInput")
        self.mixs = [self.dram("mixs%d" % l, [1280, SEQ], BF16, "Internal") for l in range(2)]
        self.x1s = self.dram("x1s", [SEQ, 1024], F32, "Internal")
        self.wouts = [self.dram("wout%d" % l, [1280, 1024], F32, "ExternalInput") for l in range(2)]
        for l in range(2):
            if l == 1:
                self.begin("oproj")
                self.outproj_pass("ext", 0, self.x1s, "x1s", False)
            for hh in range(2):
                xmode = "ext" if l == 0 else "x1"
                self.run_pass(dict(tag="_%d%d" % (l, hh), xmode=xmode, fused=True, l=l, hh=hh))
        out_d = self.dram("out", [SEQ, 1024], F32, "ExternalOutput")
        self.begin("oproj")
        self.outproj_pass("x1", 1, out_d, "out_d", True)
        self.S.emit(final_wait_ops=self.last_out)
        return self.nc

    def outproj_pass(self, src_kind, l, dst, dst_name, final):
        SEQ = self.SEQ
        pP = [self.psum("pP0", [128, 512]), self.psum("pP1", [128, 512])]
        wb = self.sb("wb", [128, 8, NW], BF16)
        wo = Tl("wb", wb.t[:, :, :].rearrange("p a b -> p (a b)")[:, 0:10240].rearrange("p (c n) -> p c n", c=10))
        wov = self.wouts[l].rearrange("(c p) n -> p c n", p=128)
        for c in range(10):
            self.dma("gpsimd", wo[:, c, :], wov[:, c, :], w=[wo])
        xts = [self.sb("xt%d" % i, [128, 1024]) for i in range(2)]
        tm_ = self.sb("tm", [128, 4, 768], BF16)
        flat = tm_.t[:, :, :].rearrange("p a b -> p (a b)")
        mpvs = [Tl("tm", flat[:, 0:1280].rearrange("p (c t) -> p c t", c=10)),
                Tl("tm", flat[:, 1280:2560].rearrange("p (c t) -> p c t", c=10))]
        for ti in range(SEQ // 128):
            xt = xts[ti % 2]
            mpv = mpvs[ti % 2]
            r0 = ti * 128
            if src_kind == "ext":
                self.dma("sync", xt[:], self.x_ext[r0:r0 + 128, :], w=[xt])
            else:
                self.dma("sync", xt[:], self.x1s[r0:r0 + 128, :], r=[("x1s", ti)], w=[xt])
            self.dma("sync", mpv[:], self.mixs[l][:, r0:r0 + 128].rearrange("(c p) t -> p c t", p=128),
                     r=[("mixs%d" % l, (0, ti // 4)), ("mixs%d" % l, (1, ti // 4))], w=[mpv])
            for hf in range(2):
                p = pP[hf]
                for c in range(10):
                    self.mm(p[:], mpv[:, c, :], wo[:, c, hf * 512:(hf + 1) * 512], c == 0, c == 9,
                            [mpv, wo], [p])
                eng = "vector"
                if eng == "gpsimd":
                    tmpo = self.scr("op_tmp", [128, 512])
                    self.cp("scalar", tmpo[:], p[:], [p], [tmpo])
                    self.tt("gpsimd", xt[:, hf * 512:(hf + 1) * 512], xt[:, hf * 512:(hf + 1) * 512], tmpo[:],
                            ALU.add, [xt, tmpo], [xt])
                else:
                    self.tt("vector", xt[:, hf * 512:(hf + 1) * 512], xt[:, hf * 512:(hf + 1) * 512], p[:],
                            ALU.add, [xt, p], [xt])
            o = self.dma("sync", dst[r0:r0 + 128, :], xt[:], r=[xt], w=[(dst_name, ti)])
            if final:
                self.last_out.append(o)

    def run_pass(self, cfg):
        nc = self.nc
        SEQ = self.SEQ
        NMC = SEQ // 512
        NCH = SEQ // 128
        tag = cfg["tag"]
        xmode = cfg["xmode"]
        fused = cfg.get("fused", False)
        has_prev = xmode == "outproj"
        wcat = self.dram("wcat" + tag, [1024, NW], F32, "ExternalInput")
        pc_d = self.dram("pc" + tag, [128, NPC], F32, "ExternalInput")
        pr_d = self.dram("pr" + tag, [128, NPR], F32, "ExternalInput")
        ng_d = self.dram("ng" + tag, [128, 8], F32, "ExternalInput")
        memg_d = self.dram("memg" + tag, [128, 8], F32, "ExternalInput")
        wkv_d = self.dram("wkv" + tag, [1024, 256], F32, "ExternalInput")
        w2_d = self.dram("w2" + tag, [16, 128], F32, "ExternalInput")
        a2_d = self.dram("a2" + tag, [16, 128], F32, "ExternalInput")
        sguw_d = self.dram("sguw" + tag, [2, 128, 128], F32, "ExternalInput")
        if fused:
            l, hh = cfg["l"], cfg["hh"]
            xin = self.x_ext
            mem_d = self.mem_ext
            mixed_d = self.mixs[l].rearrange("(g two p) t -> two p g t", two=2, p=128)[hh]
            mix_name = "mixs%d" % l
            mix_key = lambda mc: (hh, mc)
            if has_prev:
                mprev_d = self.mixs[0]
                wout_d = self.wouts[0]
                x1_d = self.x1s
        else:
            xin = self.dram("xin", [SEQ, 1024], F32, "ExternalInput")
            mem_d = self.dram("mem", [256, 1024], F32, "ExternalInput")
            mixed_d = self.dram("mixed", [640, SEQ], BF16, "ExternalOutput").rearrange("(g p) t -> p g t", p=128)
            mix_name = "mixed_d"
            mix_key = lambda mc: mc
            if has_prev:
                mprev_d = self.dram("mprev", [1280, SEQ], BF16, "ExternalInput")
                wout_d = self.dram("wout", [1280, 1024], F32, "ExternalInput")
                x1_d = self.dram("x1out", [SEQ, 1024], F32, "ExternalOutput")

        pT = self.psum("pT", [128, 1024], BF16)
        pP = [self.psum("pP0", [128, 512]), self.psum("pP1", [128, 512])]
        self.gp = [self.psum("pG%d" % i, [128, 512]) for i in range(3)]
        self.pacc = pP
        pNum = self.psum("pNum", [128, 512])
        pDen = self.psum("pDen", [128, 512])

        identf = self.sb("identf", [128, 128])
        ident = self.sb("ident", [128, 128], BF16)
        ones_bf = self.sb("ones_bf", [128, 128], BF16)
        onesf = self.sb("onesf", [128, 128])
        c64f = self.sb("c64f", [128, 128])
        c64b = self.sb("c64b", [128, 128], BF16)
        blk = self.sb("blk", [128, 128], BF16)
        m_ge = self.sb("m_ge", [128, 128], BF16)
        self.memset("vector", identf[:], 1.0, [identf])
        self.asel(identf[:], identf[:], ALU.is_equal, [identf], [identf])
        self.cp("vector", ident[:], identf[:], [identf], [ident])
        self.memset("vector", ones_bf[:], 1.0, [ones_bf])
        self.memset("vector", onesf[:], 1.0, [onesf])
        self.memset("vector", c64f[:], 1.0 / 64, [c64f])
        self.memset("vector", c64b[:], 1.0 / 64, [c64b])
        self.memset("vector", blk[:], 0.0, [blk])
        self.memset("vector", blk[0:64, 0:64], 1.0, [blk])
        self.memset("vector", blk[64:128, 64:128], 1.0, [blk])
        self.memset("vector", m_ge[:], 1.0, [m_ge])
        self.asel(m_ge[:], m_ge[:], ALU.is_ge, [m_ge], [m_ge])

        pc = self.sb("pcs", [128, NPC])
        pr = self.sb("prs", [128, NPR])
        ng = self.sb("ngs", [128, 8])
        memg = self.sb("memgs", [128, 8])
        self.dma("sync", pc[:], pc_d, w=[pc])
        self.dma("sync", pr[:], pr_d, w=[pr])
        self.dma("sync", ng[:], ng_d, w=[ng])
        self.dma("sync", memg[:], memg_d, w=[memg])
        omm = self.sb("omm", [128, 6])
        self.ts("vector", omm[:], pc[:, PC["mu_r"]:PC["mu_r"] + 6], -1.0, ALU.mult, [pc], [omm], s2=1.0, op1=ALU.add)
        nfbB = self.sb("nfbB", [128, 1])
        self.ts("vector", nfbB[:], pc[:, PC["fb_B"]:PC["fb_B"] + 1], -1.0, ALU.mult, [pc], [nfbB])
        nfbC = self.sb("nfbC", [128, 2])
        self.ts("vector", nfbC[:], pc[:, PC["fb_C0"]:PC["fb_C0"] + 2], -1.0, ALU.mult, [pc], [nfbC])
        gq8 = self.sb("gq8", [128, 128])
        self.ts("vector", gq8[:], pr[:, PR_BQG:PR_BQG + 128], 0.125, ALU.mult, [pr], [gq8])
        gmq8 = self.sb("gmq8", [128, 128])
        self.ts("vector", gmq8[:], pr[:, PR_MQG:PR_MQG + 128], 0.125, ALU.mult, [pr], [gmq8])

        wb = self.sb("wb", [128, 8, NW], BF16)
        wv = wcat.rearrange("(c p) n -> p c n", p=128)
        for c in range(8):
            self.dma("gpsimd", wb[:, c, :], wv[:, c, :], w=[(wb, c)])
        for c in range(8):
            eng = "vector" if c % 2 == 0 else "gpsimd"
            self.ts(eng, wb[:, c, :], wb[:, c, :], ng[:, c:c + 1], ALU.mult, [(wb, c), ng], [(wb, c)])
        if has_prev:
            wo = self.sb("wo", [128, 10, 1024], BF16)
            wov = wout_d.rearrange("(c p) n -> p c n", p=128)
            for c in range(10):
                self.dma("gpsimd", wo[:, c, :], wov[:, c, :], w=[(wo, c)])
        w2s = self.sb("w2s", [16, 128])
        a2s = self.sb("a2s", [16, 128])
        self.dma("sync", w2s[:], w2_d, w=[w2s])
        self.dma("sync", a2s[:], a2_d, w=[a2s])

        wsT = self.sb("wsT", [128, 2, 128], BF16)
        self.begin("setupA")
        if "A" in self.branches:
            sgw = self.scr("sgw", [128, 2, 128])
            self.dma("sync", sgw[:], sguw_d.rearrange("h t s -> t h s"), w=[sgw])
            sgwT = self.scr("sgwT", [128, 2, 128])
            for h in range(2):
                p = self.gps()
                self.tr(p[:, 0:128], sgw[:, h, :], identf[:], [sgw, identf], [p])
                self.cp("vector", sgwT[:, h, :], p[:, 0:128], [p], [sgwT])
                self.asel(sgwT[:, h, :], sgwT[:, h, :], ALU.is_ge, [sgwT], [sgwT])
            self.cp("vector", wsT[:], sgwT[:], [sgwT], [wsT])

        kTm = self.sb("kTm", [128, 256], BF16)
        vm = self.sb("vm", [128, 2, 128], BF16)

        self.alloc_state(NCH)

        xts = [self.sb("xt0", [128, 1024]), self.sb("xt1", [128, 1024])]
        hb = self.sb("hb", [128, 1024], BF16)
        hT = self.sb("hT", [128, 8, 512], BF16)
        ss = self.sb("ss", [128, 2])
        GATED = {"Az": AF.Silu, "Bz": AF.Silu, "Cz": AF.Silu, "Mz": AF.Silu, "Co": AF.Sigmoid}
        fm = {}
        for g in FM_GROUPS:
            if g in ("Cq", "Ck"):
                fm[g] = self.sb("fm_" + g, [128, 4 + 512], BF16)
            elif g in ("Dr", "Dk", "Dv", "Dz"):
                fm[g] = self.sb("fm_" + g, [128, 2 + 512], BF16)
            elif g in ("Dw", "Da"):
                fm[g] = self.sb("fm_" + g, [16, 1 + 512])
            elif g in GATED:
                fm[g] = self.sb("fm_" + g, [128, 512], BF16)
            else:
                fm[g] = self.sb("fm_" + g, [128, 512])
        tm = self.sb("tm", [128, 4, 768], BF16)
        mixo = self.sb("mixo", [128, 5, 512], BF16)
        if "M" in self.branches:
            self.setup_mem(mem_d, wkv_d, memg, pr, ident, identf, pT, kTm, vm, xts, hT, tm, hb)
        mpvs = [Tl("tm", tm.t[:, :, :].rearrange("p a b -> p (a b)")[:, 0:1280].rearrange("p (c t) -> p c t", c=10))] * 2
        for g in ("Cq", "Ck"):
            self.memset("vector", fm[g][:, 0:3], 0.0, [(fm[g], "hist")])
        for g in ("Dr", "Dk", "Dv", "Dz"):
            self.memset("vector", fm[g][:, 0:1], 0.0, [(fm[g], "hist")])
        for g in ("Dw", "Da"):
            self.memset("vector", fm[g][:, 0:1], 0.0, [(fm[g], "hist")])

        self.consts = dict(ident=ident, identf=identf, ones_bf=ones_bf, onesf=onesf, c64f=c64f, c64b=c64b,
                           blk=blk, m_ge=m_ge, pc=pc, pr=pr, omm=omm, nfbB=nfbB, nfbC=nfbC,
                           gq8=gq8, gmq8=gmq8, wsT=wsT, kTm=kTm, vm=vm, w2s=w2s, a2s=a2s, pT=pT,
                           pNum=pNum, pDen=pDen)
        last_out = self.last_out
        xi = 0
        xstate = {"xi": 0}

        def xprep(mc):
            t0 = mc * 512
            for tt in range(4):
                xi = xstate["xi"]
                xt = xts[xi % 2]
                r0 = t0 + tt * 128
                ti = mc * 4 + tt
                if xmode == "x1":
                    self.dma("sync", xt[:], self.x1s[r0:r0 + 128, :], r=[("x1s", ti)], w=[xt])
                else:
                    self.dma("sync", xt[:], xin[r0:r0 + 128, :], w=[xt])
                if has_prev:
                    mpv = mpvs[xi % 2]
                    rr = [("mixs0", (0, mc)), ("mixs0", (1, mc))] if fused else []
                    self.dma("sync", mpv[:], mprev_d[:, r0:r0 + 128].rearrange("(c p) t -> p c t", p=128),
                             r=rr, w=[mpv])
                    for hf in range(2):
                        p = pP[hf]
                        for c in range(10):
                            self.mm(p[:], mpv[:, c, :], wo[:, c, hf * 512:(hf + 1) * 512],
                                    c == 0, c == 9, [mpv, (wo, c)], [p])
                        self.tt("vector", xt[:, hf * 512:(hf + 1) * 512], xt[:, hf * 512:(hf + 1) * 512],
                                p[:], ALU.add, [xt, p], [xt])
                    o = self.dma("sync", x1_d[r0:r0 + 128, :], xt[:], r=[xt], w=[("x1s", ti)])
                    if not fused:
                        last_out.append(o)
                xstate["xi"] = xi + 1
                self.act(hb[:], xt[:], AF.Square, [xt], [hb, ss], accum_out=ss[:, 0:1])
                self.act(ss[:, 1:2], ss[:, 0:1], AF.Ln, [ss], [ss], scale=1.0 / 1024, bias=EPS)
                self.act(ss[:, 1:2], ss[:, 1:2], AF.Exp, [ss], [ss], scale=-0.5)
                self.ts("vector", hb[:], xt[:], ss[:, 1:2], ALU.mult, [xt, ss], [hb])
                yield
                for c in range(8):
                    self.tr(pT[:, c * 128:(c + 1) * 128], hb[:, c * 128:(c + 1) * 128], ident[:],
                            [hb, ident], [pT])
                eng = "vector" if tt % 2 == 0 else "scalar"
                self.cp(eng, hT[:, :, tt * 128:(tt + 1) * 128],
                        pT[:, :].rearrange("p (c t) -> p c t", c=8), [pT], [hT])
                yield

        for _ in xprep(0):
            pass
        for mc in range(NMC):
            t0 = mc * 512
            for gi, g in enumerate(FM_GROUPS):
                p = pP[gi % 2]
                wdt = FM_W[g]
                for c in range(8):
                    self.mm(p[0:wdt, :], wb[:, c, FM_OFF[g]:FM_OFF[g] + wdt], hT[:, c, :], c == 0, c == 7,
                            [(wb, c), hT], [p])
                hist = {"Cq": 3, "Ck": 3, "Dr": 1, "Dk": 1, "Dv": 1, "Dz": 1, "Dw": 1, "Da": 1}.get(g, 0)
                dst = fm[g]
                if g in GATED:
                    self.act(dst[:, :], p[:, :], GATED[g], [p], [(dst, "cur")])
                    continue
                if hist and mc > 0:
                    self.cp("vector", dst[0:wdt, 0:hist], dst[0:wdt, 512:512 + hist], [(dst, "cur")], [(dst, "hist")])
                eng = "scalar" if gi % 2 == 0 else "vector"
                self.cp(eng, dst[0:wdt, hist:hist + 512], p[0:wdt, :], [p, (dst, "hist")], [(dst, "cur")])
            for tt in range(4):
                for hf in range(2):
                    p = pP[hf]
                    for c in range(8):
                        self.mm(p[:, 0:384], hT[:, c, tt * 128:(tt + 1) * 128],
                                wb[:, c, NFM + hf * 384:NFM + (hf + 1) * 384], c == 0, c == 7, [hT, (wb, c)], [p])
                    eng = "scalar" if hf == 0 else "vector"
                    self.cp(eng, tm[:, tt, hf * 384:(hf + 1) * 384], p[:, 0:384], [p], [(tm, tt)])
            self.zero_mix = []
            if "A" in self.branches:
                self.branch_A(mc, fm, tm, mixo)
            else:
                self.memset("gpsimd", mixo[:, 0, :], 0.0, [(mixo, 0)])
            if "B" in self.branches:
                self.branch_B(mc, fm, tm, mixo)
            else:
                self.memset("gpsimd", mixo[:, 1, :], 0.0, [(mixo, 1)])
            gens = []
            if mc + 1 < NMC:
                gens.append(("X", xprep(mc + 1)))
            if "M" in self.branches:
                gens.append(("M", self.branch_M(mc, fm, tm, mixo)))
            else:
                self.memset("gpsimd", mixo[:, 4, :], 0.0, [(mixo, 4)])
            if "D" in self.branches:
                gens.append(("D", self.branch_D(mc, fm, tm, mixo)))
            else:
                self.memset("gpsimd", mixo[:, 3, :], 0.0, [(mixo, 3)])
            if "C" in self.branches:
                gens.append(("C", self.branch_C(mc, fm, tm, mixo)))
            else:
                self.memset("gpsimd", mixo[:, 2, :], 0.0, [(mixo, 2)])
            while gens:
                for item in list(gens):
                    for rep in range(3 if item[0] == "D" else 1):
                        self._br = item[0]
                        try:
                            next(item[1])
                        except StopIteration:
                            gens.remove(item)
                            break
            o = self.dma("sync", mixed_d[:, :, t0:t0 + 512], mixo[:], r=[mixo], w=[(mix_name, mix_key(mc))])
            if not fused:
                last_out.append(o)

    def head_rms_tm(self, src, dst, gain, tag, nt=4):
        sq = self.scr("hr_sq", [128, 4, 128])
        ssq = self.scr("hr_ssq", [128, 8])
        src_ap, src_r = src
        dst_ap, dst_w = dst
        g_ap, g_r = gain
        self.tt("gpsimd", sq[:, 0:nt, :], src_ap, src_ap, ALU.mult, src_r, [sq])
        ssq3 = ssq[:, 0:2 * nt].rearrange("p (t h) -> p t h", h=2)
        self.op("vector", lambda e: e.tensor_reduce(out=ssq3,
                                                     in_=sq[:, 0:nt, :].rearrange("p t (h j) -> p t h j", h=2),
                                                     axis=AX.X, op=ALU.add), [sq], [ssq])
        self.act(ssq[:, 0:2 * nt], ssq[:, 0:2 * nt], AF.Ln, [ssq], [ssq], scale=1.0 / 64, bias=EPS)
        self.act(ssq[:, 0:2 * nt], ssq[:, 0:2 * nt], AF.Exp, [ssq], [ssq], scale=-0.5)
        self.tt("vector", sq[:, 0:nt, :].rearrange("p t (h j) -> p t h j", h=2),
                src_ap.rearrange("p t (h j) -> p t h j", h=2),
                ssq3[:, :, :, None].broadcast_to([128, nt, 2, 64]), ALU.mult, src_r + [ssq], [sq])
        self.tt("vector", dst_ap, sq[:, 0:nt, :], g_ap[:, None, :].broadcast_to([128, nt, 128]), ALU.mult,
                [sq] + g_r, dst_w)

    def begin(self, br):
        self._br = br
        if not hasattr(self, "_brcount"):
            self._brcount = {}
        self._brcount.setdefault(br, {})

    def scr(self, name, shape, dt=F32):
        if not hasattr(self, "_scrmap"):
            self._scrmap = {}
            self._pools = {}
        br = getattr(self, "_br", "x")
        key = (br, name)
        if key in self._scrmap:
            return self._scrmap[key]
        esz = 4 if dt == F32 else 2
        n = 1
        for d_ in shape[1:]:
            n *= d_
        nbytes = n * esz
        cls = 256
        while cls < nbytes:
            cls *= 2
        cnt = self._brcount.setdefault(br, {})
        k = cnt.get(cls, 0)
        cnt[cls] = k + 1
        fam = br if br in ("C", "M") else ""
        pool = self._pools.setdefault((fam, cls), [])
        if k >= len(pool):
            pname = "scr%s%d_%d" % (fam, cls, k)
            pool.append((pname, self.st.enter_context(self.nc.sbuf_tensor(pname, [128, cls // 4], F32))))
        pname, raw = pool[k]
        h = raw if dt == F32 else raw.bitcast(dt)
        ap = h[0:shape[0], 0:n]
        if len(shape) == 3:
            ap = ap.rearrange("p (a b) -> p a b", a=shape[1])
        elif len(shape) == 4:
            ap = ap.rearrange("p (a b c) -> p a b c", a=shape[1], b=shape[2])
        t = Tl(pname, ap)
        self._scrmap[key] = t
        return t

    def gelu(self, dst_ap, dst_w, src_ap, src_r, shape, tag):
        self.act(dst_ap, src_ap, AF.Gelu_apprx_tanh, src_r, dst_w)

    def setup_mem(self, mem_d, wkv_d, memg, pr, ident, identf, pT, kTm, vm, xts, hT, tm, hb):
        self.begin("setup")
        for c in range(2):
            self.dma("sync", xts[c][:], mem_d[c * 128:(c + 1) * 128, :], w=[xts[c]])
        wk = Tl("hT", hT[:, :, 0:256])
        mhT = Tl("hT", hT[:, :, 256:512])
        mh = Tl("tm", tm.t[:, :, :].rearrange("p a b -> p (a b)")[:, 0:2048].rearrange("p (c d) -> p c d", c=2))
        wkv_v = wkv_d.rearrange("(c p) n -> p c n", p=128)
        for c in range(8):
            self.dma("gpsimd", wk[:, c, :], wkv_v[:, c, :], w=[wk])
        for c in range(8):
            self.ts("vector", wk[:, c, :], wk[:, c, :], memg[:, c:c + 1], ALU.mult, [wk, memg], [wk])
        mss = self.scr("m_ss", [128, 2])
        for c in range(2):
            self.act(hb[:], xts[c][:], AF.Square, [xts[c]], [hb, mss], accum_out=mss[:, c:c + 1])
        self.act(mss[:], mss[:], AF.Ln, [mss], [mss], scale=1.0 / 1024, bias=EPS)
        self.act(mss[:], mss[:], AF.Exp, [mss], [mss], scale=-0.5)
        for c in range(2):
            self.ts("vector", mh[:, c, :], xts[c][:], mss[:, c:c + 1], ALU.mult, [xts[c], mss], [mh])
        for c2 in range(2):
            for c in range(8):
                self.tr(pT[:, c * 128:(c + 1) * 128], mh[:, c2, c * 128:(c + 1) * 128], ident[:], [mh, ident], [pT])
            self.cp("vector", mhT[:, :, c2 * 128:(c2 + 1) * 128], pT[:, :].rearrange("p (c t) -> p c t", c=8),
                    [pT], [mhT])
        kvt = self.scr("m_kvt", [128, 2, 256])
        for c2 in range(2):
            p = self.gps()
            for c in range(8):
                self.mm(p[:, 0:256], mhT[:, c, c2 * 128:(c2 + 1) * 128], wk[:, c, :], c == 0, c == 7, [mhT, wk], [p])
            self.cp("vector", kvt[:, c2, :], p[:, 0:256], [p], [kvt])
        self.cp("vector", vm[:], kvt[:, :, 128:256], [kvt], [vm])
        kn = self.scr("m_kn", [128, 2, 128], BF16)
        self.head_rms_tm((kvt[:, :, 0:128], [kvt]), (kn[:], [kn]), (pr[:, PR_MKG:PR_MKG + 128], [pr]), "mk", nt=2)
        for c2 in range(2):
            self.tr(pT[:, c2 * 128:(c2 + 1) * 128], kn[:, c2, :], ident[:], [kn, ident], [pT])
        self.cp("vector", kTm[:], pT[:, 0:256], [pT], [kTm])

    def alloc_state(self, NCH):
        SEQ = self.SEQ
        if "B" in self.branches:
            self.KT = self.sb("KT", [128, SEQ], BF16)
            self.VB = self.sb("VB", [128, NCH, 128], BF16)
            self.Fcol = self.sb("Fcol", [128, 2, NCH])
            self.Fprev = self.sb("Fprev", [128, 1])
            self.memset("vector", self.Fprev[:], 0.0, [self.Fprev])
            self.c64h = [self.sb("c64h%d" % h, [128, 128], BF16) for h in range(2)]
            self.hm = self.sb("hm", [128, 2])
            self.memset("vector", self.hm[:], 0.0, [self.hm])
            for h in range(2):
                hs = slice(h * 64, (h + 1) * 64)
                fs = slice(64, 128) if h == 0 else slice(0, 64)
                self.memset("vector", self.c64h[h][:], 0.0, [self.c64h[h]])
                self.memset("vector", self.c64h[h][fs, :], 1.0 / 64, [self.c64h[h]])
                self.memset("vector", self.hm[hs, h:h + 1], 1.0, [self.hm])
        if "C" in self.branches:
            self.Cst = self.sb("Cst", [128, 128])
            self.Cstb = self.sb("Cstb", [128, 128], BF16)
            self.memset("vector", self.Cst[:], 0.0, [self.Cst])
            self.memset("vector", self.Cstb[:], 0.0, [self.Cstb])
        if "D" in self.branches:
            self.Sst = self.sb("Sst", [128, 64])
            self.Sstb = self.sb("Sstb", [128, 64], BF16)
            self.memset("vector", self.Sst[:], 0.0, [self.Sst])
            self.memset("vector", self.Sstb[:], 0.0, [self.Sstb])

    def branch_A(self, mc, fm, tm, mixo):
        self.begin("A")
        c = self.consts
        pr = c["pr"]
        gu = self.scr("A_gu", [128, 512])
        self.gelu(gu[:], [gu], fm["Au"][:, :], [(fm["Au"], "cur")], [128, 512], "u")
        gv = self.scr("A_gv", [128, 4, 128])
        self.gelu(gv[:], [gv], tm[:, :, 0:128], [tm], [128, 4, 128], "v")
        vn = self.scr("A_vn", [128, 4, 128], BF16)
        self.head_rms_tm((gv[:], [gv]), (vn[:], [vn]), (pr[:, PR_SGU_G:PR_SGU_G + 128], [pr]), "av")
        p = self.gps()
        for tt in range(4):
            for h in range(2):
                self.mm(p[h * 64:(h + 1) * 64, tt * 128:(tt + 1) * 128], vn[:, tt, h * 64:(h + 1) * 64],
                        c["wsT"][:, h, :], True, True, [vn, c["wsT"]], [p])
        ya = self.scr("A_ya", [128, 512])
        self.tt("vector", ya[:].rearrange("p (a t) -> p a t", a=4), p[:].rearrange("p (a t) -> p a t", a=4),
                pr[:, None, PR_SGUB:PR_SGUB + 128].broadcast_to([128, 4, 128]), ALU.add, [p, pr], [ya])
        self.tt("vector", ya[:], ya[:], gu[:], ALU.mult, [ya, gu], [ya])
        self.tt("vector", mixo[:, 0, :], ya[:], fm["Az"][:, :], ALU.mult, [ya, fm["Az"]], [(mixo, 0)])

    def branch_M(self, mc, fm, tm, mixo):
        self.begin("M")
        c = self.consts
        pT = c["pT"]
        qn = self.scr("M_qn", [128, 4, 128], BF16)
        self.head_rms_tm((tm[:, :, 640:768], [tm]), (qn[:], [qn]), (c["gmq8"][:], [c["gmq8"]]), "mq")
        yield
        qT = self.scr("M_qT", [128, 512], BF16)
        for tt in range(4):
            self.tr(pT[:, tt * 128:(tt + 1) * 128], qn[:, tt, :], c["ident"][:], [qn, c["ident"]], [pT])
        self.cp("vector", qT[:], pT[:, 0:512], [pT], [qT])
        yield
        pn = self.pacc[0]
        pd = self.pacc[1]
        E = self.scr("M_E", [128, 512], BF16)
        for h in range(2):
            hs = slice(h * 64, (h + 1) * 64)
            for mcx in range(2):
                ps_ = self.gps()
                self.mm(ps_[:], c["kTm"][hs, mcx * 128:(mcx + 1) * 128], qT[hs, :], True, True, [c["kTm"], qT], [ps_])
                self.act(E[:], ps_[:], AF.Exp, [ps_], [E])
                yield
                self.mm(pn[hs, :], c["vm"][:, mcx, hs], E[:], mcx == 0, mcx == 1, [c["vm"], E], [(pn, h)])
                self.mm(pd[hs, :], c["ones_bf"][:, 0:64], E[:], mcx == 0, mcx == 1, [c["ones_bf"], E], [(pd, h)])
                yield
        rd = self.scr("M_rd", [128, 512])
        self.recip(rd[:], pd[:], [pd], [rd])
        self.tt("vector", rd[:], pn[:], rd[:], ALU.mult, [pn, rd], [rd])
        yield
        self.tt("gpsimd", mixo[:, 4, :], rd[:], fm["Mz"][:, :], ALU.mult, [rd, fm["Mz"]], [(mixo, 4)])

    def branch_B(self, mc, fm, tm, mixo):
        self.begin("B")
        c = self.consts
        pT = c["pT"]
        pr = c["pr"]
        G = mc
        if getattr(self, "bstage", 9) < 1:
            return
        qn = self.scr("B_qn", [128, 4, 128], BF16)
        kn = self.scr("B_kn", [128, 4, 128], BF16)
        self.head_rms_tm((tm[:, :, 128:256], [tm]), (qn[:], [qn]), (c["gq8"][:], [c["gq8"]]), "bq")
        self.head_rms_tm((tm[:, :, 256:384], [tm]), (kn[:], [kn]), (pr[:, PR_BKG:PR_BKG + 128], [pr]), "bk")
        QT = self.scr("B_QT", [128, 512], BF16)
        if getattr(self, "bstage", 9) < 0.3:
            return
        for tt in range(4):
            self.tr(pT[:, tt * 128:(tt + 1) * 128], qn[:, tt, :], c["ident"][:], [qn, c["ident"]], [pT])
        for tt in range(4):
            self.tr(pT[:, 512 + tt * 128:512 + (tt + 1) * 128], kn[:, tt, :], c["ident"][:], [kn, c["ident"]], [pT])
        self.cp("vector", QT[:], pT[:, 0:512], [pT], [QT])
        self.cp("scalar", self.KT[:, G * 512:(G + 1) * 512], pT[:, 512:1024], [pT], [(self.KT, G)])
        QTm = [self.scr("B_QTm%d" % h, [128, 512], BF16) for h in range(2)]
        for h in range(2):
            self.ts("vector" if h == 0 else "gpsimd", QTm[h][:], QT[:], self.hm[:, h:h + 1], ALU.mult,
                    [QT, self.hm], [QTm[h]])
        for tt in range(4):
            self.cp("gpsimd", self.VB[:, G * 4 + tt, :], tm[:, tt, 384:512], [tm], [(self.VB, G * 4 + tt)])
        e1 = self.scr("B_e1", [128, 512])
        self.act(e1[:], fm["Bf"][:, :], AF.Exp, [(fm["Bf"], "cur")], [e1], scale=-1.0, bias=c["nfbB"][:, 0:1])
        self.act(e1[:], e1[:], AF.Ln, [e1], [e1], bias=1.0)
        Lc = self.scr("B_Lc", [128, 512])
        for q4 in range(4):
            qs = slice(q4 * 128, (q4 + 1) * 128)
            ini = self.Fprev[:, 0:1] if q4 == 0 else Lc[:, q4 * 128 - 1:q4 * 128]
            self.op("vector", lambda e, qs=qs, ini=ini: e.tensor_tensor_scan(
                out=Lc[:, qs], data0=c["onesf"][:, 0:128], data1=e1[:, qs], initial=ini,
                op0=ALU.mult, op1=ALU.add), [c["onesf"], e1, self.Fprev, Lc], [Lc])
        pcg = self.gps()
        for h in range(2):
            fs = slice(64, 128) if h == 0 else slice(0, 64)
            for j in range(4):
                self.mm(pcg[:, h * 8 + j:h * 8 + j + 1], Lc[fs, j * 128:(j + 1) * 128], c["c64f"][fs, 0:1],
                        True, True, [Lc, c["c64f"]], [pcg])
            for hf in range(2):
                col = hf * 256 + 127
                self.mm(pcg[:, 16 + h * 2 + hf:17 + h * 2 + hf], c["c64f"][fs, :], Lc[fs, col:col + 1], True, True,
                        [c["c64f"], Lc], [pcg])
        cg = self.scr("B_cg", [128, 4])
        for h in range(2):
            self.cp("vector", self.Fcol[:, h, G * 4:G * 4 + 4], pcg[:, h * 8:h * 8 + 4], [pcg], [(self.Fcol, G)])
        self.cp("vector", cg[:], pcg[:, 16:20], [pcg], [cg])
        self.cp("vector", self.Fprev[:], Lc[:, 511:512], [Lc], [self.Fprev])
        nkb = 4 * G + 4
        bias = self.scr("B_bias", [128, 4, self.SEQ // 128])
        for h in range(2):
            for hf in range(2):
                self.ts("vector", bias[:, h * 2 + hf, 0:nkb], self.Fcol[:, h, 0:nkb],
                        cg[:, h * 2 + hf:h * 2 + hf + 1], ALU.subtract, [self.Fcol, cg], [bias])
        pOff, pDg = c["pNum"], c["pDen"]
        Es = [self.scr("B_E%d" % i, [128, 512], BF16) for i in range(4)]
        EaccO = self.scr("B_EaccO", [128, 512])
        EaccD = self.scr("B_EaccD", [128, 512])
        EaccP = self.scr("B_EaccP", [128, 512])
        ndg = self.scr("B_ndg", [128, 512])
        rd = self.scr("B_rd", [128, 512])
        yb = self.scr("B_yb", [128, 512])
        sc = self.scr("B_sc", [128, 2])
        for h in range(2):
            self.tt("vector", sc[:, h:h + 1], cg[:, h * 2:h * 2 + 1], cg[:, h * 2 + 1:h * 2 + 2], ALU.subtract,
                    [cg], [sc])
        self.act(sc[:], sc[:], AF.Exp, [sc], [sc])
        noff = 4 * G
        pairs = [(h, kb) for h in range(2) for kb in range(nkb)]
        psl = {}

        def score(i):
            h, kb = pairs[i]
            j = kb - 4 * G
            c0 = 0 if j <= 0 else j * 128
            ps_ = self.gps()
            self.mm(ps_[:, c0:512], self.KT[:, kb * 128:(kb + 1) * 128], QTm[h][:, c0:512], True, True,
                    [self.KT, QTm[h]], [ps_])
            psl[i] = ps_
        score(0)
        for i, (h, kb) in enumerate(pairs):
            if i + 1 < len(pairs):
                score(i + 1)
            hs = slice(h * 64, (h + 1) * 64)
            j = kb - 4 * G
            c0 = 0 if j <= 0 else j * 128
            ps_ = psl.pop(i)
            E = Es[i % 4]
            if j < 0:
                self.act(E[:, :], ps_[:, :], AF.Exp, [ps_, bias], [E], bias=bias[:, h * 2, kb:kb + 1])
                self.mm(pOff[:, :], self.VB[:, kb, :], E[:, :], kb == 0, kb == noff - 1, [self.VB, E], [pOff])
                if kb == 0:
                    self.cp("vector", EaccO[:], E[:], [E], [EaccO])
                elif kb == 1:
                    self.cp("gpsimd", EaccP[:], E[:], [E], [EaccP])
                elif kb % 2 == 0:
                    self.tt("vector", EaccO[:], EaccO[:], E[:], ALU.add, [EaccO, E], [EaccO])
                else:
                    self.tt("gpsimd", EaccP[:], EaccP[:], E[:], ALU.add, [EaccP, E], [EaccP])
            else:
                for hf in range(2):
                    a_, b_ = max(c0, hf * 256), (hf + 1) * 256
                    if a_ >= b_:
                        continue
                    self.act(E[:, a_:b_], ps_[:, a_:b_], AF.Exp, [ps_, bias], [E],
                             bias=bias[:, h * 2 + hf, kb:kb + 1])
                self.tt("gpsimd", E[:, c0:c0 + 128], E[:, c0:c0 + 128], c["m_ge"][:], ALU.mult,
                        [E, c["m_ge"]], [E])
                self.mm(pDg[:, c0:512], self.VB[:, kb, :], E[:, c0:512], j == 0, kb == nkb - 1,
                        [self.VB, E], [pDg])
                if j == 0:
                    self.cp("vector", EaccD[:], E[:], [E], [EaccD])
                else:
                    self.tt("vector", EaccD[:, c0:512], EaccD[:, c0:512], E[:, c0:512], ALU.add,
                            [EaccD, E], [EaccD])
            if kb == nkb - 1:
                if noff > 0:
                    self.tt("gpsimd", EaccO[:], EaccO[:], EaccP[:], ALU.add, [EaccO, EaccP], [EaccO])
                    self.tt("vector", EaccD[:, 0:256], EaccD[:, 0:256], EaccO[:, 0:256], ALU.add,
                            [EaccD, EaccO], [EaccD])
                    self.stt(EaccD[:, 256:512], EaccO[:, 256:512], sc[:, h:h + 1], EaccD[:, 256:512], ALU.mult,
                             ALU.add, [EaccO, sc, EaccD], [EaccD])
                pd_ = self.gps()
                self.mm(pd_[hs, :], c["onesf"][:, 0:64], EaccD[:], True, True, [c["onesf"], EaccD], [pd_])
                self.recip(rd[hs, :], pd_[hs, :], [pd_], [rd])
                self.cp("scalar", ndg[hs, :], pDg[hs, :], [pDg], [ndg])
                if noff > 0:
                    self.tt("vector", ndg[hs, 0:256], ndg[hs, 0:256], pOff[hs, 0:256], ALU.add, [ndg, pOff], [ndg])
                    self.stt(ndg[hs, 256:512], pOff[hs, 256:512], sc[hs, h:h + 1], ndg[hs, 256:512], ALU.mult,
                             ALU.add, [pOff, sc, ndg], [ndg])
                self.tt("vector", yb[hs, :], ndg[hs, :], rd[hs, :], ALU.mult, [ndg, rd], [yb])
        self.tt("vector", mixo[:, 1, :], yb[:], fm["Bz"][:, :], ALU.mult, [yb, fm["Bz"]], [(mixo, 1)])

    def branch_C(self, mc, fm, tm, mixo):
        self.begin("C")
        c = self.consts
        pc = c["pc"]
        pT = c["pT"]
        qc = self.scr("C_qc", [128, 512], BF16)
        kc = self.scr("C_kc", [128, 512], BF16)
        for g, dst, wn, bn in (("Cq", qc, "cq", "cqb"), ("Ck", kc, "ck", "ckb")):
            x = fm[g]
            acc = self.scr("C_acc" + g, [128, 512])
            self.ts("vector", acc[:], x[:, 0:512], pc[:, PC[wn + "0"]:PC[wn + "0"] + 1], ALU.mult, [x, pc], [acc],
                    s2=pc[:, PC[bn]:PC[bn] + 1], op1=ALU.add)
            for j in range(1, 4):
                self.stt(acc[:], x[:, j:j + 512], pc[:, PC[wn + str(j)]:PC[wn + str(j)] + 1], acc[:], ALU.mult,
                         ALU.add, [x, pc, acc], [acc])
            if g == "Cq":
                self.act(dst[:], acc[:], AF.Silu, [acc], [dst])
            else:
                self.act(acc[:], acc[:], AF.Silu, [acc], [acc])
                self.ts("vector", dst[:], acc[:], 0.125, ALU.mult, [acc], [dst])
        yield
        Lb = []
        for h in range(2):
            e1 = fm["Cf%d" % h]
            self.act(e1[:], fm["Cf%d" % h][:, :], AF.Exp, [(fm["Cf%d" % h], "cur")], [e1], scale=-1.0,
                     bias=c["nfbC"][:, h:h + 1])
            self.act(e1[:], e1[:], AF.Ln, [e1], [e1], bias=1.0)
            L = self.scr("C_Lb%d" % h, [128, 512])
            for ch in range(4):
                cs = slice(ch * 128, (ch + 1) * 128)
                self.op("vector", lambda e, L=L, e1=e1, cs=cs: e.tensor_tensor_scan(
                    out=L[:, cs], data0=c["onesf"][:, 0:128], data1=e1[:, cs], initial=0.0,
                    op0=ALU.mult, op1=ALU.add), [c["onesf"], e1], [L])
            Lb.append(L)
        ilog = fm["Ci"]
        self.ts("vector", ilog[:], fm["Ci"][:, :], pc[:, PC["ib_C"]:PC["ib_C"] + 1], ALU.add,
                [(fm["Ci"], "cur"), pc], [ilog])
        yield
        so = fm["Co"]
        sz = fm["Cz"]
        if not hasattr(self, "vaug"):
            self.vaug = [self.sb("C_vaug%d" % h, [128, 128], BF16) for h in range(2)]
            for h in range(2):
                self.memset("vector", self.vaug[h][:], 1.0, [self.vaug[h]])
        vaug = self.vaug
        for ch in range(4):
            cs = slice(ch * 128, (ch + 1) * 128)
            pcol = self.gps()
            for h in range(2):
                hs = slice(h * 64, (h + 1) * 64)
                self.mm(pcol[:, h:h + 1], Lb[h][0:64, cs], c["c64f"][0:64, 0:1], True, True, [Lb[h], c["c64f"]], [pcol])
                self.mm(pcol[:, 2 + h:3 + h], ilog[hs, cs], c["c64f"][hs, 0:1], True, True, [ilog, c["c64f"]], [pcol])
            col = self.scr("C_col", [128, 8])
            self.cp("vector", col[:, 0:4], pcol[:, 0:4], [pcol], [col])
            yield
            self.tt("vector", col[:, 4:6], col[:, 0:2], col[:, 2:4], ALU.add, [col], [col])
            for h in range(2):
                self.ts("vector", col[:, 6 + h:7 + h], col[:, 4 + h:5 + h], Lb[h][:, ch * 128 + 127:ch * 128 + 128],
                        ALU.subtract, [col, Lb[h]], [col])
            wcol = self.scr("C_wcol", [128, 2])
            self.act(wcol[:], col[:, 6:8], AF.Exp, [col], [wcol])
            yield
            eg = self.scr("C_eg", [128, 2])
            for h in range(2):
                self.act(eg[:, h:h + 1], Lb[h][:, ch * 128 + 127:ch * 128 + 128], AF.Exp, [Lb[h]], [eg], scale=-1.0)
            ebq = self.scr("C_ebq", [128, 128])
            for h in range(2):
                hs = slice(h * 64, (h + 1) * 64)
                self.act(ebq[hs, :], Lb[h][hs, cs], AF.Exp, [Lb[h]], [ebq], scale=-1.0)
            qp = self.scr("C_qp", [128, 128], BF16)
            self.tt("vector", qp[:], qc[:, cs], ebq[:], ALU.mult, [qc, ebq], [qp])
            yield
            for h in range(2):
                hs = slice(h * 64, (h + 1) * 64)
                self.cp("gpsimd", vaug[h][:, 0:64], tm[:, ch, 512 + h * 64:512 + (h + 1) * 64], [tm], [vaug[h]])
            pS = self.gps()
            P = []
            for h in range(2):
                hs = slice(h * 64, (h + 1) * 64)
                self.mm(pS[:, h * 128:(h + 1) * 128], kc[hs, cs], qc[hs, cs], True, True, [kc, qc], [pS])
                D = self.scr("C_D%d" % h, [128, 128])
                self.ts("vector", D[:], Lb[h][:, cs], col[:, h:h + 1], ALU.subtract, [Lb[h], col], [D], s2=0.0,
                        op1=ALU.max)
                self.act(D[:], D[:], AF.Exp, [D, col], [D], scale=-1.0, bias=col[:, 2 + h:3 + h])
                self.tt("gpsimd", D[:], D[:], c["m_ge"][:], ALU.mult, [D, c["m_ge"]], [D])
                Ph = self.scr("C_P%d" % h, [128, 128], BF16)
                self.tt("vector", Ph[:], pS[:, h * 128:(h + 1) * 128], D[:], ALU.mult, [pS, D], [Ph])
                P.append(Ph)
            yield
            pnd = self.gps()
            for h in range(2):
                hs = slice(h * 64, (h + 1) * 64)
                self.mm(pnd[hs, 0:128], vaug[h][:, 0:64], P[h][:], True, False, [vaug[h], P[h]], [pnd])
                self.mm(pnd[hs, 0:128], self.Cstb[hs, 0:64], qp[hs, :], False, True, [self.Cstb, qp], [pnd])
            for h in range(2):
                hs = slice(h * 64, (h + 1) * 64)
                self.mm(pnd[hs, 128:256], c["ones_bf"][:, 0:64], P[h][:], True, False, [c["ones_bf"], P[h]], [pnd])
                self.mm(pnd[hs, 128:256], self.Cstb[hs, 64:128], qp[hs, :], False, True, [self.Cstb, qp], [pnd])
            dn = self.scr("C_dn", [128, 128])
            self.act(dn[:], pnd[:, 128:256], AF.Abs, [pnd], [dn])
            self.ts("vector", dn[:], dn[:], 1.0, ALU.max, [dn], [dn])
            self.recip(dn[:], dn[:], [dn], [dn])
            ho = self.scr("C_ho", [128, 128])
            self.tt("vector", ho[:], pnd[:, 0:128], dn[:], ALU.mult, [pnd, dn], [ho])
            yield
            self.tt("gpsimd", ho[:], ho[:], so[:, cs], ALU.mult, [ho, so], [ho])
            yield
            sq = self.scr("C_sq", [128, 128], BF16)
            self.tt("gpsimd", sq[:], ho[:], ho[:], ALU.mult, [ho], [sq])
            pq = self.gps()
            self.mm(pq[:, 0:128], c["blk"][:], sq[:], True, True, [c["blk"], sq], [pq])
            rs = self.scr("C_rs", [128, 128])
            self.act(rs[:], pq[:, 0:128], AF.Ln, [pq], [rs], scale=1.0 / 64, bias=EPS)
            self.act(rs[:], rs[:], AF.Exp, [rs], [rs], scale=-0.5)
            self.stt(ho[:], ho[:], pc[:, PC["out_g"]:PC["out_g"] + 1], rs[:], ALU.mult, ALU.mult, [ho, pc, rs], [ho])
            self.tt("vector", mixo[:, 2, cs], ho[:], sz[:, cs], ALU.mult, [ho, sz], [(mixo, 2)])
            yield
            self.tr(pT[:, 0:128], kc[:, cs], c["ident"][:], [kc, c["ident"]], [pT])
            khat = self.scr("C_khat", [128, 128], BF16)
            for h in range(2):
                hs = slice(h * 64, (h + 1) * 64)
                self.ts("vector", khat[:, hs], pT[:, h * 64:(h + 1) * 64], wcol[:, h:h + 1], ALU.mult, [pT, wcol],
                        [khat])
            yield
            pC = self.gps()
            for h in range(2):
                hs = slice(h * 64, (h + 1) * 64)
                self.mm(pC[hs, 0:128], khat[:, hs], vaug[h][:], True, True, [khat, vaug[h]], [pC])
            for h in range(2):
                hs = slice(h * 64, (h + 1) * 64)
                self.stt(self.Cst[hs, :], self.Cst[hs, :], eg[hs, h:h + 1], pC[hs, 0:128], ALU.mult, ALU.add,
                         [self.Cst, eg, pC], [self.Cst])
            self.cp("vector", self.Cstb[:], self.Cst[:], [self.Cst], [self.Cstb])
            yield

    def branch_D(self, mc, fm, tm, mixo):
        self.begin("D")
        c = self.consts
        pc = c["pc"]
        pT = c["pT"]
        omm = c["omm"]
        if not hasattr(self, "m_gt"):
            self.m_gt = self.sb("m_gt", [128, 128], BF16)
            self.mN_gt = self.sb("mN_gt", [128, 128], BF16)
            self.memset("vector", self.m_gt[:], 1.0, [self.m_gt])
            self.asel(self.m_gt[:], self.m_gt[:], ALU.is_gt, [self.m_gt], [self.m_gt])
            self.memset("vector", self.mN_gt[:], 1.0, [self.mN_gt])
            self.asel(self.mN_gt[:], self.mN_gt[:], ALU.is_gt, [self.mN_gt], [self.mN_gt], cm=1, pat=[[-1, 128]])
        m_gt, mN_gt, m_ge = self.m_gt, self.mN_gt, c["m_ge"]

        def shift(g, idx, rows, name):
            x = fm[g]
            t = self.scr("D_sh_" + name, [128, 512])
            self.ts("vector", t[0:rows, :], x[0:rows, 1:513], omm[0:rows, idx:idx + 1], ALU.mult, [x, omm], [t])
            self.stt(t[0:rows, :], x[0:rows, 0:512], pc[0:rows, PC["mu_r"] + idx:PC["mu_r"] + idx + 1], t[0:rows, :],
                     ALU.mult, ALU.add, [x, pc, t], [t])
            return t
        rs = shift("Dr", 0, 128, "r")
        ks = shift("Dk", 1, 128, "k")
        vs = shift("Dv", 2, 128, "v")
        zs = shift("Dz", 3, 128, "z")
        ws = shift("Dw", 4, 16, "w")
        as_ = shift("Da", 5, 16, "a")
        self.act(ws[0:16, :], ws[0:16, :], AF.Tanh, [ws], [ws])
        pw = self.gps()
        self.mm(pw[:], c["w2s"][:], ws[0:16, :], True, True, [c["w2s"], ws], [pw])
        lw = ws
        self.act(lw[:], pw[:], AF.Sigmoid, [pw, pc], [lw], bias=pc[:, PC["w0"]:PC["w0"] + 1])
        self.ts("gpsimd", lw[:], lw[:], -0.6065306597126334, ALU.mult, [lw], [lw])
        pa = self.gps()
        self.mm(pa[:], c["a2s"][:], as_[0:16, :], True, True, [c["a2s"], as_], [pa])
        aa = as_
        self.act(aa[:], pa[:], AF.Sigmoid, [pa, pc], [aa], bias=pc[:, PC["a0"]:PC["a0"] + 1])
        sz = zs
        self.act(sz[:], zs[:], AF.Silu, [zs], [sz])
        yield
        vb = self.scr("D_vb", [128, 512], BF16)
        self.cp("gpsimd", vb[:], vs[:], [vs], [vb])
        kx = self.scr("D_kx", [128, 512])
        self.ts("vector", kx[:], ks[:], pc[:, PC["k_k"]:PC["k_k"] + 1], ALU.mult, [ks, pc], [kx])
        sqb = self.scr("D_sqb", [128, 512], BF16)
        self.tt("gpsimd", sqb[:], kx[:], kx[:], ALU.mult, [kx], [sqb])
        pq = self.gps()
        self.mm(pq[:], c["blk"][:], sqb[:], True, True, [c["blk"], sqb], [pq])
        rn = self.scr("D_rn", [128, 512])
        self.ts("vector", rn[:], pq[:], 1e-18, ALU.max, [pq], [rn])
        self.act(rn[:], rn[:], AF.Ln, [rn], [rn])
        self.act(rn[:], rn[:], AF.Exp, [rn], [rn], scale=-0.5)
        kk = kx
        self.tt("vector", kk[:], kx[:], rn[:], ALU.mult, [kx, rn], [kk])
        k2 = rn
        self.ts("vector", k2[:], aa[:], -1.0, ALU.add, [aa, pc], [k2], s2=pc[:, PC["k_a"]:PC["k_a"] + 1], op1=ALU.mult)
        self.stt(k2[:], k2[:], 1.0, ks[:], ALU.add, ALU.mult, [k2, ks], [k2])
        bv = ks
        self.tt("gpsimd", bv[:], kk[:], aa[:], ALU.mult, [kk, aa], [bv])
        yield
        rk = self.scr("D_rk", [128, 512], BF16)
        self.stt(rk[:], rs[:], pc[:, PC["r_k"]:PC["r_k"] + 1], k2[:], ALU.mult, ALU.mult, [rs, pc, k2], [rk])
        pb = self.gps()
        self.mm(pb[:], c["blk"][:], rk[:], True, True, [c["blk"], rk], [pb])
        bon = vs
        self.tt("vector", bon[:], pb[:], vs[:], ALU.mult, [pb, vs], [bon])
        cl = self.scr("D_cl", [128, 512])
        for ch in range(4):
            cs = slice(ch * 128, (ch + 1) * 128)
            self.op("vector", lambda e, cs=cs: e.tensor_tensor_scan(
                out=cl[:, cs], data0=c["onesf"][:, 0:128], data1=lw[:, cs], initial=0.0,
                op0=ALU.mult, op1=ALU.add), [c["onesf"], lw], [cl])
        Ecl = self.scr("D_Ecl", [128, 512])
        Encl = self.scr("D_Encl", [128, 512])
        self.act(Ecl[:], cl[:], AF.Exp, [cl], [Ecl])
        self.act(Encl[:], cl[:], AF.Exp, [cl], [Encl], scale=-1.0)
        Ecx = lw
        self.tt("gpsimd", lw[:], cl[:], lw[:], ALU.subtract, [cl, lw], [lw])
        self.act(Ecx[:], lw[:], AF.Exp, [lw], [Ecx])
        yield
        clT = self.scr("D_clT", [128, 4])
        self.cp("vector", clT[:], cl[:].rearrange("p (a t) -> p a t", a=4)[:, :, 127], [cl], [clT])
        Eh = cl
        for ch in range(4):
            cs = slice(ch * 128, (ch + 1) * 128)
            self.act(Eh[:, cs], cl[:, cs], AF.Exp, [cl, clT], [Eh], scale=-1.0, bias=clT[:, ch:ch + 1])
        KR = self.scr("D_KR", [128, 4, 256], BF16)
        kt = self.scr("D_kt", [128, 512], BF16)
        bt = self.scr("D_bt", [128, 512], BF16)
        khat = self.scr("D_khat", [128, 512], BF16)
        nbh = self.scr("D_nbh", [128, 512], BF16)
        self.tt("vector", KR[:, :, 0:128], kk[:].rearrange("p (a t) -> p a t", a=4),
                Ecx[:].rearrange("p (a t) -> p a t", a=4), ALU.mult, [kk, Ecx], [KR])
        self.tt("gpsimd", KR[:, :, 128:256], rs[:].rearrange("p (a t) -> p a t", a=4),
                Ecl[:].rearrange("p (a t) -> p a t", a=4), ALU.mult, [rs, Ecl], [KR])
        self.tt("vector", kt[:], k2[:], Encl[:], ALU.mult, [k2, Encl], [kt])
        self.tt("gpsimd", bt[:], bv[:], Encl[:], ALU.mult, [bv, Encl], [bt])
        self.tt("vector", khat[:], k2[:], Eh[:], ALU.mult, [k2, Eh], [khat])
        self.stt(nbh[:], bv[:], -1.0, Eh[:], ALU.mult, ALU.mult, [bv, Eh], [nbh])
        yraw = kx
        yield
        for ch in range(4):
            cs = slice(ch * 128, (ch + 1) * 128)
            self.tr(pT[:, 0:128], KR[:, ch, 0:128], c["ident"][:], [KR, c["ident"]], [pT])
            self.tr(pT[:, 128:256], vb[:, cs], c["ident"][:], [vb, c["ident"]], [pT])
            self.tr(pT[:, 256:384], khat[:, cs], c["ident"][:], [khat, c["ident"]], [pT])
            self.tr(pT[:, 384:512], nbh[:, cs], c["ident"][:], [nbh, c["ident"]], [pT])
            TMs = self.scr("D_TMs", [128, 4, 128], BF16)
            self.cp("scalar", TMs[:], pT[:, 0:512].rearrange("p (a t) -> p a t", a=4), [pT], [TMs])
            yield
            rpT = self.scr("D_rpT", [128, 128], BF16)
            GT = self.scr("D_GT", [128, 64], BF16)
            Hs = self.scr("D_Hs", [128, 64])
            py = c["pNum"]
            pHS = c["pDen"]
            HS = [slice(0, 64), slice(64, 128)]
            LT, nQbT, LkT, QkT, LN, R, R2 = {}, {}, {}, {}, {}, {}, {}
            for h in range(2):
                hs = HS[h]
                p1 = self.gps()
                self.mm(p1[:, 0:256], bt[hs, cs], KR[hs, ch, :], True, True, [bt, KR], [p1])
                self.mm(p1[:, 256:512], kt[hs, cs], KR[hs, ch, :], True, True, [kt, KR], [p1])
                LT[h] = self.scr("D_LT%d" % h, [128, 128], BF16)
                nQbT[h] = self.scr("D_nQbT%d" % h, [128, 128], BF16)
                LkT[h] = self.scr("D_LkT%d" % h, [128, 128], BF16)
                QkT[h] = self.scr("D_QkT%d" % h, [128, 128], BF16)
                self.tt("vector", LT[h][:], p1[:, 0:128], m_gt[:], ALU.mult, [p1, m_gt], [LT[h]])
                self.tt("vector", LkT[h][:], p1[:, 256:384], m_gt[:], ALU.mult, [p1, m_gt], [LkT[h]])
                self.stt(nQbT[h][:], p1[:, 128:256], -1.0, m_ge[:], ALU.mult, ALU.mult, [p1, m_ge], [nQbT[h]])
                self.tt("vector", QkT[h][:], p1[:, 384:512], m_ge[:], ALU.mult, [p1, m_ge], [QkT[h]])
                yield
            for h in range(2):
                hs = HS[h]
                p2 = self.gps()
                self.mm(p2[:, 0:128], KR[hs, ch, 0:128], bt[hs, cs], True, True, [KR, bt], [p2])
                self.mm(p2[:, 128:192], LkT[h][:], TMs[:, 1, hs], True, True, [LkT[h], TMs], [p2])
                LN[h] = self.scr("D_LN%d" % h, [128, 128], BF16)
                self.tt("vector", LN[h][:], p2[:, 0:128], mN_gt[:], ALU.mult, [p2, mN_gt], [LN[h]])
                R[h] = self.scr("D_R%d" % h, [128, 128], BF16)
                self.cp("gpsimd", R[h][:, 0:64], TMs[:, 0, hs], [TMs], [R[h]])
                self.cp("scalar", R[h][:, 64:128], p2[:, 128:192], [p2], [R[h]])
                yield
            Rc, Rn, PTc, PNc = {}, {}, {}, {}
            for h in range(2):
                pr_ = self.gps()
                self.mm(pr_[:, 0:128], LT[h][:], R[h][:], True, True, [LT[h], R[h]], [pr_])
                R2[h] = self.scr("D_Rb%d" % h, [128, 128], BF16)
                self.tt("vector", R2[h][:], R[h][:], pr_[:, 0:128], ALU.subtract, [R[h], pr_], [R2[h]])
                Rc[h], Rn[h] = R2[h], R[h]
                PTc[h], PNc[h] = LT[h], LN[h]
            yield
            for j in range(1, 7):
                PTn, PNn = {}, {}
                for h in range(2):
                    PTn[h] = self.scr("D_PT%d_%d" % (h, j % 2), [128, 128], BF16)
                    PNn[h] = self.scr("D_PN%d_%d" % (h, j % 2), [128, 128], BF16)
                    pp = self.gps()
                    self.mm(pp[:, 0:128], PNc[h][:], PTc[h][:], True, True, [PNc[h], PTc[h]], [pp])
                    if j < 6:
                        self.mm(pp[:, 128:256], PTc[h][:], PNc[h][:], True, True, [PNc[h], PTc[h]], [pp])
                        self.cp("scalar", PTn[h][:], pp[:, 0:128], [pp], [PTn[h]])
                        self.cp("vector", PNn[h][:], pp[:, 128:256], [pp], [PNn[h]])
                    else:
                        self.cp("scalar", PTn[h][:], pp[:, 0:128], [pp], [PTn[h]])
                yield
                for h in range(2):
                    pr_ = self.gps()
                    self.mm(pr_[:, 0:128], PTn[h][:], Rc[h][:], True, True, [PTn[h], Rc[h]], [pr_])
                    self.tt("vector", Rn[h][:], Rc[h][:], pr_[:, 0:128], ALU.add, [Rc[h], pr_], [Rn[h]])
                    Rc[h], Rn[h] = Rn[h], Rc[h]
                    PTc[h], PNc[h] = PTn[h], PNn[h]
                yield
            for h in range(2):
                hs = HS[h]
                Rch = Rc[h]
                pr_ = self.gps()
                self.mm(pr_[hs, 0:128], Rch[:, 0:64], nQbT[h][:], True, True, [Rch, nQbT[h]], [pr_])
                self.tt("vector", rpT[hs, :], pr_[hs, 0:128], KR[hs, ch, 128:256], ALU.add, [pr_, KR], [rpT])
                self.mm(py[hs, 0:128], TMs[:, 1, hs], QkT[h][:], True, False, [TMs, QkT[h]], [(py, h)])
                self.mm(py[hs, 0:128], Rch[:, 64:128], nQbT[h][:], False, False, [Rch, nQbT[h]], [(py, h)])
                self.mm(py[hs, 0:128], self.Sstb[hs, :], rpT[hs, :], False, True, [self.Sstb, rpT], [(py, h)])
                pg = self.gps()
                self.mm(pg[hs, 0:64], Rch[:, 0:64], TMs[:, 3, hs], True, True, [Rch, TMs], [pg])
                self.stt(GT[hs, :], c["identf"][hs, h * 64:(h + 1) * 64], Ecl[hs, ch * 128 + 127:ch * 128 + 128],
                         pg[hs, 0:64], ALU.mult, ALU.add, [c["identf"], Ecl, pg], [GT])
                self.mm(pHS[hs, 0:64], TMs[:, 2, hs], TMs[:, 1, hs], True, False, [TMs], [(pHS, h)])
                self.mm(pHS[hs, 0:64], TMs[:, 3, hs], Rch[:, 64:128], False, True, [TMs, Rch], [(pHS, h)])
                self.cp("scalar", Hs[hs, :], pHS[hs, 0:64], [(pHS, h)], [Hs])
                self.mm(pHS[hs, 64:128], GT[hs, :], self.Sstb[hs, :], True, True, [GT, self.Sstb], [(pHS, h)])
                self.tt("vector", self.Sst[hs, :], pHS[hs, 64:128], Hs[hs, :], ALU.add, [(pHS, h), Hs], [self.Sst])
                yield
            self.cp("vector", self.Sstb[:], self.Sst[:], [self.Sst], [self.Sstb])
            self.cp("scalar", yraw[:, cs], py[:, 0:128], [py], [yraw])
        ysq = sqb
        self.tt("gpsimd", ysq[:], yraw[:], yraw[:], ALU.mult, [yraw], [ysq])
        pq2 = self.gps()
        self.mm(pq2[:], c["blk"][:], ysq[:], True, True, [c["blk"], ysq], [pq2])
        rs2 = rn
        self.act(rs2[:], pq2[:], AF.Ln, [pq2], [rs2], scale=1.0 / 64, bias=EPS)
        self.act(rs2[:], rs2[:], AF.Exp, [rs2], [rs2], scale=-0.5)
        self.stt(yraw[:], yraw[:], pc[:, PC["ln_g"]:PC["ln_g"] + 1], rs2[:], ALU.mult, ALU.mult, [yraw, pc, rs2],
                 [yraw])
        self.tt("vector", yraw[:], yraw[:], bon[:], ALU.add, [yraw, bon], [yraw])
        self.tt("gpsimd", mixo[:, 3, :], yraw[:], sz[:], ALU.mult, [yraw, sz], [(mixo, 3)])


def build_final(SEQ):
    Bd = Builder(SEQ, False)
    nc = Bd.nc
    x1 = Bd.dram("x1", [SEQ, 1024], F32, "ExternalInput")
    mp = Bd.dram("mprev", [1280, SEQ], BF16, "ExternalInput")
    wout_d = Bd.dram("wout", [1280, 1024], F32, "ExternalInput")
    out = Bd.dram("out", [SEQ, 1024], F32, "ExternalOutput")
    pP = [Bd.psum("pP0", [128, 512]), Bd.psum("pP1", [128, 512])]
    wo = Bd.sb("wo", [128, 10, 1024], BF16)
    wov = wout_d.rearrange("(c p) n -> p c n", p=128)
    for c in range(10):
        Bd.dma("gpsimd", wo[:, c, :], wov[:, c, :], w=[(wo, c)])
    xts = [Bd.sb("xt%d" % i, [128, 4, 1024]) for i in range(2)]
    mpvs = [Bd.sb("mpv%d" % i, [128, 10, 512], BF16) for i in range(2)]
    outs = []
    for mc in range(SEQ // 512):
        t0 = mc * 512
        xt = xts[mc % 2]
        mpv = mpvs[mc % 2]
        Bd.dma("sync", xt[:], x1[t0:t0 + 512, :].rearrange("(tt p) d -> p tt d", p=128), w=[xt])
        Bd.dma("sync", mpv[:], mp[:, t0:t0 + 512].rearrange("(c p) t -> p c t", p=128), w=[mpv])
        for tt in range(4):
            for hf in range(2):
                p = pP[hf]
                for c in range(10):
                    Bd.mm(p[:], mpv[:, c, tt * 128:(tt + 1) * 128], wo[:, c, hf * 512:(hf + 1) * 512],
                          c == 0, c == 9, [mpv, (wo, c)], [p])
                Bd.tt("vector", xt[:, tt, hf * 512:(hf + 1) * 512], xt[:, tt, hf * 512:(hf + 1) * 512],
                      p[:], ALU.add, [xt, p], [xt])
        o = Bd.dma("sync", out[t0:t0 + 512, :].rearrange("(tt p) d -> p tt d", p=128), xt[:], r=[xt], w=["out_d"])
        outs.append(o)
    Bd.S.emit(final_wait_ops=outs)
    return nc


_CACHE = {}


def _get_prog(key, fn):
    if key not in _CACHE:
        _CACHE[key] = fn()
    return _CACHE[key]


def kernel_unfused(**inputs):
    inp = {k: np.asarray(v) for k, v in inputs.items()}
    x = inp["x"]
    BATCH, SEQ, _ = x.shape
    n = 8
    cores = [(b, hh) for b in range(BATCH) for hh in range(2)]
    xcur = [np.ascontiguousarray(x[b]) for b in range(BATCH)]
    mprev = None
    for l in range(2):
        has_prev = l > 0
        nc = _get_prog(("layer", SEQ, has_prev), lambda: Builder(SEQ, has_prev).build())
        in_maps = []
        for (b, hh) in cores:
            d = host_layer_params(inp, l, hh)
            d["xin"] = xcur[b]
            d["mem"] = np.ascontiguousarray(inp["mem"][b])
            if has_prev:
                d["mprev"] = mprev[b]
                d["wout"] = np.ascontiguousarray(inp["w_out"][l - 1])
            in_maps.append(d)
        res = run_bass_kernel_spmd(nc, in_maps, core_ids=list(range(n)))
        new_m = []
        for b in range(BATCH):
            full = np.zeros((1280, SEQ), dtype=ml_dtypes.bfloat16)
            for hh in range(2):
                m = np.asarray(res.results[b * 2 + hh]["mixed"])
                for g in range(5):
                    full[g * 256 + hh * 128:g * 256 + hh * 128 + 128] = m[g * 128:(g + 1) * 128]
            new_m.append(full)
            if has_prev:
                xcur[b] = np.asarray(res.results[b * 2]["x1out"])
        mprev = new_m
    ncf = _get_prog(("final", SEQ), lambda: build_final(SEQ // 2))
    in_maps = []
    H = SEQ // 2
    for (b, hh) in cores:
        in_maps.append({"x1": np.ascontiguousarray(xcur[b][hh * H:(hh + 1) * H]),
                        "mprev": np.ascontiguousarray(mprev[b][:, hh * H:(hh + 1) * H]),
                        "wout": np.ascontiguousarray(inp["w_out"][1])})
    res = run_bass_kernel_spmd(ncf, in_maps, core_ids=list(range(n)))
    out = np.zeros((BATCH, SEQ, 1024), np.float32)
    for i, (b, hh) in enumerate(cores):
        out[b, hh * H:(hh + 1) * H] = np.asarray(res.results[i]["out"])
    return out


def kernel(**inputs):
    inp = {k: np.asarray(v) for k, v in inputs.items()}
    x = inp["x"]
    BATCH, SEQ, _ = x.shape
    nc = _get_prog(("fused", SEQ), lambda: Builder(SEQ, False).build_fused())
    per = {}
    for l in range(2):
        for hh in range(2):
            d = host_layer_params(inp, l, hh)
            for k, v in d.items():
                per["%s_%d%d" % (k, l, hh)] = v
    in_maps = []
    for b in range(BATCH):
        d = dict(per)
        d["xin"] = np.ascontiguousarray(x[b])
        d["mem"] = np.ascontiguousarray(inp["mem"][b])
        d["wout0"] = np.ascontiguousarray(inp["w_out"][0])
        d["wout1"] = np.ascontiguousarray(inp["w_out"][1])
        in_maps.append(d)
    res = run_bass_kernel_spmd(nc, in_maps, core_ids=list(range(BATCH)))
    out = np.stack([np.asarray(res.results[b]["out"]) for b in range(BATCH)], axis=0)
    return out.astype(np.float32)
```
